# Optimizing a Trainium2 kernel written in Bass

```python
import math
import jax, jax.numpy as jnp
from jax import lax
import numpy as np

D_MODEL = 1024
BATCH = 16
SEQ = 2048
DEPTH = 2

D_BRANCH = 512
D_MIX = 3 * D_BRANCH
A_HEADS = 4
A_QK_DIM = 64
A_V_DIM = 2 * A_QK_DIM
A_QK_COLS = A_HEADS * 2 * A_QK_DIM
A_V_COLS = A_HEADS * A_V_DIM
B_HEADS = 4
B_QK_DIM = 64
B_V_DIM = 128
B_QK_COLS = B_HEADS * B_QK_DIM
B_V_COLS = B_HEADS * B_V_DIM
RET_CHUNK = 128
ROPE_BASE = 10000.0
C_GROUPS = 4
C_GROUP_DIM = D_BRANCH // C_GROUPS
POOL_WINDOWS = (2, 4, 8, 16)
NUM_BUCKETS = 32
MAX_DISTANCE = 128
Q_BLOCK = 128
EPS = 1e-6
COLUMN_SIZES = (A_QK_COLS, A_QK_COLS, A_V_COLS, D_BRANCH,
                B_QK_COLS, B_QK_COLS, B_V_COLS, D_BRANCH,
                D_BRANCH, D_BRANCH)
D_IN = 2 * A_QK_COLS + A_V_COLS + 2 * B_QK_COLS + B_V_COLS + 4 * D_BRANCH

kernel_name = "hybrid_diffattn_retention_pool_encoder"


def rmsnorm(x, w):
    xf = x.astype(jnp.float32)
    y = xf * lax.rsqrt(jnp.mean(xf * xf, axis=-1, keepdims=True) + EPS)
    return (y * w.astype(jnp.float32)).astype(x.dtype)


def t5_bucket(rel):
    half = NUM_BUCKETS // 2
    max_exact = half // 2
    ret = jnp.where(rel > 0, half, 0)
    n = jnp.abs(rel)
    nf = jnp.maximum(n, 1).astype(jnp.float32)
    large = max_exact + (jnp.log(nf / max_exact) / math.log(MAX_DISTANCE / max_exact)
                         * (half - max_exact)).astype(jnp.int32)
    large = jnp.minimum(large, half - 1)
    return ret + jnp.where(n < max_exact, n, large)


def diff_attention(q, k, v, rel_bias, lam, lam_init, subln_w):
    B_, S_ = q.shape[0], q.shape[1]
    nb = S_ // Q_BLOCK
    scale = A_QK_DIM ** -0.5
    k_pos = jnp.arange(S_, dtype=jnp.int32)
    qb = q.reshape(B_, nb, Q_BLOCK, A_HEADS, 2, A_QK_DIM).transpose(1, 0, 2, 3, 4, 5)
    bias_table = rel_bias.astype(jnp.float32)

    def block(args):
        q_blk, i = args
        q_pos = i * Q_BLOCK + jnp.arange(Q_BLOCK, dtype=jnp.int32)
        bias = bias_table[t5_bucket(k_pos[None, :] - q_pos[:, None])]
        bias = bias.transpose(2, 0, 1)[None, :, None]
        logits = jnp.einsum('bqhmd,bkhmd->bhmqk', q_blk, k).astype(jnp.float32) * scale + bias
        p = jax.nn.softmax(logits, axis=-1)
        attn = p[:, :, 0] - lam * p[:, :, 1]
        return jnp.einsum('bhqk,bkhd->bqhd', attn.astype(v.dtype), v)

    o = lax.map(block, (qb, jnp.arange(nb, dtype=jnp.int32)))
    o = o.transpose(1, 0, 2, 3, 4).reshape(B_, S_, A_HEADS, A_V_DIM)
    o = rmsnorm(o, subln_w) * (1.0 - lam_init)
    return o.reshape(B_, S_, A_V_COLS)


def rotary(t):
    S_, d = t.shape[1], t.shape[-1]
    half = d // 2
    theta = 1.0 / (ROPE_BASE ** jnp.linspace(0.0, 1.0, half, dtype=jnp.float32))
    ang = jnp.arange(S_, dtype=jnp.float32)[:, None] * theta[None, :]
    cos = jnp.cos(ang)[None, :, None, :]
    sin = jnp.sin(ang)[None, :, None, :]
    t1, t2 = t[..., :half], t[..., half:]
    return jnp.concatenate([t1 * cos - t2 * sin, t1 * sin + t2 * cos], axis=-1)


def retention_dir(q, k, v, log_gamma, strict):
    B_, H, S_, dk = q.shape
    dv = v.shape[-1]
    C = RET_CHUNK
    nc = S_ // C
    idx = jnp.arange(C, dtype=jnp.float32)
    diff = idx[:, None] - idx[None, :]
    mask = (diff > 0) if strict else (diff >= 0)
    decay_intra = jnp.where(mask[None], jnp.exp(log_gamma[:, None, None] * jnp.maximum(diff, 0.0)[None]), 0.0)
    q_dec = jnp.exp(log_gamma[:, None] * (idx + 1.0)[None])[:, :, None]
    k_dec = jnp.exp(log_gamma[:, None] * (C - 1.0 - idx)[None])[:, :, None]
    chunk_dec = jnp.exp(log_gamma * C)[:, None, None]
    qc = q.reshape(B_, H, nc, C, dk).transpose(2, 0, 1, 3, 4)
    kc = k.reshape(B_, H, nc, C, dk).transpose(2, 0, 1, 3, 4)
    vc = v.reshape(B_, H, nc, C, dv).transpose(2, 0, 1, 3, 4)

    def step(R, inp):
        qi, ki, vi = inp
        inner = jnp.einsum('bhnd,bhmd->bhnm', qi, ki) * decay_intra
        out = (jnp.einsum('bhnm,bhmv->bhnv', inner, vi)
               + jnp.einsum('bhnd,bhdv->bhnv', qi * q_dec, R))
        R = R * chunk_dec + jnp.einsum('bhmd,bhmv->bhdv', ki * k_dec, vi)
        return R, out

    R0 = jnp.zeros((B_, H, dk, dv), jnp.float32)
    _, o = lax.scan(step, R0, (qc, kc, vc))
    return o.transpose(1, 2, 0, 3, 4).reshape(B_, H, S_, dv)


def retention(q, k, v, decay_logit):
    B_, S_ = q.shape[0], q.shape[1]
    qf = rotary(q.astype(jnp.float32))
    kf = rotary(k.astype(jnp.float32)) * (B_QK_DIM ** -0.5)
    qf = qf.transpose(0, 2, 1, 3)
    kf = kf.transpose(0, 2, 1, 3)
    vf = v.astype(jnp.float32).transpose(0, 2, 1, 3)
    log_gamma = jax.nn.log_sigmoid(decay_logit.astype(jnp.float32))
    fwd = retention_dir(qf, kf, vf, log_gamma[0], False)
    bwd = retention_dir(jnp.flip(qf, 2), jnp.flip(kf, 2), jnp.flip(vf, 2), log_gamma[1], True)
    o = fwd + jnp.flip(bwd, 2)
    o = o * lax.rsqrt(jnp.mean(o * o, axis=-1, keepdims=True) + EPS)
    return o.transpose(0, 2, 1, 3).reshape(B_, S_, B_V_COLS).astype(v.dtype)


def multiscale_pool(u, pool_w, pool_scale):
    B_, S_ = u.shape[0], u.shape[1]
    uf = u.astype(jnp.float32)
    cs = jnp.concatenate([jnp.zeros_like(uf[:, :1]), jnp.cumsum(uf, axis=1)], axis=1)
    pos = jnp.arange(S_, dtype=jnp.int32)
    outs = []
    for g, w in enumerate(POOL_WINDOWS):
        lo_c, hi_c = g * C_GROUP_DIM, (g + 1) * C_GROUP_DIM
        lo = jnp.clip(pos - w // 2, 0, S_)
        hi = jnp.clip(pos + (w - w // 2), 0, S_)
        window_sum = cs[:, hi, lo_c:hi_c] - cs[:, lo, lo_c:hi_c]
        count = (hi - lo).astype(jnp.float32)[None, :, None]
        outs.append(window_sum / count - uf[:, :, lo_c:hi_c])
    pooled = jnp.stack(outs, axis=2)
    y = jnp.einsum('bsgc,gcd->bsgd', pooled, pool_w.astype(jnp.float32)).reshape(B_, S_, D_BRANCH)
    return (y * pool_scale.astype(jnp.float32)).astype(u.dtype)


def hybrid_layer(x, layer_idx, norm_w, w_in, diff_lambda, diff_subln_w, ret_decay_logit,
                 pool_w, pool_scale, w_out, rel_bias):
    B_, S_ = x.shape[0], x.shape[1]
    h = rmsnorm(x, norm_w)
    proj = jnp.einsum('bsd,de->bse', h, w_in)
    splits = [int(s) for s in np.cumsum(COLUMN_SIZES)[:-1]]
    aq, ak, av, ag, bq, bk, bv, bg, cu, cg = jnp.split(proj, splits, axis=-1)

    lam_init = 0.8 - 0.6 * math.exp(-0.3 * layer_idx)
    lf = diff_lambda.astype(jnp.float32)
    lam = jnp.exp(jnp.sum(lf[0] * lf[1])) - jnp.exp(jnp.sum(lf[2] * lf[3])) + lam_init
    o_a = diff_attention(aq.reshape(B_, S_, A_HEADS, 2, A_QK_DIM),
                         ak.reshape(B_, S_, A_HEADS, 2, A_QK_DIM),
                         av.reshape(B_, S_, A_HEADS, A_V_DIM),
                         rel_bias, lam, lam_init, diff_subln_w)
    o_a = o_a * jax.nn.silu(ag)

    o_b = retention(bq.reshape(B_, S_, B_HEADS, B_QK_DIM),
                    bk.reshape(B_, S_, B_HEADS, B_QK_DIM),
                    bv.reshape(B_, S_, B_HEADS, B_V_DIM),
                    ret_decay_logit)
    o_b = o_b * jax.nn.silu(bg)

    o_c = multiscale_pool(cu, pool_w, pool_scale) * jax.nn.silu(cg)

    mixed = jnp.concatenate([o_a, o_b, o_c], axis=-1)
    return x + jnp.einsum('bse,ed->bsd', mixed, w_out)


def setup_inputs(seed: int = 0) -> dict:
    key = jax.random.key(seed)
    ks = jax.random.split(key, 11)
    x = jax.random.normal(ks[0], (BATCH, SEQ, D_MODEL), jnp.float32)
    norm_w = 1.0 + 0.05 * jax.random.normal(ks[1], (DEPTH, D_MODEL), jnp.float32)
    w_in = jax.random.normal(ks[2], (DEPTH, D_MODEL, D_IN), jnp.float32) * D_MODEL ** -0.5
    diff_lambda = 0.1 * jax.random.normal(ks[3], (DEPTH, 4, A_QK_DIM), jnp.float32)
    diff_subln_w = 1.0 + 0.05 * jax.random.normal(ks[4], (DEPTH, A_V_DIM), jnp.float32)
    e = jnp.stack([5.0 + jnp.arange(B_HEADS, dtype=jnp.float32),
                   5.5 + jnp.arange(B_HEADS, dtype=jnp.float32)])
    ret_decay_logit = (jnp.log(jnp.power(2.0, e) - 1.0)[None]
                       + 0.05 * jax.random.normal(ks[5], (DEPTH, 2, B_HEADS), jnp.float32))
    pool_w = jax.random.normal(ks[6], (DEPTH, C_GROUPS, C_GROUP_DIM, C_GROUP_DIM), jnp.float32) * C_GROUP_DIM ** -0.5
    pool_scale = 0.5 + 0.1 * jax.random.normal(ks[7], (DEPTH, D_BRANCH), jnp.float32)
    w_out = jax.random.normal(ks[8], (DEPTH, D_MIX, D_MODEL), jnp.float32) * D_MIX ** -0.5
    rel_bias = 0.1 * jax.random.normal(ks[9], (NUM_BUCKETS, A_HEADS), jnp.float32)
    final_norm_w = 1.0 + 0.05 * jax.random.normal(ks[10], (D_MODEL,), jnp.float32)
    return {'x': x, 'norm_w': norm_w, 'w_in': w_in, 'diff_lambda': diff_lambda,
            'diff_subln_w': diff_subln_w, 'ret_decay_logit': ret_decay_logit,
            'pool_w': pool_w, 'pool_scale': pool_scale, 'w_out': w_out,
            'rel_bias': rel_bias, 'final_norm_w': final_norm_w}


def reference(x, norm_w, w_in, diff_lambda, diff_subln_w, ret_decay_logit,
              pool_w, pool_scale, w_out, rel_bias, final_norm_w):
    h = x
    for l in range(DEPTH):
        h = hybrid_layer(h, l, norm_w[l], w_in[l], diff_lambda[l], diff_subln_w[l],
                         ret_decay_logit[l], pool_w[l], pool_scale[l], w_out[l], rel_bias)
    return rmsnorm(h, final_norm_w)
```

```python
import contextlib
import math
import numpy as np
import concourse.bass as bass
import concourse.mybir as mybir
from concourse.bass_utils import run_bass_kernel_spmd

F32 = mybir.dt.float32
BF16 = mybir.dt.bfloat16
ALU = mybir.AluOpType
AF = mybir.ActivationFunctionType
AX = mybir.AxisListType

D = 1024
S = 2048
NT = S // 128
DIN = 4608
DMIX = 1536
EPS = 1e-6
OFF = dict(aq=0, ak=512, av=1024, ag=1536, bq=2048, bk=2304, bv=2560, bg=3072, cu=3584, cg=4096)
MW = 1152

ENGS = ("pe", "act", "dve", "pool", "sp")
CUT = 99
SEM_LIMIT = 30000


class Op:
    __slots__ = ("eng", "fn", "deps", "inc", "is_dma", "seq", "eidx", "sem", "val", "name",
                 "closer", "dslot", "nofence")


class Prog:
    def __init__(self):
        self.ops = []
        self.last_w = {}
        self.readers = {}
        self.pend = {e: set() for e in ENGS}
        self.fence_idx = 0

    def fence(self):
        deps = set()
        last = {}
        for o in self.ops[self.fence_idx:]:
            if o.is_dma:
                if not o.nofence:
                    deps.add(o.seq)
            else:
                last[o.eng] = o.seq
        deps |= set(last.values())
        for e in ENGS:
            self.pend[e] |= deps
        self.fence_idx = len(self.ops)

    def _add(self, eng, fn, reads, writes, inc, is_dma, name):
        o = Op()
        o.eng, o.fn, o.inc, o.is_dma, o.name = eng, fn, inc, is_dma, name
        o.seq = len(self.ops)
        o.closer = o.seq
        o.nofence = False
        deps = set(self.pend[eng])
        self.pend[eng] = set()
        reads = list(reads)
        writes = list(writes)
        ex = [r for r in reads if isinstance(r, tuple) and r[0] == "ps"]
        reads = [r for r in reads if r not in ex]
        writes = writes + [r for r in ex if r not in writes]
        for r in reads:
            if r in self.last_w:
                deps.add(self.last_w[r])
        for w in writes:
            if w in self.last_w:
                deps.add(self.last_w[w])
            for rd in self.readers.get(w, ()):
                deps.add(rd)
        for r in reads:
            self.readers.setdefault(r, []).append(o.seq)
        for w in writes:
            self.last_w[w] = o.seq
            self.readers[w] = []
        deps.discard(o.seq)
        o.deps = deps
        self.ops.append(o)
        return o

    def op(self, eng, fn, reads=(), writes=(), inc=True, name=""):
        return self._add(eng, fn, reads, writes, inc, False, name)

    def dma(self, fn, reads=(), writes=(), queue="sp", name=""):
        return self._add(queue, fn, reads, writes, True, True, name)

    def emit(self, nc, st, ndma_sems=8):
        ops = self.ops
        per_eng = {e: [] for e in ENGS}
        for o in ops:
            o.eidx = len(per_eng[o.eng])
            per_eng[o.eng].append(o)
        nsem = [0]

        def newsem(tag):
            nsem[0] += 1
            return st.enter_context(nc.semaphore("%s_%d" % (tag, nsem[0])))

        dsem = {q: [newsem("d" + q) for _ in range(ndma_sems)] for q in ("sp", "pool")}
        for e in ENGS:
            cnt = 0
            cur = newsem("s" + e)
            dcnt = [0] * ndma_sems
            nd = 0
            pending = []
            for o in per_eng[e]:
                if o.is_dma:
                    j = nd % ndma_sems
                    nd += 1
                    dcnt[j] += 16
                    o.sem, o.val, o.dslot = dsem[e][j], dcnt[j], j
                elif o.inc:
                    if cnt >= SEM_LIMIT:
                        cur = newsem("s" + e)
                        cnt = 0
                    cnt += 1
                    o.sem, o.val = cur, cnt
                    for p in pending:
                        p.sem, p.val, p.closer = cur, cnt, o.seq
                    pending = []
                else:
                    pending.append(o)
            assert not pending, "trailing no-inc ops on " + e
        blk = st.enter_context(nc.Block())

        def make(e):
            def body(engine):
                waited = {}
                last_dma = {}

                def need(p):
                    if waited.get(p.sem, 0) >= p.val:
                        return
                    waited[p.sem] = p.val
                    engine.wait_ge(p.sem, p.val)

                for o in per_eng[e]:
                    for d in sorted(o.deps):
                        p = ops[d]
                        if p.eng == e and not p.is_dma and not o.is_dma:
                            if e != "pe" and o.eidx - p.eidx <= 2:
                                need(p)
                            continue
                        assert p.closer < o.seq, (p.name, o.name)
                        need(p)
                    if o.is_dma:
                        j = o.dslot
                        if j in last_dma:
                            need(last_dma[j])
                        last_dma[j] = o
                        o.fn(engine).then_inc(o.sem, 16)
                    else:
                        ins = o.fn(engine)
                        if o.inc:
                            ins.then_inc(o.sem, 1)
                for pv in last_dma.values():
                    need(pv)
            return body

        blk.tensor(make("pe"))
        blk.scalar(make("act"))
        blk.vector(make("dve"))
        blk.gpsimd(make("pool"))
        blk.sync(make("sp"))


def _t5_bucket(rel):
    half, max_exact = 16, 8
    ret = np.where(rel > 0, half, 0)
    n = np.abs(rel)
    nf = np.maximum(n, 1).astype(np.float32)
    large = max_exact + (np.log(nf / np.float32(max_exact)) / np.float32(math.log(128 / max_exact))
                         * np.float32(half - max_exact)).astype(np.int32)
    large = np.minimum(large, half - 1)
    return ret + np.where(n < max_exact, n, large)


def _pool_mats():
    pm = np.zeros((128, 20, 128), np.float32)
    for g, w in enumerate((2, 4, 8, 16)):
        for v in range(5):
            t = {0: 5, 1: 5, 2: 5, 3: 0, 4: NT - 1}[v]
            for i in range(128):
                gi = t * 128 + i
                lo = min(max(gi - w // 2, 0), S)
                hi = min(max(gi + (w - w // 2), 0), S)
                cnt = float(hi - lo)
                for gj in range(lo, hi):
                    tj, j = divmod(gj, 128)
                    rel = tj - t
                    if v in (0, 3, 4) and rel == 0:
                        pm[j, g * 5 + v, i] += 1.0 / cnt
                    elif v == 1 and rel == -1:
                        pm[j, g * 5 + v, i] += 1.0 / cnt
                    elif v == 2 and rel == 1:
                        pm[j, g * 5 + v, i] += 1.0 / cnt
                if v in (0, 3, 4):
                    pm[i, g * 5 + v, i] -= 1.0
    return pm


def _host_consts(inp):
    c = {}
    bc = lambda a: np.ascontiguousarray(np.broadcast_to(a, (128,) + a.shape)).astype(np.float32)
    c["nwb"] = np.ascontiguousarray(np.stack([bc(inp["norm_w"][0]), bc(inp["norm_w"][1]),
                                               bc(inp["final_norm_w"])], 0))
    c["ident"] = np.eye(128, dtype=np.float32)
    p = np.arange(128)[:, None]
    cc = np.arange(MW)[None, :]
    bidx = _t5_bucket(p - cc + 512)
    rb = np.asarray(inp["rel_bias"], np.float32)
    c["bmaster"] = np.ascontiguousarray(rb[bidx].transpose(0, 2, 1))
    c["cfar"] = bc(np.concatenate([rb[31], rb[15]]))
    half = 32
    theta = (1.0 / (np.float32(10000.0) ** np.linspace(0.0, 1.0, half, dtype=np.float32))).astype(np.float32)
    ang = (np.arange(S, dtype=np.float32)[:, None] * theta[None, :]).astype(np.float32)
    cs = np.stack([np.cos(ang), np.sin(ang)], 0).astype(np.float32)
    c["cs"] = np.ascontiguousarray(cs.reshape(2, NT, 128, half).transpose(2, 0, 1, 3))
    m = np.arange(128, dtype=np.float32)[:, None]
    n = np.arange(128, dtype=np.float32)[None, :]
    c["retc"] = np.ascontiguousarray(np.stack([np.maximum(n - m, 0), np.maximum(m - n, 0)], 1))
    i = np.arange(128, dtype=np.float32)
    c["tokidx"] = np.ascontiguousarray(np.stack([i + 1, 128 - i, 127 - i, i], 1))
    c["dlam"] = bc(np.asarray(inp["diff_lambda"], np.float32).reshape(2, 256))
    c["subw"] = bc(np.asarray(inp["diff_subln_w"], np.float32))
    c["rdl"] = bc(np.asarray(inp["ret_decay_logit"], np.float32).reshape(16))
    c["pscale"] = bc(np.asarray(inp["pool_scale"], np.float32))
    c["poolw"] = np.ascontiguousarray(np.asarray(inp["pool_w"], np.float32))
    c["poolm"] = _pool_mats()
    return c


def build_nc(nseq=2, nlayers=2, phases="ABC"):
    nc = bass.Bass("TRN2", target_bir_lowering=False)
    din = lambda name, shape: nc.dram_tensor(name, list(shape), F32, kind="ExternalInput").ap()
    x_d = din("x", [nseq, NT, 128, D])
    win_d = din("w_in", [2, D, DIN])
    wout_d = din("w_out", [2, DMIX, D])
    nwb_d = din("nwb", [3, 128, D])
    ident_d = din("ident", [128, 128])
    bm_d = din("bmaster", [128, 4, MW])
    cfar_d = din("cfar", [128, 8])
    cs_d = din("cs", [128, 2, NT, 32])
    retc_d = din("retc", [128, 2, 128])
    tokidx_d = din("tokidx", [128, 4])
    dlam_d = din("dlam", [128, 2, 256])
    subw_d = din("subw", [128, 2, 128])
    rdl_d = din("rdl", [128, 16])
    pscale_d = din("pscale", [128, 2, 512])
    poolw_d = din("poolw", [2, 4, 128, 128])
    poolm_d = din("poolm", [128, 20, 128])
    y_d = nc.dram_tensor("y", [nseq, NT, 128, D], F32, kind="ExternalOutput").ap()

    P = Prog()
    with contextlib.ExitStack() as st:
        def sb(name, shape, dt=F32):
            return st.enter_context(nc.sbuf_tensor("s_" + name, list(shape), dt))

        xres = sb("xres", [128, NT, D])
        hT = sb("hT", [128, 8, S], BF16)
        NSLOT = 6
        wslot = [sb("wslot%d" % i, [128, 2048], BF16) for i in range(NSLOT)]
        bmast = sb("bmast", [128, 4, MW], BF16)
        identb = sb("identb", [128, 128], BF16)
        cfar = sb("cfar", [128, 8])
        zero1 = sb("zero1", [128, 1])
        mhalf = sb("mhalf", [128, 16])
        tokidx = sb("tokidx", [128, 4])
        rdl = sb("rdl", [128, 16])
        poolw = sb("poolw", [128, 8, 128], BF16)
        poolm = sb("poolm", [128, 20, 128], BF16)
        lg = sb("lg", [128, 16])
        tqk = sb("tqk", [128, 2, 16])
        gsc = sb("gsc", [128, 16])
        D2T = sb("D2T", [128, 2, 4, 128])
        W2 = sb("W2", [128, 2, 2, 128])
        psh = sb("psh", [128, 2, 512])
        nlam = sb("nlam", [128, 2])
        sm = sb("sm", [128, 64])
        ss = sb("ss", [128, 2, NT])
        hb = [sb("hb0", [128, D], BF16)] * 2
        junk = sb("junk", [128, D], BF16)
        mixed = [sb("mixed%d" % i, [128, 256], BF16) for i in range(2)]
        mT = [sb("mT%d" % i, [128, 2, 128], BF16) for i in range(2)]
        gate = sb("gate", [128, NT, 256], BF16)
        th = [sb("th%d" % i, [128, 256]) for i in range(2)]
        ARENA = 43 * 1024 + 512
        arena = sb("arena", [128, ARENA], mybir.dt.uint8)

        def carve(off, shape, dt):
            n = int(np.prod(shape))
            bpe = 2 if dt == BF16 else 4
            ap = arena[:, off:off + n * bpe].bitcast(dt)
            if len(shape) > 1:
                names = " ".join("d%d" % i for i in range(len(shape)))
                kw = {"d%d" % i: shape[i] for i in range(1, len(shape))}
                ap = ap.rearrange("p (%s) -> p %s" % (names, names), **kw)
            return ap, off + n * bpe

        o = 0
        nw, o = carve(o, [D], F32)
        identf, o = carve(o, [128], F32)
        retc, o = carve(o, [2, 128], F32)
        dlam, o = carve(o, [2, 256], F32)
        subw, o = carve(o, [2, 128], F32)
        scr, o = carve(o, [256], F32)
        o = 0
        qT, o = carve(o, [2, S], BF16)
        kT, o = carve(o, [2, S], BF16)
        V1, o = carve(o, [NT, 2, 130], BF16)
        oa, o = carve(o, [4, 2, 128], F32)
        oan, o = carve(o, [4, 2, 128], F32)
        PT0, o = carve(o, [2, 512], BF16)
        PT1, o = carve(o, [2, 512], BF16)
        PT = [PT0, PT1]
        tmpA, o = carve(o, [128], F32)
        assert o <= ARENA, o
        o = 0
        qkr, o = carve(o, [NT, 256], BF16)
        vB, o = carve(o, [NT, 256], BF16)
        Rst, o = carve(o, [NT, 2, 128], BF16)
        cs, o = carve(o, [2, NT, 32], F32)
        Rcur, o = carve(o, [2, 128], F32)
        qk32 = []
        rtmp = []
        kdec = []
        qdec = []
        TT = []
        TTq2 = []
        innerT = []
        btmp = []
        for _i in range(2):
            a_, o = carve(o, [256], F32); qk32.append(a_)
            a_, o = carve(o, [4, 128], F32); rtmp.append(a_)
            a_, o = carve(o, [2, 2, 64], BF16); kdec.append(a_)
            a_, o = carve(o, [2, 2, 64], BF16); qdec.append(a_)
            a_, o = carve(o, [3, 128], BF16); TT.append(a_)
            a_, o = carve(o, [2, 128], BF16); TTq2.append(a_)
            a_, o = carve(o, [2, 128], BF16); innerT.append(a_)
            a_, o = carve(o, [2, 128], F32); btmp.append(a_)
        assert o <= ARENA, o
        o = 0
        uC, o = carve(o, [NT, 256], BF16)
        pooledT = []
        ytmp = []
        for _i in range(2):
            a_, o = carve(o, [2, 128], BF16); pooledT.append(a_)
            a_, o = carve(o, [256], F32); ytmp.append(a_)
        assert o <= ARENA, o

        psall = st.enter_context(nc.psum_tensor("psall", [128, 8, 512], F32))
        bank = [psall[:, i, :] for i in range(8)]

        def PS(*idx):
            return [("ps", i) for i in idx]


        dma = P.dma
        dma(lambda e: e.dma_start(out=identf, in_=ident_d), writes=["identf"])
        dma(lambda e: e.dma_start(out=cfar[:], in_=cfar_d), writes=["cfar"])
        dma(lambda e: e.dma_start(out=retc, in_=retc_d), writes=["retc"])
        dma(lambda e: e.dma_start(out=tokidx[:], in_=tokidx_d), writes=["tokidx"])
        dma(lambda e: e.dma_start(out=dlam, in_=dlam_d), writes=["dlam"])
        dma(lambda e: e.dma_start(out=subw, in_=subw_d), writes=["subw"])
        dma(lambda e: e.dma_start(out=rdl[:], in_=rdl_d), writes=["rdl"])
        dma(lambda e: e.dma_start(out=psh[:], in_=pscale_d), writes=["psh"])
        dma(lambda e: e.dma_start(out=poolw[:].rearrange("c (l g) d -> c l g d", l=2),
                                  in_=poolw_d.rearrange("l g c d -> c l g d")),
            writes=["poolw"], queue="pool")
        dma(lambda e: e.dma_start(out=poolm[:], in_=poolm_d), writes=["poolm"], queue="pool")
        dma(lambda e: e.dma_start(out=bmast[:], in_=bm_d), writes=["bmast"], queue="pool")

        P.op("pool", lambda e: e.memset(mhalf[:], -0.5), writes=["mhalf"])
        P.op("pool", lambda e: e.memset(zero1[:], 0.0), writes=["zero1"])
        P.op("dve", lambda e: e.tensor_copy(out=identb[:], in_=identf), reads=["identf"], writes=["identb"])

        lam_init = [0.8 - 0.6 * math.exp(-0.3 * l) for l in range(2)]
        for l in range(2):
            P.op("dve", lambda e, l=l: e.tensor_tensor(out=scr[:, 0:64], in0=dlam[:, l, 0:64], in1=dlam[:, l, 64:128], op=ALU.mult),
                 reads=["dlam"], writes=["junk"])
            P.op("dve", lambda e, l=l: e.tensor_tensor(out=scr[:, 64:128], in0=dlam[:, l, 128:192], in1=dlam[:, l, 192:256], op=ALU.mult),
                 reads=["dlam"], writes=["junk"])
            P.op("dve", lambda e: e.reduce_sum(out=sm[:, 0:2], in_=scr[:, 0:128].rearrange("p (a b) -> p a b", a=2), axis=AX.X),
                 reads=["junk"], writes=["sm"])
            P.op("act", lambda e: e.activation(out=sm[:, 2:4], in_=sm[:, 0:2], func=AF.Exp), reads=["sm"], writes=["sm"])
            P.op("dve", lambda e, l=l: e.tensor_scalar(out=sm[:, 4:5], in0=sm[:, 3:4], scalar1=-lam_init[l], scalar2=None, op0=ALU.add),
                 reads=["sm"], writes=["sm"])
            P.op("dve", lambda e, l=l: e.tensor_tensor(out=nlam[:, l:l + 1], in0=sm[:, 4:5], in1=sm[:, 2:3], op=ALU.subtract),
                 reads=["sm"], writes=["nlam"])
            for hh in range(2):
                P.op("dve", lambda e, l=l, hh=hh: e.tensor_scalar(out=W2[:, l, hh, :], in0=subw[:, l, :],
                                                                  scalar1=(1.0 - lam_init[l]) * 0.5, scalar2=None, op0=ALU.mult),
                     reads=["subw"], writes=["W2"])
            P.op("dve", lambda e, l=l: e.tensor_scalar(out=psh[:, l, :], in0=psh[:, l, :], scalar1=0.5, scalar2=None, op0=ALU.mult),
                 reads=["psh"], writes=["psh"])
        P.op("act", lambda e: e.activation(out=sm[:, 16:32], in_=rdl[:], func=AF.Exp, scale=-1.0), reads=["rdl"], writes=["sm"])
        P.op("dve", lambda e: e.tensor_scalar(out=sm[:, 32:48], in0=sm[:, 16:32], scalar1=1.0, scalar2=None, op0=ALU.add),
             reads=["sm"], writes=["sm"])
        P.op("act", lambda e: e.activation(out=sm[:, 16:32], in_=sm[:, 32:48], func=AF.Ln), reads=["sm"], writes=["sm"])
        P.op("dve", lambda e: e.tensor_scalar(out=lg[:], in0=sm[:, 16:32], scalar1=-1.0, scalar2=None, op0=ALU.mult),
             reads=["sm"], writes=["lg"])
        P.op("act", lambda e: e.activation(out=gsc[:], in_=lg[:], func=AF.Exp, scale=128.0), reads=["lg"], writes=["gsc"])
        for l in range(2):
            lf = l * 8
            lb = l * 8 + 4
            for (dst, src, ti) in ((0, lf, 0), (4, lb, 1), (8, lf, 2), (12, lb, 3)):
                P.op("dve", lambda e, l=l, dst=dst, src=src, ti=ti: e.tensor_scalar(
                    out=sm[:, 48 + dst:52 + dst], in0=lg[:, src:src + 4], scalar1=tokidx[:, ti:ti + 1], scalar2=None, op0=ALU.mult),
                    reads=["lg", "tokidx"], writes=["sm"])
            P.op("act", lambda e, l=l: e.activation(out=tqk[:, l, :], in_=sm[:, 48:64], func=AF.Exp), reads=["sm"], writes=["tqk"])
            for h in range(4):
                P.op("dve", lambda e, l=l, h=h: e.tensor_scalar(out=scr[:, 0:128], in0=retc[:, 0, :], scalar1=lg[:, l * 8 + h:l * 8 + h + 1],
                                                                scalar2=None, op0=ALU.mult), reads=["retc", "lg"], writes=["junk"])
                P.op("dve", lambda e, l=l, h=h: e.scalar_tensor_tensor(out=scr[:, 128:256], in0=retc[:, 1, :],
                                                                       scalar=lg[:, l * 8 + 4 + h:l * 8 + 5 + h], in1=scr[:, 0:128],
                                                                       op0=ALU.mult, op1=ALU.add), reads=["retc", "lg", "junk"], writes=["junk"])
                P.op("act", lambda e, l=l, h=h: e.activation(out=D2T[:, l, h, :], in_=scr[:, 128:256], func=AF.Exp),
                     reads=["junk"], writes=["D2T"])

        P.fence()
        wstate = {"n": 0}

        preloaded = {}
        sched = {"list": [], "i": 0}

        def phase_loads(ph, l, hp):
            if ph == "A":
                return [("in", l, OFF["aq"] + hp * 256), ("in", l, OFF["ak"] + hp * 256), ("in", l, OFF["av"] + hp * 256),
                        ("in", l, OFF["ag"] + hp * 256), ("out", l, hp * 256)]
            if ph == "B":
                return [("inqk", l, OFF["bq"] + hp * 128), ("in", l, OFF["bv"] + hp * 256), ("in", l, OFF["bg"] + hp * 256),
                        ("out", l, 512 + hp * 256)]
            return [("in", l, OFF["cu"] + hp * 256), ("in", l, OFF["cg"] + hp * 256), ("out", l, 1024 + hp * 256)]

        def hoist_next():
            i = sched["i"] + 1
            if i < len(sched["list"]):
                ph, l, hp = sched["list"][i]
                for k in phase_loads(ph, l, hp):
                    if k not in preloaded:
                        preloaded[k] = _load_w(*k)

        def load_w(kind, l, c0):
            k = (kind, l, c0)
            if k in preloaded:
                return preloaded.pop(k)
            return _load_w(kind, l, c0)

        def _load_w(kind, l, c0):
            n0 = len(P.ops)
            r = _load_w2(kind, l, c0)
            for o_ in P.ops[n0:]:
                o_.nofence = True
            return r

        def _load_w2(kind, l, c0):
            i = wstate["n"] % NSLOT
            wstate["n"] += 1
            sl = wslot[i]
            key = ("wslot", i)
            if kind == "in":
                v = sl[:].rearrange("p (k c) -> p k c", k=8)
                dma(lambda e: e.dma_start(out=v, in_=win_d[l, :, c0:c0 + 256].rearrange("(k p) c -> p k c", p=128)),
                    writes=[key], queue="pool")
            elif kind == "inqk":
                v = sl[:].rearrange("p (k c) -> p k c", k=8)
                dma(lambda e: e.dma_start(out=v[:, :, 0:128], in_=win_d[l, :, c0:c0 + 128].rearrange("(k p) c -> p k c", p=128)),
                    writes=[key], queue="pool")
                dma(lambda e: e.dma_start(out=v[:, :, 128:256], in_=win_d[l, :, c0 + 256:c0 + 384].rearrange("(k p) c -> p k c", p=128)),
                    reads=[key], writes=[key], queue="pool")
            else:
                v = sl[:].rearrange("p (k c) -> p k c", k=2)
                dma(lambda e: e.dma_start(out=v, in_=wout_d[l, c0:c0 + 256, :].rearrange("(k p) c -> p k c", p=128)),
                    writes=[key], queue="pool")
            return v, key

        cnt = {"tok": 0, "tail": 0, "ev": 0}

        def proj_tok(wv, wkey, t, bk, ncols=256):
            for k in range(8):
                P.op("pe", lambda e, k=k: e.matmul(bank[bk][:, 0:ncols], lhsT=hT[:, k, t * 128:(t + 1) * 128], rhs=wv[:, k, 0:ncols],
                                                   start=(k == 0), stop=(k == 7)),
                     reads=["hT", wkey], writes=PS(bk), inc=(k == 7))

        def gate_evac(bk, t, i):
            P.op("act", lambda e: e.activation(out=th[i][:], in_=bank[bk][:, 0:256], func=AF.Tanh, scale=0.5),
                 reads=PS(bk), writes=[("th", i)])
            P.op("dve", lambda e: e.scalar_tensor_tensor(out=gate[:, t, :], in0=th[i][:], scalar=1.0, in1=bank[bk][:, 0:256],
                                                         op0=ALU.add, op1=ALU.mult),
                 reads=PS(bk) + [("th", i)], writes=["gate"])

        def tail(mx, mxkey, t, wov, wokey, tb=None, tbk=7, ob=None):
            i = cnt["tail"] % 2
            cnt["tail"] += 1
            if tb is None:
                tb = bank[7].bitcast(BF16)[:, 0:256]
            for k in range(2):
                P.op("pe", lambda e, k=k: e.transpose(out=tb[:, k * 128:(k + 1) * 128], in_=mx[:, k * 128:(k + 1) * 128], identity=identb[:]),
                     reads=[mxkey, "identb"], writes=PS(tbk), inc=(k == 1))
            P.op("act", lambda e: e.copy(out=mT[i][:].rearrange("p a b -> p (a b)"), in_=tb), reads=PS(tbk), writes=[("mT", i)])
            if ob is None:
                for half in range(2):
                    for k in range(2):
                        P.op("pe", lambda e, k=k, half=half: e.matmul(bank[7], lhsT=mT[i][:, k, :], rhs=wov[:, k, half * 512:(half + 1) * 512],
                                                                      start=(k == 0), stop=(k == 1)),
                             reads=[("mT", i), wokey], writes=PS(7), inc=(k == 1))
                    P.op("dve", lambda e, half=half: e.tensor_tensor(out=xres[:, t, half * 512:(half + 1) * 512],
                                                                     in0=xres[:, t, half * 512:(half + 1) * 512], in1=bank[7], op=ALU.add),
                         reads=PS(7) + [("x", t)], writes=[("x", t)])
            else:
                for half in range(2):
                    for k in range(2):
                        P.op("pe", lambda e, k=k, half=half: e.matmul(bank[ob + half], lhsT=mT[i][:, k, :], rhs=wov[:, k, half * 512:(half + 1) * 512],
                                                                      start=(k == 0), stop=(k == 1)),
                             reads=[("mT", i), wokey], writes=PS(ob + half), inc=(k == 1))
                P.op("dve", lambda e: e.tensor_tensor(out=xres[:, t, :], in0=xres[:, t, :],
                                                      in1=psall[:, ob:ob + 2, :].rearrange("p a b -> p (a b)"), op=ALU.add),
                     reads=PS(ob, ob + 1) + [("x", t)], writes=[("x", t)])

        def rmsnorm_to_hT(l):
            dma(lambda e: e.dma_start(out=nw, in_=nwb_d[l]), writes=["nw"])
            for t in range(NT):
                P.op("act", lambda e, t=t: e.activation(out=junk[:], in_=xres[:, t, :], func=AF.Square, accum_out=ss[:, 0, t:t + 1]),
                     reads=[("x", t)], writes=["junk", ("ss", t)])
            P.op("dve", lambda e: e.tensor_scalar(out=ss[:, 1, :], in0=ss[:, 0, :], scalar1=1.0 / D, scalar2=EPS, op0=ALU.mult, op1=ALU.add),
                 reads=[("ss", t) for t in range(NT)], writes=["ss1"])
            P.op("pool", lambda e: e.tensor_tensor(out=ss[:, 0, :], in0=ss[:, 1, :], in1=mhalf[:, 0:NT], op=ALU.pow),
                 reads=["ss1", "mhalf"], writes=[("ss", t) for t in range(NT)])
            for t in range(NT):
                i = 0
                P.op("dve", lambda e, t=t, i=i: e.scalar_tensor_tensor(out=hb[i][:], in0=xres[:, t, :], scalar=ss[:, 0, t:t + 1], in1=nw,
                                                                       op0=ALU.mult, op1=ALU.mult),
                     reads=[("x", t), ("ss", t), "nw"], writes=[("hb", i)])
                bk = 5 + (t % 2)
                for k in range(8):
                    P.op("pe", lambda e, k=k, i=i, bk=bk: e.transpose(out=bank[bk][:].bitcast(BF16)[:, k * 128:(k + 1) * 128],
                                                                      in_=hb[i][:, k * 128:(k + 1) * 128], identity=identb[:]),
                         reads=[("hb", i), "identb"], writes=PS(bk), inc=(k == 7))
                P.op("act", lambda e, t=t, bk=bk: e.copy(out=hT[:, :, t * 128:(t + 1) * 128],
                                                         in_=bank[bk][:].bitcast(BF16).rearrange("p (k c) -> p k c", k=8)),
                     reads=PS(bk), writes=["hT"])

        def phase_A(l, hp):
            P.op("pool", lambda e: e.memset(V1[:, :, :, 128:129], 1.0), writes=["V1"])
            wq, wqk = load_w("in", l, OFF["aq"] + hp * 256)
            wk, wkk = load_w("in", l, OFF["ak"] + hp * 256)
            wv_, wvk = load_w("in", l, OFF["av"] + hp * 256)
            n = 0
            for (wv, wkey, dst, dkey, scl) in ((wq, wqk, qT, "qT", 0.125), (wk, wkk, kT, "kT", 1.0)):
                for hh in range(2):
                    for tc in range(4):
                        bk = n % 4
                        n += 1
                        for k in range(8):
                            P.op("pe", lambda e, k=k, bk=bk, wv=wv, hh=hh, tc=tc: e.matmul(
                                bank[bk][:, :], lhsT=wv[:, k, hh * 128:(hh + 1) * 128], rhs=hT[:, k, tc * 512:(tc + 1) * 512],
                                start=(k == 0), stop=(k == 7)), reads=["hT", wkey], writes=PS(bk), inc=(k == 7))
                        if n % 2 == 0:
                            P.op("act", lambda e, bk=bk, dst=dst, hh=hh, tc=tc, scl=scl: e.mul(out=dst[:, hh, tc * 512:(tc + 1) * 512],
                                                                                             in_=bank[bk][:, :], mul=scl),
                                 reads=PS(bk), writes=[dkey])
                        else:
                            P.op("dve", lambda e, bk=bk, dst=dst, hh=hh, tc=tc, scl=scl: e.tensor_scalar(
                                out=dst[:, hh, tc * 512:(tc + 1) * 512], in0=bank[bk][:, :], scalar1=scl, scalar2=None, op0=ALU.mult),
                                reads=PS(bk), writes=[dkey])
            wg, wgk = load_w("in", l, OFF["ag"] + hp * 256)
            wo, wok = load_w("out", l, hp * 256)
            for t in range(NT):
                bk = t % 4
                proj_tok(wv_, wvk, t, bk)
                eng = "act" if t % 2 == 0 else "dve"
                if eng == "act":
                    P.op("act", lambda e, t=t, bk=bk: e.copy(out=V1[:, t, :, 0:128], in_=bank[bk][:, 0:256].rearrange("p (a b) -> p a b", a=2)),
                         reads=PS(bk), writes=["V1"])
                else:
                    P.op("dve", lambda e, t=t, bk=bk: e.tensor_copy(out=V1[:, t, :, 0:128], in_=bank[bk][:, 0:256].rearrange("p (a b) -> p a b", a=2)),
                         reads=PS(bk), writes=["V1"])
            for t in range(NT):
                bk = t % 4
                proj_tok(wg, wgk, t, bk)
                gate_evac(bk, t, t % 2)

            hoist_next()

            def acc(r, c0=0, c1=129):
                return bank[4 + r // 3][:, (r % 3) * 129 + c0:(r % 3) * 129 + c1]

            pend = []

            def chunk_tail(qc):
                P.op("pool", lambda e: e.tensor_tensor(out=oan, in0=oa, in1=oa, op=ALU.mult), reads=["oa"], writes=["oan"])
                P.op("dve", lambda e: e.reduce_sum(out=sm[:, 16:24], in_=oan.rearrange("p a b c -> p (a b) c"), axis=AX.X),
                     reads=["oan"], writes=["sm"])
                P.op("dve", lambda e: e.tensor_scalar(out=sm[:, 24:32], in0=sm[:, 16:24], scalar1=1.0 / 128, scalar2=EPS, op0=ALU.mult, op1=ALU.add),
                     reads=["sm"], writes=["sm"])
                P.op("pool", lambda e: e.tensor_tensor(out=sm[:, 16:24], in0=sm[:, 24:32], in1=mhalf[:, 0:8], op=ALU.pow),
                     reads=["sm", "mhalf"], writes=["sm"])
                P.op("dve", lambda e: e.tensor_tensor(out=oan.rearrange("p a b c -> p (a b) c"), in0=oa.rearrange("p a b c -> p (a b) c"),
                                                      in1=sm[:, 16:24].unsqueeze(2).to_broadcast([128, 8, 128]), op=ALU.mult),
                     reads=["oa", "sm"], writes=["oan"])
                for qs in range(4):
                    t = qc * 4 + qs
                    i = cnt["tail"] % 2
                    P.op("pool", lambda e, qs=qs: e.tensor_tensor(out=oan[:, qs], in0=oan[:, qs], in1=W2[:, l], op=ALU.mult),
                         reads=["oan", "W2"], writes=["oan"])
                    P.op("dve", lambda e, qs=qs, t=t, i=i: e.tensor_tensor(out=mixed[i][:], in0=oan[:, qs].rearrange("p a b -> p (a b)"),
                                                                          in1=gate[:, t, :], op=ALU.mult),
                         reads=["oan", "gate"], writes=[("mixed", i)])
                    tail(mixed[i], ("mixed", i), t, wo, wok)

            for qc in range(4):
                for hh in range(2):
                    h = 2 * hp + hh
                    seq = []
                    for kt in range(NT):
                        d = kt - 4 * qc
                        near = -1 <= d <= 4
                        seq.append((kt, near, d))

                    def emit_qk(j, hh=hh, qc=qc, h=h):
                        kt, near, d = seq[j]
                        b0 = (j % 2) * 2
                        for m in range(2):
                            P.op("pe", lambda e, m=m, kt=kt, b0=b0: e.matmul(
                                bank[b0 + m][:, :], lhsT=kT[m * 64:(m + 1) * 64, hh, kt * 128:(kt + 1) * 128],
                                rhs=qT[m * 64:(m + 1) * 64, hh, qc * 512:(qc + 1) * 512], start=True, stop=(not near)),
                                reads=["kT", "qT"], writes=PS(b0 + m), inc=(m == 1 and not near))
                        if near:
                            base = 512 - 128 * d
                            for m in range(2):
                                P.op("pe", lambda e, m=m, b0=b0, base=base: e.matmul(
                                    bank[b0 + m][:, :], lhsT=identb[:], rhs=bmast[:, h, base:base + 512], start=False, stop=True),
                                    reads=["identb", "bmast"], writes=PS(b0 + m), inc=(m == 1))

                    def emit_exp_pv(j, hh=hh, qc=qc, h=h):
                        kt, near, d = seq[j]
                        b0 = (j % 2) * 2
                        pt = PT[j % 2]
                        if near:
                            bias_ap = zero1[:, 0:1]
                        elif d > 4:
                            bias_ap = cfar[:, h:h + 1]
                        else:
                            bias_ap = cfar[:, 4 + h:5 + h]
                        for m in range(2):
                            P.op("act", lambda e, m=m: e.activation(out=pt[:, m, :], in_=bank[b0 + m][:, :], func=AF.Exp, bias=bias_ap, scale=1.0),
                                 reads=PS(b0 + m) + ["cfar", "zero1"], writes=[("PT", j % 2, m)])
                        for m in range(2):
                            for qs in range(4):
                                r = m * 4 + qs
                                first = (kt == 0 and r % 3 == 0)
                                last = (kt == NT - 1 and m == 1 and qs == 3)
                                P.op("pe", lambda e, m=m, qs=qs, r=r, first=first: e.matmul(
                                    acc(r), lhsT=pt[:, m, qs * 128:(qs + 1) * 128], rhs=V1[:, kt, hh, 0:129],
                                    start=first, stop=(kt == NT - 1), skip_group_check=True),
                                    reads=[("PT", j % 2, m), "V1"], writes=PS(4 + r // 3), inc=(last or (m == 1 and qs == 3)))

                    emit_qk(0)
                    for j in range(NT):
                        if j + 1 < NT:
                            emit_qk(j + 1)
                        emit_exp_pv(j)
                        if j == 5 and pend:
                            chunk_tail(pend.pop(0))
                    for b in range(3):
                        nr = 3 if b < 2 else 2
                        P.op("dve", lambda e, b=b, nr=nr: e.reciprocal(
                            out=sm[:, 3 * b:3 * b + nr], in_=bank[4 + b][:, 0:nr * 129].rearrange("p (r c) -> p r c", c=129)[:, :, 128]),
                            reads=PS(4 + b), writes=["sm"])
                    P.op("dve", lambda e: e.tensor_scalar(out=sm[:, 4:8], in0=sm[:, 4:8], scalar1=nlam[:, l:l + 1], scalar2=None, op0=ALU.mult),
                         reads=["sm", "nlam"], writes=["sm"])
                    for qs in range(4):
                        r1 = 4 + qs
                        P.op("act", lambda e, qs=qs, r1=r1: e.mul(out=tmpA, in_=acc(r1, 0, 128), mul=sm[:, r1:r1 + 1]),
                             reads=PS(4 + r1 // 3) + ["sm"], writes=["tmpA"])
                        P.op("dve", lambda e, qs=qs, hh=hh: e.scalar_tensor_tensor(out=oa[:, qs, hh, :], in0=acc(qs, 0, 128), scalar=sm[:, qs:qs + 1],
                                                                                  in1=tmpA, op0=ALU.mult, op1=ALU.add),
                             reads=PS(4 + qs // 3) + ["sm", "tmpA"], writes=["oa"])
                pend.append(qc)
            while pend:
                chunk_tail(pend.pop(0))

        def phase_B(l, hp):
            wqk_, wqkk = load_w("inqk", l, OFF["bq"] + hp * 128)
            wv_, wvk = load_w("in", l, OFF["bv"] + hp * 256)
            wg, wgk = load_w("in", l, OFF["bg"] + hp * 256)
            wo, wok = load_w("out", l, 512 + hp * 256)
            dma(lambda e: e.dma_start(out=cs, in_=cs_d), writes=["cs"])
            for s2 in range(2):
                P.op("pool", lambda e, s2=s2: e.memset(TTq2[s2], 0.0), writes=[("TTq2", s2)])
            for t in range(NT):
                bk = t % 4
                i2 = t % 2
                proj_tok(wqk_, wqkk, t, bk)
                P.op("act", lambda e, bk=bk, i2=i2: e.copy(out=qk32[i2][:, 0:128], in_=bank[bk][:, 0:128]), reads=PS(bk), writes=[("qk32", i2)])
                P.op("act", lambda e, bk=bk, i2=i2: e.mul(out=qk32[i2][:, 128:256], in_=bank[bk][:, 128:256], mul=0.125),
                     reads=PS(bk), writes=[("qk32", i2)])
                src4 = qk32[i2].rearrange("p (a b c) -> p a b c", a=4, b=2)
                cos_b = cs[:, 0, t, :].unsqueeze(1).to_broadcast([128, 4, 32])
                sin_b = cs[:, 1, t, :].unsqueeze(1).to_broadcast([128, 4, 32])
                t1 = src4[:, :, 0, :]
                t2 = src4[:, :, 1, :]
                rt = [rtmp[i2][:, i].rearrange("p (a c) -> p a c", a=4) for i in range(4)]
                dst4 = qkr[:, t, :].rearrange("p (a b c) -> p a b c", a=4, b=2)
                P.op("pool", lambda e, t1=t1, cos_b=cos_b, rt=rt: e.tensor_tensor(out=rt[0], in0=t1, in1=cos_b, op=ALU.mult),
                     reads=[("qk32", i2), "cs"], writes=[("rtmp", i2, 0)])
                P.op("pool", lambda e, t2=t2, sin_b=sin_b, rt=rt: e.tensor_tensor(out=rt[1], in0=t2, in1=sin_b, op=ALU.mult),
                     reads=[("qk32", i2), "cs"], writes=[("rtmp", i2, 1)])
                P.op("dve", lambda e, t1=t1, sin_b=sin_b, rt=rt: e.tensor_tensor(out=rt[2], in0=t1, in1=sin_b, op=ALU.mult),
                     reads=[("qk32", i2), "cs"], writes=[("rtmp", i2, 2)])
                P.op("dve", lambda e, t2=t2, cos_b=cos_b, rt=rt: e.tensor_tensor(out=rt[3], in0=t2, in1=cos_b, op=ALU.mult),
                     reads=[("qk32", i2), "cs"], writes=[("rtmp", i2, 3)])
                P.op("pool", lambda e, rt=rt, dst4=dst4: e.tensor_tensor(out=dst4[:, :, 0, :], in0=rt[0], in1=rt[1], op=ALU.subtract),
                     reads=[("rtmp", i2, 0), ("rtmp", i2, 1)], writes=[("qkr", t, 0)])
                P.op("dve", lambda e, rt=rt, dst4=dst4: e.tensor_tensor(out=dst4[:, :, 1, :], in0=rt[2], in1=rt[3], op=ALU.add),
                     reads=[("rtmp", i2, 2), ("rtmp", i2, 3)], writes=[("qkr", t, 1)])
            for t in range(NT):
                bk = t % 4
                proj_tok(wv_, wvk, t, bk)
                if t % 2 == 0:
                    P.op("act", lambda e, t=t, bk=bk: e.copy(out=vB[:, t, :], in_=bank[bk][:, 0:256]), reads=PS(bk), writes=[("vB", t)])
                else:
                    P.op("dve", lambda e, t=t, bk=bk: e.tensor_copy(out=vB[:, t, :], in_=bank[bk][:, 0:256]), reads=PS(bk), writes=[("vB", t)])
            for t in range(NT):
                bk = t % 4
                proj_tok(wg, wgk, t, bk)
                gate_evac(bk, t, t % 2)
            hoist_next()
            QKR = lambda t: [("qkr", t, 0), ("qkr", t, 1)]

            def kvp(t):
                return bank[t // 2][:, (t % 2) * 256:(t % 2) * 256 + 256]

            for t in range(NT):
                i2 = t % 2
                kq = qkr[:, t, 128:256].rearrange("p (a c) -> p a c", a=2)
                for dr in range(2):
                    c0 = 8 + dr * 4 + 2 * hp
                    P.op("pool", lambda e, dr=dr, c0=c0, kq=kq, i2=i2: e.tensor_tensor(
                        out=kdec[i2][:, :, dr, :], in0=kq, in1=tqk[:, l, c0:c0 + 2].unsqueeze(2).to_broadcast([128, 2, 64]), op=ALU.mult),
                        reads=QKR(t) + ["tqk"], writes=[("kdec", i2)])
                for hh in range(2):
                    P.op("pe", lambda e, hh=hh, t=t, i2=i2: e.matmul(kvp(t)[:, hh * 128:(hh + 1) * 128], lhsT=kdec[i2][:, hh].rearrange("p a b -> p (a b)"),
                                                                    rhs=vB[:, t, hh * 128:(hh + 1) * 128], start=True, stop=True),
                         reads=[("kdec", i2), ("vB", t)], writes=PS(t // 2), inc=(hh == 1))

            P.op("pool", lambda e: e.memset(Rcur, 0.0), writes=[("Rcur", 0), ("Rcur", 64)])
            for n_ in range(NT):
                for (lo, hi, t, dr) in ((0, 64, n_, 0), (64, 128, NT - 1 - n_, 1)):
                    P.op("act", lambda e, t=t, lo=lo, hi=hi: e.copy(out=Rst[lo:hi, t], in_=Rcur[lo:hi]), reads=[("Rcur", lo)], writes=[("Rst", t, lo)])
                    if n_ == NT - 1:
                        continue
                    for hh in range(2):
                        gc = l * 8 + dr * 4 + 2 * hp + hh
                        P.op("dve", lambda e, lo=lo, hi=hi, t=t, hh=hh, gc=gc: e.scalar_tensor_tensor(
                            out=Rcur[lo:hi, hh, :], in0=Rcur[lo:hi, hh, :], scalar=gsc[lo:hi, gc:gc + 1],
                            in1=kvp(t)[lo:hi, hh * 128:(hh + 1) * 128], op0=ALU.mult, op1=ALU.add),
                            reads=PS(t // 2) + [("Rcur", lo), "gsc"], writes=[("Rcur", lo)])

            for t in range(NT):
                s2 = t % 2
                bTS = 2 * s2
                bO = 2 * s2 + 1
                qq = qkr[:, t, 0:128].rearrange("p (a c) -> p a c", a=2)
                for dr in range(2):
                    c0 = dr * 4 + 2 * hp
                    P.op("pool", lambda e, dr=dr, c0=c0, qq=qq, s2=s2: e.tensor_tensor(
                        out=qdec[s2][:, :, dr, :], in0=qq, in1=tqk[:, l, c0:c0 + 2].unsqueeze(2).to_broadcast([128, 2, 64]), op=ALU.mult),
                        reads=QKR(t) + ["tqk"], writes=[("qdec", s2)])
                bT = bank[bTS].bitcast(BF16)
                srcs = [qdec[s2][:, 0].rearrange("p a b -> p (a b)"), qdec[s2][:, 1].rearrange("p a b -> p (a b)"),
                        qkr[:, t, 128:256], qkr[:, t, 0:128]]
                for i4 in range(4):
                    P.op("pe", lambda e, i4=i4, srcs=srcs, bT=bT: e.transpose(out=bT[:, i4 * 128:(i4 + 1) * 128], in_=srcs[i4], identity=identb[:]),
                         reads=[("qdec", s2), "identb"] + QKR(t), writes=PS(bTS), inc=(i4 == 3))
                P.op("act", lambda e, bT=bT, s2=s2: e.copy(out=TT[s2].rearrange("p a b -> p (a b)"), in_=bT[:, 0:384]),
                     reads=PS(bTS), writes=[("TT", s2)])
                for hh in range(2):
                    P.op("act", lambda e, bT=bT, s2=s2, hh=hh: e.copy(out=TTq2[s2][hh * 64:(hh + 1) * 64, hh, :],
                                                                       in_=bT[hh * 64:(hh + 1) * 64, 384:512]),
                         reads=PS(bTS), writes=[("TTq2", s2)])
                P.op("pe", lambda e, s2=s2, bTS=bTS: e.matmul(bank[bTS][:, 256:512], lhsT=TT[s2][:, 2, :],
                                                             rhs=TTq2[s2].rearrange("p a b -> p (a b)"), start=True, stop=True),
                     reads=[("TT", s2), ("TTq2", s2)], writes=PS(bTS), inc=True)
                P.op("dve", lambda e, s2=s2, bTS=bTS: e.tensor_tensor(out=innerT[s2], in0=bank[bTS][:, 256:512].rearrange("p (a b) -> p a b", a=2),
                                                                     in1=D2T[:, l, 2 * hp:2 * hp + 2, :], op=ALU.mult),
                     reads=PS(bTS) + ["D2T"], writes=[("innerT", s2)])
                for hh in range(2):
                    P.op("pe", lambda e, hh=hh, t=t, s2=s2, bO=bO: e.matmul(bank[bO][:, hh * 128:(hh + 1) * 128], lhsT=innerT[s2][:, hh, :],
                                                                           rhs=vB[:, t, hh * 128:(hh + 1) * 128], start=True, stop=False),
                         reads=[("innerT", s2), ("vB", t)], writes=PS(bO), inc=False)
                    P.op("pe", lambda e, hh=hh, t=t, s2=s2, bO=bO: e.matmul(bank[bO][:, hh * 128:(hh + 1) * 128], lhsT=TT[s2][:, hh, :],
                                                                           rhs=Rst[:, t, hh, :], start=False, stop=True),
                         reads=[("TT", s2), ("Rst", t, 0), ("Rst", t, 64)], writes=PS(bO), inc=(hh == 1))
                sc = 32 + 8 * s2
                for hh in range(2):
                    P.op("act", lambda e, hh=hh, bO=bO, sc=sc: e.activation(out=junk[:, hh * 128:(hh + 1) * 128], in_=bank[bO][:, hh * 128:(hh + 1) * 128],
                                                                          func=AF.Square, accum_out=sm[:, sc + hh:sc + hh + 1]),
                         reads=PS(bO), writes=["junk", ("smB", s2)])
                P.op("dve", lambda e, sc=sc: e.tensor_scalar(out=sm[:, sc + 2:sc + 4], in0=sm[:, sc:sc + 2], scalar1=4.0 / 128, scalar2=4.0 * EPS,
                                                             op0=ALU.mult, op1=ALU.add), reads=[("smB", s2)], writes=[("smB", s2)])
                P.op("pool", lambda e, sc=sc: e.tensor_tensor(out=sm[:, sc + 4:sc + 6], in0=sm[:, sc + 2:sc + 4], in1=mhalf[:, 0:2], op=ALU.pow),
                     reads=[("smB", s2), "mhalf"], writes=[("smB", s2)])
                i = cnt["tail"] % 2
                P.op("dve", lambda e, s2=s2, bO=bO, sc=sc: e.tensor_tensor(out=btmp[s2], in0=bank[bO][:, 0:256].rearrange("p (a b) -> p a b", a=2),
                                                                          in1=sm[:, sc + 4:sc + 6].unsqueeze(2).to_broadcast([128, 2, 128]), op=ALU.mult),
                     reads=PS(bO) + [("smB", s2)], writes=[("btmp", s2)])
                P.op("pool", lambda e, i=i, t=t, s2=s2: e.tensor_tensor(out=mixed[i][:], in0=btmp[s2].rearrange("p a b -> p (a b)"), in1=gate[:, t, :], op=ALU.mult),
                     reads=[("btmp", s2), "gate"], writes=[("mixed", i)])
                tail(mixed[i], ("mixed", i), t, wo, wok, tb=bank[bO].bitcast(BF16)[:, 512:768], tbk=bO, ob=4 + 2 * s2)

        def phase_C(l, gp):
            wu, wuk = load_w("in", l, OFF["cu"] + gp * 256)
            wg, wgk = load_w("in", l, OFF["cg"] + gp * 256)
            wo, wok = load_w("out", l, 1024 + gp * 256)
            for t in range(NT):
                bk = t % 4
                proj_tok(wu, wuk, t, bk)
                if t % 2 == 0:
                    P.op("act", lambda e, t=t, bk=bk: e.copy(out=uC[:, t, :], in_=bank[bk][:, 0:256]), reads=PS(bk), writes=[("uC", t)])
                else:
                    P.op("dve", lambda e, t=t, bk=bk: e.tensor_copy(out=uC[:, t, :], in_=bank[bk][:, 0:256]), reads=PS(bk), writes=[("uC", t)])
            for t in range(NT):
                bk = t % 4
                proj_tok(wg, wgk, t, bk)
                gate_evac(bk, t, t % 2)
            hoist_next()
            for t in range(NT):
                s2 = t % 2
                bP = 2 * s2
                bY = 2 * s2 + 1
                for gg in range(2):
                    g = 2 * gp + gg
                    parts = [(t, g * 5 + (3 if t == 0 else 4 if t == NT - 1 else 0))]
                    if t > 0:
                        parts.append((t - 1, g * 5 + 1))
                    if t < NT - 1:
                        parts.append((t + 1, g * 5 + 2))
                    for pi_, (tj, mi) in enumerate(parts):
                        P.op("pe", lambda e, gg=gg, tj=tj, mi=mi, pi_=pi_, np_=len(parts), bP=bP: e.matmul(
                            bank[bP][:, gg * 128:(gg + 1) * 128], lhsT=uC[:, tj, gg * 128:(gg + 1) * 128], rhs=poolm[:, mi, :],
                            start=(pi_ == 0), stop=(pi_ == np_ - 1)),
                            reads=[("uC", tj), "poolm"], writes=PS(bP), inc=(gg == 1 and pi_ == len(parts) - 1))
                P.op("act", lambda e, bP=bP, s2=s2: e.copy(out=pooledT[s2].rearrange("p a b -> p (a b)"), in_=bank[bP][:, 0:256]),
                     reads=PS(bP), writes=[("pooledT", s2)])
                for gg in range(2):
                    g = 2 * gp + gg
                    P.op("pe", lambda e, gg=gg, g=g, s2=s2, bY=bY: e.matmul(bank[bY][:, gg * 128:(gg + 1) * 128], lhsT=pooledT[s2][:, gg, :],
                                                                           rhs=poolw[:, l * 4 + g, :], start=True, stop=True),
                         reads=[("pooledT", s2), "poolw"], writes=PS(bY), inc=(gg == 1))
                i = cnt["tail"] % 2
                P.op("dve", lambda e, s2=s2, bY=bY: e.tensor_tensor(out=ytmp[s2], in0=bank[bY][:, 0:256], in1=psh[:, l, gp * 256:(gp + 1) * 256], op=ALU.mult),
                     reads=PS(bY) + ["psh"], writes=[("ytmp", s2)])
                P.op("pool", lambda e, t=t, i=i, s2=s2: e.tensor_tensor(out=mixed[i][:], in0=ytmp[s2], in1=gate[:, t, :], op=ALU.mult),
                     reads=[("ytmp", s2), "gate"], writes=[("mixed", i)])
                tail(mixed[i], ("mixed", i), t, wo, wok, tb=bank[bY].bitcast(BF16)[:, 512:768], tbk=bY, ob=4 + 2 * s2)

        for s_ in range(nseq):
            for l in range(nlayers):
                for ph in phases:
                    for hp in range(2):
                        sched["list"].append((ph, l, hp))
        sched["i"] = -1
        hoist_next()
        fns = {"A": phase_A, "B": phase_B, "C": phase_C}
        pi = 0
        for s in range(nseq):
            for t in range(NT):
                dma(lambda e, s=s, t=t: e.dma_start(out=xres[:, t, :], in_=x_d[s, t]), writes=[("x", t)])
            for l in range(nlayers):
                P.fence()
                rmsnorm_to_hT(l)
                for ph in phases:
                    for hp in range(2):
                        if hp == 0:
                            P.fence()
                        sched["i"] = pi
                        fns[ph](l, hp)
                        pi += 1
            P.fence()
            dma(lambda e: e.dma_start(out=nw, in_=nwb_d[2]), writes=["nw"])
            for t in range(NT):
                P.op("act", lambda e, t=t: e.activation(out=junk[:], in_=xres[:, t, :], func=AF.Square, accum_out=ss[:, 0, t:t + 1]),
                     reads=[("x", t)], writes=["junk", ("ss", t)])
            P.op("dve", lambda e: e.tensor_scalar(out=ss[:, 1, :], in0=ss[:, 0, :], scalar1=1.0 / D, scalar2=EPS, op0=ALU.mult, op1=ALU.add),
                 reads=[("ss", t) for t in range(NT)], writes=["ss1"])
            P.op("pool", lambda e: e.tensor_tensor(out=ss[:, 0, :], in0=ss[:, 1, :], in1=mhalf[:, 0:NT], op=ALU.pow),
                 reads=["ss1", "mhalf"], writes=[("ss", t) for t in range(NT)])
            for t in range(NT):
                P.op("dve", lambda e, t=t: e.scalar_tensor_tensor(out=xres[:, t, :], in0=xres[:, t, :], scalar=ss[:, 0, t:t + 1], in1=nw,
                                                                  op0=ALU.mult, op1=ALU.mult),
                     reads=[("x", t), ("ss", t), "nw"], writes=[("x", t)])
                dma(lambda e, s=s, t=t: e.dma_start(out=y_d[s, t], in_=xres[:, t, :]), reads=[("x", t)], writes=[("y", s, t)])
        P.emit(nc, st)
    return nc


_NC_CACHE = {}


def kernel(**inputs):
    x = np.asarray(inputs["x"], np.float32)
    B = x.shape[0]
    ncores = 8
    nseq = B // ncores
    consts = _host_consts(inputs)
    key = (nseq,)
    if key not in _NC_CACHE:
        _NC_CACHE[key] = build_nc(nseq=nseq)
    nc = _NC_CACHE[key]
    w_in = np.ascontiguousarray(np.asarray(inputs["w_in"], np.float32))
    w_out = np.ascontiguousarray(np.asarray(inputs["w_out"], np.float32))
    in_maps = []
    for c in range(ncores):
        m = dict(consts)
        m["x"] = np.ascontiguousarray(x[c * nseq:(c + 1) * nseq].reshape(nseq, NT, 128, D))
        m["w_in"] = w_in
        m["w_out"] = w_out
        in_maps.append(m)
    res = run_bass_kernel_spmd(nc, in_maps, core_ids=list(range(ncores)))
    out = np.concatenate([np.asarray(r["y"]).reshape(nseq, S, D) for r in res.results], axis=0)
    return out.astype(np.float32)
```

```python
import contextlib
import math
import numpy as np
import concourse.bass as bass
import concourse.mybir as mybir
from concourse.bass_utils import run_bass_kernel_spmd

F32 = mybir.dt.float32
BF16 = mybir.dt.bfloat16
ALU = mybir.AluOpType
AF = mybir.ActivationFunctionType
AX = mybir.AxisListType

D = 1024
S = 2048
NT = S // 128
DIN = 4608
DMIX = 1536
EPS = 1e-6
OFF = dict(aq=0, ak=512, av=1024, ag=1536, bq=2048, bk=2304, bv=2560, bg=3072, cu=3584, cg=4096)
MW = 1152

ENGS = ("pe", "act", "dve", "pool", "sp")
CUT = 99
SEM_LIMIT = 30000


class Op:
    __slots__ = ("eng", "fn", "deps", "inc", "is_dma", "seq", "eidx", "sem", "val", "name",
                 "closer", "dslot", "nofence")


class Prog:
    def __init__(self):
        self.ops = []
        self.last_w = {}
        self.readers = {}
        self.pend = {e: set() for e in ENGS}
        self.fence_idx = 0

    def fence(self):
        deps = set()
        last = {}
        for o in self.ops[self.fence_idx:]:
            if o.is_dma:
                if not o.nofence:
                    deps.add(o.seq)
            else:
                last[o.eng] = o.seq
        deps |= set(last.values())
        for e in ENGS:
            self.pend[e] |= deps
        self.fence_idx = len(self.ops)

    def _add(self, eng, fn, reads, writes, inc, is_dma, name):
        o = Op()
        o.eng, o.fn, o.inc, o.is_dma, o.name = eng, fn, inc, is_dma, name
        o.seq = len(self.ops)
        o.closer = o.seq
        o.nofence = False
        deps = set(self.pend[eng])
        self.pend[eng] = set()
        reads = list(reads)
        writes = list(writes)
        ex = [r for r in reads if isinstance(r, tuple) and r[0] == "ps"]
        reads = [r for r in reads if r not in ex]
        writes = writes + [r for r in ex if r not in writes]
        for r in reads:
            if r in self.last_w:
                deps.add(self.last_w[r])
        for w in writes:
            if w in self.last_w:
                deps.add(self.last_w[w])
            for rd in self.readers.get(w, ()):
                deps.add(rd)
        for r in reads:
            self.readers.setdefault(r, []).append(o.seq)
        for w in writes:
            self.last_w[w] = o.seq
            self.readers[w] = []
        deps.discard(o.seq)
        o.deps = deps
        self.ops.append(o)
        return o

    def op(self, eng, fn, reads=(), writes=(), inc=True, name=""):
        return self._add(eng, fn, reads, writes, inc, False, name)

    def dma(self, fn, reads=(), writes=(), queue="sp", name=""):
        return self._add(queue, fn, reads, writes, True, True, name)

    def emit(self, nc, st, ndma_sems=8):
        ops = self.ops
        per_eng = {e: [] for e in ENGS}
        for o in ops:
            o.eidx = len(per_eng[o.eng])
            per_eng[o.eng].append(o)
        nsem = [0]

        def newsem(tag):
            nsem[0] += 1
            return st.enter_context(nc.semaphore("%s_%d" % (tag, nsem[0])))

        dsem = {q: [newsem("d" + q) for _ in range(ndma_sems)] for q in ("sp", "pool")}
        for e in ENGS:
            cnt = 0
            cur = newsem("s" + e)
            dcnt = [0] * ndma_sems
            nd = 0
            pending = []
            for o in per_eng[e]:
                if o.is_dma:
                    j = nd % ndma_sems
                    nd += 1
                    dcnt[j] += 16
                    o.sem, o.val, o.dslot = dsem[e][j], dcnt[j], j
                elif o.inc:
                    if cnt >= SEM_LIMIT:
                        cur = newsem("s" + e)
                        cnt = 0
                    cnt += 1
                    o.sem, o.val = cur, cnt
                    for p in pending:
                        p.sem, p.val, p.closer = cur, cnt, o.seq
                    pending = []
                else:
                    pending.append(o)
            assert not pending, "trailing no-inc ops on " + e
        blk = st.enter_context(nc.Block())

        def make(e):
            def body(engine):
                waited = {}
                last_dma = {}

                def need(p):
                    if waited.get(p.sem, 0) >= p.val:
                        return
                    waited[p.sem] = p.val
                    engine.wait_ge(p.sem, p.val)

                for o in per_eng[e]:
                    for d in sorted(o.deps):
                        p = ops[d]
                        if p.eng == e and not p.is_dma and not o.is_dma:
                            if e != "pe" and o.eidx - p.eidx <= 2:
                                need(p)
                            continue
                        assert p.closer < o.seq, (p.name, o.name)
                        need(p)
                    if o.is_dma:
                        j = o.dslot
                        if j in last_dma:
                            need(last_dma[j])
                        last_dma[j] = o
                        o.fn(engine).then_inc(o.sem, 16)
                    else:
                        ins = o.fn(engine)
                        if o.inc:
                            ins.then_inc(o.sem, 1)
                for pv in last_dma.values():
                    need(pv)
            return body

        blk.tensor(make("pe"))
        blk.scalar(make("act"))
        blk.vector(make("dve"))
        blk.gpsimd(make("pool"))
        blk.sync(make("sp"))


def _t5_bucket(rel):
    half, max_exact = 16, 8
    ret = np.where(rel > 0, half, 0)
    n = np.abs(rel)
    nf = np.maximum(n, 1).astype(np.float32)
    large = max_exact + (np.log(nf / np.float32(max_exact)) / np.float32(math.log(128 / max_exact))
                         * np.float32(half - max_exact)).astype(np.int32)
    large = np.minimum(large, half - 1)
    return ret + np.where(n < max_exact, n, large)


def _pool_mats():
    pm = np.zeros((128, 20, 128), np.float32)
    for g, w in enumerate((2, 4, 8, 16)):
        for v in range(5):
            t = {0: 5, 1: 5, 2: 5, 3: 0, 4: NT - 1}[v]
            for i in range(128):
                gi = t * 128 + i
                lo = min(max(gi - w // 2, 0), S)
                hi = min(max(gi + (w - w // 2), 0), S)
                cnt = float(hi - lo)
                for gj in range(lo, hi):
                    tj, j = divmod(gj, 128)
                    rel = tj - t
                    if v in (0, 3, 4) and rel == 0:
                        pm[j, g * 5 + v, i] += 1.0 / cnt
                    elif v == 1 and rel == -1:
                        pm[j, g * 5 + v, i] += 1.0 / cnt
                    elif v == 2 and rel == 1:
                        pm[j, g * 5 + v, i] += 1.0 / cnt
                if v in (0, 3, 4):
                    pm[i, g * 5 + v, i] -= 1.0
    return pm


def _host_consts(inp):
    c = {}
    bc = lambda a: np.ascontiguousarray(np.broadcast_to(a, (128,) + a.shape)).astype(np.float32)
    c["nwb"] = np.ascontiguousarray(np.stack([bc(inp["norm_w"][0]), bc(inp["norm_w"][1]),
                                               bc(inp["final_norm_w"])], 0))
    c["ident"] = np.eye(128, dtype=np.float32)
    p = np.arange(128)[:, None]
    cc = np.arange(MW)[None, :]
    bidx = _t5_bucket(p - cc + 512)
    rb = np.asarray(inp["rel_bias"], np.float32)
    c["bmaster"] = np.ascontiguousarray(rb[bidx].transpose(0, 2, 1))
    c["cfar"] = bc(np.concatenate([rb[31], rb[15]]))
    half = 32
    theta = (1.0 / (np.float32(10000.0) ** np.linspace(0.0, 1.0, half, dtype=np.float32))).astype(np.float32)
    ang = (np.arange(S, dtype=np.float32)[:, None] * theta[None, :]).astype(np.float32)
    cs = np.stack([np.cos(ang), np.sin(ang)], 0).astype(np.float32)
    c["cs"] = np.ascontiguousarray(cs.reshape(2, NT, 128, half).transpose(2, 0, 1, 3))
    m = np.arange(128, dtype=np.float32)[:, None]
    n = np.arange(128, dtype=np.float32)[None, :]
    c["retc"] = np.ascontiguousarray(np.stack([np.maximum(n - m, 0), np.maximum(m - n, 0)], 1))
    i = np.arange(128, dtype=np.float32)
    c["tokidx"] = np.ascontiguousarray(np.stack([i + 1, 128 - i, 127 - i, i], 1))
    c["dlam"] = bc(np.asarray(inp["diff_lambda"], np.float32).reshape(2, 256))
    c["subw"] = bc(np.asarray(inp["diff_subln_w"], np.float32))
    c["rdl"] = bc(np.asarray(inp["ret_decay_logit"], np.float32).reshape(16))
    c["pscale"] = bc(np.asarray(inp["pool_scale"], np.float32))
    c["poolw"] = np.ascontiguousarray(np.asarray(inp["pool_w"], np.float32))
    c["poolm"] = _pool_mats()
    return c


def build_nc(nseq=2, nlayers=2, phases="ABC"):
    nc = bass.Bass("TRN2", target_bir_lowering=False)
    din = lambda name, shape: nc.dram_tensor(name, list(shape), F32, kind="ExternalInput").ap()
    x_d = din("x", [nseq, NT, 128, D])
    win_d = din("w_in", [2, D, DIN])
    wout_d = din("w_out", [2, DMIX, D])
    nwb_d = din("nwb", [3, 128, D])
    ident_d = din("ident", [128, 128])
    bm_d = din("bmaster", [128, 4, MW])
    cfar_d = din("cfar", [128, 8])
    cs_d = din("cs", [128, 2, NT, 32])
    retc_d = din("retc", [128, 2, 128])
    tokidx_d = din("tokidx", [128, 4])
    dlam_d = din("dlam", [128, 2, 256])
    subw_d = din("subw", [128, 2, 128])
    rdl_d = din("rdl", [128, 16])
    pscale_d = din("pscale", [128, 2, 512])
    poolw_d = din("poolw", [2, 4, 128, 128])
    poolm_d = din("poolm", [128, 20, 128])
    y_d = nc.dram_tensor("y", [nseq, NT, 128, D], F32, kind="ExternalOutput").ap()

    P = Prog()
    with contextlib.ExitStack() as st:
        def sb(name, shape, dt=F32):
            return st.enter_context(nc.sbuf_tensor("s_" + name, list(shape), dt))

        xres = sb("xres", [128, NT, D])
        hT = sb("hT", [128, 8, S], BF16)
        NSLOT = 6
        wslot = [sb("wslot%d" % i, [128, 2048], BF16) for i in range(NSLOT)]
        bmast = sb("bmast", [128, 4, MW], BF16)
        identb = sb("identb", [128, 128], BF16)
        cfar = sb("cfar", [128, 8])
        zero1 = sb("zero1", [128, 1])
        mhalf = sb("mhalf", [128, 16])
        tokidx = sb("tokidx", [128, 4])
        rdl = sb("rdl", [128, 16])
        poolw = sb("poolw", [128, 8, 128], BF16)
        poolm = sb("poolm", [128, 20, 128], BF16)
        lg = sb("lg", [128, 16])
        tqk = sb("tqk", [128, 2, 16])
        gsc = sb("gsc", [128, 16])
        D2T = sb("D2T", [128, 2, 4, 128])
        W2 = sb("W2", [128, 2, 2, 128])
        psh = sb("psh", [128, 2, 512])
        nlam = sb("nlam", [128, 2])
        sm = sb("sm", [128, 64])
        ss = sb("ss", [128, 2, NT])
        hb = [sb("hb0", [128, D], BF16)] * 2
        junk = sb("junk", [128, D], BF16)
        mixed = [sb("mixed%d" % i, [128, 256], BF16) for i in range(2)]
        mT = [sb("mT%d" % i, [128, 2, 128], BF16) for i in range(2)]
        gate = sb("gate", [128, NT, 256], BF16)
        th = [sb("th%d" % i, [128, 256]) for i in range(2)]
        ARENA = 43 * 1024 + 512
        arena = sb("arena", [128, ARENA], mybir.dt.uint8)

        def carve(off, shape, dt):
            n = int(np.prod(shape))
            bpe = 2 if dt == BF16 else 4
            ap = arena[:, off:off + n * bpe].bitcast(dt)
            if len(shape) > 1:
                names = " ".join("d%d" % i for i in range(len(shape)))
                kw = {"d%d" % i: shape[i] for i in range(1, len(shape))}
                ap = ap.rearrange("p (%s) -> p %s" % (names, names), **kw)
            return ap, off + n * bpe

        o = 0
        nw, o = carve(o, [D], F32)
        identf, o = carve(o, [128], F32)
        retc, o = carve(o, [2, 128], F32)
        dlam, o = carve(o, [2, 256], F32)
        subw, o = carve(o, [2, 128], F32)
        scr, o = carve(o, [256], F32)
        o = 0
        qT, o = carve(o, [2, S], BF16)
        kT, o = carve(o, [2, S], BF16)
        V1, o = carve(o, [NT, 2, 130], BF16)
        oa, o = carve(o, [4, 2, 128], F32)
        oan, o = carve(o, [4, 2, 128], F32)
        PT0, o = carve(o, [2, 512], BF16)
        PT1, o = carve(o, [2, 512], BF16)
        PT = [PT0, PT1]
        tmpA, o = carve(o, [128], F32)
        accS, o = carve(o, [8, 129], F32)
        assert o <= ARENA, o
        o = 0
        qkr, o = carve(o, [NT, 256], BF16)
        vB, o = carve(o, [NT, 256], BF16)
        Rst, o = carve(o, [NT, 2, 128], BF16)
        cs, o = carve(o, [2, NT, 32], F32)
        Rcur, o = carve(o, [2, 128], F32)
        qk32 = []
        rtmp = []
        kdec = []
        qdec = []
        TT = []
        TTq2 = []
        innerT = []
        btmp = []
        for _i in range(2):
            a_, o = carve(o, [256], F32); qk32.append(a_)
            a_, o = carve(o, [4, 128], F32); rtmp.append(a_)
            a_, o = carve(o, [2, 2, 64], BF16); kdec.append(a_)
            a_, o = carve(o, [2, 2, 64], BF16); qdec.append(a_)
            a_, o = carve(o, [3, 128], BF16); TT.append(a_)
            a_, o = carve(o, [2, 128], BF16); TTq2.append(a_)
            a_, o = carve(o, [2, 128], BF16); innerT.append(a_)
            a_, o = carve(o, [2, 128], F32); btmp.append(a_)
        assert o <= ARENA, o
        o = 0
        uC, o = carve(o, [NT, 256], BF16)
        pooledT = []
        ytmp = []
        for _i in range(2):
            a_, o = carve(o, [2, 128], BF16); pooledT.append(a_)
            a_, o = carve(o, [256], F32); ytmp.append(a_)
        assert o <= ARENA, o

        psall = st.enter_context(nc.psum_tensor("psall", [128, 8, 512], F32))
        bank = [psall[:, i, :] for i in range(8)]

        def PS(*idx):
            return [("ps", i) for i in idx]


        dma = P.dma
        dma(lambda e: e.dma_start(out=identf, in_=ident_d), writes=["identf"])
        dma(lambda e: e.dma_start(out=cfar[:], in_=cfar_d), writes=["cfar"])
        dma(lambda e: e.dma_start(out=retc, in_=retc_d), writes=["retc"])
        dma(lambda e: e.dma_start(out=tokidx[:], in_=tokidx_d), writes=["tokidx"])
        dma(lambda e: e.dma_start(out=dlam, in_=dlam_d), writes=["dlam"])
        dma(lambda e: e.dma_start(out=subw, in_=subw_d), writes=["subw"])
        dma(lambda e: e.dma_start(out=rdl[:], in_=rdl_d), writes=["rdl"])
        dma(lambda e: e.dma_start(out=psh[:], in_=pscale_d), writes=["psh"])
        dma(lambda e: e.dma_start(out=poolw[:].rearrange("c (l g) d -> c l g d", l=2),
                                  in_=poolw_d.rearrange("l g c d -> c l g d")),
            writes=["poolw"], queue="pool")
        dma(lambda e: e.dma_start(out=poolm[:], in_=poolm_d), writes=["poolm"], queue="pool")
        dma(lambda e: e.dma_start(out=bmast[:], in_=bm_d), writes=["bmast"], queue="pool")
        for h_ in range(4):
            P.op("act", lambda e, h_=h_: e.activation(out=bmast[:, h_, :], in_=bmast[:, h_, :], func=AF.Exp), reads=["bmast"], writes=["bmast"])

        P.op("pool", lambda e: e.memset(mhalf[:], -0.5), writes=["mhalf"])
        P.op("pool", lambda e: e.memset(zero1[:], 0.0), writes=["zero1"])
        P.op("dve", lambda e: e.tensor_copy(out=identb[:], in_=identf), reads=["identf"], writes=["identb"])

        lam_init = [0.8 - 0.6 * math.exp(-0.3 * l) for l in range(2)]
        for l in range(2):
            P.op("dve", lambda e, l=l: e.tensor_tensor(out=scr[:, 0:64], in0=dlam[:, l, 0:64], in1=dlam[:, l, 64:128], op=ALU.mult),
                 reads=["dlam"], writes=["junk"])
            P.op("dve", lambda e, l=l: e.tensor_tensor(out=scr[:, 64:128], in0=dlam[:, l, 128:192], in1=dlam[:, l, 192:256], op=ALU.mult),
                 reads=["dlam"], writes=["junk"])
            P.op("dve", lambda e: e.reduce_sum(out=sm[:, 0:2], in_=scr[:, 0:128].rearrange("p (a b) -> p a b", a=2), axis=AX.X),
                 reads=["junk"], writes=["sm"])
            P.op("act", lambda e: e.activation(out=sm[:, 2:4], in_=sm[:, 0:2], func=AF.Exp), reads=["sm"], writes=["sm"])
            P.op("dve", lambda e, l=l: e.tensor_scalar(out=sm[:, 4:5], in0=sm[:, 3:4], scalar1=-lam_init[l], scalar2=None, op0=ALU.add),
                 reads=["sm"], writes=["sm"])
            P.op("dve", lambda e, l=l: e.tensor_tensor(out=nlam[:, l:l + 1], in0=sm[:, 4:5], in1=sm[:, 2:3], op=ALU.subtract),
                 reads=["sm"], writes=["nlam"])
            for hh in range(2):
                P.op("dve", lambda e, l=l, hh=hh: e.tensor_scalar(out=W2[:, l, hh, :], in0=subw[:, l, :],
                                                                  scalar1=(1.0 - lam_init[l]) * 0.5, scalar2=None, op0=ALU.mult),
                     reads=["subw"], writes=["W2"])
            P.op("dve", lambda e, l=l: e.tensor_scalar(out=psh[:, l, :], in0=psh[:, l, :], scalar1=0.5, scalar2=None, op0=ALU.mult),
                 reads=["psh"], writes=["psh"])
        P.op("act", lambda e: e.activation(out=sm[:, 16:32], in_=rdl[:], func=AF.Exp, scale=-1.0), reads=["rdl"], writes=["sm"])
        P.op("dve", lambda e: e.tensor_scalar(out=sm[:, 32:48], in0=sm[:, 16:32], scalar1=1.0, scalar2=None, op0=ALU.add),
             reads=["sm"], writes=["sm"])
        P.op("act", lambda e: e.activation(out=sm[:, 16:32], in_=sm[:, 32:48], func=AF.Ln), reads=["sm"], writes=["sm"])
        P.op("dve", lambda e: e.tensor_scalar(out=lg[:], in0=sm[:, 16:32], scalar1=-1.0, scalar2=None, op0=ALU.mult),
             reads=["sm"], writes=["lg"])
        P.op("act", lambda e: e.activation(out=gsc[:], in_=lg[:], func=AF.Exp, scale=128.0), reads=["lg"], writes=["gsc"])
        for l in range(2):
            lf = l * 8
            lb = l * 8 + 4
            for (dst, src, ti) in ((0, lf, 0), (4, lb, 1), (8, lf, 2), (12, lb, 3)):
                P.op("dve", lambda e, l=l, dst=dst, src=src, ti=ti: e.tensor_scalar(
                    out=sm[:, 48 + dst:52 + dst], in0=lg[:, src:src + 4], scalar1=tokidx[:, ti:ti + 1], scalar2=None, op0=ALU.mult),
                    reads=["lg", "tokidx"], writes=["sm"])
            P.op("act", lambda e, l=l: e.activation(out=tqk[:, l, :], in_=sm[:, 48:64], func=AF.Exp), reads=["sm"], writes=["tqk"])
            for h in range(4):
                P.op("dve", lambda e, l=l, h=h: e.tensor_scalar(out=scr[:, 0:128], in0=retc[:, 0, :], scalar1=lg[:, l * 8 + h:l * 8 + h + 1],
                                                                scalar2=None, op0=ALU.mult), reads=["retc", "lg"], writes=["junk"])
                P.op("dve", lambda e, l=l, h=h: e.scalar_tensor_tensor(out=scr[:, 128:256], in0=retc[:, 1, :],
                                                                       scalar=lg[:, l * 8 + 4 + h:l * 8 + 5 + h], in1=scr[:, 0:128],
                                                                       op0=ALU.mult, op1=ALU.add), reads=["retc", "lg", "junk"], writes=["junk"])
                P.op("act", lambda e, l=l, h=h: e.activation(out=D2T[:, l, h, :], in_=scr[:, 128:256], func=AF.Exp),
                     reads=["junk"], writes=["D2T"])

        P.fence()
        wstate = {"n": 0}

        preloaded = {}
        sched = {"list": [], "i": 0}

        def phase_loads(ph, l, hp):
            if ph == "A":
                return [("in", l, OFF["aq"] + hp * 256), ("in", l, OFF["ak"] + hp * 256), ("in", l, OFF["av"] + hp * 256),
                        ("in", l, OFF["ag"] + hp * 256), ("out", l, hp * 256)]
            if ph == "B":
                return [("inqk", l, OFF["bq"] + hp * 128), ("in", l, OFF["bv"] + hp * 256), ("in", l, OFF["bg"] + hp * 256),
                        ("out", l, 512 + hp * 256)]
            return [("in", l, OFF["cu"] + hp * 256), ("in", l, OFF["cg"] + hp * 256), ("out", l, 1024 + hp * 256)]

        def hoist_next():
            i = sched["i"] + 1
            if i < len(sched["list"]):
                ph, l, hp = sched["list"][i]
                for k in phase_loads(ph, l, hp):
                    if k not in preloaded:
                        preloaded[k] = _load_w(*k)

        def load_w(kind, l, c0):
            k = (kind, l, c0)
            if k in preloaded:
                return preloaded.pop(k)
            return _load_w(kind, l, c0)

        def _load_w(kind, l, c0):
            n0 = len(P.ops)
            r = _load_w2(kind, l, c0)
            for o_ in P.ops[n0:]:
                o_.nofence = True
            return r

        def _load_w2(kind, l, c0):
            i = wstate["n"] % NSLOT
            wstate["n"] += 1
            sl = wslot[i]
            key = ("wslot", i)
            if kind == "in":
                v = sl[:].rearrange("p (k c) -> p k c", k=8)
                dma(lambda e: e.dma_start(out=v, in_=win_d[l, :, c0:c0 + 256].rearrange("(k p) c -> p k c", p=128)),
                    writes=[key], queue="pool")
            elif kind == "inqk":
                v = sl[:].rearrange("p (k c) -> p k c", k=8)
                dma(lambda e: e.dma_start(out=v[:, :, 0:128], in_=win_d[l, :, c0:c0 + 128].rearrange("(k p) c -> p k c", p=128)),
                    writes=[key], queue="pool")
                dma(lambda e: e.dma_start(out=v[:, :, 128:256], in_=win_d[l, :, c0 + 256:c0 + 384].rearrange("(k p) c -> p k c", p=128)),
                    reads=[key], writes=[key], queue="pool")
            else:
                v = sl[:].rearrange("p (k c) -> p k c", k=2)
                dma(lambda e: e.dma_start(out=v, in_=wout_d[l, c0:c0 + 256, :].rearrange("(k p) c -> p k c", p=128)),
                    writes=[key], queue="pool")
            return v, key

        cnt = {"tok": 0, "tail": 0, "ev": 0}

        def proj_tok(wv, wkey, t, bk, ncols=256):
            for k in range(8):
                P.op("pe", lambda e, k=k: e.matmul(bank[bk][:, 0:ncols], lhsT=hT[:, k, t * 128:(t + 1) * 128], rhs=wv[:, k, 0:ncols],
                                                   start=(k == 0), stop=(k == 7)),
                     reads=["hT", wkey], writes=PS(bk), inc=(k == 7))

        def gate_evac(bk, t, i):
            P.op("act", lambda e: e.activation(out=th[i][:], in_=bank[bk][:, 0:256], func=AF.Tanh, scale=0.5),
                 reads=PS(bk), writes=[("th", i)])
            P.op("dve", lambda e: e.scalar_tensor_tensor(out=gate[:, t, :], in0=th[i][:], scalar=1.0, in1=bank[bk][:, 0:256],
                                                         op0=ALU.add, op1=ALU.mult),
                 reads=PS(bk) + [("th", i)], writes=["gate"])

        def tail(mx, mxkey, t, wov, wokey, tb=None, tbk=7, ob=None, i=None):
            if i is None:
                i = cnt["tail"] % 2
                cnt["tail"] += 1
            if tb is None:
                tb = bank[7].bitcast(BF16)[:, 0:256]
            for k in range(2):
                P.op("pe", lambda e, k=k: e.transpose(out=tb[:, k * 128:(k + 1) * 128], in_=mx[:, k * 128:(k + 1) * 128], identity=identb[:]),
                     reads=[mxkey, "identb"], writes=PS(tbk), inc=(k == 1))
            P.op("act", lambda e: e.copy(out=mT[i][:].rearrange("p a b -> p (a b)"), in_=tb), reads=PS(tbk), writes=[("mT", i)])
            if ob is None:
                for half in range(2):
                    for k in range(2):
                        P.op("pe", lambda e, k=k, half=half: e.matmul(bank[7], lhsT=mT[i][:, k, :], rhs=wov[:, k, half * 512:(half + 1) * 512],
                                                                      start=(k == 0), stop=(k == 1)),
                             reads=[("mT", i), wokey], writes=PS(7), inc=(k == 1))
                    P.op("dve", lambda e, half=half: e.tensor_tensor(out=xres[:, t, half * 512:(half + 1) * 512],
                                                                     in0=xres[:, t, half * 512:(half + 1) * 512], in1=bank[7], op=ALU.add),
                         reads=PS(7) + [("x", t)], writes=[("x", t)])
            else:
                for half in range(2):
                    for k in range(2):
                        P.op("pe", lambda e, k=k, half=half: e.matmul(bank[ob + half], lhsT=mT[i][:, k, :], rhs=wov[:, k, half * 512:(half + 1) * 512],
                                                                      start=(k == 0), stop=(k == 1)),
                             reads=[("mT", i), wokey], writes=PS(ob + half), inc=(k == 1))
                P.op("dve", lambda e: e.tensor_tensor(out=xres[:, t, :], in0=xres[:, t, :],
                                                      in1=psall[:, ob:ob + 2, :].rearrange("p a b -> p (a b)"), op=ALU.add),
                     reads=PS(ob, ob + 1) + [("x", t)], writes=[("x", t)])

        def rmsnorm_to_hT(l):
            dma(lambda e: e.dma_start(out=nw, in_=nwb_d[l]), writes=["nw"])
            for t in range(NT):
                P.op("act", lambda e, t=t: e.activation(out=junk[:], in_=xres[:, t, :], func=AF.Square, accum_out=ss[:, 0, t:t + 1]),
                     reads=[("x", t)], writes=["junk", ("ss", t)])
            P.op("dve", lambda e: e.tensor_scalar(out=ss[:, 1, :], in0=ss[:, 0, :], scalar1=1.0 / D, scalar2=EPS, op0=ALU.mult, op1=ALU.add),
                 reads=[("ss", t) for t in range(NT)], writes=["ss1"])
            P.op("pool", lambda e: e.tensor_tensor(out=ss[:, 0, :], in0=ss[:, 1, :], in1=mhalf[:, 0:NT], op=ALU.pow),
                 reads=["ss1", "mhalf"], writes=[("ss", t) for t in range(NT)])
            for t in range(NT):
                i = 0
                P.op("dve", lambda e, t=t, i=i: e.scalar_tensor_tensor(out=hb[i][:], in0=xres[:, t, :], scalar=ss[:, 0, t:t + 1], in1=nw,
                                                                       op0=ALU.mult, op1=ALU.mult),
                     reads=[("x", t), ("ss", t), "nw"], writes=[("hb", i)])
                bk = 5 + (t % 2)
                for k in range(8):
                    P.op("pe", lambda e, k=k, i=i, bk=bk: e.transpose(out=bank[bk][:].bitcast(BF16)[:, k * 128:(k + 1) * 128],
                                                                      in_=hb[i][:, k * 128:(k + 1) * 128], identity=identb[:]),
                         reads=[("hb", i), "identb"], writes=PS(bk), inc=(k == 7))
                P.op("act", lambda e, t=t, bk=bk: e.copy(out=hT[:, :, t * 128:(t + 1) * 128],
                                                         in_=bank[bk][:].bitcast(BF16).rearrange("p (k c) -> p k c", k=8)),
                     reads=PS(bk), writes=["hT"])

        def phase_A(l, hp):
            P.op("pool", lambda e: e.memset(V1[:, :, :, 128:129], 1.0), writes=["V1"])
            wq, wqk = load_w("in", l, OFF["aq"] + hp * 256)
            wk, wkk = load_w("in", l, OFF["ak"] + hp * 256)
            wv_, wvk = load_w("in", l, OFF["av"] + hp * 256)
            n = 0
            for (wv, wkey, dst, dkey, scl) in ((wq, wqk, qT, "qT", 0.125), (wk, wkk, kT, "kT", 1.0)):
                for hh in range(2):
                    for tc in range(4):
                        bk = n % 4
                        n += 1
                        for k in range(8):
                            P.op("pe", lambda e, k=k, bk=bk, wv=wv, hh=hh, tc=tc: e.matmul(
                                bank[bk][:, :], lhsT=wv[:, k, hh * 128:(hh + 1) * 128], rhs=hT[:, k, tc * 512:(tc + 1) * 512],
                                start=(k == 0), stop=(k == 7)), reads=["hT", wkey], writes=PS(bk), inc=(k == 7))
                        if n % 2 == 0:
                            P.op("act", lambda e, bk=bk, dst=dst, hh=hh, tc=tc, scl=scl: e.mul(out=dst[:, hh, tc * 512:(tc + 1) * 512],
                                                                                             in_=bank[bk][:, :], mul=scl),
                                 reads=PS(bk), writes=[dkey])
                        else:
                            P.op("dve", lambda e, bk=bk, dst=dst, hh=hh, tc=tc, scl=scl: e.tensor_scalar(
                                out=dst[:, hh, tc * 512:(tc + 1) * 512], in0=bank[bk][:, :], scalar1=scl, scalar2=None, op0=ALU.mult),
                                reads=PS(bk), writes=[dkey])
            wg, wgk = load_w("in", l, OFF["ag"] + hp * 256)
            wo, wok = load_w("out", l, hp * 256)
            for t in range(NT):
                bk = t % 4
                proj_tok(wv_, wvk, t, bk)
                eng = "act" if t % 2 == 0 else "dve"
                if eng == "act":
                    P.op("act", lambda e, t=t, bk=bk: e.copy(out=V1[:, t, :, 0:128], in_=bank[bk][:, 0:256].rearrange("p (a b) -> p a b", a=2)),
                         reads=PS(bk), writes=["V1"])
                else:
                    P.op("dve", lambda e, t=t, bk=bk: e.tensor_copy(out=V1[:, t, :, 0:128], in_=bank[bk][:, 0:256].rearrange("p (a b) -> p a b", a=2)),
                         reads=PS(bk), writes=["V1"])
            for t in range(NT):
                bk = t % 4
                proj_tok(wg, wgk, t, bk)
                gate_evac(bk, t, t % 2)

            hoist_next()

            def acc(r, c0=0, c1=129):
                return bank[4 + r // 3][:, (r % 3) * 129 + c0:(r % 3) * 129 + c1]

            pend = []

            def chunk_tail(qc):
                P.op("pool", lambda e: e.tensor_tensor(out=oan, in0=oa, in1=oa, op=ALU.mult), reads=["oa"], writes=["oan"])
                P.op("dve", lambda e: e.reduce_sum(out=sm[:, 16:24], in_=oan.rearrange("p a b c -> p (a b) c"), axis=AX.X),
                     reads=["oan"], writes=["sm"])
                P.op("dve", lambda e: e.tensor_scalar(out=sm[:, 24:32], in0=sm[:, 16:24], scalar1=1.0 / 128, scalar2=EPS, op0=ALU.mult, op1=ALU.add),
                     reads=["sm"], writes=["sm"])
                P.op("pool", lambda e: e.tensor_tensor(out=sm[:, 16:24], in0=sm[:, 24:32], in1=mhalf[:, 0:8], op=ALU.pow),
                     reads=["sm", "mhalf"], writes=["sm"])
                P.op("dve", lambda e: e.tensor_tensor(out=oan.rearrange("p a b c -> p (a b) c"), in0=oa.rearrange("p a b c -> p (a b) c"),
                                                      in1=sm[:, 16:24].unsqueeze(2).to_broadcast([128, 8, 128]), op=ALU.mult),
                     reads=["oa", "sm"], writes=["oan"])
                for qs in range(4):
                    t = qc * 4 + qs
                    i = cnt["tail"] % 2
                    P.op("pool", lambda e, qs=qs: e.tensor_tensor(out=oan[:, qs], in0=oan[:, qs], in1=W2[:, l], op=ALU.mult),
                         reads=["oan", "W2"], writes=["oan"])
                    P.op("dve", lambda e, qs=qs, t=t, i=i: e.tensor_tensor(out=mixed[i][:], in0=oan[:, qs].rearrange("p a b -> p (a b)"),
                                                                          in1=gate[:, t, :], op=ALU.mult),
                         reads=["oan", "gate"], writes=[("mixed", i)])
                    tail(mixed[i], ("mixed", i), t, wo, wok)

            for qc in range(4):
                for hh in range(2):
                    h = 2 * hp + hh
                    seq = []
                    for kt in range(NT):
                        d = kt - 4 * qc
                        near = -1 <= d <= 4
                        seq.append((kt, near, d))

                    def emit_qk(j, hh=hh, qc=qc, h=h):
                        kt, near, d = seq[j]
                        b0 = (j % 2) * 2
                        for m in range(2):
                            P.op("pe", lambda e, m=m, kt=kt, b0=b0: e.matmul(
                                bank[b0 + m][:, :], lhsT=kT[m * 64:(m + 1) * 64, hh, kt * 128:(kt + 1) * 128],
                                rhs=qT[m * 64:(m + 1) * 64, hh, qc * 512:(qc + 1) * 512], start=True, stop=True),
                                reads=["kT", "qT"], writes=PS(b0 + m), inc=(m == 1))

                    def emit_exp_pv(j, hh=hh, qc=qc, h=h):
                        kt, near, d = seq[j]
                        b0 = (j % 2) * 2
                        pt = PT[j % 2]
                        if near:
                            bias_ap = zero1[:, 0:1]
                        elif d > 4:
                            bias_ap = cfar[:, h:h + 1]
                        else:
                            bias_ap = cfar[:, 4 + h:5 + h]
                        for m in range(2):
                            P.op("act", lambda e, m=m: e.activation(out=pt[:, m, :], in_=bank[b0 + m][:, :], func=AF.Exp, bias=bias_ap, scale=1.0),
                                 reads=PS(b0 + m) + ["cfar", "zero1"], writes=[("PT", j % 2, m)])
                        if near:
                            base = 512 - 128 * d
                            P.op("dve", lambda e, base=base: e.tensor_tensor(
                                out=pt, in0=pt, in1=bmast[:, h, base:base + 512].unsqueeze(1).to_broadcast([128, 2, 512]), op=ALU.mult),
                                reads=["bmast"], writes=[("PT", j % 2, 0), ("PT", j % 2, 1)])
                        for m in range(2):
                            for qs in range(4):
                                r = m * 4 + qs
                                first = (kt == 0 and r % 3 == 0)
                                last = (kt == NT - 1 and m == 1 and qs == 3)
                                P.op("pe", lambda e, m=m, qs=qs, r=r, first=first: e.matmul(
                                    acc(r), lhsT=pt[:, m, qs * 128:(qs + 1) * 128], rhs=V1[:, kt, hh, 0:129],
                                    start=first, stop=(kt == NT - 1), skip_group_check=True),
                                    reads=[("PT", j % 2, m), "V1"], writes=PS(4 + r // 3), inc=(last or (m == 1 and qs == 3)))

                    emit_qk(0)
                    for j in range(NT):
                        if j + 1 < NT:
                            emit_qk(j + 1)
                        emit_exp_pv(j)
                        if j == 5 and pend:
                            chunk_tail(pend.pop(0))
                    P.op("act", lambda e: e.copy(out=accS[:, 0:3, :].rearrange("p a b -> p (a b)"), in_=bank[4][:, 0:387]), reads=PS(4), writes=["accS0"])
                    P.op("dve", lambda e: e.tensor_copy(out=accS[:, 3:6, :].rearrange("p a b -> p (a b)"), in_=bank[5][:, 0:387]), reads=PS(5), writes=["accS1"])
                    P.op("act", lambda e: e.copy(out=accS[:, 6:8, :].rearrange("p a b -> p (a b)"), in_=bank[6][:, 0:258]), reads=PS(6), writes=["accS2"])
                    P.op("dve", lambda e: e.reciprocal(out=sm[:, 0:8], in_=accS[:, :, 128]), reads=["accS0", "accS1", "accS2"], writes=["sm"])
                    P.op("dve", lambda e: e.tensor_scalar(out=sm[:, 4:8], in0=sm[:, 4:8], scalar1=nlam[:, l:l + 1], scalar2=None, op0=ALU.mult),
                         reads=["sm", "nlam"], writes=["sm"])
                    for qs in range(4):
                        r1 = 4 + qs
                        P.op("act", lambda e, qs=qs, r1=r1: e.mul(out=tmpA, in_=accS[:, r1, 0:128], mul=sm[:, r1:r1 + 1]),
                             reads=["accS0", "accS1", "accS2", "sm"], writes=["tmpA"])
                        P.op("dve", lambda e, qs=qs, hh=hh: e.scalar_tensor_tensor(out=oa[:, qs, hh, :], in0=accS[:, qs, 0:128], scalar=sm[:, qs:qs + 1],
                                                                                  in1=tmpA, op0=ALU.mult, op1=ALU.add),
                             reads=["accS0", "accS1", "accS2", "sm", "tmpA"], writes=["oa"])
                pend.append(qc)
            while pend:
                chunk_tail(pend.pop(0))

        def phase_B(l, hp):
            wqk_, wqkk = load_w("inqk", l, OFF["bq"] + hp * 128)
            wv_, wvk = load_w("in", l, OFF["bv"] + hp * 256)
            wg, wgk = load_w("in", l, OFF["bg"] + hp * 256)
            wo, wok = load_w("out", l, 512 + hp * 256)
            dma(lambda e: e.dma_start(out=cs, in_=cs_d), writes=["cs"])
            for s2 in range(2):
                P.op("pool", lambda e, s2=s2: e.memset(TTq2[s2], 0.0), writes=[("TTq2", s2)])
            for t in range(NT):
                bk = t % 4
                i2 = t % 2
                proj_tok(wqk_, wqkk, t, bk)
                P.op("act", lambda e, bk=bk, i2=i2: e.copy(out=qk32[i2][:, 0:128], in_=bank[bk][:, 0:128]), reads=PS(bk), writes=[("qk32", i2)])
                P.op("act", lambda e, bk=bk, i2=i2: e.mul(out=qk32[i2][:, 128:256], in_=bank[bk][:, 128:256], mul=0.125),
                     reads=PS(bk), writes=[("qk32", i2)])
                src4 = qk32[i2].rearrange("p (a b c) -> p a b c", a=4, b=2)
                cos_b = cs[:, 0, t, :].unsqueeze(1).to_broadcast([128, 4, 32])
                sin_b = cs[:, 1, t, :].unsqueeze(1).to_broadcast([128, 4, 32])
                t1 = src4[:, :, 0, :]
                t2 = src4[:, :, 1, :]
                rt = [rtmp[i2][:, i].rearrange("p (a c) -> p a c", a=4) for i in range(4)]
                dst4 = qkr[:, t, :].rearrange("p (a b c) -> p a b c", a=4, b=2)
                P.op("pool", lambda e, t1=t1, cos_b=cos_b, rt=rt: e.tensor_tensor(out=rt[0], in0=t1, in1=cos_b, op=ALU.mult),
                     reads=[("qk32", i2), "cs"], writes=[("rtmp", i2, 0)])
                P.op("pool", lambda e, t2=t2, sin_b=sin_b, rt=rt: e.tensor_tensor(out=rt[1], in0=t2, in1=sin_b, op=ALU.mult),
                     reads=[("qk32", i2), "cs"], writes=[("rtmp", i2, 1)])
                P.op("dve", lambda e, t1=t1, sin_b=sin_b, rt=rt: e.tensor_tensor(out=rt[2], in0=t1, in1=sin_b, op=ALU.mult),
                     reads=[("qk32", i2), "cs"], writes=[("rtmp", i2, 2)])
                P.op("dve", lambda e, t2=t2, cos_b=cos_b, rt=rt: e.tensor_tensor(out=rt[3], in0=t2, in1=cos_b, op=ALU.mult),
                     reads=[("qk32", i2), "cs"], writes=[("rtmp", i2, 3)])
                P.op("pool", lambda e, rt=rt, dst4=dst4: e.tensor_tensor(out=dst4[:, :, 0, :], in0=rt[0], in1=rt[1], op=ALU.subtract),
                     reads=[("rtmp", i2, 0), ("rtmp", i2, 1)], writes=[("qkr", t, 0)])
                P.op("dve", lambda e, rt=rt, dst4=dst4: e.tensor_tensor(out=dst4[:, :, 1, :], in0=rt[2], in1=rt[3], op=ALU.add),
                     reads=[("rtmp", i2, 2), ("rtmp", i2, 3)], writes=[("qkr", t, 1)])
            for t in range(NT):
                bk = t % 4
                proj_tok(wv_, wvk, t, bk)
                if t % 2 == 0:
                    P.op("act", lambda e, t=t, bk=bk: e.copy(out=vB[:, t, :], in_=bank[bk][:, 0:256]), reads=PS(bk), writes=[("vB", t)])
                else:
                    P.op("dve", lambda e, t=t, bk=bk: e.tensor_copy(out=vB[:, t, :], in_=bank[bk][:, 0:256]), reads=PS(bk), writes=[("vB", t)])
            for t in range(NT):
                bk = t % 4
                proj_tok(wg, wgk, t, bk)
                gate_evac(bk, t, t % 2)
            hoist_next()
            QKR = lambda t: [("qkr", t, 0), ("qkr", t, 1)]

            def kvp(t):
                return bank[t // 2][:, (t % 2) * 256:(t % 2) * 256 + 256]

            for t in range(NT):
                i2 = t % 2
                kq = qkr[:, t, 128:256].rearrange("p (a c) -> p a c", a=2)
                for dr in range(2):
                    c0 = 8 + dr * 4 + 2 * hp
                    P.op("pool", lambda e, dr=dr, c0=c0, kq=kq, i2=i2: e.tensor_tensor(
                        out=kdec[i2][:, :, dr, :], in0=kq, in1=tqk[:, l, c0:c0 + 2].unsqueeze(2).to_broadcast([128, 2, 64]), op=ALU.mult),
                        reads=QKR(t) + ["tqk"], writes=[("kdec", i2)])
                for hh in range(2):
                    P.op("pe", lambda e, hh=hh, t=t, i2=i2: e.matmul(kvp(t)[:, hh * 128:(hh + 1) * 128], lhsT=kdec[i2][:, hh].rearrange("p a b -> p (a b)"),
                                                                    rhs=vB[:, t, hh * 128:(hh + 1) * 128], start=True, stop=True),
                         reads=[("kdec", i2), ("vB", t)], writes=PS(t // 2), inc=(hh == 1))

            P.op("pool", lambda e: e.memset(Rcur, 0.0), writes=[("Rcur", 0), ("Rcur", 64)])
            for n_ in range(NT):
                for (lo, hi, t, dr) in ((0, 64, n_, 0), (64, 128, NT - 1 - n_, 1)):
                    P.op("act", lambda e, t=t, lo=lo, hi=hi: e.copy(out=Rst[lo:hi, t], in_=Rcur[lo:hi]), reads=[("Rcur", lo)], writes=[("Rst", t, lo)])
                    if n_ == NT - 1:
                        continue
                    for hh in range(2):
                        gc = l * 8 + dr * 4 + 2 * hp + hh
                        P.op("dve", lambda e, lo=lo, hi=hi, t=t, hh=hh, gc=gc: e.scalar_tensor_tensor(
                            out=Rcur[lo:hi, hh, :], in0=Rcur[lo:hi, hh, :], scalar=gsc[lo:hi, gc:gc + 1],
                            in1=kvp(t)[lo:hi, hh * 128:(hh + 1) * 128], op0=ALU.mult, op1=ALU.add),
                            reads=PS(t // 2) + [("Rcur", lo), "gsc"], writes=[("Rcur", lo)])

            def S0(t):
                s2 = t % 2
                bTS = 2 * s2
                bO = 2 * s2 + 1
                qq = qkr[:, t, 0:128].rearrange("p (a c) -> p a c", a=2)
                for dr in range(2):
                    c0 = dr * 4 + 2 * hp
                    P.op("pool", lambda e, dr=dr, c0=c0, qq=qq, s2=s2: e.tensor_tensor(
                        out=qdec[s2][:, :, dr, :], in0=qq, in1=tqk[:, l, c0:c0 + 2].unsqueeze(2).to_broadcast([128, 2, 64]), op=ALU.mult),
                        reads=QKR(t) + ["tqk"], writes=[("qdec", s2)])
                bT = bank[bTS].bitcast(BF16)
                srcs = [qdec[s2][:, 0].rearrange("p a b -> p (a b)"), qdec[s2][:, 1].rearrange("p a b -> p (a b)"),
                        qkr[:, t, 128:256], qkr[:, t, 0:128]]
                for i4 in range(4):
                    P.op("pe", lambda e, i4=i4, srcs=srcs, bT=bT: e.transpose(out=bT[:, i4 * 128:(i4 + 1) * 128], in_=srcs[i4], identity=identb[:]),
                         reads=[("qdec", s2), "identb"] + QKR(t), writes=PS(bTS), inc=(i4 == 3))
                P.op("act", lambda e, bT=bT, s2=s2: e.copy(out=TT[s2].rearrange("p a b -> p (a b)"), in_=bT[:, 0:384]),
                     reads=PS(bTS), writes=[("TT", s2)])
                for hh in range(2):
                    P.op("act", lambda e, bT=bT, s2=s2, hh=hh: e.copy(out=TTq2[s2][hh * 64:(hh + 1) * 64, hh, :],
                                                                       in_=bT[hh * 64:(hh + 1) * 64, 384:512]),
                         reads=PS(bTS), writes=[("TTq2", s2)])
                P.op("pe", lambda e, s2=s2, bTS=bTS: e.matmul(bank[bTS][:, 256:512], lhsT=TT[s2][:, 2, :],
                                                             rhs=TTq2[s2].rearrange("p a b -> p (a b)"), start=True, stop=True),
                     reads=[("TT", s2), ("TTq2", s2)], writes=PS(bTS), inc=True)
                P.op("dve", lambda e, s2=s2, bTS=bTS: e.tensor_tensor(out=innerT[s2], in0=bank[bTS][:, 256:512].rearrange("p (a b) -> p a b", a=2),
                                                                     in1=D2T[:, l, 2 * hp:2 * hp + 2, :], op=ALU.mult),
                     reads=PS(bTS) + ["D2T"], writes=[("innerT", s2)])
                for hh in range(2):
                    P.op("pe", lambda e, hh=hh, t=t, s2=s2, bO=bO: e.matmul(bank[bO][:, hh * 128:(hh + 1) * 128], lhsT=innerT[s2][:, hh, :],
                                                                           rhs=vB[:, t, hh * 128:(hh + 1) * 128], start=True, stop=False),
                         reads=[("innerT", s2), ("vB", t)], writes=PS(bO), inc=False)
                    P.op("pe", lambda e, hh=hh, t=t, s2=s2, bO=bO: e.matmul(bank[bO][:, hh * 128:(hh + 1) * 128], lhsT=TT[s2][:, hh, :],
                                                                           rhs=Rst[:, t, hh, :], start=False, stop=True),
                         reads=[("TT", s2), ("Rst", t, 0), ("Rst", t, 64)], writes=PS(bO), inc=(hh == 1))

            def S1(t):
                s2 = t % 2
                bO = 2 * s2 + 1
                sc = 32 + 8 * s2
                for hh in range(2):
                    P.op("act", lambda e, hh=hh, bO=bO, sc=sc: e.activation(out=junk[:, hh * 128:(hh + 1) * 128], in_=bank[bO][:, hh * 128:(hh + 1) * 128],
                                                                          func=AF.Square, accum_out=sm[:, sc + hh:sc + hh + 1]),
                         reads=PS(bO), writes=["junk", ("smB", s2)])
                P.op("dve", lambda e, sc=sc: e.tensor_scalar(out=sm[:, sc + 2:sc + 4], in0=sm[:, sc:sc + 2], scalar1=4.0 / 128, scalar2=4.0 * EPS,
                                                             op0=ALU.mult, op1=ALU.add), reads=[("smB", s2)], writes=[("smB", s2)])
                P.op("pool", lambda e, sc=sc: e.tensor_tensor(out=sm[:, sc + 4:sc + 6], in0=sm[:, sc + 2:sc + 4], in1=mhalf[:, 0:2], op=ALU.pow),
                     reads=[("smB", s2), "mhalf"], writes=[("smB", s2)])
                i = t % 2
                P.op("dve", lambda e, s2=s2, bO=bO, sc=sc: e.tensor_tensor(out=btmp[s2], in0=bank[bO][:, 0:256].rearrange("p (a b) -> p a b", a=2),
                                                                          in1=sm[:, sc + 4:sc + 6].unsqueeze(2).to_broadcast([128, 2, 128]), op=ALU.mult),
                     reads=PS(bO) + [("smB", s2)], writes=[("btmp", s2)])
                P.op("pool", lambda e, i=i, t=t, s2=s2: e.tensor_tensor(out=mixed[i][:], in0=btmp[s2].rearrange("p a b -> p (a b)"), in1=gate[:, t, :], op=ALU.mult),
                     reads=[("btmp", s2), "gate"], writes=[("mixed", i)])

            def S2(t):
                s2 = t % 2
                bO = 2 * s2 + 1
                i = t % 2
                tail(mixed[i], ("mixed", i), t, wo, wok, tb=bank[bO].bitcast(BF16)[:, 512:768], tbk=bO, ob=4 + 2 * s2, i=i)

            for it in range(NT + 2):
                if it < NT:
                    S0(it)
                if 0 <= it - 1 < NT:
                    S1(it - 1)
                if 0 <= it - 2 < NT:
                    S2(it - 2)

        def phase_C(l, gp):
            wu, wuk = load_w("in", l, OFF["cu"] + gp * 256)
            wg, wgk = load_w("in", l, OFF["cg"] + gp * 256)
            wo, wok = load_w("out", l, 1024 + gp * 256)
            for t in range(NT):
                bk = t % 4
                proj_tok(wu, wuk, t, bk)
                if t % 2 == 0:
                    P.op("act", lambda e, t=t, bk=bk: e.copy(out=uC[:, t, :], in_=bank[bk][:, 0:256]), reads=PS(bk), writes=[("uC", t)])
                else:
                    P.op("dve", lambda e, t=t, bk=bk: e.tensor_copy(out=uC[:, t, :], in_=bank[bk][:, 0:256]), reads=PS(bk), writes=[("uC", t)])
            for t in range(NT):
                bk = t % 4
                proj_tok(wg, wgk, t, bk)
                gate_evac(bk, t, t % 2)
            hoist_next()

            def S0(t):
                s2 = t % 2
                bP = 2 * s2
                bY = 2 * s2 + 1
                for gg in range(2):
                    g = 2 * gp + gg
                    parts = [(t, g * 5 + (3 if t == 0 else 4 if t == NT - 1 else 0))]
                    if t > 0:
                        parts.append((t - 1, g * 5 + 1))
                    if t < NT - 1:
                        parts.append((t + 1, g * 5 + 2))
                    for pi_, (tj, mi) in enumerate(parts):
                        P.op("pe", lambda e, gg=gg, tj=tj, mi=mi, pi_=pi_, np_=len(parts), bP=bP: e.matmul(
                            bank[bP][:, gg * 128:(gg + 1) * 128], lhsT=uC[:, tj, gg * 128:(gg + 1) * 128], rhs=poolm[:, mi, :],
                            start=(pi_ == 0), stop=(pi_ == np_ - 1)),
                            reads=[("uC", tj), "poolm"], writes=PS(bP), inc=(gg == 1 and pi_ == len(parts) - 1))
                P.op("act", lambda e, bP=bP, s2=s2: e.copy(out=pooledT[s2].rearrange("p a b -> p (a b)"), in_=bank[bP][:, 0:256]),
                     reads=PS(bP), writes=[("pooledT", s2)])
                for gg in range(2):
                    g = 2 * gp + gg
                    P.op("pe", lambda e, gg=gg, g=g, s2=s2, bY=bY: e.matmul(bank[bY][:, gg * 128:(gg + 1) * 128], lhsT=pooledT[s2][:, gg, :],
                                                                           rhs=poolw[:, l * 4 + g, :], start=True, stop=True),
                         reads=[("pooledT", s2), "poolw"], writes=PS(bY), inc=(gg == 1))

            def S1(t):
                s2 = t % 2
                bY = 2 * s2 + 1
                i = t % 2
                P.op("dve", lambda e, s2=s2, bY=bY: e.tensor_tensor(out=ytmp[s2], in0=bank[bY][:, 0:256], in1=psh[:, l, gp * 256:(gp + 1) * 256], op=ALU.mult),
                     reads=PS(bY) + ["psh"], writes=[("ytmp", s2)])
                P.op("pool", lambda e, t=t, i=i, s2=s2: e.tensor_tensor(out=mixed[i][:], in0=ytmp[s2], in1=gate[:, t, :], op=ALU.mult),
                     reads=[("ytmp", s2), "gate"], writes=[("mixed", i)])

            def S2(t):
                s2 = t % 2
                bY = 2 * s2 + 1
                i = t % 2
                tail(mixed[i], ("mixed", i), t, wo, wok, tb=bank[bY].bitcast(BF16)[:, 512:768], tbk=bY, ob=4 + 2 * s2, i=i)

            for it in range(NT + 2):
                if it < NT:
                    S0(it)
                if 0 <= it - 1 < NT:
                    S1(it - 1)
                if 0 <= it - 2 < NT:
                    S2(it - 2)

        for s_ in range(nseq):
            for l in range(nlayers):
                for ph in phases:
                    for hp in range(2):
                        sched["list"].append((ph, l, hp))
        sched["i"] = -1
        hoist_next()
        fns = {"A": phase_A, "B": phase_B, "C": phase_C}
        pi = 0
        for s in range(nseq):
            for t in range(NT):
                dma(lambda e, s=s, t=t: e.dma_start(out=xres[:, t, :], in_=x_d[s, t]), writes=[("x", t)])
            for l in range(nlayers):
                P.fence()
                rmsnorm_to_hT(l)
                for ph in phases:
                    for hp in range(2):
                        if hp == 0:
                            P.fence()
                        sched["i"] = pi
                        fns[ph](l, hp)
                        pi += 1
            P.fence()
            dma(lambda e: e.dma_start(out=nw, in_=nwb_d[2]), writes=["nw"])
            for t in range(NT):
                P.op("act", lambda e, t=t: e.activation(out=junk[:], in_=xres[:, t, :], func=AF.Square, accum_out=ss[:, 0, t:t + 1]),
                     reads=[("x", t)], writes=["junk", ("ss", t)])
            P.op("dve", lambda e: e.tensor_scalar(out=ss[:, 1, :], in0=ss[:, 0, :], scalar1=1.0 / D, scalar2=EPS, op0=ALU.mult, op1=ALU.add),
                 reads=[("ss", t) for t in range(NT)], writes=["ss1"])
            P.op("pool", lambda e: e.tensor_tensor(out=ss[:, 0, :], in0=ss[:, 1, :], in1=mhalf[:, 0:NT], op=ALU.pow),
                 reads=["ss1", "mhalf"], writes=[("ss", t) for t in range(NT)])
            for t in range(NT):
                P.op("dve", lambda e, t=t: e.scalar_tensor_tensor(out=xres[:, t, :], in0=xres[:, t, :], scalar=ss[:, 0, t:t + 1], in1=nw,
                                                                  op0=ALU.mult, op1=ALU.mult),
                     reads=[("x", t), ("ss", t), "nw"], writes=[("x", t)])
                dma(lambda e, s=s, t=t: e.dma_start(out=y_d[s, t], in_=xres[:, t, :]), reads=[("x", t)], writes=[("y", s, t)])
        P.emit(nc, st)
    return nc


_NC_CACHE = {}


def kernel(**inputs):
    x = np.asarray(inputs["x"], np.float32)
    B = x.shape[0]
    ncores = 8
    nseq = B // ncores
    consts = _host_consts(inputs)
    key = (nseq,)
    if key not in _NC_CACHE:
        _NC_CACHE[key] = build_nc(nseq=nseq)
    nc = _NC_CACHE[key]
    w_in = np.ascontiguousarray(np.asarray(inputs["w_in"], np.float32))
    w_out = np.ascontiguousarray(np.asarray(inputs["w_out"], np.float32))
    in_maps = []
    for c in range(ncores):
        m = dict(consts)
        m["x"] = np.ascontiguousarray(x[c * nseq:(c + 1) * nseq].reshape(nseq, NT, 128, D))
        m["w_in"] = w_in
        m["w_out"] = w_out
        in_maps.append(m)
    res = run_bass_kernel_spmd(nc, in_maps, core_ids=list(range(ncores)))
    out = np.concatenate([np.asarray(r["y"]).reshape(nseq, S, D) for r in res.results], axis=0)
    return out.astype(np.float32)
```

```python
import contextlib
import math
import numpy as np
import concourse.bass as bass
import concourse.mybir as mybir
from concourse.bass_utils import run_bass_kernel_spmd

F32 = mybir.dt.float32
BF16 = mybir.dt.bfloat16
ALU = mybir.AluOpType
AF = mybir.ActivationFunctionType
AX = mybir.AxisListType

D = 1024
S = 2048
NT = S // 128
DIN = 4608
DMIX = 1536
EPS = 1e-6
OFF = dict(aq=0, ak=512, av=1024, ag=1536, bq=2048, bk=2304, bv=2560, bg=3072, cu=3584, cg=4096)
MW = 1152

ENGS = ("pe", "act", "dve", "pool", "sp")
CUT = 99
SEM_LIMIT = 30000


class Op:
    __slots__ = ("eng", "fn", "deps", "inc", "is_dma", "seq", "eidx", "sem", "val", "name",
                 "closer", "dslot", "nofence")


class Prog:
    def __init__(self):
        self.ops = []
        self.last_w = {}
        self.readers = {}
        self.pend = {e: set() for e in ENGS}
        self.fence_idx = 0

    def fence(self):
        deps = set()
        last = {}
        for o in self.ops[self.fence_idx:]:
            if o.is_dma:
                if not o.nofence:
                    deps.add(o.seq)
            else:
                last[o.eng] = o.seq
        deps |= set(last.values())
        for e in ENGS:
            self.pend[e] |= deps
        self.fence_idx = len(self.ops)

    def _add(self, eng, fn, reads, writes, inc, is_dma, name):
        o = Op()
        o.eng, o.fn, o.inc, o.is_dma, o.name = eng, fn, inc, is_dma, name
        o.seq = len(self.ops)
        o.closer = o.seq
        o.nofence = False
        deps = set(self.pend[eng])
        self.pend[eng] = set()
        reads = list(reads)
        writes = list(writes)
        ex = [r for r in reads if isinstance(r, tuple) and r[0] == "ps"]
        reads = [r for r in reads if r not in ex]
        writes = writes + [r for r in ex if r not in writes]
        for r in reads:
            if r in self.last_w:
                deps.add(self.last_w[r])
        for w in writes:
            if w in self.last_w:
                deps.add(self.last_w[w])
            for rd in self.readers.get(w, ()):
                deps.add(rd)
        for r in reads:
            self.readers.setdefault(r, []).append(o.seq)
        for w in writes:
            self.last_w[w] = o.seq
            self.readers[w] = []
        deps.discard(o.seq)
        o.deps = deps
        self.ops.append(o)
        return o

    def op(self, eng, fn, reads=(), writes=(), inc=True, name=""):
        return self._add(eng, fn, reads, writes, inc, False, name)

    def dma(self, fn, reads=(), writes=(), queue="sp", name=""):
        return self._add(queue, fn, reads, writes, True, True, name)

    def emit(self, nc, st, ndma_sems=8):
        ops = self.ops
        per_eng = {e: [] for e in ENGS}
        for o in ops:
            o.eidx = len(per_eng[o.eng])
            per_eng[o.eng].append(o)
        nsem = [0]

        def newsem(tag):
            nsem[0] += 1
            return st.enter_context(nc.semaphore("%s_%d" % (tag, nsem[0])))

        dsem = {q: [newsem("d" + q) for _ in range(ndma_sems)] for q in ("sp", "pool")}
        for e in ENGS:
            cnt = 0
            cur = newsem("s" + e)
            dcnt = [0] * ndma_sems
            nd = 0
            pending = []
            for o in per_eng[e]:
                if o.is_dma:
                    j = nd % ndma_sems
                    nd += 1
                    dcnt[j] += 16
                    o.sem, o.val, o.dslot = dsem[e][j], dcnt[j], j
                elif o.inc:
                    if cnt >= SEM_LIMIT:
                        cur = newsem("s" + e)
                        cnt = 0
                    cnt += 1
                    o.sem, o.val = cur, cnt
                    for p in pending:
                        p.sem, p.val, p.closer = cur, cnt, o.seq
                    pending = []
                else:
                    pending.append(o)
            assert not pending, "trailing no-inc ops on " + e
        blk = st.enter_context(nc.Block())

        def make(e):
            def body(engine):
                waited = {}
                last_dma = {}

                def need(p):
                    if waited.get(p.sem, 0) >= p.val:
                        return
                    waited[p.sem] = p.val
                    engine.wait_ge(p.sem, p.val)

                for o in per_eng[e]:
                    for d in sorted(o.deps):
                        p = ops[d]
                        if p.eng == e and not p.is_dma and not o.is_dma:
                            if e != "pe" and o.eidx - p.eidx <= 2:
                                need(p)
                            continue
                        assert p.closer < o.seq, (p.name, o.name)
                        need(p)
                    if o.is_dma:
                        j = o.dslot
                        if j in last_dma:
                            need(last_dma[j])
                        last_dma[j] = o
                        o.fn(engine).then_inc(o.sem, 16)
                    else:
                        ins = o.fn(engine)
                        if o.inc:
                            ins.then_inc(o.sem, 1)
                for pv in last_dma.values():
                    need(pv)
            return body

        blk.tensor(make("pe"))
        blk.scalar(make("act"))
        blk.vector(make("dve"))
        blk.gpsimd(make("pool"))
        blk.sync(make("sp"))


def _t5_bucket(rel):
    half, max_exact = 16, 8
    ret = np.where(rel > 0, half, 0)
    n = np.abs(rel)
    nf = np.maximum(n, 1).astype(np.float32)
    large = max_exact + (np.log(nf / np.float32(max_exact)) / np.float32(math.log(128 / max_exact))
                         * np.float32(half - max_exact)).astype(np.int32)
    large = np.minimum(large, half - 1)
    return ret + np.where(n < max_exact, n, large)


def _pool_mats():
    pm = np.zeros((128, 20, 128), np.float32)
    for g, w in enumerate((2, 4, 8, 16)):
        for v in range(5):
            t = {0: 5, 1: 5, 2: 5, 3: 0, 4: NT - 1}[v]
            for i in range(128):
                gi = t * 128 + i
                lo = min(max(gi - w // 2, 0), S)
                hi = min(max(gi + (w - w // 2), 0), S)
                cnt = float(hi - lo)
                for gj in range(lo, hi):
                    tj, j = divmod(gj, 128)
                    rel = tj - t
                    if v in (0, 3, 4) and rel == 0:
                        pm[j, g * 5 + v, i] += 1.0 / cnt
                    elif v == 1 and rel == -1:
                        pm[j, g * 5 + v, i] += 1.0 / cnt
                    elif v == 2 and rel == 1:
                        pm[j, g * 5 + v, i] += 1.0 / cnt
                if v in (0, 3, 4):
                    pm[i, g * 5 + v, i] -= 1.0
    return pm


def _host_consts(inp):
    c = {}
    bc = lambda a: np.ascontiguousarray(np.broadcast_to(a, (128,) + a.shape)).astype(np.float32)
    c["nwb"] = np.ascontiguousarray(np.stack([bc(inp["norm_w"][0]), bc(inp["norm_w"][1]),
                                               bc(inp["final_norm_w"])], 0))
    c["ident"] = np.eye(128, dtype=np.float32)
    p = np.arange(128)[:, None]
    cc = np.arange(MW)[None, :]
    bidx = _t5_bucket(p - cc + 512)
    rb = np.asarray(inp["rel_bias"], np.float32)
    c["bmaster"] = np.ascontiguousarray(rb[bidx].transpose(0, 2, 1))
    c["cfar"] = bc(np.concatenate([rb[31], rb[15]]))
    half = 32
    theta = (1.0 / (np.float32(10000.0) ** np.linspace(0.0, 1.0, half, dtype=np.float32))).astype(np.float32)
    ang = (np.arange(S, dtype=np.float32)[:, None] * theta[None, :]).astype(np.float32)
    cs = np.stack([np.cos(ang), np.sin(ang)], 0).astype(np.float32)
    c["cs"] = np.ascontiguousarray(cs.reshape(2, NT, 128, half).transpose(2, 0, 1, 3))
    m = np.arange(128, dtype=np.float32)[:, None]
    n = np.arange(128, dtype=np.float32)[None, :]
    c["retc"] = np.ascontiguousarray(np.stack([np.maximum(n - m, 0), np.maximum(m - n, 0)], 1))
    i = np.arange(128, dtype=np.float32)
    c["tokidx"] = np.ascontiguousarray(np.stack([i + 1, 128 - i, 127 - i, i], 1))
    c["dlam"] = bc(np.asarray(inp["diff_lambda"], np.float32).reshape(2, 256))
    c["subw"] = bc(np.asarray(inp["diff_subln_w"], np.float32))
    c["rdl"] = bc(np.asarray(inp["ret_decay_logit"], np.float32).reshape(16))
    c["pscale"] = bc(np.asarray(inp["pool_scale"], np.float32))
    c["poolw"] = np.ascontiguousarray(np.asarray(inp["pool_w"], np.float32))
    c["poolm"] = _pool_mats()
    return c


def build_nc(nseq=2, nlayers=2, phases="ABC"):
    nc = bass.Bass("TRN2", target_bir_lowering=False)
    din = lambda name, shape: nc.dram_tensor(name, list(shape), F32, kind="ExternalInput").ap()
    x_d = din("x", [nseq, NT, 128, D])
    win_d = din("w_in", [2, D, DIN])
    wout_d = din("w_out", [2, DMIX, D])
    nwb_d = din("nwb", [3, 128, D])
    ident_d = din("ident", [128, 128])
    bm_d = din("bmaster", [128, 4, MW])
    cfar_d = din("cfar", [128, 8])
    cs_d = din("cs", [128, 2, NT, 32])
    retc_d = din("retc", [128, 2, 128])
    tokidx_d = din("tokidx", [128, 4])
    dlam_d = din("dlam", [128, 2, 256])
    subw_d = din("subw", [128, 2, 128])
    rdl_d = din("rdl", [128, 16])
    pscale_d = din("pscale", [128, 2, 512])
    poolw_d = din("poolw", [2, 4, 128, 128])
    poolm_d = din("poolm", [128, 20, 128])
    y_d = nc.dram_tensor("y", [nseq, NT, 128, D], F32, kind="ExternalOutput").ap()

    P = Prog()
    with contextlib.ExitStack() as st:
        def sb(name, shape, dt=F32):
            return st.enter_context(nc.sbuf_tensor("s_" + name, list(shape), dt))

        xres = sb("xres", [128, NT, D])
        hT = sb("hT", [128, 8, S], BF16)
        NSLOT = 6
        wslot = [sb("wslot%d" % i, [128, 2048], BF16) for i in range(NSLOT)]
        bmast = sb("bmast", [128, 4, MW], BF16)
        identb = sb("identb", [128, 128], BF16)
        cfar = sb("cfar", [128, 8])
        zero1 = sb("zero1", [128, 1])
        mhalf = sb("mhalf", [128, 16])
        tokidx = sb("tokidx", [128, 4])
        rdl = sb("rdl", [128, 16])
        poolw = sb("poolw", [128, 8, 128], BF16)
        poolm = sb("poolm", [128, 20, 128], BF16)
        lg = sb("lg", [128, 16])
        tqk = sb("tqk", [128, 2, 16])
        gsc = sb("gsc", [128, 16])
        D2T = sb("D2T", [128, 2, 4, 128])
        W2 = sb("W2", [128, 2, 2, 128])
        psh = sb("psh", [128, 2, 512])
        nlam = sb("nlam", [128, 2])
        sm = sb("sm", [128, 64])
        ss = sb("ss", [128, 2, NT])
        hb = [sb("hb0", [128, D], BF16)] * 2
        junk = sb("junk", [128, D], BF16)
        mixed = [sb("mixed%d" % i, [128, 256], BF16) for i in range(2)]
        mT = [sb("mT%d" % i, [128, 2, 128], BF16) for i in range(2)]
        gate = sb("gate", [128, NT, 256], BF16)
        th = [sb("th%d" % i, [128, 256]) for i in range(2)]
        ARENA = 43 * 1024 + 512
        arena = sb("arena", [128, ARENA], mybir.dt.uint8)

        def carve(off, shape, dt):
            n = int(np.prod(shape))
            bpe = 2 if dt == BF16 else 4
            ap = arena[:, off:off + n * bpe].bitcast(dt)
            if len(shape) > 1:
                names = " ".join("d%d" % i for i in range(len(shape)))
                kw = {"d%d" % i: shape[i] for i in range(1, len(shape))}
                ap = ap.rearrange("p (%s) -> p %s" % (names, names), **kw)
            return ap, off + n * bpe

        o = 0
        nw, o = carve(o, [D], F32)
        identf, o = carve(o, [128], F32)
        retc, o = carve(o, [2, 128], F32)
        dlam, o = carve(o, [2, 256], F32)
        subw, o = carve(o, [2, 128], F32)
        scr, o = carve(o, [256], F32)
        o = 0
        qT, o = carve(o, [2, S], BF16)
        kT, o = carve(o, [2, S], BF16)
        V1, o = carve(o, [NT, 2, 130], BF16)
        oa, o = carve(o, [4, 2, 128], F32)
        oan, o = carve(o, [4, 2, 128], F32)
        PT0, o = carve(o, [2, 512], BF16)
        PT1, o = carve(o, [2, 512], BF16)
        PT = [PT0, PT1]
        tmpA, o = carve(o, [128], F32)
        accS, o = carve(o, [8, 129], F32)
        assert o <= ARENA, o
        o = 0
        qkr, o = carve(o, [NT, 256], BF16)
        vB, o = carve(o, [NT, 256], BF16)
        Rst, o = carve(o, [NT, 2, 128], BF16)
        cs, o = carve(o, [2, NT, 32], F32)
        Rcur, o = carve(o, [2, 128], F32)
        qk32 = []
        rtmp = []
        kdec = []
        qdec = []
        TT = []
        TTq2 = []
        innerT = []
        btmp = []
        for _i in range(2):
            a_, o = carve(o, [256], F32); qk32.append(a_)
            a_, o = carve(o, [4, 128], F32); rtmp.append(a_)
            a_, o = carve(o, [2, 2, 64], BF16); kdec.append(a_)
            a_, o = carve(o, [2, 2, 64], BF16); qdec.append(a_)
            a_, o = carve(o, [3, 128], BF16); TT.append(a_)
            a_, o = carve(o, [2, 128], BF16); TTq2.append(a_)
            a_, o = carve(o, [2, 128], BF16); innerT.append(a_)
            a_, o = carve(o, [2, 128], F32); btmp.append(a_)
        assert o <= ARENA, o
        o = 0
        uC, o = carve(o, [NT, 256], BF16)
        pooledT = []
        ytmp = []
        for _i in range(2):
            a_, o = carve(o, [2, 128], BF16); pooledT.append(a_)
            a_, o = carve(o, [256], F32); ytmp.append(a_)
        assert o <= ARENA, o

        psall = st.enter_context(nc.psum_tensor("psall", [128, 8, 512], F32))
        bank = [psall[:, i, :] for i in range(8)]

        def PS(*idx):
            return [("ps", i) for i in idx]


        dma = P.dma
        dma(lambda e: e.dma_start(out=identf, in_=ident_d), writes=["identf"])
        dma(lambda e: e.dma_start(out=cfar[:], in_=cfar_d), writes=["cfar"])
        dma(lambda e: e.dma_start(out=retc, in_=retc_d), writes=["retc"])
        dma(lambda e: e.dma_start(out=tokidx[:], in_=tokidx_d), writes=["tokidx"])
        dma(lambda e: e.dma_start(out=dlam, in_=dlam_d), writes=["dlam"])
        dma(lambda e: e.dma_start(out=subw, in_=subw_d), writes=["subw"])
        dma(lambda e: e.dma_start(out=rdl[:], in_=rdl_d), writes=["rdl"])
        dma(lambda e: e.dma_start(out=psh[:], in_=pscale_d), writes=["psh"])
        dma(lambda e: e.dma_start(out=poolw[:].rearrange("c (l g) d -> c l g d", l=2),
                                  in_=poolw_d.rearrange("l g c d -> c l g d")),
            writes=["poolw"], queue="pool")
        dma(lambda e: e.dma_start(out=poolm[:], in_=poolm_d), writes=["poolm"], queue="pool")
        dma(lambda e: e.dma_start(out=bmast[:], in_=bm_d), writes=["bmast"], queue="pool")
        for h_ in range(4):
            P.op("act", lambda e, h_=h_: e.activation(out=bmast[:, h_, :], in_=bmast[:, h_, :], func=AF.Exp), reads=["bmast"], writes=["bmast"])

        P.op("pool", lambda e: e.memset(mhalf[:], -0.5), writes=["mhalf"])
        P.op("pool", lambda e: e.memset(zero1[:], 0.0), writes=["zero1"])
        P.op("dve", lambda e: e.tensor_copy(out=identb[:], in_=identf), reads=["identf"], writes=["identb"])

        lam_init = [0.8 - 0.6 * math.exp(-0.3 * l) for l in range(2)]
        for l in range(2):
            P.op("dve", lambda e, l=l: e.tensor_tensor(out=scr[:, 0:64], in0=dlam[:, l, 0:64], in1=dlam[:, l, 64:128], op=ALU.mult),
                 reads=["dlam"], writes=["junk"])
            P.op("dve", lambda e, l=l: e.tensor_tensor(out=scr[:, 64:128], in0=dlam[:, l, 128:192], in1=dlam[:, l, 192:256], op=ALU.mult),
                 reads=["dlam"], writes=["junk"])
            P.op("dve", lambda e: e.reduce_sum(out=sm[:, 0:2], in_=scr[:, 0:128].rearrange("p (a b) -> p a b", a=2), axis=AX.X),
                 reads=["junk"], writes=["sm"])
            P.op("act", lambda e: e.activation(out=sm[:, 2:4], in_=sm[:, 0:2], func=AF.Exp), reads=["sm"], writes=["sm"])
            P.op("dve", lambda e, l=l: e.tensor_scalar(out=sm[:, 4:5], in0=sm[:, 3:4], scalar1=-lam_init[l], scalar2=None, op0=ALU.add),
                 reads=["sm"], writes=["sm"])
            P.op("dve", lambda e, l=l: e.tensor_tensor(out=nlam[:, l:l + 1], in0=sm[:, 4:5], in1=sm[:, 2:3], op=ALU.subtract),
                 reads=["sm"], writes=["nlam"])
            for hh in range(2):
                P.op("dve", lambda e, l=l, hh=hh: e.tensor_scalar(out=W2[:, l, hh, :], in0=subw[:, l, :],
                                                                  scalar1=(1.0 - lam_init[l]) * 0.5, scalar2=None, op0=ALU.mult),
                     reads=["subw"], writes=["W2"])
            P.op("dve", lambda e, l=l: e.tensor_scalar(out=psh[:, l, :], in0=psh[:, l, :], scalar1=0.5, scalar2=None, op0=ALU.mult),
                 reads=["psh"], writes=["psh"])
        P.op("act", lambda e: e.activation(out=sm[:, 16:32], in_=rdl[:], func=AF.Exp, scale=-1.0), reads=["rdl"], writes=["sm"])
        P.op("dve", lambda e: e.tensor_scalar(out=sm[:, 32:48], in0=sm[:, 16:32], scalar1=1.0, scalar2=None, op0=ALU.add),
             reads=["sm"], writes=["sm"])
        P.op("act", lambda e: e.activation(out=sm[:, 16:32], in_=sm[:, 32:48], func=AF.Ln), reads=["sm"], writes=["sm"])
        P.op("dve", lambda e: e.tensor_scalar(out=lg[:], in0=sm[:, 16:32], scalar1=-1.0, scalar2=None, op0=ALU.mult),
             reads=["sm"], writes=["lg"])
        P.op("act", lambda e: e.activation(out=gsc[:], in_=lg[:], func=AF.Exp, scale=128.0), reads=["lg"], writes=["gsc"])
        for l in range(2):
            lf = l * 8
            lb = l * 8 + 4
            for (dst, src, ti) in ((0, lf, 0), (4, lb, 1), (8, lf, 2), (12, lb, 3)):
                P.op("dve", lambda e, l=l, dst=dst, src=src, ti=ti: e.tensor_scalar(
                    out=sm[:, 48 + dst:52 + dst], in0=lg[:, src:src + 4], scalar1=tokidx[:, ti:ti + 1], scalar2=None, op0=ALU.mult),
                    reads=["lg", "tokidx"], writes=["sm"])
            P.op("act", lambda e, l=l: e.activation(out=tqk[:, l, :], in_=sm[:, 48:64], func=AF.Exp), reads=["sm"], writes=["tqk"])
            for h in range(4):
                P.op("dve", lambda e, l=l, h=h: e.tensor_scalar(out=scr[:, 0:128], in0=retc[:, 0, :], scalar1=lg[:, l * 8 + h:l * 8 + h + 1],
                                                                scalar2=None, op0=ALU.mult), reads=["retc", "lg"], writes=["junk"])
                P.op("dve", lambda e, l=l, h=h: e.scalar_tensor_tensor(out=scr[:, 128:256], in0=retc[:, 1, :],
                                                                       scalar=lg[:, l * 8 + 4 + h:l * 8 + 5 + h], in1=scr[:, 0:128],
                                                                       op0=ALU.mult, op1=ALU.add), reads=["retc", "lg", "junk"], writes=["junk"])
                P.op("act", lambda e, l=l, h=h: e.activation(out=D2T[:, l, h, :], in_=scr[:, 128:256], func=AF.Exp),
                     reads=["junk"], writes=["D2T"])

        P.fence()
        wstate = {"n": 0}

        preloaded = {}
        sched = {"list": [], "i": 0}

        def phase_loads(ph, l, hp):
            if ph == "A":
                return [("in", l, OFF["aq"] + hp * 256), ("in", l, OFF["ak"] + hp * 256), ("in", l, OFF["av"] + hp * 256),
                        ("in", l, OFF["ag"] + hp * 256), ("out", l, hp * 256)]
            if ph == "B":
                return [("inqk", l, OFF["bq"] + hp * 128), ("in", l, OFF["bv"] + hp * 256), ("in", l, OFF["bg"] + hp * 256),
                        ("out", l, 512 + hp * 256)]
            return [("in", l, OFF["cu"] + hp * 256), ("in", l, OFF["cg"] + hp * 256), ("out", l, 1024 + hp * 256)]

        def hoist_next():
            i = sched["i"] + 1
            if i < len(sched["list"]):
                ph, l, hp = sched["list"][i]
                for k in phase_loads(ph, l, hp):
                    if k not in preloaded:
                        preloaded[k] = _load_w(*k)

        def load_w(kind, l, c0):
            k = (kind, l, c0)
            if k in preloaded:
                return preloaded.pop(k)
            return _load_w(kind, l, c0)

        def _load_w(kind, l, c0):
            n0 = len(P.ops)
            r = _load_w2(kind, l, c0)
            for o_ in P.ops[n0:]:
                o_.nofence = True
            return r

        def _load_w2(kind, l, c0):
            i = wstate["n"] % NSLOT
            wstate["n"] += 1
            sl = wslot[i]
            key = ("wslot", i)
            if kind == "in":
                v = sl[:].rearrange("p (k c) -> p k c", k=8)
                dma(lambda e: e.dma_start(out=v, in_=win_d[l, :, c0:c0 + 256].rearrange("(k p) c -> p k c", p=128)),
                    writes=[key], queue="pool")
            elif kind == "inqk":
                v = sl[:].rearrange("p (k c) -> p k c", k=8)
                dma(lambda e: e.dma_start(out=v[:, :, 0:128], in_=win_d[l, :, c0:c0 + 128].rearrange("(k p) c -> p k c", p=128)),
                    writes=[key], queue="pool")
                dma(lambda e: e.dma_start(out=v[:, :, 128:256], in_=win_d[l, :, c0 + 256:c0 + 384].rearrange("(k p) c -> p k c", p=128)),
                    reads=[key], writes=[key], queue="pool")
            else:
                v = sl[:].rearrange("p (k c) -> p k c", k=2)
                dma(lambda e: e.dma_start(out=v, in_=wout_d[l, c0:c0 + 256, :].rearrange("(k p) c -> p k c", p=128)),
                    writes=[key], queue="pool")
            return v, key

        cnt = {"tok": 0, "tail": 0, "ev": 0}

        def proj_tok(wv, wkey, t, bk, ncols=256):
            for k in range(8):
                P.op("pe", lambda e, k=k: e.matmul(bank[bk][:, 0:ncols], lhsT=hT[:, k, t * 128:(t + 1) * 128], rhs=wv[:, k, 0:ncols],
                                                   start=(k == 0), stop=(k == 7)),
                     reads=["hT", wkey], writes=PS(bk), inc=(k == 7))

        def gate_evac(bk, t, i):
            P.op("act", lambda e: e.activation(out=th[i][:], in_=bank[bk][:, 0:256], func=AF.Tanh, scale=0.5),
                 reads=PS(bk), writes=[("th", i)])
            P.op("dve", lambda e: e.scalar_tensor_tensor(out=gate[:, t, :], in0=th[i][:], scalar=1.0, in1=bank[bk][:, 0:256],
                                                         op0=ALU.add, op1=ALU.mult),
                 reads=PS(bk) + [("th", i)], writes=["gate"])

        def tail(mx, mxkey, t, wov, wokey, tb=None, tbk=7, ob=None, i=None):
            if i is None:
                i = cnt["tail"] % 2
                cnt["tail"] += 1
            if tb is None:
                tb = bank[7].bitcast(BF16)[:, 0:256]
            for k in range(2):
                P.op("pe", lambda e, k=k: e.transpose(out=tb[:, k * 128:(k + 1) * 128], in_=mx[:, k * 128:(k + 1) * 128], identity=identb[:]),
                     reads=[mxkey, "identb"], writes=PS(tbk), inc=(k == 1))
            P.op("act", lambda e: e.copy(out=mT[i][:].rearrange("p a b -> p (a b)"), in_=tb), reads=PS(tbk), writes=[("mT", i)])
            if ob is None:
                for half in range(2):
                    for k in range(2):
                        P.op("pe", lambda e, k=k, half=half: e.matmul(bank[7], lhsT=mT[i][:, k, :], rhs=wov[:, k, half * 512:(half + 1) * 512],
                                                                      start=(k == 0), stop=(k == 1)),
                             reads=[("mT", i), wokey], writes=PS(7), inc=(k == 1))
                    P.op("dve", lambda e, half=half: e.tensor_tensor(out=xres[:, t, half * 512:(half + 1) * 512],
                                                                     in0=xres[:, t, half * 512:(half + 1) * 512], in1=bank[7], op=ALU.add),
                         reads=PS(7) + [("x", t)], writes=[("x", t)])
            else:
                for half in range(2):
                    for k in range(2):
                        P.op("pe", lambda e, k=k, half=half: e.matmul(bank[ob + half], lhsT=mT[i][:, k, :], rhs=wov[:, k, half * 512:(half + 1) * 512],
                                                                      start=(k == 0), stop=(k == 1)),
                             reads=[("mT", i), wokey], writes=PS(ob + half), inc=(k == 1))
                P.op("dve", lambda e: e.tensor_tensor(out=xres[:, t, :], in0=xres[:, t, :],
                                                      in1=psall[:, ob:ob + 2, :].rearrange("p a b -> p (a b)"), op=ALU.add),
                     reads=PS(ob, ob + 1) + [("x", t)], writes=[("x", t)])

        def rmsnorm_to_hT(l):
            dma(lambda e: e.dma_start(out=nw, in_=nwb_d[l]), writes=["nw"])
            for t in range(NT):
                P.op("act", lambda e, t=t: e.activation(out=junk[:], in_=xres[:, t, :], func=AF.Square, accum_out=ss[:, 0, t:t + 1]),
                     reads=[("x", t)], writes=["junk", ("ss", t)])
            P.op("dve", lambda e: e.tensor_scalar(out=ss[:, 1, :], in0=ss[:, 0, :], scalar1=1.0 / D, scalar2=EPS, op0=ALU.mult, op1=ALU.add),
                 reads=[("ss", t) for t in range(NT)], writes=["ss1"])
            P.op("pool", lambda e: e.tensor_tensor(out=ss[:, 0, :], in0=ss[:, 1, :], in1=mhalf[:, 0:NT], op=ALU.pow),
                 reads=["ss1", "mhalf"], writes=[("ss", t) for t in range(NT)])
            for t in range(NT):
                i = 0
                P.op("dve", lambda e, t=t, i=i: e.scalar_tensor_tensor(out=hb[i][:], in0=xres[:, t, :], scalar=ss[:, 0, t:t + 1], in1=nw,
                                                                       op0=ALU.mult, op1=ALU.mult),
                     reads=[("x", t), ("ss", t), "nw"], writes=[("hb", i)])
                bk = 5 + (t % 2)
                for k in range(8):
                    P.op("pe", lambda e, k=k, i=i, bk=bk: e.transpose(out=bank[bk][:].bitcast(BF16)[:, k * 128:(k + 1) * 128],
                                                                      in_=hb[i][:, k * 128:(k + 1) * 128], identity=identb[:]),
                         reads=[("hb", i), "identb"], writes=PS(bk), inc=(k == 7))
                P.op("act", lambda e, t=t, bk=bk: e.copy(out=hT[:, :, t * 128:(t + 1) * 128],
                                                         in_=bank[bk][:].bitcast(BF16).rearrange("p (k c) -> p k c", k=8)),
                     reads=PS(bk), writes=["hT"])

        def phase_A(l, hp):
            P.op("pool", lambda e: e.memset(V1[:, :, :, 128:129], 1.0), writes=["V1"])
            wq, wqk = load_w("in", l, OFF["aq"] + hp * 256)
            wk, wkk = load_w("in", l, OFF["ak"] + hp * 256)
            wv_, wvk = load_w("in", l, OFF["av"] + hp * 256)
            n = 0
            for (wv, wkey, dst, dkey, scl) in ((wq, wqk, qT, "qT", 0.125), (wk, wkk, kT, "kT", 1.0)):
                for hh in range(2):
                    for tc in range(4):
                        bk = n % 4
                        n += 1
                        for k in range(8):
                            P.op("pe", lambda e, k=k, bk=bk, wv=wv, hh=hh, tc=tc: e.matmul(
                                bank[bk][:, :], lhsT=wv[:, k, hh * 128:(hh + 1) * 128], rhs=hT[:, k, tc * 512:(tc + 1) * 512],
                                start=(k == 0), stop=(k == 7)), reads=["hT", wkey], writes=PS(bk), inc=(k == 7))
                        if n % 2 == 0:
                            P.op("act", lambda e, bk=bk, dst=dst, hh=hh, tc=tc, scl=scl: e.mul(out=dst[:, hh, tc * 512:(tc + 1) * 512],
                                                                                             in_=bank[bk][:, :], mul=scl),
                                 reads=PS(bk), writes=[dkey])
                        else:
                            P.op("dve", lambda e, bk=bk, dst=dst, hh=hh, tc=tc, scl=scl: e.tensor_scalar(
                                out=dst[:, hh, tc * 512:(tc + 1) * 512], in0=bank[bk][:, :], scalar1=scl, scalar2=None, op0=ALU.mult),
                                reads=PS(bk), writes=[dkey])
            wg, wgk = load_w("in", l, OFF["ag"] + hp * 256)
            wo, wok = load_w("out", l, hp * 256)
            for t in range(NT):
                bk = t % 4
                proj_tok(wv_, wvk, t, bk)
                eng = "act" if t % 2 == 0 else "dve"
                if eng == "act":
                    P.op("act", lambda e, t=t, bk=bk: e.copy(out=V1[:, t, :, 0:128], in_=bank[bk][:, 0:256].rearrange("p (a b) -> p a b", a=2)),
                         reads=PS(bk), writes=["V1"])
                else:
                    P.op("dve", lambda e, t=t, bk=bk: e.tensor_copy(out=V1[:, t, :, 0:128], in_=bank[bk][:, 0:256].rearrange("p (a b) -> p a b", a=2)),
                         reads=PS(bk), writes=["V1"])
            for t in range(NT):
                bk = t % 4
                proj_tok(wg, wgk, t, bk)
                gate_evac(bk, t, t % 2)

            hoist_next()

            def acc(r, c0=0, c1=129):
                return bank[4 + r // 3][:, (r % 3) * 129 + c0:(r % 3) * 129 + c1]

            pend = []

            def sched_chunk_tail(qc):
                items = []

                def pre():
                    P.op("pool", lambda e: e.tensor_tensor(out=oan, in0=oa, in1=oa, op=ALU.mult), reads=["oa"], writes=["oan"])
                    P.op("dve", lambda e: e.reduce_sum(out=sm[:, 16:24], in_=oan.rearrange("p a b c -> p (a b) c"), axis=AX.X),
                         reads=["oan"], writes=["sm"])
                    P.op("dve", lambda e: e.tensor_scalar(out=sm[:, 24:32], in0=sm[:, 16:24], scalar1=1.0 / 128, scalar2=EPS, op0=ALU.mult, op1=ALU.add),
                         reads=["sm"], writes=["sm"])
                    P.op("pool", lambda e: e.tensor_tensor(out=sm[:, 16:24], in0=sm[:, 24:32], in1=mhalf[:, 0:8], op=ALU.pow),
                         reads=["sm", "mhalf"], writes=["sm"])
                    P.op("dve", lambda e: e.tensor_tensor(out=oan.rearrange("p a b c -> p (a b) c"), in0=oa.rearrange("p a b c -> p (a b) c"),
                                                          in1=sm[:, 16:24].unsqueeze(2).to_broadcast([128, 8, 128]), op=ALU.mult),
                         reads=["oa", "sm"], writes=["oan"])
                items.append((0, pre))
                tb = bank[7].bitcast(BF16)[:, 0:256]
                for qs in range(4):
                    t = qc * 4 + qs
                    i = qs % 2
                    g0 = 1 + 7 * qs

                    def T1(qs=qs, t=t, i=i):
                        P.op("pool", lambda e: e.tensor_tensor(out=oan[:, qs], in0=oan[:, qs], in1=W2[:, l], op=ALU.mult),
                             reads=["oan", "W2"], writes=["oan"])
                        P.op("dve", lambda e: e.tensor_tensor(out=mixed[i][:], in0=oan[:, qs].rearrange("p a b -> p (a b)"),
                                                              in1=gate[:, t, :], op=ALU.mult),
                             reads=["oan", "gate"], writes=[("mixed", i)])

                    def T2(i=i):
                        for k in range(2):
                            P.op("pe", lambda e, k=k: e.transpose(out=tb[:, k * 128:(k + 1) * 128], in_=mixed[i][:, k * 128:(k + 1) * 128],
                                                                  identity=identb[:]),
                                 reads=[("mixed", i), "identb"], writes=PS(7), inc=(k == 1))
                        P.op("dve", lambda e: e.tensor_copy(out=mT[i][:].rearrange("p a b -> p (a b)"), in_=tb), reads=PS(7), writes=[("mT", i)])

                    def T4(half, t=t, i=i):
                        for k in range(2):
                            P.op("pe", lambda e, k=k: e.matmul(bank[7], lhsT=mT[i][:, k, :], rhs=wo[:, k, half * 512:(half + 1) * 512],
                                                               start=(k == 0), stop=(k == 1)),
                                 reads=[("mT", i), wok], writes=PS(7), inc=(k == 1))
                        P.op("dve", lambda e: e.tensor_tensor(out=xres[:, t, half * 512:(half + 1) * 512],
                                                              in0=xres[:, t, half * 512:(half + 1) * 512], in1=bank[7], op=ALU.add),
                             reads=PS(7) + [("x", t)], writes=[("x", t)])
                    items += [(g0, T1), (g0 + 2, T2), (g0 + 4, lambda T4=T4: T4(0)), (g0 + 6, lambda T4=T4: T4(1))]
                return items

            for qc in range(4):
                for hh in range(2):
                    h = 2 * hp + hh
                    seq = []
                    for kt in range(NT):
                        d = kt - 4 * qc
                        near = -1 <= d <= 4
                        seq.append((kt, near, d))

                    def emit_qk(j, hh=hh, qc=qc, h=h):
                        kt, near, d = seq[j]
                        b0 = (j % 2) * 2
                        for m in range(2):
                            P.op("pe", lambda e, m=m, kt=kt, b0=b0: e.matmul(
                                bank[b0 + m][:, :], lhsT=kT[m * 64:(m + 1) * 64, hh, kt * 128:(kt + 1) * 128],
                                rhs=qT[m * 64:(m + 1) * 64, hh, qc * 512:(qc + 1) * 512], start=True, stop=True),
                                reads=["kT", "qT"], writes=PS(b0 + m), inc=(m == 1))

                    def emit_exp_pv(j, hh=hh, qc=qc, h=h):
                        kt, near, d = seq[j]
                        b0 = (j % 2) * 2
                        pt = PT[j % 2]
                        if near:
                            bias_ap = zero1[:, 0:1]
                        elif d > 4:
                            bias_ap = cfar[:, h:h + 1]
                        else:
                            bias_ap = cfar[:, 4 + h:5 + h]
                        for m in range(2):
                            P.op("act", lambda e, m=m: e.activation(out=pt[:, m, :], in_=bank[b0 + m][:, :], func=AF.Exp, bias=bias_ap, scale=1.0),
                                 reads=PS(b0 + m) + ["cfar", "zero1"], writes=[("PT", j % 2, m)])
                        if near:
                            base = 512 - 128 * d
                            P.op("dve", lambda e, base=base: e.tensor_tensor(
                                out=pt, in0=pt, in1=bmast[:, h, base:base + 512].unsqueeze(1).to_broadcast([128, 2, 512]), op=ALU.mult),
                                reads=["bmast"], writes=[("PT", j % 2, 0), ("PT", j % 2, 1)])
                        for m in range(2):
                            for qs in range(4):
                                r = m * 4 + qs
                                first = (kt == 0 and r % 3 == 0)
                                last = (kt == NT - 1 and m == 1 and qs == 3)
                                P.op("pe", lambda e, m=m, qs=qs, r=r, first=first: e.matmul(
                                    acc(r), lhsT=pt[:, m, qs * 128:(qs + 1) * 128], rhs=V1[:, kt, hh, 0:129],
                                    start=first, stop=(kt == NT - 1), skip_group_check=True),
                                    reads=[("PT", j % 2, m), "V1"], writes=PS(4 + r // 3), inc=(last or (m == 1 and qs == 3)))

                    emit_qk(0)
                    for j in range(NT):
                        if j + 1 < NT:
                            emit_qk(j + 1)
                        emit_exp_pv(j)
                        g_ = hh * NT + j
                        for it_ in [x for x in pend if x[0] == g_]:
                            pend.remove(it_)
                            it_[1]()
                    P.op("act", lambda e: e.copy(out=accS[:, 0:3, :].rearrange("p a b -> p (a b)"), in_=bank[4][:, 0:387]), reads=PS(4), writes=["accS0"])
                    P.op("dve", lambda e: e.tensor_copy(out=accS[:, 3:6, :].rearrange("p a b -> p (a b)"), in_=bank[5][:, 0:387]), reads=PS(5), writes=["accS1"])
                    P.op("act", lambda e: e.copy(out=accS[:, 6:8, :].rearrange("p a b -> p (a b)"), in_=bank[6][:, 0:258]), reads=PS(6), writes=["accS2"])
                    P.op("dve", lambda e: e.reciprocal(out=sm[:, 0:8], in_=accS[:, :, 128]), reads=["accS0", "accS1", "accS2"], writes=["sm"])
                    P.op("dve", lambda e: e.tensor_scalar(out=sm[:, 4:8], in0=sm[:, 4:8], scalar1=nlam[:, l:l + 1], scalar2=None, op0=ALU.mult),
                         reads=["sm", "nlam"], writes=["sm"])
                    for qs in range(4):
                        r1 = 4 + qs
                        P.op("dve", lambda e, qs=qs, r1=r1: e.tensor_scalar(out=tmpA, in0=accS[:, r1, 0:128], scalar1=sm[:, r1:r1 + 1],
                                                                            scalar2=None, op0=ALU.mult),
                             reads=["accS0", "accS1", "accS2", "sm"], writes=["tmpA"])
                        P.op("dve", lambda e, qs=qs, hh=hh: e.scalar_tensor_tensor(out=oa[:, qs, hh, :], in0=accS[:, qs, 0:128], scalar=sm[:, qs:qs + 1],
                                                                                  in1=tmpA, op0=ALU.mult, op1=ALU.add),
                             reads=["accS0", "accS1", "accS2", "sm", "tmpA"], writes=["oa"])
                for it_ in sorted(pend, key=lambda x: x[0]):
                    it_[1]()
                pend = sched_chunk_tail(qc)
            for it_ in sorted(pend, key=lambda x: x[0]):
                it_[1]()
            pend = []

        def phase_B(l, hp):
            wqk_, wqkk = load_w("inqk", l, OFF["bq"] + hp * 128)
            wv_, wvk = load_w("in", l, OFF["bv"] + hp * 256)
            wg, wgk = load_w("in", l, OFF["bg"] + hp * 256)
            wo, wok = load_w("out", l, 512 + hp * 256)
            dma(lambda e: e.dma_start(out=cs, in_=cs_d), writes=["cs"])
            for s2 in range(2):
                P.op("pool", lambda e, s2=s2: e.memset(TTq2[s2], 0.0), writes=[("TTq2", s2)])
            for t in range(NT):
                bk = t % 4
                i2 = t % 2
                proj_tok(wqk_, wqkk, t, bk)
                P.op("act", lambda e, bk=bk, i2=i2: e.copy(out=qk32[i2][:, 0:128], in_=bank[bk][:, 0:128]), reads=PS(bk), writes=[("qk32", i2)])
                P.op("act", lambda e, bk=bk, i2=i2: e.mul(out=qk32[i2][:, 128:256], in_=bank[bk][:, 128:256], mul=0.125),
                     reads=PS(bk), writes=[("qk32", i2)])
                src4 = qk32[i2].rearrange("p (a b c) -> p a b c", a=4, b=2)
                cos_b = cs[:, 0, t, :].unsqueeze(1).to_broadcast([128, 4, 32])
                sin_b = cs[:, 1, t, :].unsqueeze(1).to_broadcast([128, 4, 32])
                t1 = src4[:, :, 0, :]
                t2 = src4[:, :, 1, :]
                rt = [rtmp[i2][:, i].rearrange("p (a c) -> p a c", a=4) for i in range(4)]
                dst4 = qkr[:, t, :].rearrange("p (a b c) -> p a b c", a=4, b=2)
                P.op("pool", lambda e, t1=t1, cos_b=cos_b, rt=rt: e.tensor_tensor(out=rt[0], in0=t1, in1=cos_b, op=ALU.mult),
                     reads=[("qk32", i2), "cs"], writes=[("rtmp", i2, 0)])
                P.op("pool", lambda e, t2=t2, sin_b=sin_b, rt=rt: e.tensor_tensor(out=rt[1], in0=t2, in1=sin_b, op=ALU.mult),
                     reads=[("qk32", i2), "cs"], writes=[("rtmp", i2, 1)])
                P.op("dve", lambda e, t1=t1, sin_b=sin_b, rt=rt: e.tensor_tensor(out=rt[2], in0=t1, in1=sin_b, op=ALU.mult),
                     reads=[("qk32", i2), "cs"], writes=[("rtmp", i2, 2)])
                P.op("dve", lambda e, t2=t2, cos_b=cos_b, rt=rt: e.tensor_tensor(out=rt[3], in0=t2, in1=cos_b, op=ALU.mult),
                     reads=[("qk32", i2), "cs"], writes=[("rtmp", i2, 3)])
                P.op("pool", lambda e, rt=rt, dst4=dst4: e.tensor_tensor(out=dst4[:, :, 0, :], in0=rt[0], in1=rt[1], op=ALU.subtract),
                     reads=[("rtmp", i2, 0), ("rtmp", i2, 1)], writes=[("qkr", t, 0)])
                P.op("dve", lambda e, rt=rt, dst4=dst4: e.tensor_tensor(out=dst4[:, :, 1, :], in0=rt[2], in1=rt[3], op=ALU.add),
                     reads=[("rtmp", i2, 2), ("rtmp", i2, 3)], writes=[("qkr", t, 1)])
            for t in range(NT):
                bk = t % 4
                proj_tok(wv_, wvk, t, bk)
                if t % 2 == 0:
                    P.op("act", lambda e, t=t, bk=bk: e.copy(out=vB[:, t, :], in_=bank[bk][:, 0:256]), reads=PS(bk), writes=[("vB", t)])
                else:
                    P.op("dve", lambda e, t=t, bk=bk: e.tensor_copy(out=vB[:, t, :], in_=bank[bk][:, 0:256]), reads=PS(bk), writes=[("vB", t)])
            for t in range(NT):
                bk = t % 4
                proj_tok(wg, wgk, t, bk)
                gate_evac(bk, t, t % 2)
            hoist_next()
            QKR = lambda t: [("qkr", t, 0), ("qkr", t, 1)]

            def kvp(t):
                return bank[t // 2][:, (t % 2) * 256:(t % 2) * 256 + 256]

            for t in range(NT):
                i2 = t % 2
                kq = qkr[:, t, 128:256].rearrange("p (a c) -> p a c", a=2)
                for dr in range(2):
                    c0 = 8 + dr * 4 + 2 * hp
                    P.op("pool", lambda e, dr=dr, c0=c0, kq=kq, i2=i2: e.tensor_tensor(
                        out=kdec[i2][:, :, dr, :], in0=kq, in1=tqk[:, l, c0:c0 + 2].unsqueeze(2).to_broadcast([128, 2, 64]), op=ALU.mult),
                        reads=QKR(t) + ["tqk"], writes=[("kdec", i2)])
                for hh in range(2):
                    P.op("pe", lambda e, hh=hh, t=t, i2=i2: e.matmul(kvp(t)[:, hh * 128:(hh + 1) * 128], lhsT=kdec[i2][:, hh].rearrange("p a b -> p (a b)"),
                                                                    rhs=vB[:, t, hh * 128:(hh + 1) * 128], start=True, stop=True),
                         reads=[("kdec", i2), ("vB", t)], writes=PS(t // 2), inc=(hh == 1))

            P.op("pool", lambda e: e.memset(Rcur, 0.0), writes=[("Rcur", 0), ("Rcur", 64)])
            for n_ in range(NT):
                for (lo, hi, t, dr) in ((0, 64, n_, 0), (64, 128, NT - 1 - n_, 1)):
                    P.op("act", lambda e, t=t, lo=lo, hi=hi: e.copy(out=Rst[lo:hi, t], in_=Rcur[lo:hi]), reads=[("Rcur", lo)], writes=[("Rst", t, lo)])
                    if n_ == NT - 1:
                        continue
                    for hh in range(2):
                        gc = l * 8 + dr * 4 + 2 * hp + hh
                        P.op("dve", lambda e, lo=lo, hi=hi, t=t, hh=hh, gc=gc: e.scalar_tensor_tensor(
                            out=Rcur[lo:hi, hh, :], in0=Rcur[lo:hi, hh, :], scalar=gsc[lo:hi, gc:gc + 1],
                            in1=kvp(t)[lo:hi, hh * 128:(hh + 1) * 128], op0=ALU.mult, op1=ALU.add),
                            reads=PS(t // 2) + [("Rcur", lo), "gsc"], writes=[("Rcur", lo)])

            def S0(t):
                s2 = t % 2
                bTS = 2 * s2
                bO = 2 * s2 + 1
                qq = qkr[:, t, 0:128].rearrange("p (a c) -> p a c", a=2)
                for dr in range(2):
                    c0 = dr * 4 + 2 * hp
                    P.op("pool", lambda e, dr=dr, c0=c0, qq=qq, s2=s2: e.tensor_tensor(
                        out=qdec[s2][:, :, dr, :], in0=qq, in1=tqk[:, l, c0:c0 + 2].unsqueeze(2).to_broadcast([128, 2, 64]), op=ALU.mult),
                        reads=QKR(t) + ["tqk"], writes=[("qdec", s2)])
                bT = bank[bTS].bitcast(BF16)
                srcs = [qdec[s2][:, 0].rearrange("p a b -> p (a b)"), qdec[s2][:, 1].rearrange("p a b -> p (a b)"),
                        qkr[:, t, 128:256], qkr[:, t, 0:128]]
                for i4 in range(4):
                    P.op("pe", lambda e, i4=i4, srcs=srcs, bT=bT: e.transpose(out=bT[:, i4 * 128:(i4 + 1) * 128], in_=srcs[i4], identity=identb[:]),
                         reads=[("qdec", s2), "identb"] + QKR(t), writes=PS(bTS), inc=(i4 == 3))
                P.op("act", lambda e, bT=bT, s2=s2: e.copy(out=TT[s2].rearrange("p a b -> p (a b)"), in_=bT[:, 0:384]),
                     reads=PS(bTS), writes=[("TT", s2)])
                for hh in range(2):
                    P.op("act", lambda e, bT=bT, s2=s2, hh=hh: e.copy(out=TTq2[s2][hh * 64:(hh + 1) * 64, hh, :],
                                                                       in_=bT[hh * 64:(hh + 1) * 64, 384:512]),
                         reads=PS(bTS), writes=[("TTq2", s2)])
                P.op("pe", lambda e, s2=s2, bTS=bTS: e.matmul(bank[bTS][:, 256:512], lhsT=TT[s2][:, 2, :],
                                                             rhs=TTq2[s2].rearrange("p a b -> p (a b)"), start=True, stop=True),
                     reads=[("TT", s2), ("TTq2", s2)], writes=PS(bTS), inc=True)
                P.op("dve", lambda e, s2=s2, bTS=bTS: e.tensor_tensor(out=innerT[s2], in0=bank[bTS][:, 256:512].rearrange("p (a b) -> p a b", a=2),
                                                                     in1=D2T[:, l, 2 * hp:2 * hp + 2, :], op=ALU.mult),
                     reads=PS(bTS) + ["D2T"], writes=[("innerT", s2)])
                for hh in range(2):
                    P.op("pe", lambda e, hh=hh, t=t, s2=s2, bO=bO: e.matmul(bank[bO][:, hh * 128:(hh + 1) * 128], lhsT=innerT[s2][:, hh, :],
                                                                           rhs=vB[:, t, hh * 128:(hh + 1) * 128], start=True, stop=False),
                         reads=[("innerT", s2), ("vB", t)], writes=PS(bO), inc=False)
                    P.op("pe", lambda e, hh=hh, t=t, s2=s2, bO=bO: e.matmul(bank[bO][:, hh * 128:(hh + 1) * 128], lhsT=TT[s2][:, hh, :],
                                                                           rhs=Rst[:, t, hh, :], start=False, stop=True),
                         reads=[("TT", s2), ("Rst", t, 0), ("Rst", t, 64)], writes=PS(bO), inc=(hh == 1))

            def S1(t):
                s2 = t % 2
                bO = 2 * s2 + 1
                sc = 32 + 8 * s2
                for hh in range(2):
                    P.op("act", lambda e, hh=hh, bO=bO, sc=sc: e.activation(out=junk[:, hh * 128:(hh + 1) * 128], in_=bank[bO][:, hh * 128:(hh + 1) * 128],
                                                                          func=AF.Square, accum_out=sm[:, sc + hh:sc + hh + 1]),
                         reads=PS(bO), writes=["junk", ("smB", s2)])
                P.op("dve", lambda e, sc=sc: e.tensor_scalar(out=sm[:, sc + 2:sc + 4], in0=sm[:, sc:sc + 2], scalar1=4.0 / 128, scalar2=4.0 * EPS,
                                                             op0=ALU.mult, op1=ALU.add), reads=[("smB", s2)], writes=[("smB", s2)])
                P.op("pool", lambda e, sc=sc: e.tensor_tensor(out=sm[:, sc + 4:sc + 6], in0=sm[:, sc + 2:sc + 4], in1=mhalf[:, 0:2], op=ALU.pow),
                     reads=[("smB", s2), "mhalf"], writes=[("smB", s2)])
                i = t % 2
                P.op("dve", lambda e, s2=s2, bO=bO, sc=sc: e.tensor_tensor(out=btmp[s2], in0=bank[bO][:, 0:256].rearrange("p (a b) -> p a b", a=2),
                                                                          in1=sm[:, sc + 4:sc + 6].unsqueeze(2).to_broadcast([128, 2, 128]), op=ALU.mult),
                     reads=PS(bO) + [("smB", s2)], writes=[("btmp", s2)])
                P.op("pool", lambda e, i=i, t=t, s2=s2: e.tensor_tensor(out=mixed[i][:], in0=btmp[s2].rearrange("p a b -> p (a b)"), in1=gate[:, t, :], op=ALU.mult),
                     reads=[("btmp", s2), "gate"], writes=[("mixed", i)])

            def S2(t):
                s2 = t % 2
                bO = 2 * s2 + 1
                i = t % 2
                tail(mixed[i], ("mixed", i), t, wo, wok, tb=bank[bO].bitcast(BF16)[:, 512:768], tbk=bO, ob=4 + 2 * s2, i=i)

            for it in range(NT + 2):
                if it < NT:
                    S0(it)
                if 0 <= it - 1 < NT:
                    S1(it - 1)
                if 0 <= it - 2 < NT:
                    S2(it - 2)

        def phase_C(l, gp):
            wu, wuk = load_w("in", l, OFF["cu"] + gp * 256)
            wg, wgk = load_w("in", l, OFF["cg"] + gp * 256)
            wo, wok = load_w("out", l, 1024 + gp * 256)
            for t in range(NT):
                bk = t % 4
                proj_tok(wu, wuk, t, bk)
                if t % 2 == 0:
                    P.op("act", lambda e, t=t, bk=bk: e.copy(out=uC[:, t, :], in_=bank[bk][:, 0:256]), reads=PS(bk), writes=[("uC", t)])
                else:
                    P.op("dve", lambda e, t=t, bk=bk: e.tensor_copy(out=uC[:, t, :], in_=bank[bk][:, 0:256]), reads=PS(bk), writes=[("uC", t)])
            for t in range(NT):
                bk = t % 4
                proj_tok(wg, wgk, t, bk)
                gate_evac(bk, t, t % 2)
            hoist_next()

            def S0(t):
                s2 = t % 2
                bP = 2 * s2
                bY = 2 * s2 + 1
                for gg in range(2):
                    g = 2 * gp + gg
                    parts = [(t, g * 5 + (3 if t == 0 else 4 if t == NT - 1 else 0))]
                    if t > 0:
                        parts.append((t - 1, g * 5 + 1))
                    if t < NT - 1:
                        parts.append((t + 1, g * 5 + 2))
                    for pi_, (tj, mi) in enumerate(parts):
                        P.op("pe", lambda e, gg=gg, tj=tj, mi=mi, pi_=pi_, np_=len(parts), bP=bP: e.matmul(
                            bank[bP][:, gg * 128:(gg + 1) * 128], lhsT=uC[:, tj, gg * 128:(gg + 1) * 128], rhs=poolm[:, mi, :],
                            start=(pi_ == 0), stop=(pi_ == np_ - 1)),
                            reads=[("uC", tj), "poolm"], writes=PS(bP), inc=(gg == 1 and pi_ == len(parts) - 1))
                P.op("act", lambda e, bP=bP, s2=s2: e.copy(out=pooledT[s2].rearrange("p a b -> p (a b)"), in_=bank[bP][:, 0:256]),
                     reads=PS(bP), writes=[("pooledT", s2)])
                for gg in range(2):
                    g = 2 * gp + gg
                    P.op("pe", lambda e, gg=gg, g=g, s2=s2, bY=bY: e.matmul(bank[bY][:, gg * 128:(gg + 1) * 128], lhsT=pooledT[s2][:, gg, :],
                                                                           rhs=poolw[:, l * 4 + g, :], start=True, stop=True),
                         reads=[("pooledT", s2), "poolw"], writes=PS(bY), inc=(gg == 1))

            def S1(t):
                s2 = t % 2
                bY = 2 * s2 + 1
                i = t % 2
                P.op("dve", lambda e, s2=s2, bY=bY: e.tensor_tensor(out=ytmp[s2], in0=bank[bY][:, 0:256], in1=psh[:, l, gp * 256:(gp + 1) * 256], op=ALU.mult),
                     reads=PS(bY) + ["psh"], writes=[("ytmp", s2)])
                P.op("pool", lambda e, t=t, i=i, s2=s2: e.tensor_tensor(out=mixed[i][:], in0=ytmp[s2], in1=gate[:, t, :], op=ALU.mult),
                     reads=[("ytmp", s2), "gate"], writes=[("mixed", i)])

            def S2(t):
                s2 = t % 2
                bY = 2 * s2 + 1
                i = t % 2
                tail(mixed[i], ("mixed", i), t, wo, wok, tb=bank[bY].bitcast(BF16)[:, 512:768], tbk=bY, ob=4 + 2 * s2, i=i)

            for it in range(NT + 2):
                if it < NT:
                    S0(it)
                if 0 <= it - 1 < NT:
                    S1(it - 1)
                if 0 <= it - 2 < NT:
                    S2(it - 2)

        for s_ in range(nseq):
            for l in range(nlayers):
                for ph in phases:
                    for hp in range(2):
                        sched["list"].append((ph, l, hp))
        sched["i"] = -1
        hoist_next()
        fns = {"A": phase_A, "B": phase_B, "C": phase_C}
        pi = 0
        for s in range(nseq):
            for t in range(NT):
                dma(lambda e, s=s, t=t: e.dma_start(out=xres[:, t, :], in_=x_d[s, t]), writes=[("x", t)])
            for l in range(nlayers):
                P.fence()
                rmsnorm_to_hT(l)
                for ph in phases:
                    for hp in range(2):
                        if hp == 0:
                            P.fence()
                        sched["i"] = pi
                        fns[ph](l, hp)
                        pi += 1
            P.fence()
            dma(lambda e: e.dma_start(out=nw, in_=nwb_d[2]), writes=["nw"])
            for t in range(NT):
                P.op("act", lambda e, t=t: e.activation(out=junk[:], in_=xres[:, t, :], func=AF.Square, accum_out=ss[:, 0, t:t + 1]),
                     reads=[("x", t)], writes=["junk", ("ss", t)])
            P.op("dve", lambda e: e.tensor_scalar(out=ss[:, 1, :], in0=ss[:, 0, :], scalar1=1.0 / D, scalar2=EPS, op0=ALU.mult, op1=ALU.add),
                 reads=[("ss", t) for t in range(NT)], writes=["ss1"])
            P.op("pool", lambda e: e.tensor_tensor(out=ss[:, 0, :], in0=ss[:, 1, :], in1=mhalf[:, 0:NT], op=ALU.pow),
                 reads=["ss1", "mhalf"], writes=[("ss", t) for t in range(NT)])
            for t in range(NT):
                P.op("dve", lambda e, t=t: e.scalar_tensor_tensor(out=xres[:, t, :], in0=xres[:, t, :], scalar=ss[:, 0, t:t + 1], in1=nw,
                                                                  op0=ALU.mult, op1=ALU.mult),
                     reads=[("x", t), ("ss", t), "nw"], writes=[("x", t)])
                dma(lambda e, s=s, t=t: e.dma_start(out=y_d[s, t], in_=xres[:, t, :]), reads=[("x", t)], writes=[("y", s, t)])
        P.emit(nc, st)
    return nc


_NC_CACHE = {}


def kernel(**inputs):
    x = np.asarray(inputs["x"], np.float32)
    B = x.shape[0]
    ncores = 8
    nseq = B // ncores
    consts = _host_consts(inputs)
    key = (nseq,)
    if key not in _NC_CACHE:
        _NC_CACHE[key] = build_nc(nseq=nseq)
    nc = _NC_CACHE[key]
    w_in = np.ascontiguousarray(np.asarray(inputs["w_in"], np.float32))
    w_out = np.ascontiguousarray(np.asarray(inputs["w_out"], np.float32))
    in_maps = []
    for c in range(ncores):
        m = dict(consts)
        m["x"] = np.ascontiguousarray(x[c * nseq:(c + 1) * nseq].reshape(nseq, NT, 128, D))
        m["w_in"] = w_in
        m["w_out"] = w_out
        in_maps.append(m)
    res = run_bass_kernel_spmd(nc, in_maps, core_ids=list(range(ncores)))
    out = np.concatenate([np.asarray(r["y"]).reshape(nseq, S, D) for r in res.results], axis=0)
    return out.astype(np.float32)
```

```python
import contextlib
import math
import numpy as np
import concourse.bass as bass
import concourse.mybir as mybir
from concourse.bass_utils import run_bass_kernel_spmd

F32 = mybir.dt.float32
BF16 = mybir.dt.bfloat16
ALU = mybir.AluOpType
AF = mybir.ActivationFunctionType
AX = mybir.AxisListType

D = 1024
S = 2048
NT = S // 128
DIN = 4608
DMIX = 1536
EPS = 1e-6
OFF = dict(aq=0, ak=512, av=1024, ag=1536, bq=2048, bk=2304, bv=2560, bg=3072, cu=3584, cg=4096)
MW = 1152

ENGS = ("pe", "act", "dve", "pool", "sp")
CUT = 99
SEM_LIMIT = 30000


class Op:
    __slots__ = ("eng", "fn", "deps", "inc", "is_dma", "seq", "eidx", "sem", "val", "name",
                 "closer", "dslot", "nofence")


class Prog:
    def __init__(self):
        self.ops = []
        self.last_w = {}
        self.readers = {}
        self.pend = {e: set() for e in ENGS}
        self.fence_idx = 0

    def fence(self):
        deps = set()
        last = {}
        for o in self.ops[self.fence_idx:]:
            if o.is_dma:
                if not o.nofence:
                    deps.add(o.seq)
            else:
                last[o.eng] = o.seq
        deps |= set(last.values())
        for e in ENGS:
            self.pend[e] |= deps
        self.fence_idx = len(self.ops)

    def _add(self, eng, fn, reads, writes, inc, is_dma, name):
        o = Op()
        o.eng, o.fn, o.inc, o.is_dma, o.name = eng, fn, inc, is_dma, name
        o.seq = len(self.ops)
        o.closer = o.seq
        o.nofence = False
        deps = set(self.pend[eng])
        self.pend[eng] = set()
        reads = list(reads)
        writes = list(writes)
        ex = [r for r in reads if isinstance(r, tuple) and r[0] == "ps"]
        reads = [r for r in reads if r not in ex]
        writes = writes + [r for r in ex if r not in writes]
        for r in reads:
            if r in self.last_w:
                deps.add(self.last_w[r])
        for w in writes:
            if w in self.last_w:
                deps.add(self.last_w[w])
            for rd in self.readers.get(w, ()):
                deps.add(rd)
        for r in reads:
            self.readers.setdefault(r, []).append(o.seq)
        for w in writes:
            self.last_w[w] = o.seq
            self.readers[w] = []
        deps.discard(o.seq)
        o.deps = deps
        self.ops.append(o)
        return o

    def op(self, eng, fn, reads=(), writes=(), inc=True, name=""):
        return self._add(eng, fn, reads, writes, inc, False, name)

    def dma(self, fn, reads=(), writes=(), queue="sp", name=""):
        return self._add(queue, fn, reads, writes, True, True, name)

    def emit(self, nc, st, ndma_sems=8):
        ops = self.ops
        per_eng = {e: [] for e in ENGS}
        for o in ops:
            o.eidx = len(per_eng[o.eng])
            per_eng[o.eng].append(o)
        nsem = [0]

        def newsem(tag):
            nsem[0] += 1
            return st.enter_context(nc.semaphore("%s_%d" % (tag, nsem[0])))

        dsem = {q: [newsem("d" + q) for _ in range(ndma_sems)] for q in ("sp", "pool")}
        for e in ENGS:
            cnt = 0
            cur = newsem("s" + e)
            dcnt = [0] * ndma_sems
            nd = 0
            pending = []
            for o in per_eng[e]:
                if o.is_dma:
                    j = nd % ndma_sems
                    nd += 1
                    dcnt[j] += 16
                    o.sem, o.val, o.dslot = dsem[e][j], dcnt[j], j
                elif o.inc:
                    if cnt >= SEM_LIMIT:
                        cur = newsem("s" + e)
                        cnt = 0
                    cnt += 1
                    o.sem, o.val = cur, cnt
                    for p in pending:
                        p.sem, p.val, p.closer = cur, cnt, o.seq
                    pending = []
                else:
                    pending.append(o)
            assert not pending, "trailing no-inc ops on " + e
        blk = st.enter_context(nc.Block())

        def make(e):
            def body(engine):
                waited = {}
                last_dma = {}

                def need(p):
                    if waited.get(p.sem, 0) >= p.val:
                        return
                    waited[p.sem] = p.val
                    engine.wait_ge(p.sem, p.val)

                for o in per_eng[e]:
                    for d in sorted(o.deps):
                        p = ops[d]
                        if p.eng == e and not p.is_dma and not o.is_dma:
                            if e != "pe" and o.eidx - p.eidx <= 2:
                                need(p)
                            continue
                        assert p.closer < o.seq, (p.name, o.name)
                        need(p)
                    if o.is_dma:
                        j = o.dslot
                        if j in last_dma:
                            need(last_dma[j])
                        last_dma[j] = o
                        o.fn(engine).then_inc(o.sem, 16)
                    else:
                        ins = o.fn(engine)
                        if o.inc:
                            ins.then_inc(o.sem, 1)
                for pv in last_dma.values():
                    need(pv)
            return body

        blk.tensor(make("pe"))
        blk.scalar(make("act"))
        blk.vector(make("dve"))
        blk.gpsimd(make("pool"))
        blk.sync(make("sp"))


def _t5_bucket(rel):
    half, max_exact = 16, 8
    ret = np.where(rel > 0, half, 0)
    n = np.abs(rel)
    nf = np.maximum(n, 1).astype(np.float32)
    large = max_exact + (np.log(nf / np.float32(max_exact)) / np.float32(math.log(128 / max_exact))
                         * np.float32(half - max_exact)).astype(np.int32)
    large = np.minimum(large, half - 1)
    return ret + np.where(n < max_exact, n, large)


def _pool_mats():
    pm = np.zeros((128, 20, 128), np.float32)
    for g, w in enumerate((2, 4, 8, 16)):
        for v in range(5):
            t = {0: 5, 1: 5, 2: 5, 3: 0, 4: NT - 1}[v]
            for i in range(128):
                gi = t * 128 + i
                lo = min(max(gi - w // 2, 0), S)
                hi = min(max(gi + (w - w // 2), 0), S)
                cnt = float(hi - lo)
                for gj in range(lo, hi):
                    tj, j = divmod(gj, 128)
                    rel = tj - t
                    if v in (0, 3, 4) and rel == 0:
                        pm[j, g * 5 + v, i] += 1.0 / cnt
                    elif v == 1 and rel == -1:
                        pm[j, g * 5 + v, i] += 1.0 / cnt
                    elif v == 2 and rel == 1:
                        pm[j, g * 5 + v, i] += 1.0 / cnt
                if v in (0, 3, 4):
                    pm[i, g * 5 + v, i] -= 1.0
    return pm


def _host_consts(inp):
    c = {}
    bc = lambda a: np.ascontiguousarray(np.broadcast_to(a, (128,) + a.shape)).astype(np.float32)
    c["nwb"] = np.ascontiguousarray(np.stack([bc(inp["norm_w"][0]), bc(inp["norm_w"][1]),
                                               bc(inp["final_norm_w"])], 0))
    c["ident"] = np.eye(128, dtype=np.float32)
    p = np.arange(128)[:, None]
    cc = np.arange(MW)[None, :]
    bidx = _t5_bucket(p - cc + 512)
    rb = np.asarray(inp["rel_bias"], np.float32)
    c["bmaster"] = np.ascontiguousarray(rb[bidx].transpose(0, 2, 1))
    c["cfar"] = bc(np.concatenate([rb[31], rb[15]]))
    half = 32
    theta = (1.0 / (np.float32(10000.0) ** np.linspace(0.0, 1.0, half, dtype=np.float32))).astype(np.float32)
    ang = (np.arange(S, dtype=np.float32)[:, None] * theta[None, :]).astype(np.float32)
    cs = np.stack([np.cos(ang), np.sin(ang)], 0).astype(np.float32)
    c["cs"] = np.ascontiguousarray(cs.reshape(2, NT, 128, half).transpose(2, 0, 1, 3))
    m = np.arange(128, dtype=np.float32)[:, None]
    n = np.arange(128, dtype=np.float32)[None, :]
    c["retc"] = np.ascontiguousarray(np.stack([np.maximum(n - m, 0), np.maximum(m - n, 0)], 1))
    i = np.arange(128, dtype=np.float32)
    c["tokidx"] = np.ascontiguousarray(np.stack([i + 1, 128 - i, 127 - i, i], 1))
    c["dlam"] = bc(np.asarray(inp["diff_lambda"], np.float32).reshape(2, 256))
    c["subw"] = bc(np.asarray(inp["diff_subln_w"], np.float32))
    c["rdl"] = bc(np.asarray(inp["ret_decay_logit"], np.float32).reshape(16))
    c["pscale"] = bc(np.asarray(inp["pool_scale"], np.float32))
    c["poolw"] = np.ascontiguousarray(np.asarray(inp["pool_w"], np.float32))
    c["poolm"] = _pool_mats()
    return c


def build_nc(nseq=2, nlayers=2, phases="ABC"):
    nc = bass.Bass("TRN2", target_bir_lowering=False)
    din = lambda name, shape: nc.dram_tensor(name, list(shape), F32, kind="ExternalInput").ap()
    x_d = din("x", [nseq, NT, 128, D])
    win_d = din("w_in", [2, D, DIN])
    wout_d = din("w_out", [2, DMIX, D])
    nwb_d = din("nwb", [3, 128, D])
    ident_d = din("ident", [128, 128])
    bm_d = din("bmaster", [128, 4, MW])
    cfar_d = din("cfar", [128, 8])
    cs_d = din("cs", [128, 2, NT, 32])
    retc_d = din("retc", [128, 2, 128])
    tokidx_d = din("tokidx", [128, 4])
    dlam_d = din("dlam", [128, 2, 256])
    subw_d = din("subw", [128, 2, 128])
    rdl_d = din("rdl", [128, 16])
    pscale_d = din("pscale", [128, 2, 512])
    poolw_d = din("poolw", [2, 4, 128, 128])
    poolm_d = din("poolm", [128, 20, 128])
    y_d = nc.dram_tensor("y", [nseq, NT, 128, D], F32, kind="ExternalOutput").ap()

    P = Prog()
    with contextlib.ExitStack() as st:
        def sb(name, shape, dt=F32):
            return st.enter_context(nc.sbuf_tensor("s_" + name, list(shape), dt))

        xres = sb("xres", [128, NT, D])
        hT = sb("hT", [128, 8, S], BF16)
        NSLOT = 6
        wslot = [sb("wslot%d" % i, [128, 2048], BF16) for i in range(NSLOT)]
        bmast = sb("bmast", [128, 4, MW], BF16)
        identb = sb("identb", [128, 128], BF16)
        cfar = sb("cfar", [128, 8])
        zero1 = sb("zero1", [128, 1])
        mhalf = sb("mhalf", [128, 16])
        tokidx = sb("tokidx", [128, 4])
        rdl = sb("rdl", [128, 16])
        poolw = sb("poolw", [128, 8, 128], BF16)
        poolm = sb("poolm", [128, 20, 128], BF16)
        lg = sb("lg", [128, 16])
        tqk = sb("tqk", [128, 2, 16])
        gsc = sb("gsc", [128, 16])
        D2T = sb("D2T", [128, 2, 4, 128])
        W2 = sb("W2", [128, 2, 2, 128])
        psh = sb("psh", [128, 2, 512])
        nlam = sb("nlam", [128, 2])
        sm = sb("sm", [128, 64])
        ss = sb("ss", [128, 2, NT])
        hb = [sb("hb0", [128, D], BF16)] * 2
        junk = sb("junk", [128, D], BF16)
        mixed = [sb("mixed%d" % i, [128, 256], BF16) for i in range(2)]
        mT = [sb("mT%d" % i, [128, 2, 128], BF16) for i in range(2)]
        gate = sb("gate", [128, NT, 256], BF16)
        th = [sb("th%d" % i, [128, 256]) for i in range(2)]
        ARENA = 43 * 1024 + 512
        arena = sb("arena", [128, ARENA], mybir.dt.uint8)

        def carve(off, shape, dt):
            n = int(np.prod(shape))
            bpe = 2 if dt == BF16 else 4
            ap = arena[:, off:off + n * bpe].bitcast(dt)
            if len(shape) > 1:
                names = " ".join("d%d" % i for i in range(len(shape)))
                kw = {"d%d" % i: shape[i] for i in range(1, len(shape))}
                ap = ap.rearrange("p (%s) -> p %s" % (names, names), **kw)
            return ap, off + n * bpe

        o = 0
        nw, o = carve(o, [D], F32)
        identf, o = carve(o, [128], F32)
        retc, o = carve(o, [2, 128], F32)
        dlam, o = carve(o, [2, 256], F32)
        subw, o = carve(o, [2, 128], F32)
        scr, o = carve(o, [256], F32)
        o = 0
        qT, o = carve(o, [2, S], BF16)
        kT, o = carve(o, [2, S], BF16)
        V1, o = carve(o, [NT, 2, 130], BF16)
        oa, o = carve(o, [4, 2, 128], F32)
        oan, o = carve(o, [4, 2, 128], F32)
        PT0, o = carve(o, [2, 512], BF16)
        PT1, o = carve(o, [2, 512], BF16)
        PT = [PT0, PT1]
        tmpA, o = carve(o, [128], F32)
        accS, o = carve(o, [8, 129], F32)
        assert o <= ARENA, o
        o = 0
        qkr, o = carve(o, [NT, 256], BF16)
        vB, o = carve(o, [NT, 256], BF16)
        Rst, o = carve(o, [NT, 2, 128], BF16)
        cs, o = carve(o, [2, NT, 32], F32)
        Rcur, o = carve(o, [2, 128], F32)
        qk32 = []
        rtmp = []
        kdec = []
        qdec = []
        TT = []
        TTq2 = []
        innerT = []
        btmp = []
        for _i in range(2):
            a_, o = carve(o, [256], F32); qk32.append(a_)
            a_, o = carve(o, [4, 128], F32); rtmp.append(a_)
            a_, o = carve(o, [2, 2, 64], BF16); kdec.append(a_)
            a_, o = carve(o, [2, 2, 64], BF16); qdec.append(a_)
            a_, o = carve(o, [2, 128], BF16); TTq2.append(a_)
            a_, o = carve(o, [2, 128], BF16); innerT.append(a_)
            a_, o = carve(o, [2, 128], F32); btmp.append(a_)
        for _i in range(3):
            a_, o = carve(o, [3, 128], BF16); TT.append(a_)
        assert o <= ARENA, o
        o = 0
        uC, o = carve(o, [NT, 256], BF16)
        pooledT = []
        ytmp = []
        for _i in range(2):
            a_, o = carve(o, [2, 128], BF16); pooledT.append(a_)
            a_, o = carve(o, [256], F32); ytmp.append(a_)
        assert o <= ARENA, o

        psall = st.enter_context(nc.psum_tensor("psall", [128, 8, 512], F32))
        bank = [psall[:, i, :] for i in range(8)]

        def PS(*idx):
            return [("ps", i) for i in idx]


        dma = P.dma
        dma(lambda e: e.dma_start(out=identf, in_=ident_d), writes=["identf"])
        dma(lambda e: e.dma_start(out=cfar[:], in_=cfar_d), writes=["cfar"])
        dma(lambda e: e.dma_start(out=retc, in_=retc_d), writes=["retc"])
        dma(lambda e: e.dma_start(out=tokidx[:], in_=tokidx_d), writes=["tokidx"])
        dma(lambda e: e.dma_start(out=dlam, in_=dlam_d), writes=["dlam"])
        dma(lambda e: e.dma_start(out=subw, in_=subw_d), writes=["subw"])
        dma(lambda e: e.dma_start(out=rdl[:], in_=rdl_d), writes=["rdl"])
        dma(lambda e: e.dma_start(out=psh[:], in_=pscale_d), writes=["psh"])
        dma(lambda e: e.dma_start(out=poolw[:].rearrange("c (l g) d -> c l g d", l=2),
                                  in_=poolw_d.rearrange("l g c d -> c l g d")),
            writes=["poolw"], queue="pool")
        dma(lambda e: e.dma_start(out=poolm[:], in_=poolm_d), writes=["poolm"], queue="pool")
        dma(lambda e: e.dma_start(out=bmast[:], in_=bm_d), writes=["bmast"], queue="pool")
        for h_ in range(4):
            P.op("act", lambda e, h_=h_: e.activation(out=bmast[:, h_, :], in_=bmast[:, h_, :], func=AF.Exp), reads=["bmast"], writes=["bmast"])

        P.op("pool", lambda e: e.memset(mhalf[:], -0.5), writes=["mhalf"])
        P.op("pool", lambda e: e.memset(zero1[:], 0.0), writes=["zero1"])
        P.op("dve", lambda e: e.tensor_copy(out=identb[:], in_=identf), reads=["identf"], writes=["identb"])

        lam_init = [0.8 - 0.6 * math.exp(-0.3 * l) for l in range(2)]
        for l in range(2):
            P.op("dve", lambda e, l=l: e.tensor_tensor(out=scr[:, 0:64], in0=dlam[:, l, 0:64], in1=dlam[:, l, 64:128], op=ALU.mult),
                 reads=["dlam"], writes=["junk"])
            P.op("dve", lambda e, l=l: e.tensor_tensor(out=scr[:, 64:128], in0=dlam[:, l, 128:192], in1=dlam[:, l, 192:256], op=ALU.mult),
                 reads=["dlam"], writes=["junk"])
            P.op("dve", lambda e: e.reduce_sum(out=sm[:, 0:2], in_=scr[:, 0:128].rearrange("p (a b) -> p a b", a=2), axis=AX.X),
                 reads=["junk"], writes=["sm"])
            P.op("act", lambda e: e.activation(out=sm[:, 2:4], in_=sm[:, 0:2], func=AF.Exp), reads=["sm"], writes=["sm"])
            P.op("dve", lambda e, l=l: e.tensor_scalar(out=sm[:, 4:5], in0=sm[:, 3:4], scalar1=-lam_init[l], scalar2=None, op0=ALU.add),
                 reads=["sm"], writes=["sm"])
            P.op("dve", lambda e, l=l: e.tensor_tensor(out=nlam[:, l:l + 1], in0=sm[:, 4:5], in1=sm[:, 2:3], op=ALU.subtract),
                 reads=["sm"], writes=["nlam"])
            for hh in range(2):
                P.op("dve", lambda e, l=l, hh=hh: e.tensor_scalar(out=W2[:, l, hh, :], in0=subw[:, l, :],
                                                                  scalar1=(1.0 - lam_init[l]) * 0.5, scalar2=None, op0=ALU.mult),
                     reads=["subw"], writes=["W2"])
            P.op("dve", lambda e, l=l: e.tensor_scalar(out=psh[:, l, :], in0=psh[:, l, :], scalar1=0.5, scalar2=None, op0=ALU.mult),
                 reads=["psh"], writes=["psh"])
        P.op("act", lambda e: e.activation(out=sm[:, 16:32], in_=rdl[:], func=AF.Exp, scale=-1.0), reads=["rdl"], writes=["sm"])
        P.op("dve", lambda e: e.tensor_scalar(out=sm[:, 32:48], in0=sm[:, 16:32], scalar1=1.0, scalar2=None, op0=ALU.add),
             reads=["sm"], writes=["sm"])
        P.op("act", lambda e: e.activation(out=sm[:, 16:32], in_=sm[:, 32:48], func=AF.Ln), reads=["sm"], writes=["sm"])
        P.op("dve", lambda e: e.tensor_scalar(out=lg[:], in0=sm[:, 16:32], scalar1=-1.0, scalar2=None, op0=ALU.mult),
             reads=["sm"], writes=["lg"])
        P.op("act", lambda e: e.activation(out=gsc[:], in_=lg[:], func=AF.Exp, scale=128.0), reads=["lg"], writes=["gsc"])
        for l in range(2):
            lf = l * 8
            lb = l * 8 + 4
            for (dst, src, ti) in ((0, lf, 0), (4, lb, 1), (8, lf, 2), (12, lb, 3)):
                P.op("dve", lambda e, l=l, dst=dst, src=src, ti=ti: e.tensor_scalar(
                    out=sm[:, 48 + dst:52 + dst], in0=lg[:, src:src + 4], scalar1=tokidx[:, ti:ti + 1], scalar2=None, op0=ALU.mult),
                    reads=["lg", "tokidx"], writes=["sm"])
            P.op("act", lambda e, l=l: e.activation(out=tqk[:, l, :], in_=sm[:, 48:64], func=AF.Exp), reads=["sm"], writes=["tqk"])
            for h in range(4):
                P.op("dve", lambda e, l=l, h=h: e.tensor_scalar(out=scr[:, 0:128], in0=retc[:, 0, :], scalar1=lg[:, l * 8 + h:l * 8 + h + 1],
                                                                scalar2=None, op0=ALU.mult), reads=["retc", "lg"], writes=["junk"])
                P.op("dve", lambda e, l=l, h=h: e.scalar_tensor_tensor(out=scr[:, 128:256], in0=retc[:, 1, :],
                                                                       scalar=lg[:, l * 8 + 4 + h:l * 8 + 5 + h], in1=scr[:, 0:128],
                                                                       op0=ALU.mult, op1=ALU.add), reads=["retc", "lg", "junk"], writes=["junk"])
                P.op("act", lambda e, l=l, h=h: e.activation(out=D2T[:, l, h, :], in_=scr[:, 128:256], func=AF.Exp),
                     reads=["junk"], writes=["D2T"])

        P.fence()
        wstate = {"n": 0}

        preloaded = {}
        sched = {"list": [], "i": 0}

        def phase_loads(ph, l, hp):
            if ph == "A":
                return [("in", l, OFF["aq"] + hp * 256), ("in", l, OFF["ak"] + hp * 256), ("in", l, OFF["av"] + hp * 256),
                        ("in", l, OFF["ag"] + hp * 256), ("out", l, hp * 256)]
            if ph == "B":
                return [("inqk", l, OFF["bq"] + hp * 128), ("in", l, OFF["bv"] + hp * 256), ("in", l, OFF["bg"] + hp * 256),
                        ("out", l, 512 + hp * 256)]
            return [("in", l, OFF["cu"] + hp * 256), ("in", l, OFF["cg"] + hp * 256), ("out", l, 1024 + hp * 256)]

        def hoist_next():
            i = sched["i"] + 1
            if i < len(sched["list"]):
                ph, l, hp = sched["list"][i]
                for k in phase_loads(ph, l, hp):
                    if k not in preloaded:
                        preloaded[k] = _load_w(*k)

        def load_w(kind, l, c0):
            k = (kind, l, c0)
            if k in preloaded:
                return preloaded.pop(k)
            return _load_w(kind, l, c0)

        def _load_w(kind, l, c0):
            n0 = len(P.ops)
            r = _load_w2(kind, l, c0)
            for o_ in P.ops[n0:]:
                o_.nofence = True
            return r

        def _load_w2(kind, l, c0):
            i = wstate["n"] % NSLOT
            wstate["n"] += 1
            sl = wslot[i]
            key = ("wslot", i)
            if kind == "in":
                v = sl[:].rearrange("p (k c) -> p k c", k=8)
                dma(lambda e: e.dma_start(out=v, in_=win_d[l, :, c0:c0 + 256].rearrange("(k p) c -> p k c", p=128)),
                    writes=[key], queue="pool")
            elif kind == "inqk":
                v = sl[:].rearrange("p (k c) -> p k c", k=8)
                dma(lambda e: e.dma_start(out=v[:, :, 0:128], in_=win_d[l, :, c0:c0 + 128].rearrange("(k p) c -> p k c", p=128)),
                    writes=[key], queue="pool")
                dma(lambda e: e.dma_start(out=v[:, :, 128:256], in_=win_d[l, :, c0 + 256:c0 + 384].rearrange("(k p) c -> p k c", p=128)),
                    reads=[key], writes=[key], queue="pool")
            else:
                v = sl[:].rearrange("p (k c) -> p k c", k=2)
                dma(lambda e: e.dma_start(out=v, in_=wout_d[l, c0:c0 + 256, :].rearrange("(k p) c -> p k c", p=128)),
                    writes=[key], queue="pool")
            return v, key

        cnt = {"tok": 0, "tail": 0, "ev": 0}

        def proj_tok(wv, wkey, t, bk, ncols=256):
            for k in range(8):
                P.op("pe", lambda e, k=k: e.matmul(bank[bk][:, 0:ncols], lhsT=hT[:, k, t * 128:(t + 1) * 128], rhs=wv[:, k, 0:ncols],
                                                   start=(k == 0), stop=(k == 7)),
                     reads=["hT", wkey], writes=PS(bk), inc=(k == 7))

        def gate_evac(bk, t, i):
            P.op("act", lambda e: e.activation(out=th[i][:], in_=bank[bk][:, 0:256], func=AF.Tanh, scale=0.5),
                 reads=PS(bk), writes=[("th", i)])
            P.op("dve", lambda e: e.scalar_tensor_tensor(out=gate[:, t, :], in0=th[i][:], scalar=1.0, in1=bank[bk][:, 0:256],
                                                         op0=ALU.add, op1=ALU.mult),
                 reads=PS(bk) + [("th", i)], writes=["gate"])

        def tail(mx, mxkey, t, wov, wokey, tb=None, tbk=7, ob=None, i=None):
            if i is None:
                i = cnt["tail"] % 2
                cnt["tail"] += 1
            if tb is None:
                tb = bank[7].bitcast(BF16)[:, 0:256]
            for k in range(2):
                P.op("pe", lambda e, k=k: e.transpose(out=tb[:, k * 128:(k + 1) * 128], in_=mx[:, k * 128:(k + 1) * 128], identity=identb[:]),
                     reads=[mxkey, "identb"], writes=PS(tbk), inc=(k == 1))
            P.op("act", lambda e: e.copy(out=mT[i][:].rearrange("p a b -> p (a b)"), in_=tb), reads=PS(tbk), writes=[("mT", i)])
            if ob is None:
                for half in range(2):
                    for k in range(2):
                        P.op("pe", lambda e, k=k, half=half: e.matmul(bank[7], lhsT=mT[i][:, k, :], rhs=wov[:, k, half * 512:(half + 1) * 512],
                                                                      start=(k == 0), stop=(k == 1)),
                             reads=[("mT", i), wokey], writes=PS(7), inc=(k == 1))
                    P.op("dve", lambda e, half=half: e.tensor_tensor(out=xres[:, t, half * 512:(half + 1) * 512],
                                                                     in0=xres[:, t, half * 512:(half + 1) * 512], in1=bank[7], op=ALU.add),
                         reads=PS(7) + [("x", t)], writes=[("x", t)])
            else:
                for half in range(2):
                    for k in range(2):
                        P.op("pe", lambda e, k=k, half=half: e.matmul(bank[ob + half], lhsT=mT[i][:, k, :], rhs=wov[:, k, half * 512:(half + 1) * 512],
                                                                      start=(k == 0), stop=(k == 1)),
                             reads=[("mT", i), wokey], writes=PS(ob + half), inc=(k == 1))
                P.op("dve", lambda e: e.tensor_tensor(out=xres[:, t, :], in0=xres[:, t, :],
                                                      in1=psall[:, ob:ob + 2, :].rearrange("p a b -> p (a b)"), op=ALU.add),
                     reads=PS(ob, ob + 1) + [("x", t)], writes=[("x", t)])

        def rmsnorm_to_hT(l):
            dma(lambda e: e.dma_start(out=nw, in_=nwb_d[l]), writes=["nw"])
            for t in range(NT):
                P.op("act", lambda e, t=t: e.activation(out=junk[:], in_=xres[:, t, :], func=AF.Square, accum_out=ss[:, 0, t:t + 1]),
                     reads=[("x", t)], writes=["junk", ("ss", t)])
            P.op("dve", lambda e: e.tensor_scalar(out=ss[:, 1, :], in0=ss[:, 0, :], scalar1=1.0 / D, scalar2=EPS, op0=ALU.mult, op1=ALU.add),
                 reads=[("ss", t) for t in range(NT)], writes=["ss1"])
            P.op("pool", lambda e: e.tensor_tensor(out=ss[:, 0, :], in0=ss[:, 1, :], in1=mhalf[:, 0:NT], op=ALU.pow),
                 reads=["ss1", "mhalf"], writes=[("ss", t) for t in range(NT)])
            for t in range(NT):
                i = 0
                P.op("dve", lambda e, t=t, i=i: e.scalar_tensor_tensor(out=hb[i][:], in0=xres[:, t, :], scalar=ss[:, 0, t:t + 1], in1=nw,
                                                                       op0=ALU.mult, op1=ALU.mult),
                     reads=[("x", t), ("ss", t), "nw"], writes=[("hb", i)])
                bk = 5 + (t % 2)
                for k in range(8):
                    P.op("pe", lambda e, k=k, i=i, bk=bk: e.transpose(out=bank[bk][:].bitcast(BF16)[:, k * 128:(k + 1) * 128],
                                                                      in_=hb[i][:, k * 128:(k + 1) * 128], identity=identb[:]),
                         reads=[("hb", i), "identb"], writes=PS(bk), inc=(k == 7))
                P.op("act", lambda e, t=t, bk=bk: e.copy(out=hT[:, :, t * 128:(t + 1) * 128],
                                                         in_=bank[bk][:].bitcast(BF16).rearrange("p (k c) -> p k c", k=8)),
                     reads=PS(bk), writes=["hT"])

        def phase_A(l, hp):
            P.op("pool", lambda e: e.memset(V1[:, :, :, 128:129], 1.0), writes=["V1"])
            wq, wqk = load_w("in", l, OFF["aq"] + hp * 256)
            wk, wkk = load_w("in", l, OFF["ak"] + hp * 256)
            wv_, wvk = load_w("in", l, OFF["av"] + hp * 256)
            n = 0
            for (wv, wkey, dst, dkey, scl) in ((wq, wqk, qT, "qT", 0.125), (wk, wkk, kT, "kT", 1.0)):
                for hh in range(2):
                    for tc in range(4):
                        bk = n % 4
                        n += 1
                        for k in range(8):
                            P.op("pe", lambda e, k=k, bk=bk, wv=wv, hh=hh, tc=tc: e.matmul(
                                bank[bk][:, :], lhsT=wv[:, k, hh * 128:(hh + 1) * 128], rhs=hT[:, k, tc * 512:(tc + 1) * 512],
                                start=(k == 0), stop=(k == 7)), reads=["hT", wkey], writes=PS(bk), inc=(k == 7))
                        if n % 2 == 0:
                            P.op("act", lambda e, bk=bk, dst=dst, hh=hh, tc=tc, scl=scl: e.mul(out=dst[:, hh, tc * 512:(tc + 1) * 512],
                                                                                             in_=bank[bk][:, :], mul=scl),
                                 reads=PS(bk), writes=[dkey])
                        else:
                            P.op("dve", lambda e, bk=bk, dst=dst, hh=hh, tc=tc, scl=scl: e.tensor_scalar(
                                out=dst[:, hh, tc * 512:(tc + 1) * 512], in0=bank[bk][:, :], scalar1=scl, scalar2=None, op0=ALU.mult),
                                reads=PS(bk), writes=[dkey])
            wg, wgk = load_w("in", l, OFF["ag"] + hp * 256)
            wo, wok = load_w("out", l, hp * 256)
            for t in range(NT):
                bk = t % 4
                proj_tok(wv_, wvk, t, bk)
                eng = "act" if t % 2 == 0 else "dve"
                if eng == "act":
                    P.op("act", lambda e, t=t, bk=bk: e.copy(out=V1[:, t, :, 0:128], in_=bank[bk][:, 0:256].rearrange("p (a b) -> p a b", a=2)),
                         reads=PS(bk), writes=["V1"])
                else:
                    P.op("dve", lambda e, t=t, bk=bk: e.tensor_copy(out=V1[:, t, :, 0:128], in_=bank[bk][:, 0:256].rearrange("p (a b) -> p a b", a=2)),
                         reads=PS(bk), writes=["V1"])
            for t in range(NT):
                bk = t % 4
                proj_tok(wg, wgk, t, bk)
                gate_evac(bk, t, t % 2)

            hoist_next()

            def acc(r, c0=0, c1=129):
                return bank[4 + r // 3][:, (r % 3) * 129 + c0:(r % 3) * 129 + c1]

            pend = []

            def sched_chunk_tail(qc):
                items = []

                def pre():
                    P.op("pool", lambda e: e.tensor_tensor(out=oan, in0=oa, in1=oa, op=ALU.mult), reads=["oa"], writes=["oan"])
                    P.op("dve", lambda e: e.reduce_sum(out=sm[:, 16:24], in_=oan.rearrange("p a b c -> p (a b) c"), axis=AX.X),
                         reads=["oan"], writes=["sm"])
                    P.op("dve", lambda e: e.tensor_scalar(out=sm[:, 24:32], in0=sm[:, 16:24], scalar1=1.0 / 128, scalar2=EPS, op0=ALU.mult, op1=ALU.add),
                         reads=["sm"], writes=["sm"])
                    P.op("pool", lambda e: e.tensor_tensor(out=sm[:, 16:24], in0=sm[:, 24:32], in1=mhalf[:, 0:8], op=ALU.pow),
                         reads=["sm", "mhalf"], writes=["sm"])
                    P.op("dve", lambda e: e.tensor_tensor(out=oan.rearrange("p a b c -> p (a b) c"), in0=oa.rearrange("p a b c -> p (a b) c"),
                                                          in1=sm[:, 16:24].unsqueeze(2).to_broadcast([128, 8, 128]), op=ALU.mult),
                         reads=["oa", "sm"], writes=["oan"])
                items.append((0, pre))
                tb = bank[7].bitcast(BF16)[:, 0:256]
                for qs in range(4):
                    t = qc * 4 + qs
                    i = qs % 2
                    g0 = 6 + 6 * qs

                    def T1(qs=qs, t=t, i=i):
                        P.op("pool", lambda e: e.tensor_tensor(out=oan[:, qs], in0=oan[:, qs], in1=W2[:, l], op=ALU.mult),
                             reads=["oan", "W2"], writes=["oan"])
                        P.op("dve", lambda e: e.tensor_tensor(out=mixed[i][:], in0=oan[:, qs].rearrange("p a b -> p (a b)"),
                                                              in1=gate[:, t, :], op=ALU.mult),
                             reads=["oan", "gate"], writes=[("mixed", i)])

                    def T2(i=i):
                        for k in range(2):
                            P.op("pe", lambda e, k=k: e.transpose(out=tb[:, k * 128:(k + 1) * 128], in_=mixed[i][:, k * 128:(k + 1) * 128],
                                                                  identity=identb[:]),
                                 reads=[("mixed", i), "identb"], writes=PS(7), inc=(k == 1))
                        P.op("dve", lambda e: e.tensor_copy(out=mT[i][:].rearrange("p a b -> p (a b)"), in_=tb), reads=PS(7), writes=[("mT", i)])

                    def T4(half, t=t, i=i):
                        for k in range(2):
                            P.op("pe", lambda e, k=k: e.matmul(bank[7], lhsT=mT[i][:, k, :], rhs=wo[:, k, half * 512:(half + 1) * 512],
                                                               start=(k == 0), stop=(k == 1)),
                                 reads=[("mT", i), wok], writes=PS(7), inc=(k == 1))
                        P.op("dve", lambda e: e.tensor_tensor(out=xres[:, t, half * 512:(half + 1) * 512],
                                                              in0=xres[:, t, half * 512:(half + 1) * 512], in1=bank[7], op=ALU.add),
                             reads=PS(7) + [("x", t)], writes=[("x", t)])
                    items += [(g0, T1), (g0 + 2, T2), (g0 + 4, lambda T4=T4: T4(0)), (g0 + 5, lambda T4=T4: T4(1))]
                return items

            for qc in range(4):
                for hh in range(2):
                    h = 2 * hp + hh
                    seq = []
                    for kt in range(NT):
                        d = kt - 4 * qc
                        near = -1 <= d <= 4
                        seq.append((kt, near, d))

                    def emit_qk(j, hh=hh, qc=qc, h=h):
                        kt, near, d = seq[j]
                        b0 = (j % 2) * 2
                        for m in range(2):
                            P.op("pe", lambda e, m=m, kt=kt, b0=b0: e.matmul(
                                bank[b0 + m][:, :], lhsT=kT[m * 64:(m + 1) * 64, hh, kt * 128:(kt + 1) * 128],
                                rhs=qT[m * 64:(m + 1) * 64, hh, qc * 512:(qc + 1) * 512], start=True, stop=True),
                                reads=["kT", "qT"], writes=PS(b0 + m), inc=(m == 1))

                    def emit_exp_pv(j, hh=hh, qc=qc, h=h):
                        kt, near, d = seq[j]
                        b0 = (j % 2) * 2
                        pt = PT[j % 2]
                        if near:
                            bias_ap = zero1[:, 0:1]
                        elif d > 4:
                            bias_ap = cfar[:, h:h + 1]
                        else:
                            bias_ap = cfar[:, 4 + h:5 + h]
                        for m in range(2):
                            P.op("act", lambda e, m=m: e.activation(out=pt[:, m, :], in_=bank[b0 + m][:, :], func=AF.Exp, bias=bias_ap, scale=1.0),
                                 reads=PS(b0 + m) + ["cfar", "zero1"], writes=[("PT", j % 2, m)])
                        if near:
                            base = 512 - 128 * d
                            for m in range(2):
                                P.op("dve", lambda e, base=base, m=m: e.tensor_tensor(
                                    out=pt[:, m, :], in0=pt[:, m, :], in1=bmast[:, h, base:base + 512], op=ALU.mult),
                                    reads=["bmast"], writes=[("PT", j % 2, m)])
                        for m in range(2):
                            for qs in range(4):
                                r = m * 4 + qs
                                first = (kt == 0 and r % 3 == 0)
                                last = (kt == NT - 1 and m == 1 and qs == 3)
                                P.op("pe", lambda e, m=m, qs=qs, r=r, first=first: e.matmul(
                                    acc(r), lhsT=pt[:, m, qs * 128:(qs + 1) * 128], rhs=V1[:, kt, hh, 0:129],
                                    start=first, stop=(kt == NT - 1), skip_group_check=True),
                                    reads=[("PT", j % 2, m), "V1"], writes=PS(4 + r // 3), inc=(last or (m == 1 and qs == 3)))

                    emit_qk(0)
                    for j in range(NT):
                        if j + 1 < NT:
                            emit_qk(j + 1)
                        emit_exp_pv(j)
                        g_ = hh * NT + j
                        for it_ in [x for x in pend if x[0] == g_]:
                            pend.remove(it_)
                            it_[1]()
                    P.op("act", lambda e: e.copy(out=accS[:, 0:3, :].rearrange("p a b -> p (a b)"), in_=bank[4][:, 0:387]), reads=PS(4), writes=["accS0"])
                    P.op("dve", lambda e: e.tensor_copy(out=accS[:, 3:6, :].rearrange("p a b -> p (a b)"), in_=bank[5][:, 0:387]), reads=PS(5), writes=["accS1"])
                    P.op("act", lambda e: e.copy(out=accS[:, 6:8, :].rearrange("p a b -> p (a b)"), in_=bank[6][:, 0:258]), reads=PS(6), writes=["accS2"])
                    P.op("dve", lambda e: e.reciprocal(out=sm[:, 0:8], in_=accS[:, :, 128]), reads=["accS0", "accS1", "accS2"], writes=["sm"])
                    P.op("dve", lambda e: e.tensor_scalar(out=sm[:, 4:8], in0=sm[:, 4:8], scalar1=nlam[:, l:l + 1], scalar2=None, op0=ALU.mult),
                         reads=["sm", "nlam"], writes=["sm"])
                    for qs in range(4):
                        r1 = 4 + qs
                        P.op("dve", lambda e, qs=qs, r1=r1: e.tensor_scalar(out=tmpA, in0=accS[:, r1, 0:128], scalar1=sm[:, r1:r1 + 1],
                                                                            scalar2=None, op0=ALU.mult),
                             reads=["accS0", "accS1", "accS2", "sm"], writes=["tmpA"])
                        P.op("dve", lambda e, qs=qs, hh=hh: e.scalar_tensor_tensor(out=oa[:, qs, hh, :], in0=accS[:, qs, 0:128], scalar=sm[:, qs:qs + 1],
                                                                                  in1=tmpA, op0=ALU.mult, op1=ALU.add),
                             reads=["accS0", "accS1", "accS2", "sm", "tmpA"], writes=["oa"])
                for it_ in sorted(pend, key=lambda x: x[0]):
                    it_[1]()
                pend = sched_chunk_tail(qc)
            for it_ in sorted(pend, key=lambda x: x[0]):
                it_[1]()
            pend = []

        def phase_B(l, hp):
            wqk_, wqkk = load_w("inqk", l, OFF["bq"] + hp * 128)
            wv_, wvk = load_w("in", l, OFF["bv"] + hp * 256)
            wg, wgk = load_w("in", l, OFF["bg"] + hp * 256)
            wo, wok = load_w("out", l, 512 + hp * 256)
            dma(lambda e: e.dma_start(out=cs, in_=cs_d), writes=["cs"])
            for s2 in range(2):
                P.op("pool", lambda e, s2=s2: e.memset(TTq2[s2], 0.0), writes=[("TTq2", s2)])
            for t in range(NT):
                bk = t % 4
                i2 = t % 2
                proj_tok(wqk_, wqkk, t, bk)
                P.op("act", lambda e, bk=bk, i2=i2: e.copy(out=qk32[i2][:, 0:128], in_=bank[bk][:, 0:128]), reads=PS(bk), writes=[("qk32", i2)])
                P.op("act", lambda e, bk=bk, i2=i2: e.mul(out=qk32[i2][:, 128:256], in_=bank[bk][:, 128:256], mul=0.125),
                     reads=PS(bk), writes=[("qk32", i2)])
                src4 = qk32[i2].rearrange("p (a b c) -> p a b c", a=4, b=2)
                cos_b = cs[:, 0, t, :].unsqueeze(1).to_broadcast([128, 4, 32])
                sin_b = cs[:, 1, t, :].unsqueeze(1).to_broadcast([128, 4, 32])
                t1 = src4[:, :, 0, :]
                t2 = src4[:, :, 1, :]
                rt = [rtmp[i2][:, i].rearrange("p (a c) -> p a c", a=4) for i in range(4)]
                dst4 = qkr[:, t, :].rearrange("p (a b c) -> p a b c", a=4, b=2)
                P.op("pool", lambda e, t1=t1, cos_b=cos_b, rt=rt: e.tensor_tensor(out=rt[0], in0=t1, in1=cos_b, op=ALU.mult),
                     reads=[("qk32", i2), "cs"], writes=[("rtmp", i2, 0)])
                P.op("pool", lambda e, t2=t2, sin_b=sin_b, rt=rt: e.tensor_tensor(out=rt[1], in0=t2, in1=sin_b, op=ALU.mult),
                     reads=[("qk32", i2), "cs"], writes=[("rtmp", i2, 1)])
                P.op("dve", lambda e, t1=t1, sin_b=sin_b, rt=rt: e.tensor_tensor(out=rt[2], in0=t1, in1=sin_b, op=ALU.mult),
                     reads=[("qk32", i2), "cs"], writes=[("rtmp", i2, 2)])
                P.op("dve", lambda e, t2=t2, cos_b=cos_b, rt=rt: e.tensor_tensor(out=rt[3], in0=t2, in1=cos_b, op=ALU.mult),
                     reads=[("qk32", i2), "cs"], writes=[("rtmp", i2, 3)])
                P.op("pool", lambda e, rt=rt, dst4=dst4: e.tensor_tensor(out=dst4[:, :, 0, :], in0=rt[0], in1=rt[1], op=ALU.subtract),
                     reads=[("rtmp", i2, 0), ("rtmp", i2, 1)], writes=[("qkr", t, 0)])
                P.op("dve", lambda e, rt=rt, dst4=dst4: e.tensor_tensor(out=dst4[:, :, 1, :], in0=rt[2], in1=rt[3], op=ALU.add),
                     reads=[("rtmp", i2, 2), ("rtmp", i2, 3)], writes=[("qkr", t, 1)])
            for t in range(NT):
                bk = t % 4
                proj_tok(wv_, wvk, t, bk)
                if t % 2 == 0:
                    P.op("act", lambda e, t=t, bk=bk: e.copy(out=vB[:, t, :], in_=bank[bk][:, 0:256]), reads=PS(bk), writes=[("vB", t)])
                else:
                    P.op("dve", lambda e, t=t, bk=bk: e.tensor_copy(out=vB[:, t, :], in_=bank[bk][:, 0:256]), reads=PS(bk), writes=[("vB", t)])
            for t in range(NT):
                bk = t % 4
                proj_tok(wg, wgk, t, bk)
                gate_evac(bk, t, t % 2)
            hoist_next()
            QKR = lambda t: [("qkr", t, 0), ("qkr", t, 1)]

            def kvp(t):
                return bank[t // 2][:, (t % 2) * 256:(t % 2) * 256 + 256]

            for t in range(NT):
                i2 = t % 2
                kq = qkr[:, t, 128:256].rearrange("p (a c) -> p a c", a=2)
                for dr in range(2):
                    c0 = 8 + dr * 4 + 2 * hp
                    P.op("pool", lambda e, dr=dr, c0=c0, kq=kq, i2=i2: e.tensor_tensor(
                        out=kdec[i2][:, :, dr, :], in0=kq, in1=tqk[:, l, c0:c0 + 2].unsqueeze(2).to_broadcast([128, 2, 64]), op=ALU.mult),
                        reads=QKR(t) + ["tqk"], writes=[("kdec", i2)])
                for hh in range(2):
                    P.op("pe", lambda e, hh=hh, t=t, i2=i2: e.matmul(kvp(t)[:, hh * 128:(hh + 1) * 128], lhsT=kdec[i2][:, hh].rearrange("p a b -> p (a b)"),
                                                                    rhs=vB[:, t, hh * 128:(hh + 1) * 128], start=True, stop=True),
                         reads=[("kdec", i2), ("vB", t)], writes=PS(t // 2), inc=(hh == 1))

            P.op("pool", lambda e: e.memset(Rcur, 0.0), writes=[("Rcur", 0), ("Rcur", 64)])
            for n_ in range(NT):
                for (lo, hi, t, dr) in ((0, 64, n_, 0), (64, 128, NT - 1 - n_, 1)):
                    P.op("act", lambda e, t=t, lo=lo, hi=hi: e.copy(out=Rst[lo:hi, t], in_=Rcur[lo:hi]), reads=[("Rcur", lo)], writes=[("Rst", t, lo)])
                    if n_ == NT - 1:
                        continue
                    for hh in range(2):
                        gc = l * 8 + dr * 4 + 2 * hp + hh
                        P.op("dve", lambda e, lo=lo, hi=hi, t=t, hh=hh, gc=gc: e.scalar_tensor_tensor(
                            out=Rcur[lo:hi, hh, :], in0=Rcur[lo:hi, hh, :], scalar=gsc[lo:hi, gc:gc + 1],
                            in1=kvp(t)[lo:hi, hh * 128:(hh + 1) * 128], op0=ALU.mult, op1=ALU.add),
                            reads=PS(t // 2) + [("Rcur", lo), "gsc"], writes=[("Rcur", lo)])

            def S0a(t):
                s2 = t % 2
                s3 = t % 3
                bTS = 2 * s2
                qq = qkr[:, t, 0:128].rearrange("p (a c) -> p a c", a=2)
                for dr in range(2):
                    c0 = dr * 4 + 2 * hp
                    P.op("pool", lambda e, dr=dr, c0=c0, qq=qq, s2=s2: e.tensor_tensor(
                        out=qdec[s2][:, :, dr, :], in0=qq, in1=tqk[:, l, c0:c0 + 2].unsqueeze(2).to_broadcast([128, 2, 64]), op=ALU.mult),
                        reads=QKR(t) + ["tqk"], writes=[("qdec", s2)])
                bT = bank[bTS].bitcast(BF16)
                srcs = [qdec[s2][:, 0].rearrange("p a b -> p (a b)"), qdec[s2][:, 1].rearrange("p a b -> p (a b)"),
                        qkr[:, t, 128:256], qkr[:, t, 0:128]]
                for i4 in range(4):
                    P.op("pe", lambda e, i4=i4, srcs=srcs, bT=bT: e.transpose(out=bT[:, i4 * 128:(i4 + 1) * 128], in_=srcs[i4], identity=identb[:]),
                         reads=[("qdec", s2), "identb"] + QKR(t), writes=PS(bTS), inc=(i4 == 3))
                P.op("act", lambda e, bT=bT, s3=s3: e.copy(out=TT[s3].rearrange("p a b -> p (a b)"), in_=bT[:, 0:384]),
                     reads=PS(bTS), writes=[("TT", s3)])
                for hh in range(2):
                    P.op("act", lambda e, bT=bT, s2=s2, hh=hh: e.copy(out=TTq2[s2][hh * 64:(hh + 1) * 64, hh, :],
                                                                       in_=bT[hh * 64:(hh + 1) * 64, 384:512]),
                         reads=PS(bTS), writes=[("TTq2", s2)])

            def S0b(t):
                s2 = t % 2
                s3 = t % 3
                bTS = 2 * s2
                P.op("pe", lambda e, s2=s2, s3=s3, bTS=bTS: e.matmul(bank[bTS][:, 256:512], lhsT=TT[s3][:, 2, :],
                                                                    rhs=TTq2[s2].rearrange("p a b -> p (a b)"), start=True, stop=True),
                     reads=[("TT", s3), ("TTq2", s2)], writes=PS(bTS), inc=True)
                P.op("dve", lambda e, s2=s2, bTS=bTS: e.tensor_tensor(out=innerT[s2], in0=bank[bTS][:, 256:512].rearrange("p (a b) -> p a b", a=2),
                                                                     in1=D2T[:, l, 2 * hp:2 * hp + 2, :], op=ALU.mult),
                     reads=PS(bTS) + ["D2T"], writes=[("innerT", s2)])

            def S0c(t):
                s2 = t % 2
                s3 = t % 3
                bO = 2 * s2 + 1
                for hh in range(2):
                    P.op("pe", lambda e, hh=hh, t=t, s2=s2, bO=bO: e.matmul(bank[bO][:, hh * 128:(hh + 1) * 128], lhsT=innerT[s2][:, hh, :],
                                                                           rhs=vB[:, t, hh * 128:(hh + 1) * 128], start=True, stop=False),
                         reads=[("innerT", s2), ("vB", t)], writes=PS(bO), inc=False)
                    P.op("pe", lambda e, hh=hh, t=t, s3=s3, bO=bO: e.matmul(bank[bO][:, hh * 128:(hh + 1) * 128], lhsT=TT[s3][:, hh, :],
                                                                           rhs=Rst[:, t, hh, :], start=False, stop=True),
                         reads=[("TT", s3), ("Rst", t, 0), ("Rst", t, 64)], writes=PS(bO), inc=(hh == 1))

            def S1(t):
                s2 = t % 2
                bO = 2 * s2 + 1
                sc = 32 + 8 * s2
                for hh in range(2):
                    P.op("act", lambda e, hh=hh, bO=bO, sc=sc: e.activation(out=junk[:, hh * 128:(hh + 1) * 128], in_=bank[bO][:, hh * 128:(hh + 1) * 128],
                                                                          func=AF.Square, accum_out=sm[:, sc + hh:sc + hh + 1]),
                         reads=PS(bO), writes=["junk", ("smB", s2)])
                P.op("dve", lambda e, sc=sc: e.tensor_scalar(out=sm[:, sc + 2:sc + 4], in0=sm[:, sc:sc + 2], scalar1=4.0 / 128, scalar2=4.0 * EPS,
                                                             op0=ALU.mult, op1=ALU.add), reads=[("smB", s2)], writes=[("smB", s2)])
                P.op("pool", lambda e, sc=sc: e.tensor_tensor(out=sm[:, sc + 4:sc + 6], in0=sm[:, sc + 2:sc + 4], in1=mhalf[:, 0:2], op=ALU.pow),
                     reads=[("smB", s2), "mhalf"], writes=[("smB", s2)])
                i = t % 2
                P.op("dve", lambda e, s2=s2, bO=bO, sc=sc: e.tensor_tensor(out=btmp[s2], in0=bank[bO][:, 0:256].rearrange("p (a b) -> p a b", a=2),
                                                                          in1=sm[:, sc + 4:sc + 6].unsqueeze(2).to_broadcast([128, 2, 128]), op=ALU.mult),
                     reads=PS(bO) + [("smB", s2)], writes=[("btmp", s2)])
                P.op("pool", lambda e, i=i, t=t, s2=s2: e.tensor_tensor(out=mixed[i][:], in0=btmp[s2].rearrange("p a b -> p (a b)"), in1=gate[:, t, :], op=ALU.mult),
                     reads=[("btmp", s2), "gate"], writes=[("mixed", i)])

            def S2(t):
                s2 = t % 2
                bO = 2 * s2 + 1
                i = t % 2
                tail(mixed[i], ("mixed", i), t, wo, wok, tb=bank[bO].bitcast(BF16)[:, 512:768], tbk=bO, ob=4 + 2 * s2, i=i)

            for it in range(NT + 4):
                if it < NT:
                    S0a(it)
                if 0 <= it - 1 < NT:
                    S0b(it - 1)
                if 0 <= it - 2 < NT:
                    S0c(it - 2)
                if 0 <= it - 3 < NT:
                    S1(it - 3)
                if 0 <= it - 4 < NT:
                    S2(it - 4)

        def phase_C(l, gp):
            wu, wuk = load_w("in", l, OFF["cu"] + gp * 256)
            wg, wgk = load_w("in", l, OFF["cg"] + gp * 256)
            wo, wok = load_w("out", l, 1024 + gp * 256)
            for t in range(NT):
                bk = t % 4
                proj_tok(wu, wuk, t, bk)
                if t % 2 == 0:
                    P.op("act", lambda e, t=t, bk=bk: e.copy(out=uC[:, t, :], in_=bank[bk][:, 0:256]), reads=PS(bk), writes=[("uC", t)])
                else:
                    P.op("dve", lambda e, t=t, bk=bk: e.tensor_copy(out=uC[:, t, :], in_=bank[bk][:, 0:256]), reads=PS(bk), writes=[("uC", t)])
            for t in range(NT):
                bk = t % 4
                proj_tok(wg, wgk, t, bk)
                gate_evac(bk, t, t % 2)
            hoist_next()

            def S0(t):
                s2 = t % 2
                bP = 2 * s2
                bY = 2 * s2 + 1
                for gg in range(2):
                    g = 2 * gp + gg
                    parts = [(t, g * 5 + (3 if t == 0 else 4 if t == NT - 1 else 0))]
                    if t > 0:
                        parts.append((t - 1, g * 5 + 1))
                    if t < NT - 1:
                        parts.append((t + 1, g * 5 + 2))
                    for pi_, (tj, mi) in enumerate(parts):
                        P.op("pe", lambda e, gg=gg, tj=tj, mi=mi, pi_=pi_, np_=len(parts), bP=bP: e.matmul(
                            bank[bP][:, gg * 128:(gg + 1) * 128], lhsT=uC[:, tj, gg * 128:(gg + 1) * 128], rhs=poolm[:, mi, :],
                            start=(pi_ == 0), stop=(pi_ == np_ - 1)),
                            reads=[("uC", tj), "poolm"], writes=PS(bP), inc=(gg == 1 and pi_ == len(parts) - 1))
                P.op("act", lambda e, bP=bP, s2=s2: e.copy(out=pooledT[s2].rearrange("p a b -> p (a b)"), in_=bank[bP][:, 0:256]),
                     reads=PS(bP), writes=[("pooledT", s2)])
                for gg in range(2):
                    g = 2 * gp + gg
                    P.op("pe", lambda e, gg=gg, g=g, s2=s2, bY=bY: e.matmul(bank[bY][:, gg * 128:(gg + 1) * 128], lhsT=pooledT[s2][:, gg, :],
                                                                           rhs=poolw[:, l * 4 + g, :], start=True, stop=True),
                         reads=[("pooledT", s2), "poolw"], writes=PS(bY), inc=(gg == 1))

            def S1(t):
                s2 = t % 2
                bY = 2 * s2 + 1
                i = t % 2
                P.op("dve", lambda e, s2=s2, bY=bY: e.tensor_tensor(out=ytmp[s2], in0=bank[bY][:, 0:256], in1=psh[:, l, gp * 256:(gp + 1) * 256], op=ALU.mult),
                     reads=PS(bY) + ["psh"], writes=[("ytmp", s2)])
                P.op("pool", lambda e, t=t, i=i, s2=s2: e.tensor_tensor(out=mixed[i][:], in0=ytmp[s2], in1=gate[:, t, :], op=ALU.mult),
                     reads=[("ytmp", s2), "gate"], writes=[("mixed", i)])

            def S2(t):
                s2 = t % 2
                bY = 2 * s2 + 1
                i = t % 2
                tail(mixed[i], ("mixed", i), t, wo, wok, tb=bank[bY].bitcast(BF16)[:, 512:768], tbk=bY, ob=4 + 2 * s2, i=i)

            for it in range(NT + 2):
                if it < NT:
                    S0(it)
                if 0 <= it - 1 < NT:
                    S1(it - 1)
                if 0 <= it - 2 < NT:
                    S2(it - 2)

        for s_ in range(nseq):
            for l in range(nlayers):
                for ph in phases:
                    for hp in range(2):
                        sched["list"].append((ph, l, hp))
        sched["i"] = -1
        hoist_next()
        fns = {"A": phase_A, "B": phase_B, "C": phase_C}
        pi = 0
        for s in range(nseq):
            for t in range(NT):
                dma(lambda e, s=s, t=t: e.dma_start(out=xres[:, t, :], in_=x_d[s, t]), writes=[("x", t)])
            for l in range(nlayers):
                P.fence()
                rmsnorm_to_hT(l)
                for ph in phases:
                    for hp in range(2):
                        if hp == 0:
                            P.fence()
                        sched["i"] = pi
                        fns[ph](l, hp)
                        pi += 1
            P.fence()
            dma(lambda e: e.dma_start(out=nw, in_=nwb_d[2]), writes=["nw"])
            for t in range(NT):
                P.op("act", lambda e, t=t: e.activation(out=junk[:], in_=xres[:, t, :], func=AF.Square, accum_out=ss[:, 0, t:t + 1]),
                     reads=[("x", t)], writes=["junk", ("ss", t)])
            P.op("dve", lambda e: e.tensor_scalar(out=ss[:, 1, :], in0=ss[:, 0, :], scalar1=1.0 / D, scalar2=EPS, op0=ALU.mult, op1=ALU.add),
                 reads=[("ss", t) for t in range(NT)], writes=["ss1"])
            P.op("pool", lambda e: e.tensor_tensor(out=ss[:, 0, :], in0=ss[:, 1, :], in1=mhalf[:, 0:NT], op=ALU.pow),
                 reads=["ss1", "mhalf"], writes=[("ss", t) for t in range(NT)])
            for t in range(NT):
                P.op("dve", lambda e, t=t: e.scalar_tensor_tensor(out=xres[:, t, :], in0=xres[:, t, :], scalar=ss[:, 0, t:t + 1], in1=nw,
                                                                  op0=ALU.mult, op1=ALU.mult),
                     reads=[("x", t), ("ss", t), "nw"], writes=[("x", t)])
                dma(lambda e, s=s, t=t: e.dma_start(out=y_d[s, t], in_=xres[:, t, :]), reads=[("x", t)], writes=[("y", s, t)])
        P.emit(nc, st)
    return nc


_NC_CACHE = {}


def kernel(**inputs):
    x = np.asarray(inputs["x"], np.float32)
    B = x.shape[0]
    ncores = 8
    nseq = B // ncores
    consts = _host_consts(inputs)
    key = (nseq,)
    if key not in _NC_CACHE:
        _NC_CACHE[key] = build_nc(nseq=nseq)
    nc = _NC_CACHE[key]
    w_in = np.ascontiguousarray(np.asarray(inputs["w_in"], np.float32))
    w_out = np.ascontiguousarray(np.asarray(inputs["w_out"], np.float32))
    in_maps = []
    for c in range(ncores):
        m = dict(consts)
        m["x"] = np.ascontiguousarray(x[c * nseq:(c + 1) * nseq].reshape(nseq, NT, 128, D))
        m["w_in"] = w_in
        m["w_out"] = w_out
        in_maps.append(m)
    res = run_bass_kernel_spmd(nc, in_maps, core_ids=list(range(ncores)))
    out = np.concatenate([np.asarray(r["y"]).reshape(nseq, S, D) for r in res.results], axis=0)
    return out.astype(np.float32)
```

```python
import contextlib
import math
import numpy as np
import concourse.bass as bass
import concourse.mybir as mybir
from concourse.bass_utils import run_bass_kernel_spmd

F32 = mybir.dt.float32
BF16 = mybir.dt.bfloat16
ALU = mybir.AluOpType
AF = mybir.ActivationFunctionType
AX = mybir.AxisListType

D = 1024
S = 2048
NT = S // 128
DIN = 4608
DMIX = 1536
EPS = 1e-6
OFF = dict(aq=0, ak=512, av=1024, ag=1536, bq=2048, bk=2304, bv=2560, bg=3072, cu=3584, cg=4096)
MW = 1152

ENGS = ("pe", "act", "dve", "pool", "sp")
CUT = 99
SEM_LIMIT = 30000


class Op:
    __slots__ = ("eng", "fn", "deps", "inc", "is_dma", "seq", "eidx", "sem", "val", "name",
                 "closer", "dslot", "nofence")


class Prog:
    def __init__(self):
        self.ops = []
        self.last_w = {}
        self.readers = {}
        self.pend = {e: set() for e in ENGS}
        self.fence_idx = 0

    def fence(self):
        deps = set()
        last = {}
        for o in self.ops[self.fence_idx:]:
            if o.is_dma:
                if not o.nofence:
                    deps.add(o.seq)
            else:
                last[o.eng] = o.seq
        deps |= set(last.values())
        for e in ENGS:
            self.pend[e] |= deps
        self.fence_idx = len(self.ops)

    def _add(self, eng, fn, reads, writes, inc, is_dma, name):
        o = Op()
        o.eng, o.fn, o.inc, o.is_dma, o.name = eng, fn, inc, is_dma, name
        o.seq = len(self.ops)
        o.closer = o.seq
        o.nofence = False
        deps = set(self.pend[eng])
        self.pend[eng] = set()
        reads = list(reads)
        writes = list(writes)
        ex = [r for r in reads if isinstance(r, tuple) and r[0] == "ps"]
        reads = [r for r in reads if r not in ex]
        writes = writes + [r for r in ex if r not in writes]
        for r in reads:
            if r in self.last_w:
                deps.add(self.last_w[r])
        for w in writes:
            if w in self.last_w:
                deps.add(self.last_w[w])
            for rd in self.readers.get(w, ()):
                deps.add(rd)
        for r in reads:
            self.readers.setdefault(r, []).append(o.seq)
        for w in writes:
            self.last_w[w] = o.seq
            self.readers[w] = []
        deps.discard(o.seq)
        o.deps = deps
        self.ops.append(o)
        return o

    def op(self, eng, fn, reads=(), writes=(), inc=True, name=""):
        return self._add(eng, fn, reads, writes, inc, False, name)

    def dma(self, fn, reads=(), writes=(), queue="sp", name=""):
        return self._add(queue, fn, reads, writes, True, True, name)

    def emit(self, nc, st, ndma_sems=8):
        ops = self.ops
        per_eng = {e: [] for e in ENGS}
        for o in ops:
            o.eidx = len(per_eng[o.eng])
            per_eng[o.eng].append(o)
        nsem = [0]

        def newsem(tag):
            nsem[0] += 1
            return st.enter_context(nc.semaphore("%s_%d" % (tag, nsem[0])))

        dsem = {q: [newsem("d" + q) for _ in range(ndma_sems)] for q in ("sp", "pool")}
        for e in ENGS:
            cnt = 0
            cur = newsem("s" + e)
            dcnt = [0] * ndma_sems
            nd = 0
            pending = []
            for o in per_eng[e]:
                if o.is_dma:
                    j = nd % ndma_sems
                    nd += 1
                    dcnt[j] += 16
                    o.sem, o.val, o.dslot = dsem[e][j], dcnt[j], j
                elif o.inc:
                    if cnt >= SEM_LIMIT:
                        cur = newsem("s" + e)
                        cnt = 0
                    cnt += 1
                    o.sem, o.val = cur, cnt
                    for p in pending:
                        p.sem, p.val, p.closer = cur, cnt, o.seq
                    pending = []
                else:
                    pending.append(o)
            assert not pending, "trailing no-inc ops on " + e
        blk = st.enter_context(nc.Block())

        def make(e):
            def body(engine):
                waited = {}
                last_dma = {}

                def need(p):
                    if waited.get(p.sem, 0) >= p.val:
                        return
                    waited[p.sem] = p.val
                    engine.wait_ge(p.sem, p.val)

                for o in per_eng[e]:
                    for d in sorted(o.deps):
                        p = ops[d]
                        if p.eng == e and not p.is_dma and not o.is_dma:
                            if e != "pe" and o.eidx - p.eidx <= 2:
                                need(p)
                            continue
                        assert p.closer < o.seq, (p.name, o.name)
                        need(p)
                    if o.is_dma:
                        j = o.dslot
                        if j in last_dma:
                            need(last_dma[j])
                        last_dma[j] = o
                        o.fn(engine).then_inc(o.sem, 16)
                    else:
                        ins = o.fn(engine)
                        if o.inc:
                            ins.then_inc(o.sem, 1)
                for pv in last_dma.values():
                    need(pv)
            return body

        blk.tensor(make("pe"))
        blk.scalar(make("act"))
        blk.vector(make("dve"))
        blk.gpsimd(make("pool"))
        blk.sync(make("sp"))


def _t5_bucket(rel):
    half, max_exact = 16, 8
    ret = np.where(rel > 0, half, 0)
    n = np.abs(rel)
    nf = np.maximum(n, 1).astype(np.float32)
    large = max_exact + (np.log(nf / np.float32(max_exact)) / np.float32(math.log(128 / max_exact))
                         * np.float32(half - max_exact)).astype(np.int32)
    large = np.minimum(large, half - 1)
    return ret + np.where(n < max_exact, n, large)


def _pool_mats():
    pm = np.zeros((128, 20, 128), np.float32)
    for g, w in enumerate((2, 4, 8, 16)):
        for v in range(5):
            t = {0: 5, 1: 5, 2: 5, 3: 0, 4: NT - 1}[v]
            for i in range(128):
                gi = t * 128 + i
                lo = min(max(gi - w // 2, 0), S)
                hi = min(max(gi + (w - w // 2), 0), S)
                cnt = float(hi - lo)
                for gj in range(lo, hi):
                    tj, j = divmod(gj, 128)
                    rel = tj - t
                    if v in (0, 3, 4) and rel == 0:
                        pm[j, g * 5 + v, i] += 1.0 / cnt
                    elif v == 1 and rel == -1:
                        pm[j, g * 5 + v, i] += 1.0 / cnt
                    elif v == 2 and rel == 1:
                        pm[j, g * 5 + v, i] += 1.0 / cnt
                if v in (0, 3, 4):
                    pm[i, g * 5 + v, i] -= 1.0
    return pm


def _host_consts(inp):
    c = {}
    bc = lambda a: np.ascontiguousarray(np.broadcast_to(a, (128,) + a.shape)).astype(np.float32)
    c["nwb"] = np.ascontiguousarray(np.stack([bc(inp["norm_w"][0]), bc(inp["norm_w"][1]),
                                               bc(inp["final_norm_w"])], 0))
    c["ident"] = np.eye(128, dtype=np.float32)
    p = np.arange(128)[:, None]
    cc = np.arange(MW)[None, :]
    bidx = _t5_bucket(p - cc + 512)
    rb = np.asarray(inp["rel_bias"], np.float32)
    c["bmaster"] = np.ascontiguousarray(rb[bidx].transpose(0, 2, 1))
    c["cfar"] = bc(np.concatenate([rb[31], rb[15]]))
    half = 32
    theta = (1.0 / (np.float32(10000.0) ** np.linspace(0.0, 1.0, half, dtype=np.float32))).astype(np.float32)
    ang = (np.arange(S, dtype=np.float32)[:, None] * theta[None, :]).astype(np.float32)
    cs = np.stack([np.cos(ang), np.sin(ang)], 0).astype(np.float32)
    c["cs"] = np.ascontiguousarray(cs.reshape(2, NT, 128, half).transpose(2, 0, 1, 3))
    m = np.arange(128, dtype=np.float32)[:, None]
    n = np.arange(128, dtype=np.float32)[None, :]
    c["retc"] = np.ascontiguousarray(np.stack([np.maximum(n - m, 0), np.maximum(m - n, 0)], 1))
    i = np.arange(128, dtype=np.float32)
    c["tokidx"] = np.ascontiguousarray(np.stack([i + 1, 128 - i, 127 - i, i], 1))
    c["dlam"] = bc(np.asarray(inp["diff_lambda"], np.float32).reshape(2, 256))
    c["subw"] = bc(np.asarray(inp["diff_subln_w"], np.float32))
    c["rdl"] = bc(np.asarray(inp["ret_decay_logit"], np.float32).reshape(16))
    c["pscale"] = bc(np.asarray(inp["pool_scale"], np.float32))
    c["poolw"] = np.ascontiguousarray(np.asarray(inp["pool_w"], np.float32))
    c["poolm"] = _pool_mats()
    return c


def build_nc(nseq=2, nlayers=2, phases="ABC"):
    nc = bass.Bass("TRN2", target_bir_lowering=False)
    din = lambda name, shape: nc.dram_tensor(name, list(shape), F32, kind="ExternalInput").ap()
    x_d = din("x", [nseq, NT, 128, D])
    win_d = din("w_in", [2, D, DIN])
    wout_d = din("w_out", [2, DMIX, D])
    nwb_d = din("nwb", [3, 128, D])
    ident_d = din("ident", [128, 128])
    bm_d = din("bmaster", [128, 4, MW])
    cfar_d = din("cfar", [128, 8])
    cs_d = din("cs", [128, 2, NT, 32])
    retc_d = din("retc", [128, 2, 128])
    tokidx_d = din("tokidx", [128, 4])
    dlam_d = din("dlam", [128, 2, 256])
    subw_d = din("subw", [128, 2, 128])
    rdl_d = din("rdl", [128, 16])
    pscale_d = din("pscale", [128, 2, 512])
    poolw_d = din("poolw", [2, 4, 128, 128])
    poolm_d = din("poolm", [128, 20, 128])
    y_d = nc.dram_tensor("y", [nseq, NT, 128, D], F32, kind="ExternalOutput").ap()

    P = Prog()
    with contextlib.ExitStack() as st:
        def sb(name, shape, dt=F32):
            return st.enter_context(nc.sbuf_tensor("s_" + name, list(shape), dt))

        xres = sb("xres", [128, NT, D])
        hT = sb("hT", [128, 8, S], BF16)
        NSLOT = 6
        wslot = [sb("wslot%d" % i, [128, 2048], BF16) for i in range(NSLOT)]
        bmast = sb("bmast", [128, 4, MW], BF16)
        identb = sb("identb", [128, 128], BF16)
        cfar = sb("cfar", [128, 8])
        zero1 = sb("zero1", [128, 1])
        mhalf = sb("mhalf", [128, 16])
        tokidx = sb("tokidx", [128, 4])
        rdl = sb("rdl", [128, 16])
        poolw = sb("poolw", [128, 8, 128], BF16)
        poolm = sb("poolm", [128, 20, 128], BF16)
        lg = sb("lg", [128, 16])
        tqk = sb("tqk", [128, 2, 16])
        gsc = sb("gsc", [128, 16])
        D2T = sb("D2T", [128, 2, 4, 128])
        W2 = sb("W2", [128, 2, 2, 128])
        psh = sb("psh", [128, 2, 512])
        nlam = sb("nlam", [128, 2])
        sm = sb("sm", [128, 64])
        ss = sb("ss", [128, 2, NT])
        hb = [sb("hb0", [128, D], BF16)] * 2
        junk = sb("junk", [128, D], BF16)
        mixed = [sb("mixed%d" % i, [128, 256], BF16) for i in range(2)]
        mT = [sb("mT%d" % i, [128, 2, 128], BF16) for i in range(2)]
        gate = sb("gate", [128, NT, 256], BF16)
        th = [sb("th%d" % i, [128, 256]) for i in range(2)]
        ARENA = 43 * 1024 + 512
        arena = sb("arena", [128, ARENA], mybir.dt.uint8)

        def carve(off, shape, dt):
            n = int(np.prod(shape))
            bpe = 2 if dt == BF16 else 4
            ap = arena[:, off:off + n * bpe].bitcast(dt)
            if len(shape) > 1:
                names = " ".join("d%d" % i for i in range(len(shape)))
                kw = {"d%d" % i: shape[i] for i in range(1, len(shape))}
                ap = ap.rearrange("p (%s) -> p %s" % (names, names), **kw)
            return ap, off + n * bpe

        o = 0
        nw, o = carve(o, [D], F32)
        identf, o = carve(o, [128], F32)
        retc, o = carve(o, [2, 128], F32)
        dlam, o = carve(o, [2, 256], F32)
        subw, o = carve(o, [2, 128], F32)
        scr, o = carve(o, [256], F32)
        o = 0
        qT, o = carve(o, [2, S], BF16)
        kT, o = carve(o, [2, S], BF16)
        V1, o = carve(o, [NT, 2, 130], BF16)
        oa, o = carve(o, [4, 2, 128], F32)
        oan, o = carve(o, [4, 2, 128], F32)
        PT0, o = carve(o, [2, 512], BF16)
        PT1, o = carve(o, [2, 512], BF16)
        PT2, o = carve(o, [2, 512], BF16)
        PT = [PT0, PT1, PT2]
        tmpA, o = carve(o, [128], F32)
        accS, o = carve(o, [8, 129], F32)
        assert o <= ARENA, o
        o = 0
        qkr, o = carve(o, [NT, 256], BF16)
        vB, o = carve(o, [NT, 256], BF16)
        Rst, o = carve(o, [NT, 2, 128], BF16)
        cs, o = carve(o, [2, NT, 32], F32)
        Rcur, o = carve(o, [2, 128], F32)
        qk32 = []
        rtmp = []
        kdec = []
        qdec = []
        TT = []
        TTq2 = []
        innerT = []
        btmp = []
        for _i in range(2):
            a_, o = carve(o, [256], F32); qk32.append(a_)
            a_, o = carve(o, [4, 128], F32); rtmp.append(a_)
            a_, o = carve(o, [2, 2, 64], BF16); kdec.append(a_)
            a_, o = carve(o, [2, 2, 64], BF16); qdec.append(a_)
            a_, o = carve(o, [2, 128], BF16); TTq2.append(a_)
            a_, o = carve(o, [2, 128], BF16); innerT.append(a_)
            a_, o = carve(o, [2, 128], F32); btmp.append(a_)
        for _i in range(3):
            a_, o = carve(o, [3, 128], BF16); TT.append(a_)
        assert o <= ARENA, o
        o = 0
        uC, o = carve(o, [NT, 256], BF16)
        pooledT = []
        ytmp = []
        for _i in range(2):
            a_, o = carve(o, [2, 128], BF16); pooledT.append(a_)
            a_, o = carve(o, [256], F32); ytmp.append(a_)
        assert o <= ARENA, o

        psall = st.enter_context(nc.psum_tensor("psall", [128, 8, 512], F32))
        bank = [psall[:, i, :] for i in range(8)]

        def PS(*idx):
            return [("ps", i) for i in idx]


        dma = P.dma
        dma(lambda e: e.dma_start(out=identf, in_=ident_d), writes=["identf"])
        dma(lambda e: e.dma_start(out=cfar[:], in_=cfar_d), writes=["cfar"])
        dma(lambda e: e.dma_start(out=retc, in_=retc_d), writes=["retc"])
        dma(lambda e: e.dma_start(out=tokidx[:], in_=tokidx_d), writes=["tokidx"])
        dma(lambda e: e.dma_start(out=dlam, in_=dlam_d), writes=["dlam"])
        dma(lambda e: e.dma_start(out=subw, in_=subw_d), writes=["subw"])
        dma(lambda e: e.dma_start(out=rdl[:], in_=rdl_d), writes=["rdl"])
        dma(lambda e: e.dma_start(out=psh[:], in_=pscale_d), writes=["psh"])
        dma(lambda e: e.dma_start(out=poolw[:].rearrange("c (l g) d -> c l g d", l=2),
                                  in_=poolw_d.rearrange("l g c d -> c l g d")),
            writes=["poolw"], queue="pool")
        dma(lambda e: e.dma_start(out=poolm[:], in_=poolm_d), writes=["poolm"], queue="pool")
        dma(lambda e: e.dma_start(out=bmast[:], in_=bm_d), writes=["bmast"], queue="pool")
        for h_ in range(4):
            P.op("act", lambda e, h_=h_: e.activation(out=bmast[:, h_, :], in_=bmast[:, h_, :], func=AF.Exp), reads=["bmast"], writes=["bmast"])

        P.op("pool", lambda e: e.memset(mhalf[:], -0.5), writes=["mhalf"])
        P.op("pool", lambda e: e.memset(zero1[:], 0.0), writes=["zero1"])
        P.op("dve", lambda e: e.tensor_copy(out=identb[:], in_=identf), reads=["identf"], writes=["identb"])

        lam_init = [0.8 - 0.6 * math.exp(-0.3 * l) for l in range(2)]
        for l in range(2):
            P.op("dve", lambda e, l=l: e.tensor_tensor(out=scr[:, 0:64], in0=dlam[:, l, 0:64], in1=dlam[:, l, 64:128], op=ALU.mult),
                 reads=["dlam"], writes=["junk"])
            P.op("dve", lambda e, l=l: e.tensor_tensor(out=scr[:, 64:128], in0=dlam[:, l, 128:192], in1=dlam[:, l, 192:256], op=ALU.mult),
                 reads=["dlam"], writes=["junk"])
            P.op("dve", lambda e: e.reduce_sum(out=sm[:, 0:2], in_=scr[:, 0:128].rearrange("p (a b) -> p a b", a=2), axis=AX.X),
                 reads=["junk"], writes=["sm"])
            P.op("act", lambda e: e.activation(out=sm[:, 2:4], in_=sm[:, 0:2], func=AF.Exp), reads=["sm"], writes=["sm"])
            P.op("dve", lambda e, l=l: e.tensor_scalar(out=sm[:, 4:5], in0=sm[:, 3:4], scalar1=-lam_init[l], scalar2=None, op0=ALU.add),
                 reads=["sm"], writes=["sm"])
            P.op("dve", lambda e, l=l: e.tensor_tensor(out=nlam[:, l:l + 1], in0=sm[:, 4:5], in1=sm[:, 2:3], op=ALU.subtract),
                 reads=["sm"], writes=["nlam"])
            for hh in range(2):
                P.op("dve", lambda e, l=l, hh=hh: e.tensor_scalar(out=W2[:, l, hh, :], in0=subw[:, l, :],
                                                                  scalar1=(1.0 - lam_init[l]) * 0.5, scalar2=None, op0=ALU.mult),
                     reads=["subw"], writes=["W2"])
            P.op("dve", lambda e, l=l: e.tensor_scalar(out=psh[:, l, :], in0=psh[:, l, :], scalar1=0.5, scalar2=None, op0=ALU.mult),
                 reads=["psh"], writes=["psh"])
        P.op("act", lambda e: e.activation(out=sm[:, 16:32], in_=rdl[:], func=AF.Exp, scale=-1.0), reads=["rdl"], writes=["sm"])
        P.op("dve", lambda e: e.tensor_scalar(out=sm[:, 32:48], in0=sm[:, 16:32], scalar1=1.0, scalar2=None, op0=ALU.add),
             reads=["sm"], writes=["sm"])
        P.op("act", lambda e: e.activation(out=sm[:, 16:32], in_=sm[:, 32:48], func=AF.Ln), reads=["sm"], writes=["sm"])
        P.op("dve", lambda e: e.tensor_scalar(out=lg[:], in0=sm[:, 16:32], scalar1=-1.0, scalar2=None, op0=ALU.mult),
             reads=["sm"], writes=["lg"])
        P.op("act", lambda e: e.activation(out=gsc[:], in_=lg[:], func=AF.Exp, scale=128.0), reads=["lg"], writes=["gsc"])
        for l in range(2):
            lf = l * 8
            lb = l * 8 + 4
            for (dst, src, ti) in ((0, lf, 0), (4, lb, 1), (8, lf, 2), (12, lb, 3)):
                P.op("dve", lambda e, l=l, dst=dst, src=src, ti=ti: e.tensor_scalar(
                    out=sm[:, 48 + dst:52 + dst], in0=lg[:, src:src + 4], scalar1=tokidx[:, ti:ti + 1], scalar2=None, op0=ALU.mult),
                    reads=["lg", "tokidx"], writes=["sm"])
            P.op("act", lambda e, l=l: e.activation(out=tqk[:, l, :], in_=sm[:, 48:64], func=AF.Exp), reads=["sm"], writes=["tqk"])
            for h in range(4):
                P.op("dve", lambda e, l=l, h=h: e.tensor_scalar(out=scr[:, 0:128], in0=retc[:, 0, :], scalar1=lg[:, l * 8 + h:l * 8 + h + 1],
                                                                scalar2=None, op0=ALU.mult), reads=["retc", "lg"], writes=["junk"])
                P.op("dve", lambda e, l=l, h=h: e.scalar_tensor_tensor(out=scr[:, 128:256], in0=retc[:, 1, :],
                                                                       scalar=lg[:, l * 8 + 4 + h:l * 8 + 5 + h], in1=scr[:, 0:128],
                                                                       op0=ALU.mult, op1=ALU.add), reads=["retc", "lg", "junk"], writes=["junk"])
                P.op("act", lambda e, l=l, h=h: e.activation(out=D2T[:, l, h, :], in_=scr[:, 128:256], func=AF.Exp),
                     reads=["junk"], writes=["D2T"])

        P.fence()
        wstate = {"n": 0}

        preloaded = {}
        sched = {"list": [], "i": 0}

        def phase_loads(ph, l, hp):
            if ph == "A":
                return [("in", l, OFF["aq"] + hp * 256), ("in", l, OFF["ak"] + hp * 256), ("in", l, OFF["av"] + hp * 256),
                        ("in", l, OFF["ag"] + hp * 256), ("out", l, hp * 256)]
            if ph == "B":
                return [("inqk", l, OFF["bq"] + hp * 128), ("in", l, OFF["bv"] + hp * 256), ("in", l, OFF["bg"] + hp * 256),
                        ("out", l, 512 + hp * 256)]
            return [("in", l, OFF["cu"] + hp * 256), ("in", l, OFF["cg"] + hp * 256), ("out", l, 1024 + hp * 256)]

        def hoist_next():
            i = sched["i"] + 1
            if i < len(sched["list"]):
                ph, l, hp = sched["list"][i]
                for k in phase_loads(ph, l, hp):
                    if k not in preloaded:
                        preloaded[k] = _load_w(*k)

        def load_w(kind, l, c0):
            k = (kind, l, c0)
            if k in preloaded:
                return preloaded.pop(k)
            return _load_w(kind, l, c0)

        def _load_w(kind, l, c0):
            n0 = len(P.ops)
            r = _load_w2(kind, l, c0)
            for o_ in P.ops[n0:]:
                o_.nofence = True
            return r

        def _load_w2(kind, l, c0):
            i = wstate["n"] % NSLOT
            wstate["n"] += 1
            sl = wslot[i]
            key = ("wslot", i)
            if kind == "in":
                v = sl[:].rearrange("p (k c) -> p k c", k=8)
                dma(lambda e: e.dma_start(out=v, in_=win_d[l, :, c0:c0 + 256].rearrange("(k p) c -> p k c", p=128)),
                    writes=[key], queue="pool")
            elif kind == "inqk":
                v = sl[:].rearrange("p (k c) -> p k c", k=8)
                dma(lambda e: e.dma_start(out=v[:, :, 0:128], in_=win_d[l, :, c0:c0 + 128].rearrange("(k p) c -> p k c", p=128)),
                    writes=[key], queue="pool")
                dma(lambda e: e.dma_start(out=v[:, :, 128:256], in_=win_d[l, :, c0 + 256:c0 + 384].rearrange("(k p) c -> p k c", p=128)),
                    reads=[key], writes=[key], queue="pool")
            else:
                v = sl[:].rearrange("p (k c) -> p k c", k=2)
                dma(lambda e: e.dma_start(out=v, in_=wout_d[l, c0:c0 + 256, :].rearrange("(k p) c -> p k c", p=128)),
                    writes=[key], queue="pool")
            return v, key

        cnt = {"tok": 0, "tail": 0, "ev": 0}

        def proj_tok(wv, wkey, t, bk, ncols=256):
            for k in range(8):
                P.op("pe", lambda e, k=k: e.matmul(bank[bk][:, 0:ncols], lhsT=hT[:, k, t * 128:(t + 1) * 128], rhs=wv[:, k, 0:ncols],
                                                   start=(k == 0), stop=(k == 7)),
                     reads=["hT", wkey], writes=PS(bk), inc=(k == 7))

        def gate_evac(bk, t, i):
            P.op("act", lambda e: e.activation(out=th[i][:], in_=bank[bk][:, 0:256], func=AF.Tanh, scale=0.5),
                 reads=PS(bk), writes=[("th", i)])
            P.op("dve", lambda e: e.scalar_tensor_tensor(out=gate[:, t, :], in0=th[i][:], scalar=1.0, in1=bank[bk][:, 0:256],
                                                         op0=ALU.add, op1=ALU.mult),
                 reads=PS(bk) + [("th", i)], writes=["gate"])

        def tail(mx, mxkey, t, wov, wokey, tb=None, tbk=7, ob=None, i=None):
            if i is None:
                i = cnt["tail"] % 2
                cnt["tail"] += 1
            if tb is None:
                tb = bank[7].bitcast(BF16)[:, 0:256]
            for k in range(2):
                P.op("pe", lambda e, k=k: e.transpose(out=tb[:, k * 128:(k + 1) * 128], in_=mx[:, k * 128:(k + 1) * 128], identity=identb[:]),
                     reads=[mxkey, "identb"], writes=PS(tbk), inc=(k == 1))
            P.op("act", lambda e: e.copy(out=mT[i][:].rearrange("p a b -> p (a b)"), in_=tb), reads=PS(tbk), writes=[("mT", i)])
            if ob is None:
                for half in range(2):
                    for k in range(2):
                        P.op("pe", lambda e, k=k, half=half: e.matmul(bank[7], lhsT=mT[i][:, k, :], rhs=wov[:, k, half * 512:(half + 1) * 512],
                                                                      start=(k == 0), stop=(k == 1)),
                             reads=[("mT", i), wokey], writes=PS(7), inc=(k == 1))
                    P.op("dve", lambda e, half=half: e.tensor_tensor(out=xres[:, t, half * 512:(half + 1) * 512],
                                                                     in0=xres[:, t, half * 512:(half + 1) * 512], in1=bank[7], op=ALU.add),
                         reads=PS(7) + [("x", t)], writes=[("x", t)])
            else:
                for half in range(2):
                    for k in range(2):
                        P.op("pe", lambda e, k=k, half=half: e.matmul(bank[ob + half], lhsT=mT[i][:, k, :], rhs=wov[:, k, half * 512:(half + 1) * 512],
                                                                      start=(k == 0), stop=(k == 1)),
                             reads=[("mT", i), wokey], writes=PS(ob + half), inc=(k == 1))
                P.op("dve", lambda e: e.tensor_tensor(out=xres[:, t, :], in0=xres[:, t, :],
                                                      in1=psall[:, ob:ob + 2, :].rearrange("p a b -> p (a b)"), op=ALU.add),
                     reads=PS(ob, ob + 1) + [("x", t)], writes=[("x", t)])

        def rmsnorm_to_hT(l):
            dma(lambda e: e.dma_start(out=nw, in_=nwb_d[l]), writes=["nw"])
            for t in range(NT):
                P.op("act", lambda e, t=t: e.activation(out=junk[:], in_=xres[:, t, :], func=AF.Square, accum_out=ss[:, 0, t:t + 1]),
                     reads=[("x", t)], writes=["junk", ("ss", t)])
            P.op("dve", lambda e: e.tensor_scalar(out=ss[:, 1, :], in0=ss[:, 0, :], scalar1=1.0 / D, scalar2=EPS, op0=ALU.mult, op1=ALU.add),
                 reads=[("ss", t) for t in range(NT)], writes=["ss1"])
            P.op("pool", lambda e: e.tensor_tensor(out=ss[:, 0, :], in0=ss[:, 1, :], in1=mhalf[:, 0:NT], op=ALU.pow),
                 reads=["ss1", "mhalf"], writes=[("ss", t) for t in range(NT)])
            for t in range(NT):
                i = 0
                P.op("dve", lambda e, t=t, i=i: e.scalar_tensor_tensor(out=hb[i][:], in0=xres[:, t, :], scalar=ss[:, 0, t:t + 1], in1=nw,
                                                                       op0=ALU.mult, op1=ALU.mult),
                     reads=[("x", t), ("ss", t), "nw"], writes=[("hb", i)])
                bk = 5 + (t % 2)
                for k in range(8):
                    P.op("pe", lambda e, k=k, i=i, bk=bk: e.transpose(out=bank[bk][:].bitcast(BF16)[:, k * 128:(k + 1) * 128],
                                                                      in_=hb[i][:, k * 128:(k + 1) * 128], identity=identb[:]),
                         reads=[("hb", i), "identb"], writes=PS(bk), inc=(k == 7))
                P.op("act", lambda e, t=t, bk=bk: e.copy(out=hT[:, :, t * 128:(t + 1) * 128],
                                                         in_=bank[bk][:].bitcast(BF16).rearrange("p (k c) -> p k c", k=8)),
                     reads=PS(bk), writes=["hT"])

        def phase_A(l, hp):
            P.op("pool", lambda e: e.memset(V1[:, :, :, 128:129], 1.0), writes=["V1"])
            wq, wqk = load_w("in", l, OFF["aq"] + hp * 256)
            wk, wkk = load_w("in", l, OFF["ak"] + hp * 256)
            wv_, wvk = load_w("in", l, OFF["av"] + hp * 256)
            n = 0
            for (wv, wkey, dst, dkey, scl) in ((wq, wqk, qT, "qT", 0.125), (wk, wkk, kT, "kT", 1.0)):
                for hh in range(2):
                    for tc in range(4):
                        bk = n % 4
                        n += 1
                        for k in range(8):
                            P.op("pe", lambda e, k=k, bk=bk, wv=wv, hh=hh, tc=tc: e.matmul(
                                bank[bk][:, :], lhsT=wv[:, k, hh * 128:(hh + 1) * 128], rhs=hT[:, k, tc * 512:(tc + 1) * 512],
                                start=(k == 0), stop=(k == 7)), reads=["hT", wkey], writes=PS(bk), inc=(k == 7))
                        if n % 2 == 0:
                            P.op("act", lambda e, bk=bk, dst=dst, hh=hh, tc=tc, scl=scl: e.mul(out=dst[:, hh, tc * 512:(tc + 1) * 512],
                                                                                             in_=bank[bk][:, :], mul=scl),
                                 reads=PS(bk), writes=[dkey])
                        else:
                            P.op("dve", lambda e, bk=bk, dst=dst, hh=hh, tc=tc, scl=scl: e.tensor_scalar(
                                out=dst[:, hh, tc * 512:(tc + 1) * 512], in0=bank[bk][:, :], scalar1=scl, scalar2=None, op0=ALU.mult),
                                reads=PS(bk), writes=[dkey])
            wg, wgk = load_w("in", l, OFF["ag"] + hp * 256)
            wo, wok = load_w("out", l, hp * 256)
            for t in range(NT):
                bk = t % 4
                proj_tok(wv_, wvk, t, bk)
                eng = "act" if t % 2 == 0 else "dve"
                if eng == "act":
                    P.op("act", lambda e, t=t, bk=bk: e.copy(out=V1[:, t, :, 0:128], in_=bank[bk][:, 0:256].rearrange("p (a b) -> p a b", a=2)),
                         reads=PS(bk), writes=["V1"])
                else:
                    P.op("dve", lambda e, t=t, bk=bk: e.tensor_copy(out=V1[:, t, :, 0:128], in_=bank[bk][:, 0:256].rearrange("p (a b) -> p a b", a=2)),
                         reads=PS(bk), writes=["V1"])
            for t in range(NT):
                bk = t % 4
                proj_tok(wg, wgk, t, bk)
                gate_evac(bk, t, t % 2)

            hoist_next()

            def acc(r, c0=0, c1=129):
                return bank[4 + r // 3][:, (r % 3) * 129 + c0:(r % 3) * 129 + c1]

            pend = []

            def sched_chunk_tail(qc):
                items = []

                def pre():
                    P.op("pool", lambda e: e.tensor_tensor(out=oan, in0=oa, in1=oa, op=ALU.mult), reads=["oa"], writes=["oan"])
                    P.op("dve", lambda e: e.reduce_sum(out=sm[:, 16:24], in_=oan.rearrange("p a b c -> p (a b) c"), axis=AX.X),
                         reads=["oan"], writes=["sm"])
                    P.op("dve", lambda e: e.tensor_scalar(out=sm[:, 24:32], in0=sm[:, 16:24], scalar1=1.0 / 128, scalar2=EPS, op0=ALU.mult, op1=ALU.add),
                         reads=["sm"], writes=["sm"])
                    P.op("pool", lambda e: e.tensor_tensor(out=sm[:, 16:24], in0=sm[:, 24:32], in1=mhalf[:, 0:8], op=ALU.pow),
                         reads=["sm", "mhalf"], writes=["sm"])
                    P.op("dve", lambda e: e.tensor_tensor(out=oan.rearrange("p a b c -> p (a b) c"), in0=oa.rearrange("p a b c -> p (a b) c"),
                                                          in1=sm[:, 16:24].unsqueeze(2).to_broadcast([128, 8, 128]), op=ALU.mult),
                         reads=["oa", "sm"], writes=["oan"])
                items.append((0, pre))
                tb = bank[7].bitcast(BF16)[:, 0:256]
                for qs in range(4):
                    t = qc * 4 + qs
                    i = qs % 2
                    g0 = 6 + 6 * qs

                    def T1(qs=qs, t=t, i=i):
                        P.op("pool", lambda e: e.tensor_tensor(out=oan[:, qs], in0=oan[:, qs], in1=W2[:, l], op=ALU.mult),
                             reads=["oan", "W2"], writes=["oan"])
                        P.op("dve", lambda e: e.tensor_tensor(out=mixed[i][:], in0=oan[:, qs].rearrange("p a b -> p (a b)"),
                                                              in1=gate[:, t, :], op=ALU.mult),
                             reads=["oan", "gate"], writes=[("mixed", i)])

                    def T2(i=i):
                        for k in range(2):
                            P.op("pe", lambda e, k=k: e.transpose(out=tb[:, k * 128:(k + 1) * 128], in_=mixed[i][:, k * 128:(k + 1) * 128],
                                                                  identity=identb[:]),
                                 reads=[("mixed", i), "identb"], writes=PS(7), inc=(k == 1))
                        P.op("dve", lambda e: e.tensor_copy(out=mT[i][:].rearrange("p a b -> p (a b)"), in_=tb), reads=PS(7), writes=[("mT", i)])

                    def T4(half, t=t, i=i):
                        for k in range(2):
                            P.op("pe", lambda e, k=k: e.matmul(bank[7], lhsT=mT[i][:, k, :], rhs=wo[:, k, half * 512:(half + 1) * 512],
                                                               start=(k == 0), stop=(k == 1)),
                                 reads=[("mT", i), wok], writes=PS(7), inc=(k == 1))
                        P.op("dve", lambda e: e.tensor_tensor(out=xres[:, t, half * 512:(half + 1) * 512],
                                                              in0=xres[:, t, half * 512:(half + 1) * 512], in1=bank[7], op=ALU.add),
                             reads=PS(7) + [("x", t)], writes=[("x", t)])
                    items += [(g0, T1), (g0 + 2, T2), (g0 + 4, lambda T4=T4: T4(0)), (g0 + 5, lambda T4=T4: T4(1))]
                return items

            for qc in range(4):
                for hh in range(2):
                    h = 2 * hp + hh
                    seq = []
                    for kt in range(NT):
                        d = kt - 4 * qc
                        near = -1 <= d <= 4
                        seq.append((kt, near, d))

                    def emit_qk(j, hh=hh, qc=qc, h=h):
                        kt, near, d = seq[j]
                        b0 = (j % 2) * 2
                        for m in range(2):
                            P.op("pe", lambda e, m=m, kt=kt, b0=b0: e.matmul(
                                bank[b0 + m][:, :], lhsT=kT[m * 64:(m + 1) * 64, hh, kt * 128:(kt + 1) * 128],
                                rhs=qT[m * 64:(m + 1) * 64, hh, qc * 512:(qc + 1) * 512], start=True, stop=True),
                                reads=["kT", "qT"], writes=PS(b0 + m), inc=(m == 1))

                    def emit_exp(j, hh=hh, qc=qc, h=h):
                        kt, near, d = seq[j]
                        b0 = (j % 2) * 2
                        pt = PT[j % 3]
                        if near:
                            bias_ap = zero1[:, 0:1]
                        elif d > 4:
                            bias_ap = cfar[:, h:h + 1]
                        else:
                            bias_ap = cfar[:, 4 + h:5 + h]
                        P.op("act", lambda e: e.activation(out=pt.rearrange("p a b -> p (a b)"),
                                                           in_=psall[:, b0:b0 + 2, :].rearrange("p a b -> p (a b)"),
                                                           func=AF.Exp, bias=bias_ap, scale=1.0),
                             reads=PS(b0, b0 + 1) + ["cfar", "zero1"], writes=[("PT", j % 3, 0), ("PT", j % 3, 1)])
                        if near:
                            base = 512 - 128 * d
                            for m in range(2):
                                P.op("dve", lambda e, base=base, m=m: e.tensor_tensor(
                                    out=pt[:, m, :], in0=pt[:, m, :], in1=bmast[:, h, base:base + 512], op=ALU.mult),
                                    reads=["bmast"], writes=[("PT", j % 3, m)])

                    def emit_pv(j, hh=hh, qc=qc, h=h):
                        kt, near, d = seq[j]
                        pt = PT[j % 3]
                        for m in range(2):
                            for qs in range(4):
                                r = m * 4 + qs
                                first = (kt == 0 and r % 3 == 0)
                                P.op("pe", lambda e, m=m, qs=qs, r=r, first=first: e.matmul(
                                    acc(r), lhsT=pt[:, m, qs * 128:(qs + 1) * 128], rhs=V1[:, kt, hh, 0:129],
                                    start=first, stop=(kt == NT - 1), skip_group_check=True),
                                    reads=[("PT", j % 3, m), "V1"], writes=PS(4 + r // 3), inc=(m == 1 and qs == 3))

                    emit_qk(0)
                    for j in range(NT + 1):
                        if j + 1 < NT:
                            emit_qk(j + 1)
                        if j < NT:
                            emit_exp(j)
                        if j >= 1:
                            emit_pv(j - 1)
                        g_ = hh * NT + j
                        if j < NT:
                            for it_ in [x for x in pend if x[0] == g_]:
                                pend.remove(it_)
                                it_[1]()
                    P.op("act", lambda e: e.copy(out=accS[:, 0:3, :].rearrange("p a b -> p (a b)"), in_=bank[4][:, 0:387]), reads=PS(4), writes=["accS0"])
                    P.op("dve", lambda e: e.tensor_copy(out=accS[:, 3:6, :].rearrange("p a b -> p (a b)"), in_=bank[5][:, 0:387]), reads=PS(5), writes=["accS1"])
                    P.op("act", lambda e: e.copy(out=accS[:, 6:8, :].rearrange("p a b -> p (a b)"), in_=bank[6][:, 0:258]), reads=PS(6), writes=["accS2"])
                    P.op("dve", lambda e: e.reciprocal(out=sm[:, 0:8], in_=accS[:, :, 128]), reads=["accS0", "accS1", "accS2"], writes=["sm"])
                    P.op("dve", lambda e: e.tensor_scalar(out=sm[:, 4:8], in0=sm[:, 4:8], scalar1=nlam[:, l:l + 1], scalar2=None, op0=ALU.mult),
                         reads=["sm", "nlam"], writes=["sm"])
                    for qs in range(4):
                        r1 = 4 + qs
                        P.op("dve", lambda e, qs=qs, r1=r1: e.tensor_scalar(out=tmpA, in0=accS[:, r1, 0:128], scalar1=sm[:, r1:r1 + 1],
                                                                            scalar2=None, op0=ALU.mult),
                             reads=["accS0", "accS1", "accS2", "sm"], writes=["tmpA"])
                        P.op("dve", lambda e, qs=qs, hh=hh: e.scalar_tensor_tensor(out=oa[:, qs, hh, :], in0=accS[:, qs, 0:128], scalar=sm[:, qs:qs + 1],
                                                                                  in1=tmpA, op0=ALU.mult, op1=ALU.add),
                             reads=["accS0", "accS1", "accS2", "sm", "tmpA"], writes=["oa"])
                for it_ in sorted(pend, key=lambda x: x[0]):
                    it_[1]()
                pend = sched_chunk_tail(qc)
            for it_ in sorted(pend, key=lambda x: x[0]):
                it_[1]()
            pend = []

        def phase_B(l, hp):
            wqk_, wqkk = load_w("inqk", l, OFF["bq"] + hp * 128)
            wv_, wvk = load_w("in", l, OFF["bv"] + hp * 256)
            wg, wgk = load_w("in", l, OFF["bg"] + hp * 256)
            wo, wok = load_w("out", l, 512 + hp * 256)
            dma(lambda e: e.dma_start(out=cs, in_=cs_d), writes=["cs"])
            for s2 in range(2):
                P.op("pool", lambda e, s2=s2: e.memset(TTq2[s2], 0.0), writes=[("TTq2", s2)])
            for t in range(NT):
                bk = t % 4
                i2 = t % 2
                proj_tok(wqk_, wqkk, t, bk)
                P.op("act", lambda e, bk=bk, i2=i2: e.copy(out=qk32[i2][:, 0:128], in_=bank[bk][:, 0:128]), reads=PS(bk), writes=[("qk32", i2)])
                P.op("act", lambda e, bk=bk, i2=i2: e.mul(out=qk32[i2][:, 128:256], in_=bank[bk][:, 128:256], mul=0.125),
                     reads=PS(bk), writes=[("qk32", i2)])
                src4 = qk32[i2].rearrange("p (a b c) -> p a b c", a=4, b=2)
                cos_b = cs[:, 0, t, :].unsqueeze(1).to_broadcast([128, 4, 32])
                sin_b = cs[:, 1, t, :].unsqueeze(1).to_broadcast([128, 4, 32])
                t1 = src4[:, :, 0, :]
                t2 = src4[:, :, 1, :]
                rt = [rtmp[i2][:, i].rearrange("p (a c) -> p a c", a=4) for i in range(4)]
                dst4 = qkr[:, t, :].rearrange("p (a b c) -> p a b c", a=4, b=2)
                P.op("pool", lambda e, t1=t1, cos_b=cos_b, rt=rt: e.tensor_tensor(out=rt[0], in0=t1, in1=cos_b, op=ALU.mult),
                     reads=[("qk32", i2), "cs"], writes=[("rtmp", i2, 0)])
                P.op("pool", lambda e, t2=t2, sin_b=sin_b, rt=rt: e.tensor_tensor(out=rt[1], in0=t2, in1=sin_b, op=ALU.mult),
                     reads=[("qk32", i2), "cs"], writes=[("rtmp", i2, 1)])
                P.op("dve", lambda e, t1=t1, sin_b=sin_b, rt=rt: e.tensor_tensor(out=rt[2], in0=t1, in1=sin_b, op=ALU.mult),
                     reads=[("qk32", i2), "cs"], writes=[("rtmp", i2, 2)])
                P.op("dve", lambda e, t2=t2, cos_b=cos_b, rt=rt: e.tensor_tensor(out=rt[3], in0=t2, in1=cos_b, op=ALU.mult),
                     reads=[("qk32", i2), "cs"], writes=[("rtmp", i2, 3)])
                P.op("pool", lambda e, rt=rt, dst4=dst4: e.tensor_tensor(out=dst4[:, :, 0, :], in0=rt[0], in1=rt[1], op=ALU.subtract),
                     reads=[("rtmp", i2, 0), ("rtmp", i2, 1)], writes=[("qkr", t, 0)])
                P.op("dve", lambda e, rt=rt, dst4=dst4: e.tensor_tensor(out=dst4[:, :, 1, :], in0=rt[2], in1=rt[3], op=ALU.add),
                     reads=[("rtmp", i2, 2), ("rtmp", i2, 3)], writes=[("qkr", t, 1)])
            for t in range(NT):
                bk = t % 4
                proj_tok(wv_, wvk, t, bk)
                if t % 2 == 0:
                    P.op("act", lambda e, t=t, bk=bk: e.copy(out=vB[:, t, :], in_=bank[bk][:, 0:256]), reads=PS(bk), writes=[("vB", t)])
                else:
                    P.op("dve", lambda e, t=t, bk=bk: e.tensor_copy(out=vB[:, t, :], in_=bank[bk][:, 0:256]), reads=PS(bk), writes=[("vB", t)])
            for t in range(NT):
                bk = t % 4
                proj_tok(wg, wgk, t, bk)
                gate_evac(bk, t, t % 2)
            hoist_next()
            QKR = lambda t: [("qkr", t, 0), ("qkr", t, 1)]

            def kvp(t):
                return bank[t // 2][:, (t % 2) * 256:(t % 2) * 256 + 256]

            for t in range(NT):
                i2 = t % 2
                kq = qkr[:, t, 128:256].rearrange("p (a c) -> p a c", a=2)
                for dr in range(2):
                    c0 = 8 + dr * 4 + 2 * hp
                    P.op("pool", lambda e, dr=dr, c0=c0, kq=kq, i2=i2: e.tensor_tensor(
                        out=kdec[i2][:, :, dr, :], in0=kq, in1=tqk[:, l, c0:c0 + 2].unsqueeze(2).to_broadcast([128, 2, 64]), op=ALU.mult),
                        reads=QKR(t) + ["tqk"], writes=[("kdec", i2)])
                for hh in range(2):
                    P.op("pe", lambda e, hh=hh, t=t, i2=i2: e.matmul(kvp(t)[:, hh * 128:(hh + 1) * 128], lhsT=kdec[i2][:, hh].rearrange("p a b -> p (a b)"),
                                                                    rhs=vB[:, t, hh * 128:(hh + 1) * 128], start=True, stop=True),
                         reads=[("kdec", i2), ("vB", t)], writes=PS(t // 2), inc=(hh == 1))

            P.op("pool", lambda e: e.memset(Rcur, 0.0), writes=[("Rcur", 0), ("Rcur", 64)])
            for n_ in range(NT):
                for (lo, hi, t, dr) in ((0, 64, n_, 0), (64, 128, NT - 1 - n_, 1)):
                    P.op("act", lambda e, t=t, lo=lo, hi=hi: e.copy(out=Rst[lo:hi, t], in_=Rcur[lo:hi]), reads=[("Rcur", lo)], writes=[("Rst", t, lo)])
                    if n_ == NT - 1:
                        continue
                    for hh in range(2):
                        gc = l * 8 + dr * 4 + 2 * hp + hh
                        P.op("dve", lambda e, lo=lo, hi=hi, t=t, hh=hh, gc=gc: e.scalar_tensor_tensor(
                            out=Rcur[lo:hi, hh, :], in0=Rcur[lo:hi, hh, :], scalar=gsc[lo:hi, gc:gc + 1],
                            in1=kvp(t)[lo:hi, hh * 128:(hh + 1) * 128], op0=ALU.mult, op1=ALU.add),
                            reads=PS(t // 2) + [("Rcur", lo), "gsc"], writes=[("Rcur", lo)])

            def S0a(t):
                s2 = t % 2
                s3 = t % 3
                bTS = 2 * s2
                qq = qkr[:, t, 0:128].rearrange("p (a c) -> p a c", a=2)
                for dr in range(2):
                    c0 = dr * 4 + 2 * hp
                    P.op("pool", lambda e, dr=dr, c0=c0, qq=qq, s2=s2: e.tensor_tensor(
                        out=qdec[s2][:, :, dr, :], in0=qq, in1=tqk[:, l, c0:c0 + 2].unsqueeze(2).to_broadcast([128, 2, 64]), op=ALU.mult),
                        reads=QKR(t) + ["tqk"], writes=[("qdec", s2)])
                bT = bank[bTS].bitcast(BF16)
                srcs = [qdec[s2][:, 0].rearrange("p a b -> p (a b)"), qdec[s2][:, 1].rearrange("p a b -> p (a b)"),
                        qkr[:, t, 128:256], qkr[:, t, 0:128]]
                for i4 in range(4):
                    P.op("pe", lambda e, i4=i4, srcs=srcs, bT=bT: e.transpose(out=bT[:, i4 * 128:(i4 + 1) * 128], in_=srcs[i4], identity=identb[:]),
                         reads=[("qdec", s2), "identb"] + QKR(t), writes=PS(bTS), inc=(i4 == 3))
                P.op("act", lambda e, bT=bT, s3=s3: e.copy(out=TT[s3].rearrange("p a b -> p (a b)"), in_=bT[:, 0:384]),
                     reads=PS(bTS), writes=[("TT", s3)])
                for hh in range(2):
                    P.op("act", lambda e, bT=bT, s2=s2, hh=hh: e.copy(out=TTq2[s2][hh * 64:(hh + 1) * 64, hh, :],
                                                                       in_=bT[hh * 64:(hh + 1) * 64, 384:512]),
                         reads=PS(bTS), writes=[("TTq2", s2)])

            def S0b(t):
                s2 = t % 2
                s3 = t % 3
                bTS = 2 * s2
                P.op("pe", lambda e, s2=s2, s3=s3, bTS=bTS: e.matmul(bank[bTS][:, 256:512], lhsT=TT[s3][:, 2, :],
                                                                    rhs=TTq2[s2].rearrange("p a b -> p (a b)"), start=True, stop=True),
                     reads=[("TT", s3), ("TTq2", s2)], writes=PS(bTS), inc=True)
                P.op("dve", lambda e, s2=s2, bTS=bTS: e.tensor_tensor(out=innerT[s2], in0=bank[bTS][:, 256:512].rearrange("p (a b) -> p a b", a=2),
                                                                     in1=D2T[:, l, 2 * hp:2 * hp + 2, :], op=ALU.mult),
                     reads=PS(bTS) + ["D2T"], writes=[("innerT", s2)])

            def S0c(t):
                s2 = t % 2
                s3 = t % 3
                bO = 2 * s2 + 1
                for hh in range(2):
                    P.op("pe", lambda e, hh=hh, t=t, s2=s2, bO=bO: e.matmul(bank[bO][:, hh * 128:(hh + 1) * 128], lhsT=innerT[s2][:, hh, :],
                                                                           rhs=vB[:, t, hh * 128:(hh + 1) * 128], start=True, stop=False),
                         reads=[("innerT", s2), ("vB", t)], writes=PS(bO), inc=False)
                    P.op("pe", lambda e, hh=hh, t=t, s3=s3, bO=bO: e.matmul(bank[bO][:, hh * 128:(hh + 1) * 128], lhsT=TT[s3][:, hh, :],
                                                                           rhs=Rst[:, t, hh, :], start=False, stop=True),
                         reads=[("TT", s3), ("Rst", t, 0), ("Rst", t, 64)], writes=PS(bO), inc=(hh == 1))

            def S1(t):
                s2 = t % 2
                bO = 2 * s2 + 1
                sc = 32 + 8 * s2
                for hh in range(2):
                    P.op("act", lambda e, hh=hh, bO=bO, sc=sc: e.activation(out=junk[:, hh * 128:(hh + 1) * 128], in_=bank[bO][:, hh * 128:(hh + 1) * 128],
                                                                          func=AF.Square, accum_out=sm[:, sc + hh:sc + hh + 1]),
                         reads=PS(bO), writes=["junk", ("smB", s2)])
                P.op("dve", lambda e, sc=sc: e.tensor_scalar(out=sm[:, sc + 2:sc + 4], in0=sm[:, sc:sc + 2], scalar1=4.0 / 128, scalar2=4.0 * EPS,
                                                             op0=ALU.mult, op1=ALU.add), reads=[("smB", s2)], writes=[("smB", s2)])
                P.op("pool", lambda e, sc=sc: e.tensor_tensor(out=sm[:, sc + 4:sc + 6], in0=sm[:, sc + 2:sc + 4], in1=mhalf[:, 0:2], op=ALU.pow),
                     reads=[("smB", s2), "mhalf"], writes=[("smB", s2)])
                i = t % 2
                P.op("dve", lambda e, s2=s2, bO=bO, sc=sc: e.tensor_tensor(out=btmp[s2], in0=bank[bO][:, 0:256].rearrange("p (a b) -> p a b", a=2),
                                                                          in1=sm[:, sc + 4:sc + 6].unsqueeze(2).to_broadcast([128, 2, 128]), op=ALU.mult),
                     reads=PS(bO) + [("smB", s2)], writes=[("btmp", s2)])
                P.op("pool", lambda e, i=i, t=t, s2=s2: e.tensor_tensor(out=mixed[i][:], in0=btmp[s2].rearrange("p a b -> p (a b)"), in1=gate[:, t, :], op=ALU.mult),
                     reads=[("btmp", s2), "gate"], writes=[("mixed", i)])

            def S2(t):
                s2 = t % 2
                bO = 2 * s2 + 1
                i = t % 2
                tail(mixed[i], ("mixed", i), t, wo, wok, tb=bank[bO].bitcast(BF16)[:, 512:768], tbk=bO, ob=4 + 2 * s2, i=i)

            for it in range(NT + 4):
                if it < NT:
                    S0a(it)
                if 0 <= it - 1 < NT:
                    S0b(it - 1)
                if 0 <= it - 2 < NT:
                    S0c(it - 2)
                if 0 <= it - 3 < NT:
                    S1(it - 3)
                if 0 <= it - 4 < NT:
                    S2(it - 4)

        def phase_C(l, gp):
            wu, wuk = load_w("in", l, OFF["cu"] + gp * 256)
            wg, wgk = load_w("in", l, OFF["cg"] + gp * 256)
            wo, wok = load_w("out", l, 1024 + gp * 256)
            for t in range(NT):
                bk = t % 4
                proj_tok(wu, wuk, t, bk)
                if t % 2 == 0:
                    P.op("act", lambda e, t=t, bk=bk: e.copy(out=uC[:, t, :], in_=bank[bk][:, 0:256]), reads=PS(bk), writes=[("uC", t)])
                else:
                    P.op("dve", lambda e, t=t, bk=bk: e.tensor_copy(out=uC[:, t, :], in_=bank[bk][:, 0:256]), reads=PS(bk), writes=[("uC", t)])
            for t in range(NT):
                bk = t % 4
                proj_tok(wg, wgk, t, bk)
                gate_evac(bk, t, t % 2)
            hoist_next()

            def S0(t):
                s2 = t % 2
                bP = 2 * s2
                bY = 2 * s2 + 1
                for gg in range(2):
                    g = 2 * gp + gg
                    parts = [(t, g * 5 + (3 if t == 0 else 4 if t == NT - 1 else 0))]
                    if t > 0:
                        parts.append((t - 1, g * 5 + 1))
                    if t < NT - 1:
                        parts.append((t + 1, g * 5 + 2))
                    for pi_, (tj, mi) in enumerate(parts):
                        P.op("pe", lambda e, gg=gg, tj=tj, mi=mi, pi_=pi_, np_=len(parts), bP=bP: e.matmul(
                            bank[bP][:, gg * 128:(gg + 1) * 128], lhsT=uC[:, tj, gg * 128:(gg + 1) * 128], rhs=poolm[:, mi, :],
                            start=(pi_ == 0), stop=(pi_ == np_ - 1)),
                            reads=[("uC", tj), "poolm"], writes=PS(bP), inc=(gg == 1 and pi_ == len(parts) - 1))
                P.op("act", lambda e, bP=bP, s2=s2: e.copy(out=pooledT[s2].rearrange("p a b -> p (a b)"), in_=bank[bP][:, 0:256]),
                     reads=PS(bP), writes=[("pooledT", s2)])
                for gg in range(2):
                    g = 2 * gp + gg
                    P.op("pe", lambda e, gg=gg, g=g, s2=s2, bY=bY: e.matmul(bank[bY][:, gg * 128:(gg + 1) * 128], lhsT=pooledT[s2][:, gg, :],
                                                                           rhs=poolw[:, l * 4 + g, :], start=True, stop=True),
                         reads=[("pooledT", s2), "poolw"], writes=PS(bY), inc=(gg == 1))

            def S1(t):
                s2 = t % 2
                bY = 2 * s2 + 1
                i = t % 2
                P.op("dve", lambda e, s2=s2, bY=bY: e.tensor_tensor(out=ytmp[s2], in0=bank[bY][:, 0:256], in1=psh[:, l, gp * 256:(gp + 1) * 256], op=ALU.mult),
                     reads=PS(bY) + ["psh"], writes=[("ytmp", s2)])
                P.op("pool", lambda e, t=t, i=i, s2=s2: e.tensor_tensor(out=mixed[i][:], in0=ytmp[s2], in1=gate[:, t, :], op=ALU.mult),
                     reads=[("ytmp", s2), "gate"], writes=[("mixed", i)])

            def S2(t):
                s2 = t % 2
                bY = 2 * s2 + 1
                i = t % 2
                tail(mixed[i], ("mixed", i), t, wo, wok, tb=bank[bY].bitcast(BF16)[:, 512:768], tbk=bY, ob=4 + 2 * s2, i=i)

            for it in range(NT + 2):
                if it < NT:
                    S0(it)
                if 0 <= it - 1 < NT:
                    S1(it - 1)
                if 0 <= it - 2 < NT:
                    S2(it - 2)

        for s_ in range(nseq):
            for l in range(nlayers):
                for ph in phases:
                    for hp in range(2):
                        sched["list"].append((ph, l, hp))
        sched["i"] = -1
        hoist_next()
        fns = {"A": phase_A, "B": phase_B, "C": phase_C}
        pi = 0
        for s in range(nseq):
            for t in range(NT):
                dma(lambda e, s=s, t=t: e.dma_start(out=xres[:, t, :], in_=x_d[s, t]), writes=[("x", t)])
            for l in range(nlayers):
                P.fence()
                rmsnorm_to_hT(l)
                for ph in phases:
                    for hp in range(2):
                        if hp == 0:
                            P.fence()
                        sched["i"] = pi
                        fns[ph](l, hp)
                        pi += 1
            P.fence()
            dma(lambda e: e.dma_start(out=nw, in_=nwb_d[2]), writes=["nw"])
            for t in range(NT):
                P.op("act", lambda e, t=t: e.activation(out=junk[:], in_=xres[:, t, :], func=AF.Square, accum_out=ss[:, 0, t:t + 1]),
                     reads=[("x", t)], writes=["junk", ("ss", t)])
            P.op("dve", lambda e: e.tensor_scalar(out=ss[:, 1, :], in0=ss[:, 0, :], scalar1=1.0 / D, scalar2=EPS, op0=ALU.mult, op1=ALU.add),
                 reads=[("ss", t) for t in range(NT)], writes=["ss1"])
            P.op("pool", lambda e: e.tensor_tensor(out=ss[:, 0, :], in0=ss[:, 1, :], in1=mhalf[:, 0:NT], op=ALU.pow),
                 reads=["ss1", "mhalf"], writes=[("ss", t) for t in range(NT)])
            for t in range(NT):
                P.op("dve", lambda e, t=t: e.scalar_tensor_tensor(out=xres[:, t, :], in0=xres[:, t, :], scalar=ss[:, 0, t:t + 1], in1=nw,
                                                                  op0=ALU.mult, op1=ALU.mult),
                     reads=[("x", t), ("ss", t), "nw"], writes=[("x", t)])
                dma(lambda e, s=s, t=t: e.dma_start(out=y_d[s, t], in_=xres[:, t, :]), reads=[("x", t)], writes=[("y", s, t)])
        P.emit(nc, st)
    return nc


_NC_CACHE = {}


def kernel(**inputs):
    x = np.asarray(inputs["x"], np.float32)
    B = x.shape[0]
    ncores = 8
    nseq = B // ncores
    consts = _host_consts(inputs)
    key = (nseq,)
    if key not in _NC_CACHE:
        _NC_CACHE[key] = build_nc(nseq=nseq)
    nc = _NC_CACHE[key]
    w_in = np.ascontiguousarray(np.asarray(inputs["w_in"], np.float32))
    w_out = np.ascontiguousarray(np.asarray(inputs["w_out"], np.float32))
    in_maps = []
    for c in range(ncores):
        m = dict(consts)
        m["x"] = np.ascontiguousarray(x[c * nseq:(c + 1) * nseq].reshape(nseq, NT, 128, D))
        m["w_in"] = w_in
        m["w_out"] = w_out
        in_maps.append(m)
    res = run_bass_kernel_spmd(nc, in_maps, core_ids=list(range(ncores)))
    out = np.concatenate([np.asarray(r["y"]).reshape(nseq, S, D) for r in res.results], axis=0)
    return out.astype(np.float32)
```

```python
import contextlib
import math
import numpy as np
import concourse.bass as bass
import concourse.mybir as mybir
from concourse.bass_utils import run_bass_kernel_spmd

F32 = mybir.dt.float32
BF16 = mybir.dt.bfloat16
ALU = mybir.AluOpType
AF = mybir.ActivationFunctionType
AX = mybir.AxisListType

D = 1024
S = 2048
NT = S // 128
DIN = 4608
DMIX = 1536
EPS = 1e-6
OFF = dict(aq=0, ak=512, av=1024, ag=1536, bq=2048, bk=2304, bv=2560, bg=3072, cu=3584, cg=4096)
MW = 1152

ENGS = ("pe", "act", "dve", "pool", "sp")
CUT = 99
SEM_LIMIT = 30000


class Op:
    __slots__ = ("eng", "fn", "deps", "inc", "is_dma", "seq", "eidx", "sem", "val", "name",
                 "closer", "dslot", "nofence")


class Prog:
    def __init__(self):
        self.ops = []
        self.last_w = {}
        self.readers = {}
        self.pend = {e: set() for e in ENGS}
        self.fence_idx = 0

    def fence(self):
        deps = set()
        last = {}
        for o in self.ops[self.fence_idx:]:
            if o.is_dma:
                if not o.nofence:
                    deps.add(o.seq)
            else:
                last[o.eng] = o.seq
        deps |= set(last.values())
        for e in ENGS:
            self.pend[e] |= deps
        self.fence_idx = len(self.ops)

    def _add(self, eng, fn, reads, writes, inc, is_dma, name):
        o = Op()
        o.eng, o.fn, o.inc, o.is_dma, o.name = eng, fn, inc, is_dma, name
        o.seq = len(self.ops)
        o.closer = o.seq
        o.nofence = False
        deps = set(self.pend[eng])
        self.pend[eng] = set()
        reads = list(reads)
        writes = list(writes)
        ex = [r for r in reads if isinstance(r, tuple) and r[0] == "ps"]
        reads = [r for r in reads if r not in ex]
        writes = writes + [r for r in ex if r not in writes]
        for r in reads:
            if r in self.last_w:
                deps.add(self.last_w[r])
        for w in writes:
            if w in self.last_w:
                deps.add(self.last_w[w])
            for rd in self.readers.get(w, ()):
                deps.add(rd)
        for r in reads:
            self.readers.setdefault(r, []).append(o.seq)
        for w in writes:
            self.last_w[w] = o.seq
            self.readers[w] = []
        deps.discard(o.seq)
        o.deps = deps
        self.ops.append(o)
        return o

    def op(self, eng, fn, reads=(), writes=(), inc=True, name=""):
        return self._add(eng, fn, reads, writes, inc, False, name)

    def dma(self, fn, reads=(), writes=(), queue="sp", name=""):
        return self._add(queue, fn, reads, writes, True, True, name)

    def emit(self, nc, st, ndma_sems=8):
        ops = self.ops
        per_eng = {e: [] for e in ENGS}
        for o in ops:
            o.eidx = len(per_eng[o.eng])
            per_eng[o.eng].append(o)
        nsem = [0]

        def newsem(tag):
            nsem[0] += 1
            return st.enter_context(nc.semaphore("%s_%d" % (tag, nsem[0])))

        dsem = {q: [newsem("d" + q) for _ in range(ndma_sems)] for q in ("sp", "pool")}
        for e in ENGS:
            cnt = 0
            cur = newsem("s" + e)
            dcnt = [0] * ndma_sems
            nd = 0
            pending = []
            for o in per_eng[e]:
                if o.is_dma:
                    j = nd % ndma_sems
                    nd += 1
                    dcnt[j] += 16
                    o.sem, o.val, o.dslot = dsem[e][j], dcnt[j], j
                elif o.inc:
                    if cnt >= SEM_LIMIT:
                        cur = newsem("s" + e)
                        cnt = 0
                    cnt += 1
                    o.sem, o.val = cur, cnt
                    for p in pending:
                        p.sem, p.val, p.closer = cur, cnt, o.seq
                    pending = []
                else:
                    pending.append(o)
            assert not pending, "trailing no-inc ops on " + e
        blk = st.enter_context(nc.Block())

        def make(e):
            def body(engine):
                waited = {}
                last_dma = {}

                def need(p):
                    if waited.get(p.sem, 0) >= p.val:
                        return
                    waited[p.sem] = p.val
                    engine.wait_ge(p.sem, p.val)

                for o in per_eng[e]:
                    for d in sorted(o.deps):
                        p = ops[d]
                        if p.eng == e and not p.is_dma and not o.is_dma:
                            if e != "pe" and o.eidx - p.eidx <= 2:
                                need(p)
                            continue
                        assert p.closer < o.seq, (p.name, o.name)
                        need(p)
                    if o.is_dma:
                        j = o.dslot
                        if j in last_dma:
                            need(last_dma[j])
                        last_dma[j] = o
                        o.fn(engine).then_inc(o.sem, 16)
                    else:
                        ins = o.fn(engine)
                        if o.inc:
                            ins.then_inc(o.sem, 1)
                for pv in last_dma.values():
                    need(pv)
            return body

        blk.tensor(make("pe"))
        blk.scalar(make("act"))
        blk.vector(make("dve"))
        blk.gpsimd(make("pool"))
        blk.sync(make("sp"))


def _t5_bucket(rel):
    half, max_exact = 16, 8
    ret = np.where(rel > 0, half, 0)
    n = np.abs(rel)
    nf = np.maximum(n, 1).astype(np.float32)
    large = max_exact + (np.log(nf / np.float32(max_exact)) / np.float32(math.log(128 / max_exact))
                         * np.float32(half - max_exact)).astype(np.int32)
    large = np.minimum(large, half - 1)
    return ret + np.where(n < max_exact, n, large)


def _pool_mats():
    pm = np.zeros((128, 20, 128), np.float32)
    for g, w in enumerate((2, 4, 8, 16)):
        for v in range(5):
            t = {0: 5, 1: 5, 2: 5, 3: 0, 4: NT - 1}[v]
            for i in range(128):
                gi = t * 128 + i
                lo = min(max(gi - w // 2, 0), S)
                hi = min(max(gi + (w - w // 2), 0), S)
                cnt = float(hi - lo)
                for gj in range(lo, hi):
                    tj, j = divmod(gj, 128)
                    rel = tj - t
                    if v in (0, 3, 4) and rel == 0:
                        pm[j, g * 5 + v, i] += 1.0 / cnt
                    elif v == 1 and rel == -1:
                        pm[j, g * 5 + v, i] += 1.0 / cnt
                    elif v == 2 and rel == 1:
                        pm[j, g * 5 + v, i] += 1.0 / cnt
                if v in (0, 3, 4):
                    pm[i, g * 5 + v, i] -= 1.0
    return pm


def _host_consts(inp):
    c = {}
    bc = lambda a: np.ascontiguousarray(np.broadcast_to(a, (128,) + a.shape)).astype(np.float32)
    c["nwb"] = np.ascontiguousarray(np.stack([bc(inp["norm_w"][0]), bc(inp["norm_w"][1]),
                                               bc(inp["final_norm_w"])], 0))
    c["ident"] = np.eye(128, dtype=np.float32)
    p = np.arange(128)[:, None]
    cc = np.arange(MW)[None, :]
    bidx = _t5_bucket(p - cc + 512)
    rb = np.asarray(inp["rel_bias"], np.float32)
    c["bmaster"] = np.ascontiguousarray(rb[bidx].transpose(0, 2, 1))
    c["cfar"] = bc(np.concatenate([rb[31], rb[15]]))
    half = 32
    theta = (1.0 / (np.float32(10000.0) ** np.linspace(0.0, 1.0, half, dtype=np.float32))).astype(np.float32)
    ang = (np.arange(S, dtype=np.float32)[:, None] * theta[None, :]).astype(np.float32)
    cs = np.stack([np.cos(ang), np.sin(ang)], 0).astype(np.float32)
    c["cs"] = np.ascontiguousarray(cs.reshape(2, NT, 128, half).transpose(2, 0, 1, 3))
    m = np.arange(128, dtype=np.float32)[:, None]
    n = np.arange(128, dtype=np.float32)[None, :]
    c["retc"] = np.ascontiguousarray(np.stack([np.maximum(n - m, 0), np.maximum(m - n, 0)], 1))
    i = np.arange(128, dtype=np.float32)
    c["tokidx"] = np.ascontiguousarray(np.stack([i + 1, 128 - i, 127 - i, i], 1))
    c["dlam"] = bc(np.asarray(inp["diff_lambda"], np.float32).reshape(2, 256))
    c["subw"] = bc(np.asarray(inp["diff_subln_w"], np.float32))
    c["rdl"] = bc(np.asarray(inp["ret_decay_logit"], np.float32).reshape(16))
    c["pscale"] = bc(np.asarray(inp["pool_scale"], np.float32))
    c["poolw"] = np.ascontiguousarray(np.asarray(inp["pool_w"], np.float32))
    c["poolm"] = _pool_mats()
    return c


def build_nc(nseq=2, nlayers=2, phases="ABC"):
    nc = bass.Bass("TRN2", target_bir_lowering=False)
    din = lambda name, shape: nc.dram_tensor(name, list(shape), F32, kind="ExternalInput").ap()
    x_d = din("x", [nseq, NT, 128, D])
    win_d = din("w_in", [2, D, DIN])
    wout_d = din("w_out", [2, DMIX, D])
    nwb_d = din("nwb", [3, 128, D])
    ident_d = din("ident", [128, 128])
    bm_d = din("bmaster", [128, 4, MW])
    cfar_d = din("cfar", [128, 8])
    cs_d = din("cs", [128, 2, NT, 32])
    retc_d = din("retc", [128, 2, 128])
    tokidx_d = din("tokidx", [128, 4])
    dlam_d = din("dlam", [128, 2, 256])
    subw_d = din("subw", [128, 2, 128])
    rdl_d = din("rdl", [128, 16])
    pscale_d = din("pscale", [128, 2, 512])
    poolw_d = din("poolw", [2, 4, 128, 128])
    poolm_d = din("poolm", [128, 20, 128])
    y_d = nc.dram_tensor("y", [nseq, NT, 128, D], F32, kind="ExternalOutput").ap()

    P = Prog()
    with contextlib.ExitStack() as st:
        def sb(name, shape, dt=F32):
            return st.enter_context(nc.sbuf_tensor("s_" + name, list(shape), dt))

        xres = sb("xres", [128, NT, D])
        hT = sb("hT", [128, 8, S], BF16)
        NSLOT = 6
        wslot = [sb("wslot%d" % i, [128, 2048], BF16) for i in range(NSLOT)]
        bmast = sb("bmast", [128, 4, MW], BF16)
        identb = sb("identb", [128, 128], BF16)
        cfar = sb("cfar", [128, 8])
        zero1 = sb("zero1", [128, 1])
        mhalf = sb("mhalf", [128, 16])
        tokidx = sb("tokidx", [128, 4])
        rdl = sb("rdl", [128, 16])
        poolw = sb("poolw", [128, 8, 128], BF16)
        poolm = sb("poolm", [128, 20, 128], BF16)
        lg = sb("lg", [128, 16])
        tqk = sb("tqk", [128, 2, 16])
        gsc = sb("gsc", [128, 16])
        D2T = sb("D2T", [128, 2, 4, 128])
        W2 = sb("W2", [128, 2, 2, 128])
        psh = sb("psh", [128, 2, 512])
        nlam = sb("nlam", [128, 2])
        sm = sb("sm", [128, 64])
        ss = sb("ss", [128, 2, NT])
        hb = [sb("hb0", [128, D], BF16)] * 2
        junk = sb("junk", [128, D], BF16)
        mixed = [sb("mixed%d" % i, [128, 256], BF16) for i in range(2)]
        mT = [sb("mT%d" % i, [128, 2, 128], BF16) for i in range(2)]
        gate = sb("gate", [128, NT, 256], BF16)
        th = [sb("th%d" % i, [128, 256]) for i in range(2)]
        ARENA = 43 * 1024 + 768
        arena = sb("arena", [128, ARENA], mybir.dt.uint8)

        def carve(off, shape, dt):
            n = int(np.prod(shape))
            bpe = 2 if dt == BF16 else 4
            ap = arena[:, off:off + n * bpe].bitcast(dt)
            if len(shape) > 1:
                names = " ".join("d%d" % i for i in range(len(shape)))
                kw = {"d%d" % i: shape[i] for i in range(1, len(shape))}
                ap = ap.rearrange("p (%s) -> p %s" % (names, names), **kw)
            return ap, off + n * bpe

        o = 0
        nw, o = carve(o, [D], F32)
        identf, o = carve(o, [128], F32)
        retc, o = carve(o, [2, 128], F32)
        dlam, o = carve(o, [2, 256], F32)
        subw, o = carve(o, [2, 128], F32)
        scr, o = carve(o, [256], F32)
        o = 0
        qT, o = carve(o, [2, S], BF16)
        kT, o = carve(o, [2, S], BF16)
        V1, o = carve(o, [NT, 2, 130], BF16)
        oa, o = carve(o, [4, 2, 128], F32)
        oan, o = carve(o, [4, 2, 128], F32)
        PT0, o = carve(o, [2, 512], BF16)
        PT1, o = carve(o, [2, 512], BF16)
        PT2, o = carve(o, [2, 512], BF16)
        PT = [PT0, PT1, PT2]
        tmpA, o = carve(o, [128], F32)
        accS, o = carve(o, [8, 129], F32)
        assert o <= ARENA, o
        o = 0
        qkr, o = carve(o, [NT, 256], BF16)
        vB, o = carve(o, [NT, 256], BF16)
        Rst, o = carve(o, [NT, 2, 128], BF16)
        cs, o = carve(o, [2, NT, 32], F32)
        Rcur, o = carve(o, [2, 128], F32)
        qk32 = []
        rtmp = []
        kdec = []
        qdec = []
        TT = []
        TTq2 = []
        innerT = []
        btmp = []
        for _i in range(2):
            a_, o = carve(o, [256], F32); qk32.append(a_)
            a_, o = carve(o, [4, 128], F32); rtmp.append(a_)
            a_, o = carve(o, [2, 2, 64], BF16); kdec.append(a_)
            a_, o = carve(o, [2, 2, 64], BF16); qdec.append(a_)
            a_, o = carve(o, [2, 128], BF16); TTq2.append(a_)
            a_, o = carve(o, [2, 128], BF16); innerT.append(a_)
            a_, o = carve(o, [2, 128], BF16); btmp.append(a_)
        Osb = []
        for _i in range(3):
            a_, o = carve(o, [3, 128], BF16); TT.append(a_)
            a_, o = carve(o, [2, 128], BF16); Osb.append(a_)
        assert o <= ARENA, o
        o = 0
        uC, o = carve(o, [NT, 256], BF16)
        pooledT = []
        ytmp = []
        for _i in range(2):
            a_, o = carve(o, [2, 128], BF16); pooledT.append(a_)
            a_, o = carve(o, [256], F32); ytmp.append(a_)
        assert o <= ARENA, o

        psall = st.enter_context(nc.psum_tensor("psall", [128, 8, 512], F32))
        bank = [psall[:, i, :] for i in range(8)]

        def PS(*idx):
            return [("ps", i) for i in idx]


        dma = P.dma
        dma(lambda e: e.dma_start(out=identf, in_=ident_d), writes=["identf"])
        dma(lambda e: e.dma_start(out=cfar[:], in_=cfar_d), writes=["cfar"])
        dma(lambda e: e.dma_start(out=retc, in_=retc_d), writes=["retc"])
        dma(lambda e: e.dma_start(out=tokidx[:], in_=tokidx_d), writes=["tokidx"])
        dma(lambda e: e.dma_start(out=dlam, in_=dlam_d), writes=["dlam"])
        dma(lambda e: e.dma_start(out=subw, in_=subw_d), writes=["subw"])
        dma(lambda e: e.dma_start(out=rdl[:], in_=rdl_d), writes=["rdl"])
        dma(lambda e: e.dma_start(out=psh[:], in_=pscale_d), writes=["psh"])
        dma(lambda e: e.dma_start(out=poolw[:].rearrange("c (l g) d -> c l g d", l=2),
                                  in_=poolw_d.rearrange("l g c d -> c l g d")),
            writes=["poolw"], queue="pool")
        dma(lambda e: e.dma_start(out=poolm[:], in_=poolm_d), writes=["poolm"], queue="pool")
        dma(lambda e: e.dma_start(out=bmast[:], in_=bm_d), writes=["bmast"], queue="pool")
        for h_ in range(4):
            P.op("act", lambda e, h_=h_: e.activation(out=bmast[:, h_, :], in_=bmast[:, h_, :], func=AF.Exp), reads=["bmast"], writes=["bmast"])

        P.op("pool", lambda e: e.memset(mhalf[:], -0.5), writes=["mhalf"])
        P.op("pool", lambda e: e.memset(zero1[:], 0.0), writes=["zero1"])
        P.op("dve", lambda e: e.tensor_copy(out=identb[:], in_=identf), reads=["identf"], writes=["identb"])

        lam_init = [0.8 - 0.6 * math.exp(-0.3 * l) for l in range(2)]
        for l in range(2):
            P.op("dve", lambda e, l=l: e.tensor_tensor(out=scr[:, 0:64], in0=dlam[:, l, 0:64], in1=dlam[:, l, 64:128], op=ALU.mult),
                 reads=["dlam"], writes=["junk"])
            P.op("dve", lambda e, l=l: e.tensor_tensor(out=scr[:, 64:128], in0=dlam[:, l, 128:192], in1=dlam[:, l, 192:256], op=ALU.mult),
                 reads=["dlam"], writes=["junk"])
            P.op("dve", lambda e: e.reduce_sum(out=sm[:, 0:2], in_=scr[:, 0:128].rearrange("p (a b) -> p a b", a=2), axis=AX.X),
                 reads=["junk"], writes=["sm"])
            P.op("act", lambda e: e.activation(out=sm[:, 2:4], in_=sm[:, 0:2], func=AF.Exp), reads=["sm"], writes=["sm"])
            P.op("dve", lambda e, l=l: e.tensor_scalar(out=sm[:, 4:5], in0=sm[:, 3:4], scalar1=-lam_init[l], scalar2=None, op0=ALU.add),
                 reads=["sm"], writes=["sm"])
            P.op("dve", lambda e, l=l: e.tensor_tensor(out=nlam[:, l:l + 1], in0=sm[:, 4:5], in1=sm[:, 2:3], op=ALU.subtract),
                 reads=["sm"], writes=["nlam"])
            for hh in range(2):
                P.op("dve", lambda e, l=l, hh=hh: e.tensor_scalar(out=W2[:, l, hh, :], in0=subw[:, l, :],
                                                                  scalar1=(1.0 - lam_init[l]) * 0.5, scalar2=None, op0=ALU.mult),
                     reads=["subw"], writes=["W2"])
            P.op("dve", lambda e, l=l: e.tensor_scalar(out=psh[:, l, :], in0=psh[:, l, :], scalar1=0.5, scalar2=None, op0=ALU.mult),
                 reads=["psh"], writes=["psh"])
        P.op("act", lambda e: e.activation(out=sm[:, 16:32], in_=rdl[:], func=AF.Exp, scale=-1.0), reads=["rdl"], writes=["sm"])
        P.op("dve", lambda e: e.tensor_scalar(out=sm[:, 32:48], in0=sm[:, 16:32], scalar1=1.0, scalar2=None, op0=ALU.add),
             reads=["sm"], writes=["sm"])
        P.op("act", lambda e: e.activation(out=sm[:, 16:32], in_=sm[:, 32:48], func=AF.Ln), reads=["sm"], writes=["sm"])
        P.op("dve", lambda e: e.tensor_scalar(out=lg[:], in0=sm[:, 16:32], scalar1=-1.0, scalar2=None, op0=ALU.mult),
             reads=["sm"], writes=["lg"])
        P.op("act", lambda e: e.activation(out=gsc[:], in_=lg[:], func=AF.Exp, scale=128.0), reads=["lg"], writes=["gsc"])
        for l in range(2):
            lf = l * 8
            lb = l * 8 + 4
            for (dst, src, ti) in ((0, lf, 0), (4, lb, 1), (8, lf, 2), (12, lb, 3)):
                P.op("dve", lambda e, l=l, dst=dst, src=src, ti=ti: e.tensor_scalar(
                    out=sm[:, 48 + dst:52 + dst], in0=lg[:, src:src + 4], scalar1=tokidx[:, ti:ti + 1], scalar2=None, op0=ALU.mult),
                    reads=["lg", "tokidx"], writes=["sm"])
            P.op("act", lambda e, l=l: e.activation(out=tqk[:, l, :], in_=sm[:, 48:64], func=AF.Exp), reads=["sm"], writes=["tqk"])
            for h in range(4):
                P.op("dve", lambda e, l=l, h=h: e.tensor_scalar(out=scr[:, 0:128], in0=retc[:, 0, :], scalar1=lg[:, l * 8 + h:l * 8 + h + 1],
                                                                scalar2=None, op0=ALU.mult), reads=["retc", "lg"], writes=["junk"])
                P.op("dve", lambda e, l=l, h=h: e.scalar_tensor_tensor(out=scr[:, 128:256], in0=retc[:, 1, :],
                                                                       scalar=lg[:, l * 8 + 4 + h:l * 8 + 5 + h], in1=scr[:, 0:128],
                                                                       op0=ALU.mult, op1=ALU.add), reads=["retc", "lg", "junk"], writes=["junk"])
                P.op("act", lambda e, l=l, h=h: e.activation(out=D2T[:, l, h, :], in_=scr[:, 128:256], func=AF.Exp),
                     reads=["junk"], writes=["D2T"])

        P.fence()
        wstate = {"n": 0}

        preloaded = {}
        sched = {"list": [], "i": 0}

        def phase_loads(ph, l, hp):
            if ph == "A":
                return [("in", l, OFF["aq"] + hp * 256), ("in", l, OFF["ak"] + hp * 256), ("in", l, OFF["av"] + hp * 256),
                        ("in", l, OFF["ag"] + hp * 256), ("out", l, hp * 256)]
            if ph == "B":
                return [("inqk", l, OFF["bq"] + hp * 128), ("in", l, OFF["bv"] + hp * 256), ("in", l, OFF["bg"] + hp * 256),
                        ("out", l, 512 + hp * 256)]
            return [("in", l, OFF["cu"] + hp * 256), ("in", l, OFF["cg"] + hp * 256), ("out", l, 1024 + hp * 256)]

        def hoist_next():
            i = sched["i"] + 1
            if i < len(sched["list"]):
                ph, l, hp = sched["list"][i]
                for k in phase_loads(ph, l, hp):
                    if k not in preloaded:
                        preloaded[k] = _load_w(*k)

        def load_w(kind, l, c0):
            k = (kind, l, c0)
            if k in preloaded:
                return preloaded.pop(k)
            return _load_w(kind, l, c0)

        def _load_w(kind, l, c0):
            n0 = len(P.ops)
            r = _load_w2(kind, l, c0)
            for o_ in P.ops[n0:]:
                o_.nofence = True
            return r

        def _load_w2(kind, l, c0):
            i = wstate["n"] % NSLOT
            wstate["n"] += 1
            sl = wslot[i]
            key = ("wslot", i)
            if kind == "in":
                v = sl[:].rearrange("p (k c) -> p k c", k=8)
                dma(lambda e: e.dma_start(out=v, in_=win_d[l, :, c0:c0 + 256].rearrange("(k p) c -> p k c", p=128)),
                    writes=[key], queue="pool")
            elif kind == "inqk":
                v = sl[:].rearrange("p (k c) -> p k c", k=8)
                dma(lambda e: e.dma_start(out=v[:, :, 0:128], in_=win_d[l, :, c0:c0 + 128].rearrange("(k p) c -> p k c", p=128)),
                    writes=[key], queue="pool")
                dma(lambda e: e.dma_start(out=v[:, :, 128:256], in_=win_d[l, :, c0 + 256:c0 + 384].rearrange("(k p) c -> p k c", p=128)),
                    reads=[key], writes=[key], queue="pool")
            else:
                v = sl[:].rearrange("p (k c) -> p k c", k=2)
                dma(lambda e: e.dma_start(out=v, in_=wout_d[l, c0:c0 + 256, :].rearrange("(k p) c -> p k c", p=128)),
                    writes=[key], queue="pool")
            return v, key

        cnt = {"tok": 0, "tail": 0, "ev": 0}

        def proj_tok(wv, wkey, t, bk, ncols=256):
            for k in range(8):
                P.op("pe", lambda e, k=k: e.matmul(bank[bk][:, 0:ncols], lhsT=hT[:, k, t * 128:(t + 1) * 128], rhs=wv[:, k, 0:ncols],
                                                   start=(k == 0), stop=(k == 7)),
                     reads=["hT", wkey], writes=PS(bk), inc=(k == 7))

        def gate_evac(bk, t, i):
            P.op("act", lambda e: e.activation(out=th[i][:], in_=bank[bk][:, 0:256], func=AF.Tanh, scale=0.5),
                 reads=PS(bk), writes=[("th", i)])
            P.op("dve", lambda e: e.scalar_tensor_tensor(out=gate[:, t, :], in0=th[i][:], scalar=1.0, in1=bank[bk][:, 0:256],
                                                         op0=ALU.add, op1=ALU.mult),
                 reads=PS(bk) + [("th", i)], writes=["gate"])

        def tail(mx, mxkey, t, wov, wokey, tb=None, tbk=7, ob=None, i=None):
            if i is None:
                i = cnt["tail"] % 2
                cnt["tail"] += 1
            if tb is None:
                tb = bank[7].bitcast(BF16)[:, 0:256]
            for k in range(2):
                P.op("pe", lambda e, k=k: e.transpose(out=tb[:, k * 128:(k + 1) * 128], in_=mx[:, k * 128:(k + 1) * 128], identity=identb[:]),
                     reads=[mxkey, "identb"], writes=PS(tbk), inc=(k == 1))
            P.op("act", lambda e: e.copy(out=mT[i][:].rearrange("p a b -> p (a b)"), in_=tb), reads=PS(tbk), writes=[("mT", i)])
            if ob is None:
                for half in range(2):
                    for k in range(2):
                        P.op("pe", lambda e, k=k, half=half: e.matmul(bank[7], lhsT=mT[i][:, k, :], rhs=wov[:, k, half * 512:(half + 1) * 512],
                                                                      start=(k == 0), stop=(k == 1)),
                             reads=[("mT", i), wokey], writes=PS(7), inc=(k == 1))
                    P.op("dve", lambda e, half=half: e.tensor_tensor(out=xres[:, t, half * 512:(half + 1) * 512],
                                                                     in0=xres[:, t, half * 512:(half + 1) * 512], in1=bank[7], op=ALU.add),
                         reads=PS(7) + [("x", t)], writes=[("x", t)])
            else:
                for half in range(2):
                    for k in range(2):
                        P.op("pe", lambda e, k=k, half=half: e.matmul(bank[ob + half], lhsT=mT[i][:, k, :], rhs=wov[:, k, half * 512:(half + 1) * 512],
                                                                      start=(k == 0), stop=(k == 1)),
                             reads=[("mT", i), wokey], writes=PS(ob + half), inc=(k == 1))
                P.op("dve", lambda e: e.tensor_tensor(out=xres[:, t, :], in0=xres[:, t, :],
                                                      in1=psall[:, ob:ob + 2, :].rearrange("p a b -> p (a b)"), op=ALU.add),
                     reads=PS(ob, ob + 1) + [("x", t)], writes=[("x", t)])

        def rmsnorm_to_hT(l):
            dma(lambda e: e.dma_start(out=nw, in_=nwb_d[l]), writes=["nw"])
            for t in range(NT):
                P.op("act", lambda e, t=t: e.activation(out=junk[:], in_=xres[:, t, :], func=AF.Square, accum_out=ss[:, 0, t:t + 1]),
                     reads=[("x", t)], writes=["junk", ("ss", t)])
            P.op("dve", lambda e: e.tensor_scalar(out=ss[:, 1, :], in0=ss[:, 0, :], scalar1=1.0 / D, scalar2=EPS, op0=ALU.mult, op1=ALU.add),
                 reads=[("ss", t) for t in range(NT)], writes=["ss1"])
            P.op("pool", lambda e: e.tensor_tensor(out=ss[:, 0, :], in0=ss[:, 1, :], in1=mhalf[:, 0:NT], op=ALU.pow),
                 reads=["ss1", "mhalf"], writes=[("ss", t) for t in range(NT)])
            for t in range(NT):
                i = 0
                P.op("dve", lambda e, t=t, i=i: e.scalar_tensor_tensor(out=hb[i][:], in0=xres[:, t, :], scalar=ss[:, 0, t:t + 1], in1=nw,
                                                                       op0=ALU.mult, op1=ALU.mult),
                     reads=[("x", t), ("ss", t), "nw"], writes=[("hb", i)])
                bk = 5 + (t % 2)
                for k in range(8):
                    P.op("pe", lambda e, k=k, i=i, bk=bk: e.transpose(out=bank[bk][:].bitcast(BF16)[:, k * 128:(k + 1) * 128],
                                                                      in_=hb[i][:, k * 128:(k + 1) * 128], identity=identb[:]),
                         reads=[("hb", i), "identb"], writes=PS(bk), inc=(k == 7))
                P.op("act", lambda e, t=t, bk=bk: e.copy(out=hT[:, :, t * 128:(t + 1) * 128],
                                                         in_=bank[bk][:].bitcast(BF16).rearrange("p (k c) -> p k c", k=8)),
                     reads=PS(bk), writes=["hT"])

        def phase_A(l, hp):
            P.op("pool", lambda e: e.memset(V1[:, :, :, 128:129], 1.0), writes=["V1"])
            wq, wqk = load_w("in", l, OFF["aq"] + hp * 256)
            wk, wkk = load_w("in", l, OFF["ak"] + hp * 256)
            wv_, wvk = load_w("in", l, OFF["av"] + hp * 256)
            n = 0
            for (wv, wkey, dst, dkey, scl) in ((wq, wqk, qT, "qT", 0.125), (wk, wkk, kT, "kT", 1.0)):
                for hh in range(2):
                    for tc in range(4):
                        bk = n % 4
                        n += 1
                        for k in range(8):
                            P.op("pe", lambda e, k=k, bk=bk, wv=wv, hh=hh, tc=tc: e.matmul(
                                bank[bk][:, :], lhsT=wv[:, k, hh * 128:(hh + 1) * 128], rhs=hT[:, k, tc * 512:(tc + 1) * 512],
                                start=(k == 0), stop=(k == 7)), reads=["hT", wkey], writes=PS(bk), inc=(k == 7))
                        if n % 2 == 0:
                            P.op("act", lambda e, bk=bk, dst=dst, hh=hh, tc=tc, scl=scl: e.mul(out=dst[:, hh, tc * 512:(tc + 1) * 512],
                                                                                             in_=bank[bk][:, :], mul=scl),
                                 reads=PS(bk), writes=[dkey])
                        else:
                            P.op("dve", lambda e, bk=bk, dst=dst, hh=hh, tc=tc, scl=scl: e.tensor_scalar(
                                out=dst[:, hh, tc * 512:(tc + 1) * 512], in0=bank[bk][:, :], scalar1=scl, scalar2=None, op0=ALU.mult),
                                reads=PS(bk), writes=[dkey])
            wg, wgk = load_w("in", l, OFF["ag"] + hp * 256)
            wo, wok = load_w("out", l, hp * 256)
            for t in range(NT):
                bk = t % 4
                proj_tok(wv_, wvk, t, bk)
                eng = "act" if t % 2 == 0 else "dve"
                if eng == "act":
                    P.op("act", lambda e, t=t, bk=bk: e.copy(out=V1[:, t, :, 0:128], in_=bank[bk][:, 0:256].rearrange("p (a b) -> p a b", a=2)),
                         reads=PS(bk), writes=["V1"])
                else:
                    P.op("dve", lambda e, t=t, bk=bk: e.tensor_copy(out=V1[:, t, :, 0:128], in_=bank[bk][:, 0:256].rearrange("p (a b) -> p a b", a=2)),
                         reads=PS(bk), writes=["V1"])
            for t in range(NT):
                bk = t % 4
                proj_tok(wg, wgk, t, bk)
                gate_evac(bk, t, t % 2)

            hoist_next()

            def acc(r, c0=0, c1=129):
                return bank[4 + r // 3][:, (r % 3) * 129 + c0:(r % 3) * 129 + c1]

            pend = []

            def sched_chunk_tail(qc):
                items = []

                def pre():
                    P.op("pool", lambda e: e.tensor_tensor(out=oan, in0=oa, in1=oa, op=ALU.mult), reads=["oa"], writes=["oan"])
                    P.op("dve", lambda e: e.reduce_sum(out=sm[:, 16:24], in_=oan.rearrange("p a b c -> p (a b) c"), axis=AX.X),
                         reads=["oan"], writes=["sm"])
                    P.op("dve", lambda e: e.tensor_scalar(out=sm[:, 24:32], in0=sm[:, 16:24], scalar1=1.0 / 128, scalar2=EPS, op0=ALU.mult, op1=ALU.add),
                         reads=["sm"], writes=["sm"])
                    P.op("pool", lambda e: e.tensor_tensor(out=sm[:, 16:24], in0=sm[:, 24:32], in1=mhalf[:, 0:8], op=ALU.pow),
                         reads=["sm", "mhalf"], writes=["sm"])
                    P.op("dve", lambda e: e.tensor_tensor(out=oan.rearrange("p a b c -> p (a b) c"), in0=oa.rearrange("p a b c -> p (a b) c"),
                                                          in1=sm[:, 16:24].unsqueeze(2).to_broadcast([128, 8, 128]), op=ALU.mult),
                         reads=["oa", "sm"], writes=["oan"])
                items.append((0, pre))
                tb = bank[7].bitcast(BF16)[:, 0:256]
                for qs in range(4):
                    t = qc * 4 + qs
                    i = qs % 2
                    g0 = (6, 9, 19, 22)[qs]

                    def T1(qs=qs, t=t, i=i):
                        P.op("pool", lambda e: e.tensor_tensor(out=oan[:, qs], in0=oan[:, qs], in1=W2[:, l], op=ALU.mult),
                             reads=["oan", "W2"], writes=["oan"])
                        P.op("dve", lambda e: e.tensor_tensor(out=mixed[i][:], in0=oan[:, qs].rearrange("p a b -> p (a b)"),
                                                              in1=gate[:, t, :], op=ALU.mult),
                             reads=["oan", "gate"], writes=[("mixed", i)])

                    def T2(i=i):
                        for k in range(2):
                            P.op("pe", lambda e, k=k: e.transpose(out=tb[:, k * 128:(k + 1) * 128], in_=mixed[i][:, k * 128:(k + 1) * 128],
                                                                  identity=identb[:]),
                                 reads=[("mixed", i), "identb"], writes=PS(7), inc=(k == 1))
                        P.op("dve", lambda e: e.tensor_copy(out=mT[i][:].rearrange("p a b -> p (a b)"), in_=tb), reads=PS(7), writes=[("mT", i)])

                    def T4(half, t=t, i=i):
                        for k in range(2):
                            P.op("pe", lambda e, k=k: e.matmul(bank[7], lhsT=mT[i][:, k, :], rhs=wo[:, k, half * 512:(half + 1) * 512],
                                                               start=(k == 0), stop=(k == 1)),
                                 reads=[("mT", i), wok], writes=PS(7), inc=(k == 1))
                        P.op("dve", lambda e: e.tensor_tensor(out=xres[:, t, half * 512:(half + 1) * 512],
                                                              in0=xres[:, t, half * 512:(half + 1) * 512], in1=bank[7], op=ALU.add),
                             reads=PS(7) + [("x", t)], writes=[("x", t)])
                    items += [(g0, T1), (g0 + 2, T2), (g0 + 4, lambda T4=T4: T4(0)), (g0 + 5, lambda T4=T4: T4(1))]
                return items

            for qc in range(4):
                for hh in range(2):
                    h = 2 * hp + hh
                    seq = []
                    for kt in range(NT):
                        d = kt - 4 * qc
                        near = -1 <= d <= 4
                        seq.append((kt, near, d))

                    def emit_qk(j, hh=hh, qc=qc, h=h):
                        kt, near, d = seq[j]
                        b0 = (j % 2) * 2
                        for m in range(2):
                            P.op("pe", lambda e, m=m, kt=kt, b0=b0: e.matmul(
                                bank[b0 + m][:, :], lhsT=kT[m * 64:(m + 1) * 64, hh, kt * 128:(kt + 1) * 128],
                                rhs=qT[m * 64:(m + 1) * 64, hh, qc * 512:(qc + 1) * 512], start=True, stop=True),
                                reads=["kT", "qT"], writes=PS(b0 + m), inc=(m == 1))

                    def emit_exp(j, hh=hh, qc=qc, h=h):
                        kt, near, d = seq[j]
                        b0 = (j % 2) * 2
                        pt = PT[j % 3]
                        if near:
                            bias_ap = zero1[:, 0:1]
                        elif d > 4:
                            bias_ap = cfar[:, h:h + 1]
                        else:
                            bias_ap = cfar[:, 4 + h:5 + h]
                        P.op("act", lambda e: e.activation(out=pt.rearrange("p a b -> p (a b)"),
                                                           in_=psall[:, b0:b0 + 2, :].rearrange("p a b -> p (a b)"),
                                                           func=AF.Exp, bias=bias_ap, scale=1.0),
                             reads=PS(b0, b0 + 1) + ["cfar", "zero1"], writes=[("PT", j % 3, 0), ("PT", j % 3, 1)])
                        if near:
                            base = 512 - 128 * d
                            for m in range(2):
                                P.op("dve", lambda e, base=base, m=m: e.tensor_tensor(
                                    out=pt[:, m, :], in0=pt[:, m, :], in1=bmast[:, h, base:base + 512], op=ALU.mult),
                                    reads=["bmast"], writes=[("PT", j % 3, m)])

                    def emit_pv(j, hh=hh, qc=qc, h=h):
                        kt, near, d = seq[j]
                        pt = PT[j % 3]
                        for m in range(2):
                            for qs in range(4):
                                r = m * 4 + qs
                                first = (kt == 0 and r % 3 == 0)
                                P.op("pe", lambda e, m=m, qs=qs, r=r, first=first: e.matmul(
                                    acc(r), lhsT=pt[:, m, qs * 128:(qs + 1) * 128], rhs=V1[:, kt, hh, 0:129],
                                    start=first, stop=(kt == NT - 1), skip_group_check=True),
                                    reads=[("PT", j % 3, m), "V1"], writes=PS(4 + r // 3), inc=(m == 1 and qs == 3))

                    emit_qk(0)
                    for j in range(NT + 1):
                        if j + 1 < NT:
                            emit_qk(j + 1)
                        if j < NT:
                            emit_exp(j)
                        if j >= 1:
                            emit_pv(j - 1)
                        g_ = hh * NT + j
                        if j < NT:
                            for it_ in [x for x in pend if x[0] == g_]:
                                pend.remove(it_)
                                it_[1]()
                    P.op("act", lambda e: e.copy(out=accS[:, 0:3, :].rearrange("p a b -> p (a b)"), in_=bank[4][:, 0:387]), reads=PS(4), writes=["accS0"])
                    P.op("dve", lambda e: e.tensor_copy(out=accS[:, 3:6, :].rearrange("p a b -> p (a b)"), in_=bank[5][:, 0:387]), reads=PS(5), writes=["accS1"])
                    P.op("act", lambda e: e.copy(out=accS[:, 6:8, :].rearrange("p a b -> p (a b)"), in_=bank[6][:, 0:258]), reads=PS(6), writes=["accS2"])
                    P.op("dve", lambda e: e.reciprocal(out=sm[:, 0:8], in_=accS[:, :, 128]), reads=["accS0", "accS1", "accS2"], writes=["sm"])
                    P.op("dve", lambda e: e.tensor_scalar(out=sm[:, 4:8], in0=sm[:, 4:8], scalar1=nlam[:, l:l + 1], scalar2=None, op0=ALU.mult),
                         reads=["sm", "nlam"], writes=["sm"])
                    for qs in range(4):
                        r1 = 4 + qs
                        P.op("dve", lambda e, qs=qs, r1=r1: e.tensor_scalar(out=tmpA, in0=accS[:, r1, 0:128], scalar1=sm[:, r1:r1 + 1],
                                                                            scalar2=None, op0=ALU.mult),
                             reads=["accS0", "accS1", "accS2", "sm"], writes=["tmpA"])
                        P.op("dve", lambda e, qs=qs, hh=hh: e.scalar_tensor_tensor(out=oa[:, qs, hh, :], in0=accS[:, qs, 0:128], scalar=sm[:, qs:qs + 1],
                                                                                  in1=tmpA, op0=ALU.mult, op1=ALU.add),
                             reads=["accS0", "accS1", "accS2", "sm", "tmpA"], writes=["oa"])
                for it_ in sorted(pend, key=lambda x: x[0]):
                    it_[1]()
                pend = sched_chunk_tail(qc)
            for it_ in sorted(pend, key=lambda x: x[0]):
                it_[1]()
            pend = []

        def phase_B(l, hp):
            wqk_, wqkk = load_w("inqk", l, OFF["bq"] + hp * 128)
            wv_, wvk = load_w("in", l, OFF["bv"] + hp * 256)
            wg, wgk = load_w("in", l, OFF["bg"] + hp * 256)
            wo, wok = load_w("out", l, 512 + hp * 256)
            dma(lambda e: e.dma_start(out=cs, in_=cs_d), writes=["cs"])
            for s2 in range(2):
                P.op("pool", lambda e, s2=s2: e.memset(TTq2[s2], 0.0), writes=[("TTq2", s2)])
            for t in range(NT):
                bk = t % 4
                i2 = t % 2
                proj_tok(wqk_, wqkk, t, bk)
                P.op("act", lambda e, bk=bk, i2=i2: e.copy(out=qk32[i2][:, 0:128], in_=bank[bk][:, 0:128]), reads=PS(bk), writes=[("qk32", i2)])
                P.op("act", lambda e, bk=bk, i2=i2: e.mul(out=qk32[i2][:, 128:256], in_=bank[bk][:, 128:256], mul=0.125),
                     reads=PS(bk), writes=[("qk32", i2)])
                src4 = qk32[i2].rearrange("p (a b c) -> p a b c", a=4, b=2)
                cos_b = cs[:, 0, t, :].unsqueeze(1).to_broadcast([128, 4, 32])
                sin_b = cs[:, 1, t, :].unsqueeze(1).to_broadcast([128, 4, 32])
                t1 = src4[:, :, 0, :]
                t2 = src4[:, :, 1, :]
                rt = [rtmp[i2][:, i].rearrange("p (a c) -> p a c", a=4) for i in range(4)]
                dst4 = qkr[:, t, :].rearrange("p (a b c) -> p a b c", a=4, b=2)
                P.op("pool", lambda e, t1=t1, cos_b=cos_b, rt=rt: e.tensor_tensor(out=rt[0], in0=t1, in1=cos_b, op=ALU.mult),
                     reads=[("qk32", i2), "cs"], writes=[("rtmp", i2, 0)])
                P.op("pool", lambda e, t2=t2, sin_b=sin_b, rt=rt: e.tensor_tensor(out=rt[1], in0=t2, in1=sin_b, op=ALU.mult),
                     reads=[("qk32", i2), "cs"], writes=[("rtmp", i2, 1)])
                P.op("dve", lambda e, t1=t1, sin_b=sin_b, rt=rt: e.tensor_tensor(out=rt[2], in0=t1, in1=sin_b, op=ALU.mult),
                     reads=[("qk32", i2), "cs"], writes=[("rtmp", i2, 2)])
                P.op("dve", lambda e, t2=t2, cos_b=cos_b, rt=rt: e.tensor_tensor(out=rt[3], in0=t2, in1=cos_b, op=ALU.mult),
                     reads=[("qk32", i2), "cs"], writes=[("rtmp", i2, 3)])
                P.op("pool", lambda e, rt=rt, dst4=dst4: e.tensor_tensor(out=dst4[:, :, 0, :], in0=rt[0], in1=rt[1], op=ALU.subtract),
                     reads=[("rtmp", i2, 0), ("rtmp", i2, 1)], writes=[("qkr", t, 0)])
                P.op("dve", lambda e, rt=rt, dst4=dst4: e.tensor_tensor(out=dst4[:, :, 1, :], in0=rt[2], in1=rt[3], op=ALU.add),
                     reads=[("rtmp", i2, 2), ("rtmp", i2, 3)], writes=[("qkr", t, 1)])
            for t in range(NT):
                bk = t % 4
                proj_tok(wv_, wvk, t, bk)
                if t % 2 == 0:
                    P.op("act", lambda e, t=t, bk=bk: e.copy(out=vB[:, t, :], in_=bank[bk][:, 0:256]), reads=PS(bk), writes=[("vB", t)])
                else:
                    P.op("dve", lambda e, t=t, bk=bk: e.tensor_copy(out=vB[:, t, :], in_=bank[bk][:, 0:256]), reads=PS(bk), writes=[("vB", t)])
            for t in range(NT):
                bk = t % 4
                proj_tok(wg, wgk, t, bk)
                gate_evac(bk, t, t % 2)
            hoist_next()
            QKR = lambda t: [("qkr", t, 0), ("qkr", t, 1)]

            def kvp(t):
                return bank[t // 2][:, (t % 2) * 256:(t % 2) * 256 + 256]

            for t in range(NT):
                i2 = t % 2
                kq = qkr[:, t, 128:256].rearrange("p (a c) -> p a c", a=2)
                for dr in range(2):
                    c0 = 8 + dr * 4 + 2 * hp
                    P.op("pool", lambda e, dr=dr, c0=c0, kq=kq, i2=i2: e.tensor_tensor(
                        out=kdec[i2][:, :, dr, :], in0=kq, in1=tqk[:, l, c0:c0 + 2].unsqueeze(2).to_broadcast([128, 2, 64]), op=ALU.mult),
                        reads=QKR(t) + ["tqk"], writes=[("kdec", i2)])
                for hh in range(2):
                    P.op("pe", lambda e, hh=hh, t=t, i2=i2: e.matmul(kvp(t)[:, hh * 128:(hh + 1) * 128], lhsT=kdec[i2][:, hh].rearrange("p a b -> p (a b)"),
                                                                    rhs=vB[:, t, hh * 128:(hh + 1) * 128], start=True, stop=True),
                         reads=[("kdec", i2), ("vB", t)], writes=PS(t // 2), inc=(hh == 1))

            P.op("pool", lambda e: e.memset(Rcur, 0.0), writes=[("Rcur", 0), ("Rcur", 64)])
            for n_ in range(NT):
                for (lo, hi, t, dr) in ((0, 64, n_, 0), (64, 128, NT - 1 - n_, 1)):
                    P.op("act", lambda e, t=t, lo=lo, hi=hi: e.copy(out=Rst[lo:hi, t], in_=Rcur[lo:hi]), reads=[("Rcur", lo)], writes=[("Rst", t, lo)])
                    if n_ == NT - 1:
                        continue
                    for hh in range(2):
                        gc = l * 8 + dr * 4 + 2 * hp + hh
                        P.op("dve", lambda e, lo=lo, hi=hi, t=t, hh=hh, gc=gc: e.scalar_tensor_tensor(
                            out=Rcur[lo:hi, hh, :], in0=Rcur[lo:hi, hh, :], scalar=gsc[lo:hi, gc:gc + 1],
                            in1=kvp(t)[lo:hi, hh * 128:(hh + 1) * 128], op0=ALU.mult, op1=ALU.add),
                            reads=PS(t // 2) + [("Rcur", lo), "gsc"], writes=[("Rcur", lo)])

            def S0a(t):
                s2 = t % 2
                s3 = t % 3
                bTS = 2 * s2
                qq = qkr[:, t, 0:128].rearrange("p (a c) -> p a c", a=2)
                for dr in range(2):
                    c0 = dr * 4 + 2 * hp
                    P.op("pool", lambda e, dr=dr, c0=c0, qq=qq, s2=s2: e.tensor_tensor(
                        out=qdec[s2][:, :, dr, :], in0=qq, in1=tqk[:, l, c0:c0 + 2].unsqueeze(2).to_broadcast([128, 2, 64]), op=ALU.mult),
                        reads=QKR(t) + ["tqk"], writes=[("qdec", s2)])
                bT = bank[bTS].bitcast(BF16)
                srcs = [qdec[s2][:, 0].rearrange("p a b -> p (a b)"), qdec[s2][:, 1].rearrange("p a b -> p (a b)"),
                        qkr[:, t, 128:256], qkr[:, t, 0:128]]
                for i4 in range(4):
                    P.op("pe", lambda e, i4=i4, srcs=srcs, bT=bT: e.transpose(out=bT[:, i4 * 128:(i4 + 1) * 128], in_=srcs[i4], identity=identb[:]),
                         reads=[("qdec", s2), "identb"] + QKR(t), writes=PS(bTS), inc=(i4 == 3))
                P.op("act", lambda e, bT=bT, s3=s3: e.copy(out=TT[s3].rearrange("p a b -> p (a b)"), in_=bT[:, 0:384]),
                     reads=PS(bTS), writes=[("TT", s3)])
                for hh in range(2):
                    P.op("act", lambda e, bT=bT, s2=s2, hh=hh: e.copy(out=TTq2[s2][hh * 64:(hh + 1) * 64, hh, :],
                                                                       in_=bT[hh * 64:(hh + 1) * 64, 384:512]),
                         reads=PS(bTS), writes=[("TTq2", s2)])

            def S0b(t):
                s2 = t % 2
                s3 = t % 3
                bTS = 2 * s2
                P.op("pe", lambda e, s2=s2, s3=s3, bTS=bTS: e.matmul(bank[bTS][:, 256:512], lhsT=TT[s3][:, 2, :],
                                                                    rhs=TTq2[s2].rearrange("p a b -> p (a b)"), start=True, stop=True),
                     reads=[("TT", s3), ("TTq2", s2)], writes=PS(bTS), inc=True)
                P.op("dve", lambda e, s2=s2, bTS=bTS: e.tensor_tensor(out=innerT[s2], in0=bank[bTS][:, 256:512].rearrange("p (a b) -> p a b", a=2),
                                                                     in1=D2T[:, l, 2 * hp:2 * hp + 2, :], op=ALU.mult),
                     reads=PS(bTS) + ["D2T"], writes=[("innerT", s2)])

            def S0c(t):
                s2 = t % 2
                s3 = t % 3
                bO = 2 * s2 + 1
                for hh in range(2):
                    P.op("pe", lambda e, hh=hh, t=t, s2=s2, bO=bO: e.matmul(bank[bO][:, hh * 128:(hh + 1) * 128], lhsT=innerT[s2][:, hh, :],
                                                                           rhs=vB[:, t, hh * 128:(hh + 1) * 128], start=True, stop=False),
                         reads=[("innerT", s2), ("vB", t)], writes=PS(bO), inc=False)
                    P.op("pe", lambda e, hh=hh, t=t, s3=s3, bO=bO: e.matmul(bank[bO][:, hh * 128:(hh + 1) * 128], lhsT=TT[s3][:, hh, :],
                                                                           rhs=Rst[:, t, hh, :], start=False, stop=True),
                         reads=[("TT", s3), ("Rst", t, 0), ("Rst", t, 64)], writes=PS(bO), inc=(hh == 1))

            def S1a(t):
                s2 = t % 2
                s3 = t % 3
                bO = 2 * s2 + 1
                sc = 32 + 8 * s3
                for hh in range(2):
                    P.op("act", lambda e, hh=hh, bO=bO, sc=sc: e.activation(out=junk[:, hh * 128:(hh + 1) * 128], in_=bank[bO][:, hh * 128:(hh + 1) * 128],
                                                                          func=AF.Square, accum_out=sm[:, sc + hh:sc + hh + 1]),
                         reads=PS(bO), writes=["junk", ("smB", s3)])
                P.op("act", lambda e, bO=bO, s3=s3: e.copy(out=Osb[s3].rearrange("p a b -> p (a b)"), in_=bank[bO][:, 0:256]),
                     reads=PS(bO), writes=[("Osb", s3)])
                P.op("dve", lambda e, sc=sc: e.tensor_scalar(out=sm[:, sc + 2:sc + 4], in0=sm[:, sc:sc + 2], scalar1=4.0 / 128, scalar2=4.0 * EPS,
                                                             op0=ALU.mult, op1=ALU.add), reads=[("smB", s3)], writes=[("smB", s3)])

            def S1b(t):
                s3 = t % 3
                sc = 32 + 8 * s3
                P.op("pool", lambda e, sc=sc: e.tensor_tensor(out=sm[:, sc + 4:sc + 6], in0=sm[:, sc + 2:sc + 4], in1=mhalf[:, 0:2], op=ALU.pow),
                     reads=[("smB", s3), "mhalf"], writes=[("smB", s3)])

            def S1c(t):
                s2 = t % 2
                s3 = t % 3
                sc = 32 + 8 * s3
                P.op("dve", lambda e, s2=s2, s3=s3, sc=sc: e.tensor_tensor(out=btmp[s2], in0=Osb[s3],
                                                                          in1=sm[:, sc + 4:sc + 6].unsqueeze(2).to_broadcast([128, 2, 128]), op=ALU.mult),
                     reads=[("Osb", s3), ("smB", s3)], writes=[("btmp", s2)])

            def S1d(t):
                s2 = t % 2
                i = t % 2
                P.op("pool", lambda e, i=i, t=t, s2=s2: e.tensor_tensor(out=mixed[i][:], in0=btmp[s2].rearrange("p a b -> p (a b)"), in1=gate[:, t, :], op=ALU.mult),
                     reads=[("btmp", s2), "gate"], writes=[("mixed", i)])

            def S2(t):
                s2 = t % 2
                bO = 2 * s2 + 1
                i = t % 2
                tail(mixed[i], ("mixed", i), t, wo, wok, tb=bank[bO].bitcast(BF16)[:, 512:768], tbk=bO, ob=4 + 2 * s2, i=i)

            stages = [S0a, S0b, S0c, S1a, S1b, S1c, S1d, S2]
            for it in range(NT + len(stages) - 1):
                for k_, fn_ in enumerate(stages):
                    if 0 <= it - k_ < NT:
                        fn_(it - k_)

        def phase_C(l, gp):
            wu, wuk = load_w("in", l, OFF["cu"] + gp * 256)
            wg, wgk = load_w("in", l, OFF["cg"] + gp * 256)
            wo, wok = load_w("out", l, 1024 + gp * 256)
            for t in range(NT):
                bk = t % 4
                proj_tok(wu, wuk, t, bk)
                if t % 2 == 0:
                    P.op("act", lambda e, t=t, bk=bk: e.copy(out=uC[:, t, :], in_=bank[bk][:, 0:256]), reads=PS(bk), writes=[("uC", t)])
                else:
                    P.op("dve", lambda e, t=t, bk=bk: e.tensor_copy(out=uC[:, t, :], in_=bank[bk][:, 0:256]), reads=PS(bk), writes=[("uC", t)])
            for t in range(NT):
                bk = t % 4
                proj_tok(wg, wgk, t, bk)
                gate_evac(bk, t, t % 2)
            hoist_next()

            def S0(t):
                s2 = t % 2
                bP = 2 * s2
                bY = 2 * s2 + 1
                for gg in range(2):
                    g = 2 * gp + gg
                    parts = [(t, g * 5 + (3 if t == 0 else 4 if t == NT - 1 else 0))]
                    if t > 0:
                        parts.append((t - 1, g * 5 + 1))
                    if t < NT - 1:
                        parts.append((t + 1, g * 5 + 2))
                    for pi_, (tj, mi) in enumerate(parts):
                        P.op("pe", lambda e, gg=gg, tj=tj, mi=mi, pi_=pi_, np_=len(parts), bP=bP: e.matmul(
                            bank[bP][:, gg * 128:(gg + 1) * 128], lhsT=uC[:, tj, gg * 128:(gg + 1) * 128], rhs=poolm[:, mi, :],
                            start=(pi_ == 0), stop=(pi_ == np_ - 1)),
                            reads=[("uC", tj), "poolm"], writes=PS(bP), inc=(gg == 1 and pi_ == len(parts) - 1))
                P.op("act", lambda e, bP=bP, s2=s2: e.copy(out=pooledT[s2].rearrange("p a b -> p (a b)"), in_=bank[bP][:, 0:256]),
                     reads=PS(bP), writes=[("pooledT", s2)])
                for gg in range(2):
                    g = 2 * gp + gg
                    P.op("pe", lambda e, gg=gg, g=g, s2=s2, bY=bY: e.matmul(bank[bY][:, gg * 128:(gg + 1) * 128], lhsT=pooledT[s2][:, gg, :],
                                                                           rhs=poolw[:, l * 4 + g, :], start=True, stop=True),
                         reads=[("pooledT", s2), "poolw"], writes=PS(bY), inc=(gg == 1))

            def S1(t):
                s2 = t % 2
                bY = 2 * s2 + 1
                i = t % 2
                P.op("dve", lambda e, s2=s2, bY=bY: e.tensor_tensor(out=ytmp[s2], in0=bank[bY][:, 0:256], in1=psh[:, l, gp * 256:(gp + 1) * 256], op=ALU.mult),
                     reads=PS(bY) + ["psh"], writes=[("ytmp", s2)])
                P.op("pool", lambda e, t=t, i=i, s2=s2: e.tensor_tensor(out=mixed[i][:], in0=ytmp[s2], in1=gate[:, t, :], op=ALU.mult),
                     reads=[("ytmp", s2), "gate"], writes=[("mixed", i)])

            def S2(t):
                s2 = t % 2
                bY = 2 * s2 + 1
                i = t % 2
                tail(mixed[i], ("mixed", i), t, wo, wok, tb=bank[bY].bitcast(BF16)[:, 512:768], tbk=bY, ob=4 + 2 * s2, i=i)

            for it in range(NT + 2):
                if it < NT:
                    S0(it)
                if 0 <= it - 1 < NT:
                    S1(it - 1)
                if 0 <= it - 2 < NT:
                    S2(it - 2)

        for s_ in range(nseq):
            for l in range(nlayers):
                for ph in phases:
                    for hp in range(2):
                        sched["list"].append((ph, l, hp))
        sched["i"] = -1
        hoist_next()
        fns = {"A": phase_A, "B": phase_B, "C": phase_C}
        pi = 0
        for s in range(nseq):
            for t in range(NT):
                dma(lambda e, s=s, t=t: e.dma_start(out=xres[:, t, :], in_=x_d[s, t]), writes=[("x", t)])
            for l in range(nlayers):
                P.fence()
                rmsnorm_to_hT(l)
                for ph in phases:
                    for hp in range(2):
                        if hp == 0:
                            P.fence()
                        sched["i"] = pi
                        fns[ph](l, hp)
                        pi += 1
            P.fence()
            dma(lambda e: e.dma_start(out=nw, in_=nwb_d[2]), writes=["nw"])
            for t in range(NT):
                P.op("act", lambda e, t=t: e.activation(out=junk[:], in_=xres[:, t, :], func=AF.Square, accum_out=ss[:, 0, t:t + 1]),
                     reads=[("x", t)], writes=["junk", ("ss", t)])
            P.op("dve", lambda e: e.tensor_scalar(out=ss[:, 1, :], in0=ss[:, 0, :], scalar1=1.0 / D, scalar2=EPS, op0=ALU.mult, op1=ALU.add),
                 reads=[("ss", t) for t in range(NT)], writes=["ss1"])
            P.op("pool", lambda e: e.tensor_tensor(out=ss[:, 0, :], in0=ss[:, 1, :], in1=mhalf[:, 0:NT], op=ALU.pow),
                 reads=["ss1", "mhalf"], writes=[("ss", t) for t in range(NT)])
            for t in range(NT):
                P.op("dve", lambda e, t=t: e.scalar_tensor_tensor(out=xres[:, t, :], in0=xres[:, t, :], scalar=ss[:, 0, t:t + 1], in1=nw,
                                                                  op0=ALU.mult, op1=ALU.mult),
                     reads=[("x", t), ("ss", t), "nw"], writes=[("x", t)])
                dma(lambda e, s=s, t=t: e.dma_start(out=y_d[s, t], in_=xres[:, t, :]), reads=[("x", t)], writes=[("y", s, t)])
        P.emit(nc, st)
    return nc


_NC_CACHE = {}


def kernel(**inputs):
    x = np.asarray(inputs["x"], np.float32)
    B = x.shape[0]
    ncores = 8
    nseq = B // ncores
    consts = _host_consts(inputs)
    key = (nseq,)
    if key not in _NC_CACHE:
        _NC_CACHE[key] = build_nc(nseq=nseq)
    nc = _NC_CACHE[key]
    w_in = np.ascontiguousarray(np.asarray(inputs["w_in"], np.float32))
    w_out = np.ascontiguousarray(np.asarray(inputs["w_out"], np.float32))
    in_maps = []
    for c in range(ncores):
        m = dict(consts)
        m["x"] = np.ascontiguousarray(x[c * nseq:(c + 1) * nseq].reshape(nseq, NT, 128, D))
        m["w_in"] = w_in
        m["w_out"] = w_out
        in_maps.append(m)
    res = run_bass_kernel_spmd(nc, in_maps, core_ids=list(range(ncores)))
    out = np.concatenate([np.asarray(r["y"]).reshape(nseq, S, D) for r in res.results], axis=0)
    return out.astype(np.float32)
```

```python
import contextlib
import math
import numpy as np
import concourse.bass as bass
import concourse.mybir as mybir
from concourse.bass_utils import run_bass_kernel_spmd

F32 = mybir.dt.float32
BF16 = mybir.dt.bfloat16
ALU = mybir.AluOpType
AF = mybir.ActivationFunctionType
AX = mybir.AxisListType

D = 1024
S = 2048
NT = S // 128
DIN = 4608
DMIX = 1536
EPS = 1e-6
OFF = dict(aq=0, ak=512, av=1024, ag=1536, bq=2048, bk=2304, bv=2560, bg=3072, cu=3584, cg=4096)
MW = 1152

ENGS = ("pe", "act", "dve", "pool", "sp")
CUT = 99
SEM_LIMIT = 30000


class Op:
    __slots__ = ("eng", "fn", "deps", "inc", "is_dma", "seq", "eidx", "sem", "val", "name",
                 "closer", "dslot", "nofence")


class Prog:
    def __init__(self):
        self.ops = []
        self.last_w = {}
        self.readers = {}
        self.pend = {e: set() for e in ENGS}
        self.fence_idx = 0

    def fence(self):
        deps = set()
        last = {}
        for o in self.ops[self.fence_idx:]:
            if o.is_dma:
                if not o.nofence:
                    deps.add(o.seq)
            else:
                last[o.eng] = o.seq
        deps |= set(last.values())
        for e in ENGS:
            self.pend[e] |= deps
        self.fence_idx = len(self.ops)

    def _add(self, eng, fn, reads, writes, inc, is_dma, name):
        o = Op()
        o.eng, o.fn, o.inc, o.is_dma, o.name = eng, fn, inc, is_dma, name
        o.seq = len(self.ops)
        o.closer = o.seq
        o.nofence = False
        deps = set(self.pend[eng])
        self.pend[eng] = set()
        reads = list(reads)
        writes = list(writes)
        ex = [r for r in reads if isinstance(r, tuple) and r[0] == "ps"]
        reads = [r for r in reads if r not in ex]
        writes = writes + [r for r in ex if r not in writes]
        for r in reads:
            if r in self.last_w:
                deps.add(self.last_w[r])
        for w in writes:
            if w in self.last_w:
                deps.add(self.last_w[w])
            for rd in self.readers.get(w, ()):
                deps.add(rd)
        for r in reads:
            self.readers.setdefault(r, []).append(o.seq)
        for w in writes:
            self.last_w[w] = o.seq
            self.readers[w] = []
        deps.discard(o.seq)
        o.deps = deps
        self.ops.append(o)
        return o

    def op(self, eng, fn, reads=(), writes=(), inc=True, name=""):
        return self._add(eng, fn, reads, writes, inc, False, name)

    def dma(self, fn, reads=(), writes=(), queue="sp", name=""):
        return self._add(queue, fn, reads, writes, True, True, name)

    def emit(self, nc, st, ndma_sems=8):
        ops = self.ops
        per_eng = {e: [] for e in ENGS}
        for o in ops:
            o.eidx = len(per_eng[o.eng])
            per_eng[o.eng].append(o)
        nsem = [0]

        def newsem(tag):
            nsem[0] += 1
            return st.enter_context(nc.semaphore("%s_%d" % (tag, nsem[0])))

        dsem = {q: [newsem("d" + q) for _ in range(ndma_sems)] for q in ("sp", "pool")}
        for e in ENGS:
            cnt = 0
            cur = newsem("s" + e)
            dcnt = [0] * ndma_sems
            nd = 0
            pending = []
            for o in per_eng[e]:
                if o.is_dma:
                    j = nd % ndma_sems
                    nd += 1
                    dcnt[j] += 16
                    o.sem, o.val, o.dslot = dsem[e][j], dcnt[j], j
                elif o.inc:
                    if cnt >= SEM_LIMIT:
                        cur = newsem("s" + e)
                        cnt = 0
                    cnt += 1
                    o.sem, o.val = cur, cnt
                    for p in pending:
                        p.sem, p.val, p.closer = cur, cnt, o.seq
                    pending = []
                else:
                    pending.append(o)
            assert not pending, "trailing no-inc ops on " + e
        blk = st.enter_context(nc.Block())

        def make(e):
            def body(engine):
                waited = {}
                last_dma = {}

                def need(p):
                    if waited.get(p.sem, 0) >= p.val:
                        return
                    waited[p.sem] = p.val
                    engine.wait_ge(p.sem, p.val)

                for o in per_eng[e]:
                    for d in sorted(o.deps):
                        p = ops[d]
                        if p.eng == e and not p.is_dma and not o.is_dma:
                            if e != "pe" and o.eidx - p.eidx <= 2:
                                need(p)
                            continue
                        assert p.closer < o.seq, (p.name, o.name)
                        need(p)
                    if o.is_dma:
                        j = o.dslot
                        if j in last_dma:
                            need(last_dma[j])
                        last_dma[j] = o
                        o.fn(engine).then_inc(o.sem, 16)
                    else:
                        ins = o.fn(engine)
                        if o.inc:
                            ins.then_inc(o.sem, 1)
                for pv in last_dma.values():
                    need(pv)
            return body

        blk.tensor(make("pe"))
        blk.scalar(make("act"))
        blk.vector(make("dve"))
        blk.gpsimd(make("pool"))
        blk.sync(make("sp"))


def _t5_bucket(rel):
    half, max_exact = 16, 8
    ret = np.where(rel > 0, half, 0)
    n = np.abs(rel)
    nf = np.maximum(n, 1).astype(np.float32)
    large = max_exact + (np.log(nf / np.float32(max_exact)) / np.float32(math.log(128 / max_exact))
                         * np.float32(half - max_exact)).astype(np.int32)
    large = np.minimum(large, half - 1)
    return ret + np.where(n < max_exact, n, large)


def _pool_mats():
    pm = np.zeros((128, 20, 128), np.float32)
    for g, w in enumerate((2, 4, 8, 16)):
        for v in range(5):
            t = {0: 5, 1: 5, 2: 5, 3: 0, 4: NT - 1}[v]
            for i in range(128):
                gi = t * 128 + i
                lo = min(max(gi - w // 2, 0), S)
                hi = min(max(gi + (w - w // 2), 0), S)
                cnt = float(hi - lo)
                for gj in range(lo, hi):
                    tj, j = divmod(gj, 128)
                    rel = tj - t
                    if v in (0, 3, 4) and rel == 0:
                        pm[j, g * 5 + v, i] += 1.0 / cnt
                    elif v == 1 and rel == -1:
                        pm[j, g * 5 + v, i] += 1.0 / cnt
                    elif v == 2 and rel == 1:
                        pm[j, g * 5 + v, i] += 1.0 / cnt
                if v in (0, 3, 4):
                    pm[i, g * 5 + v, i] -= 1.0
    return pm


def _host_consts(inp):
    c = {}
    bc = lambda a: np.ascontiguousarray(np.broadcast_to(a, (128,) + a.shape)).astype(np.float32)
    c["nwb"] = np.ascontiguousarray(np.stack([bc(inp["norm_w"][0]), bc(inp["norm_w"][1]),
                                               bc(inp["final_norm_w"])], 0))
    c["ident"] = np.eye(128, dtype=np.float32)
    p = np.arange(128)[:, None]
    cc = np.arange(MW)[None, :]
    bidx = _t5_bucket(p - cc + 512)
    rb = np.asarray(inp["rel_bias"], np.float32)
    c["bmaster"] = np.ascontiguousarray(rb[bidx].transpose(0, 2, 1))
    c["cfar"] = bc(np.concatenate([rb[31], rb[15]]))
    half = 32
    theta = (1.0 / (np.float32(10000.0) ** np.linspace(0.0, 1.0, half, dtype=np.float32))).astype(np.float32)
    ang = (np.arange(S, dtype=np.float32)[:, None] * theta[None, :]).astype(np.float32)
    cs = np.stack([np.cos(ang), np.sin(ang)], 0).astype(np.float32)
    c["cs"] = np.ascontiguousarray(cs.reshape(2, NT, 128, half).transpose(2, 0, 1, 3))
    m = np.arange(128, dtype=np.float32)[:, None]
    n = np.arange(128, dtype=np.float32)[None, :]
    c["retc"] = np.ascontiguousarray(np.stack([np.maximum(n - m, 0), np.maximum(m - n, 0)], 1))
    i = np.arange(128, dtype=np.float32)
    c["tokidx"] = np.ascontiguousarray(np.stack([i + 1, 128 - i, 127 - i, i], 1))
    c["dlam"] = bc(np.asarray(inp["diff_lambda"], np.float32).reshape(2, 256))
    c["subw"] = bc(np.asarray(inp["diff_subln_w"], np.float32))
    c["rdl"] = bc(np.asarray(inp["ret_decay_logit"], np.float32).reshape(16))
    c["pscale"] = bc(np.asarray(inp["pool_scale"], np.float32))
    c["poolw"] = np.ascontiguousarray(np.asarray(inp["pool_w"], np.float32))
    c["poolm"] = _pool_mats()
    return c


def build_nc(nseq=2, nlayers=2, phases="ABC"):
    nc = bass.Bass("TRN2", target_bir_lowering=False)
    din = lambda name, shape: nc.dram_tensor(name, list(shape), F32, kind="ExternalInput").ap()
    x_d = din("x", [nseq, NT, 128, D])
    win_d = din("w_in", [2, D, DIN])
    wout_d = din("w_out", [2, DMIX, D])
    nwb_d = din("nwb", [3, 128, D])
    ident_d = din("ident", [128, 128])
    bm_d = din("bmaster", [128, 4, MW])
    cfar_d = din("cfar", [128, 8])
    cs_d = din("cs", [128, 2, NT, 32])
    retc_d = din("retc", [128, 2, 128])
    tokidx_d = din("tokidx", [128, 4])
    dlam_d = din("dlam", [128, 2, 256])
    subw_d = din("subw", [128, 2, 128])
    rdl_d = din("rdl", [128, 16])
    pscale_d = din("pscale", [128, 2, 512])
    poolw_d = din("poolw", [2, 4, 128, 128])
    poolm_d = din("poolm", [128, 20, 128])
    y_d = nc.dram_tensor("y", [nseq, NT, 128, D], F32, kind="ExternalOutput").ap()

    P = Prog()
    with contextlib.ExitStack() as st:
        def sb(name, shape, dt=F32):
            return st.enter_context(nc.sbuf_tensor("s_" + name, list(shape), dt))

        xres = sb("xres", [128, NT, D])
        hT = sb("hT", [128, 8, S], BF16)
        NSLOT = 6
        wslot = [sb("wslot%d" % i, [128, 2048], BF16) for i in range(NSLOT)]
        bmast = sb("bmast", [128, 4, MW], BF16)
        identb = sb("identb", [128, 128], BF16)
        cfar = sb("cfar", [128, 8])
        zero1 = sb("zero1", [128, 1])
        mhalf = sb("mhalf", [128, 16])
        tokidx = sb("tokidx", [128, 4])
        rdl = sb("rdl", [128, 16])
        poolw = sb("poolw", [128, 8, 128], BF16)
        poolm = sb("poolm", [128, 20, 128], BF16)
        lg = sb("lg", [128, 16])
        tqk = sb("tqk", [128, 2, 16])
        gsc = sb("gsc", [128, 16])
        D2T = sb("D2T", [128, 2, 4, 128])
        W2 = sb("W2", [128, 2, 2, 128])
        psh = sb("psh", [128, 2, 512])
        nlam = sb("nlam", [128, 2])
        sm = sb("sm", [128, 64])
        ss = sb("ss", [128, 2, NT])
        hb = [sb("hb0", [128, D], BF16)] * 2
        junk = sb("junk", [128, D], BF16)
        mixed = [sb("mixed%d" % i, [128, 256], BF16) for i in range(2)]
        mT = [sb("mT%d" % i, [128, 2, 128], BF16) for i in range(2)]
        gate = sb("gate", [128, NT, 256], BF16)
        th = [sb("th%d" % i, [128, 256]) for i in range(2)]
        ARENA = 43 * 1024 + 768
        arena = sb("arena", [128, ARENA], mybir.dt.uint8)

        def carve(off, shape, dt):
            n = int(np.prod(shape))
            bpe = 2 if dt == BF16 else 4
            ap = arena[:, off:off + n * bpe].bitcast(dt)
            if len(shape) > 1:
                names = " ".join("d%d" % i for i in range(len(shape)))
                kw = {"d%d" % i: shape[i] for i in range(1, len(shape))}
                ap = ap.rearrange("p (%s) -> p %s" % (names, names), **kw)
            return ap, off + n * bpe

        o = 0
        nw, o = carve(o, [D], F32)
        identf, o = carve(o, [128], F32)
        retc, o = carve(o, [2, 128], F32)
        dlam, o = carve(o, [2, 256], F32)
        subw, o = carve(o, [2, 128], F32)
        scr, o = carve(o, [256], F32)
        o = 0
        qT, o = carve(o, [2, S], BF16)
        kT, o = carve(o, [2, S], BF16)
        V1, o = carve(o, [NT, 2, 130], BF16)
        oa, o = carve(o, [4, 2, 128], F32)
        oan, o = carve(o, [4, 2, 128], F32)
        PT0, o = carve(o, [2, 512], BF16)
        PT1, o = carve(o, [2, 512], BF16)
        PT2, o = carve(o, [2, 512], BF16)
        PT = [PT0, PT1, PT2]
        tmpA, o = carve(o, [128], F32)
        accS, o = carve(o, [8, 129], F32)
        assert o <= ARENA, o
        o = 0
        qkr, o = carve(o, [NT, 256], BF16)
        vB, o = carve(o, [NT, 256], BF16)
        Rst, o = carve(o, [NT, 2, 128], BF16)
        cs, o = carve(o, [2, NT, 32], F32)
        Rcur, o = carve(o, [2, 128], F32)
        qk32 = []
        rtmp = []
        kdec = []
        qdec = []
        TT = []
        TTq2 = []
        innerT = []
        btmp = []
        for _i in range(2):
            a_, o = carve(o, [256], F32); qk32.append(a_)
            a_, o = carve(o, [4, 128], F32); rtmp.append(a_)
            a_, o = carve(o, [2, 2, 64], BF16); kdec.append(a_)
            a_, o = carve(o, [2, 2, 64], BF16); qdec.append(a_)
            a_, o = carve(o, [2, 128], BF16); TTq2.append(a_)
            a_, o = carve(o, [2, 128], BF16); innerT.append(a_)
            a_, o = carve(o, [2, 128], BF16); btmp.append(a_)
        Osb = []
        for _i in range(3):
            a_, o = carve(o, [3, 128], BF16); TT.append(a_)
            a_, o = carve(o, [2, 128], BF16); Osb.append(a_)
        assert o <= ARENA, o
        o = 0
        uC, o = carve(o, [NT, 256], BF16)
        pooledT = []
        ytmp = []
        for _i in range(2):
            a_, o = carve(o, [2, 128], BF16); pooledT.append(a_)
            a_, o = carve(o, [256], F32); ytmp.append(a_)
        assert o <= ARENA, o

        psall = st.enter_context(nc.psum_tensor("psall", [128, 8, 512], F32))
        bank = [psall[:, i, :] for i in range(8)]

        def PS(*idx):
            return [("ps", i) for i in idx]


        dma = P.dma
        dma(lambda e: e.dma_start(out=identf, in_=ident_d), writes=["identf"])
        dma(lambda e: e.dma_start(out=cfar[:], in_=cfar_d), writes=["cfar"])
        dma(lambda e: e.dma_start(out=retc, in_=retc_d), writes=["retc"])
        dma(lambda e: e.dma_start(out=tokidx[:], in_=tokidx_d), writes=["tokidx"])
        dma(lambda e: e.dma_start(out=dlam, in_=dlam_d), writes=["dlam"])
        dma(lambda e: e.dma_start(out=subw, in_=subw_d), writes=["subw"])
        dma(lambda e: e.dma_start(out=rdl[:], in_=rdl_d), writes=["rdl"])
        dma(lambda e: e.dma_start(out=psh[:], in_=pscale_d), writes=["psh"])
        dma(lambda e: e.dma_start(out=poolw[:].rearrange("c (l g) d -> c l g d", l=2),
                                  in_=poolw_d.rearrange("l g c d -> c l g d")),
            writes=["poolw"], queue="pool")
        dma(lambda e: e.dma_start(out=poolm[:], in_=poolm_d), writes=["poolm"], queue="pool")
        dma(lambda e: e.dma_start(out=bmast[:], in_=bm_d), writes=["bmast"], queue="pool")
        for h_ in range(4):
            P.op("act", lambda e, h_=h_: e.activation(out=bmast[:, h_, :], in_=bmast[:, h_, :], func=AF.Exp), reads=["bmast"], writes=["bmast"])

        P.op("pool", lambda e: e.memset(mhalf[:], -0.5), writes=["mhalf"])
        P.op("pool", lambda e: e.memset(zero1[:], 0.0), writes=["zero1"])
        P.op("dve", lambda e: e.tensor_copy(out=identb[:], in_=identf), reads=["identf"], writes=["identb"])

        lam_init = [0.8 - 0.6 * math.exp(-0.3 * l) for l in range(2)]
        for l in range(2):
            P.op("dve", lambda e, l=l: e.tensor_tensor(out=scr[:, 0:64], in0=dlam[:, l, 0:64], in1=dlam[:, l, 64:128], op=ALU.mult),
                 reads=["dlam"], writes=["junk"])
            P.op("dve", lambda e, l=l: e.tensor_tensor(out=scr[:, 64:128], in0=dlam[:, l, 128:192], in1=dlam[:, l, 192:256], op=ALU.mult),
                 reads=["dlam"], writes=["junk"])
            P.op("dve", lambda e: e.reduce_sum(out=sm[:, 0:2], in_=scr[:, 0:128].rearrange("p (a b) -> p a b", a=2), axis=AX.X),
                 reads=["junk"], writes=["sm"])
            P.op("act", lambda e: e.activation(out=sm[:, 2:4], in_=sm[:, 0:2], func=AF.Exp), reads=["sm"], writes=["sm"])
            P.op("dve", lambda e, l=l: e.tensor_scalar(out=sm[:, 4:5], in0=sm[:, 3:4], scalar1=-lam_init[l], scalar2=None, op0=ALU.add),
                 reads=["sm"], writes=["sm"])
            P.op("dve", lambda e, l=l: e.tensor_tensor(out=nlam[:, l:l + 1], in0=sm[:, 4:5], in1=sm[:, 2:3], op=ALU.subtract),
                 reads=["sm"], writes=["nlam"])
            for hh in range(2):
                P.op("dve", lambda e, l=l, hh=hh: e.tensor_scalar(out=W2[:, l, hh, :], in0=subw[:, l, :],
                                                                  scalar1=(1.0 - lam_init[l]) * 0.5, scalar2=None, op0=ALU.mult),
                     reads=["subw"], writes=["W2"])
            P.op("dve", lambda e, l=l: e.tensor_scalar(out=psh[:, l, :], in0=psh[:, l, :], scalar1=0.5, scalar2=None, op0=ALU.mult),
                 reads=["psh"], writes=["psh"])
        P.op("act", lambda e: e.activation(out=sm[:, 16:32], in_=rdl[:], func=AF.Exp, scale=-1.0), reads=["rdl"], writes=["sm"])
        P.op("dve", lambda e: e.tensor_scalar(out=sm[:, 32:48], in0=sm[:, 16:32], scalar1=1.0, scalar2=None, op0=ALU.add),
             reads=["sm"], writes=["sm"])
        P.op("act", lambda e: e.activation(out=sm[:, 16:32], in_=sm[:, 32:48], func=AF.Ln), reads=["sm"], writes=["sm"])
        P.op("dve", lambda e: e.tensor_scalar(out=lg[:], in0=sm[:, 16:32], scalar1=-1.0, scalar2=None, op0=ALU.mult),
             reads=["sm"], writes=["lg"])
        P.op("act", lambda e: e.activation(out=gsc[:], in_=lg[:], func=AF.Exp, scale=128.0), reads=["lg"], writes=["gsc"])
        for l in range(2):
            lf = l * 8
            lb = l * 8 + 4
            for (dst, src, ti) in ((0, lf, 0), (4, lb, 1), (8, lf, 2), (12, lb, 3)):
                P.op("dve", lambda e, l=l, dst=dst, src=src, ti=ti: e.tensor_scalar(
                    out=sm[:, 48 + dst:52 + dst], in0=lg[:, src:src + 4], scalar1=tokidx[:, ti:ti + 1], scalar2=None, op0=ALU.mult),
                    reads=["lg", "tokidx"], writes=["sm"])
            P.op("act", lambda e, l=l: e.activation(out=tqk[:, l, :], in_=sm[:, 48:64], func=AF.Exp), reads=["sm"], writes=["tqk"])
            for h in range(4):
                P.op("dve", lambda e, l=l, h=h: e.tensor_scalar(out=scr[:, 0:128], in0=retc[:, 0, :], scalar1=lg[:, l * 8 + h:l * 8 + h + 1],
                                                                scalar2=None, op0=ALU.mult), reads=["retc", "lg"], writes=["junk"])
                P.op("dve", lambda e, l=l, h=h: e.scalar_tensor_tensor(out=scr[:, 128:256], in0=retc[:, 1, :],
                                                                       scalar=lg[:, l * 8 + 4 + h:l * 8 + 5 + h], in1=scr[:, 0:128],
                                                                       op0=ALU.mult, op1=ALU.add), reads=["retc", "lg", "junk"], writes=["junk"])
                P.op("act", lambda e, l=l, h=h: e.activation(out=D2T[:, l, h, :], in_=scr[:, 128:256], func=AF.Exp),
                     reads=["junk"], writes=["D2T"])

        P.fence()
        wstate = {"n": 0}

        preloaded = {}
        sched = {"list": [], "i": 0}

        def phase_loads(ph, l, hp):
            if ph == "A":
                return [("in", l, OFF["aq"] + hp * 256), ("in", l, OFF["ak"] + hp * 256), ("in", l, OFF["av"] + hp * 256),
                        ("in", l, OFF["ag"] + hp * 256), ("out", l, hp * 256)]
            if ph == "B":
                return [("inqk", l, OFF["bq"] + hp * 128), ("in", l, OFF["bv"] + hp * 256), ("in", l, OFF["bg"] + hp * 256),
                        ("out", l, 512 + hp * 256)]
            return [("in", l, OFF["cu"] + hp * 256), ("in", l, OFF["cg"] + hp * 256), ("out", l, 1024 + hp * 256)]

        def hoist_next():
            i = sched["i"] + 1
            if i < len(sched["list"]):
                ph, l, hp = sched["list"][i]
                for k in phase_loads(ph, l, hp):
                    if k not in preloaded:
                        preloaded[k] = _load_w(*k)

        def load_w(kind, l, c0):
            k = (kind, l, c0)
            if k in preloaded:
                return preloaded.pop(k)
            return _load_w(kind, l, c0)

        def _load_w(kind, l, c0):
            n0 = len(P.ops)
            r = _load_w2(kind, l, c0)
            for o_ in P.ops[n0:]:
                o_.nofence = True
            return r

        def _load_w2(kind, l, c0):
            i = wstate["n"] % NSLOT
            wstate["n"] += 1
            sl = wslot[i]
            key = ("wslot", i)
            if kind == "in":
                v = sl[:].rearrange("p (k c) -> p k c", k=8)
                dma(lambda e: e.dma_start(out=v, in_=win_d[l, :, c0:c0 + 256].rearrange("(k p) c -> p k c", p=128)),
                    writes=[key], queue="pool")
            elif kind == "inqk":
                v = sl[:].rearrange("p (k c) -> p k c", k=8)
                dma(lambda e: e.dma_start(out=v[:, :, 0:128], in_=win_d[l, :, c0:c0 + 128].rearrange("(k p) c -> p k c", p=128)),
                    writes=[key], queue="pool")
                dma(lambda e: e.dma_start(out=v[:, :, 128:256], in_=win_d[l, :, c0 + 256:c0 + 384].rearrange("(k p) c -> p k c", p=128)),
                    reads=[key], writes=[key], queue="pool")
            else:
                v = sl[:].rearrange("p (k c) -> p k c", k=2)
                dma(lambda e: e.dma_start(out=v, in_=wout_d[l, c0:c0 + 256, :].rearrange("(k p) c -> p k c", p=128)),
                    writes=[key], queue="pool")
            return v, key

        cnt = {"tok": 0, "tail": 0, "ev": 0}

        def proj_tok(wv, wkey, t, bk, ncols=256):
            for k in range(8):
                P.op("pe", lambda e, k=k: e.matmul(bank[bk][:, 0:ncols], lhsT=hT[:, k, t * 128:(t + 1) * 128], rhs=wv[:, k, 0:ncols],
                                                   start=(k == 0), stop=(k == 7)),
                     reads=["hT", wkey], writes=PS(bk), inc=(k == 7))

        def gate_evac(bk, t, i):
            P.op("act", lambda e: e.activation(out=th[i][:], in_=bank[bk][:, 0:256], func=AF.Tanh, scale=0.5),
                 reads=PS(bk), writes=[("th", i)])
            P.op("dve", lambda e: e.scalar_tensor_tensor(out=gate[:, t, :], in0=th[i][:], scalar=1.0, in1=bank[bk][:, 0:256],
                                                         op0=ALU.add, op1=ALU.mult),
                 reads=PS(bk) + [("th", i)], writes=["gate"])

        def tail(mx, mxkey, t, wov, wokey, tb=None, tbk=7, ob=None, i=None):
            if i is None:
                i = cnt["tail"] % 2
                cnt["tail"] += 1
            if tb is None:
                tb = bank[7].bitcast(BF16)[:, 0:256]
            for k in range(2):
                P.op("pe", lambda e, k=k: e.transpose(out=tb[:, k * 128:(k + 1) * 128], in_=mx[:, k * 128:(k + 1) * 128], identity=identb[:]),
                     reads=[mxkey, "identb"], writes=PS(tbk), inc=(k == 1))
            P.op("act", lambda e: e.copy(out=mT[i][:].rearrange("p a b -> p (a b)"), in_=tb), reads=PS(tbk), writes=[("mT", i)])
            if ob is None:
                for half in range(2):
                    for k in range(2):
                        P.op("pe", lambda e, k=k, half=half: e.matmul(bank[7], lhsT=mT[i][:, k, :], rhs=wov[:, k, half * 512:(half + 1) * 512],
                                                                      start=(k == 0), stop=(k == 1)),
                             reads=[("mT", i), wokey], writes=PS(7), inc=(k == 1))
                    P.op("dve", lambda e, half=half: e.tensor_tensor(out=xres[:, t, half * 512:(half + 1) * 512],
                                                                     in0=xres[:, t, half * 512:(half + 1) * 512], in1=bank[7], op=ALU.add),
                         reads=PS(7) + [("x", t)], writes=[("x", t)])
            else:
                for half in range(2):
                    for k in range(2):
                        P.op("pe", lambda e, k=k, half=half: e.matmul(bank[ob + half], lhsT=mT[i][:, k, :], rhs=wov[:, k, half * 512:(half + 1) * 512],
                                                                      start=(k == 0), stop=(k == 1)),
                             reads=[("mT", i), wokey], writes=PS(ob + half), inc=(k == 1))
                P.op("dve", lambda e: e.tensor_tensor(out=xres[:, t, :], in0=xres[:, t, :],
                                                      in1=psall[:, ob:ob + 2, :].rearrange("p a b -> p (a b)"), op=ALU.add),
                     reads=PS(ob, ob + 1) + [("x", t)], writes=[("x", t)])

        NG = 4

        def ms_sq(t):
            P.op("act", lambda e, t=t: e.activation(out=junk[:], in_=xres[:, t, :], func=AF.Square, accum_out=ss[:, 0, t:t + 1]),
                 reads=[("x", t)], writes=["junk", ("ss", t)])

        def ms_stats(g):
            t0, t1 = g * NG, (g + 1) * NG
            P.op("dve", lambda e: e.tensor_scalar(out=ss[:, 1, t0:t1], in0=ss[:, 0, t0:t1], scalar1=1.0 / D, scalar2=EPS, op0=ALU.mult, op1=ALU.add),
                 reads=[("ss", t) for t in range(t0, t1)], writes=[("ss1", g)])
            P.op("pool", lambda e: e.tensor_tensor(out=ss[:, 0, t0:t1], in0=ss[:, 1, t0:t1], in1=mhalf[:, 0:NG], op=ALU.pow),
                 reads=[("ss1", g), "mhalf"], writes=[("ss", t) for t in range(t0, t1)])

        def rmsnorm_to_hT(l):
            dma(lambda e: e.dma_start(out=nw, in_=nwb_d[l]), writes=["nw"])
            for t in range(NG):
                ms_sq(t)
            for t in range(NT):
                if t % NG == 0:
                    ms_stats(t // NG)
                if t + NG < NT:
                    ms_sq(t + NG)
                i = 0
                P.op("dve", lambda e, t=t, i=i: e.scalar_tensor_tensor(out=hb[i][:], in0=xres[:, t, :], scalar=ss[:, 0, t:t + 1], in1=nw,
                                                                       op0=ALU.mult, op1=ALU.mult),
                     reads=[("x", t), ("ss", t), "nw"], writes=[("hb", i)])
                bk = 5 + (t % 2)
                for k in range(8):
                    P.op("pe", lambda e, k=k, i=i, bk=bk: e.transpose(out=bank[bk][:].bitcast(BF16)[:, k * 128:(k + 1) * 128],
                                                                      in_=hb[i][:, k * 128:(k + 1) * 128], identity=identb[:]),
                         reads=[("hb", i), "identb"], writes=PS(bk), inc=(k == 7))
                if t % 2 == 0:
                    P.op("act", lambda e, t=t, bk=bk: e.copy(out=hT[:, :, t * 128:(t + 1) * 128],
                                                             in_=bank[bk][:].bitcast(BF16).rearrange("p (k c) -> p k c", k=8)),
                         reads=PS(bk), writes=[("hTw", t)])
                else:
                    P.op("dve", lambda e, t=t, bk=bk: e.tensor_copy(out=hT[:, :, t * 128:(t + 1) * 128],
                                                                    in_=bank[bk][:].bitcast(BF16).rearrange("p (k c) -> p k c", k=8)),
                         reads=PS(bk), writes=[("hTw", t)])

        def phase_A(l, hp):
            P.op("pool", lambda e: e.memset(V1[:, :, :, 128:129], 1.0), writes=["V1"])
            wq, wqk = load_w("in", l, OFF["aq"] + hp * 256)
            wk, wkk = load_w("in", l, OFF["ak"] + hp * 256)
            wv_, wvk = load_w("in", l, OFF["av"] + hp * 256)
            n = 0
            for (wv, wkey, dst, dkey, scl) in ((wq, wqk, qT, "qT", 0.125), (wk, wkk, kT, "kT", 1.0)):
                for hh in range(2):
                    for tc in range(4):
                        bk = n % 4
                        n += 1
                        for k in range(8):
                            P.op("pe", lambda e, k=k, bk=bk, wv=wv, hh=hh, tc=tc: e.matmul(
                                bank[bk][:, :], lhsT=wv[:, k, hh * 128:(hh + 1) * 128], rhs=hT[:, k, tc * 512:(tc + 1) * 512],
                                start=(k == 0), stop=(k == 7)), reads=["hT", wkey], writes=PS(bk), inc=(k == 7))
                        if n % 2 == 0:
                            P.op("act", lambda e, bk=bk, dst=dst, hh=hh, tc=tc, scl=scl: e.mul(out=dst[:, hh, tc * 512:(tc + 1) * 512],
                                                                                             in_=bank[bk][:, :], mul=scl),
                                 reads=PS(bk), writes=[dkey])
                        else:
                            P.op("dve", lambda e, bk=bk, dst=dst, hh=hh, tc=tc, scl=scl: e.tensor_scalar(
                                out=dst[:, hh, tc * 512:(tc + 1) * 512], in0=bank[bk][:, :], scalar1=scl, scalar2=None, op0=ALU.mult),
                                reads=PS(bk), writes=[dkey])
            wg, wgk = load_w("in", l, OFF["ag"] + hp * 256)
            wo, wok = load_w("out", l, hp * 256)
            for t in range(NT):
                bk = t % 4
                proj_tok(wv_, wvk, t, bk)
                eng = "act" if t % 2 == 0 else "dve"
                if eng == "act":
                    P.op("act", lambda e, t=t, bk=bk: e.copy(out=V1[:, t, :, 0:128], in_=bank[bk][:, 0:256].rearrange("p (a b) -> p a b", a=2)),
                         reads=PS(bk), writes=["V1"])
                else:
                    P.op("dve", lambda e, t=t, bk=bk: e.tensor_copy(out=V1[:, t, :, 0:128], in_=bank[bk][:, 0:256].rearrange("p (a b) -> p a b", a=2)),
                         reads=PS(bk), writes=["V1"])
            for t in range(NT):
                bk = t % 4
                proj_tok(wg, wgk, t, bk)
                gate_evac(bk, t, t % 2)

            hoist_next()

            def acc(r, c0=0, c1=129):
                return bank[4 + r // 3][:, (r % 3) * 129 + c0:(r % 3) * 129 + c1]

            pend = []

            def sched_chunk_tail(qc):
                items = []

                def pre():
                    P.op("pool", lambda e: e.tensor_tensor(out=oan, in0=oa, in1=oa, op=ALU.mult), reads=["oa"], writes=["oan"])
                    P.op("dve", lambda e: e.reduce_sum(out=sm[:, 16:24], in_=oan.rearrange("p a b c -> p (a b) c"), axis=AX.X),
                         reads=["oan"], writes=["sm"])
                    P.op("dve", lambda e: e.tensor_scalar(out=sm[:, 24:32], in0=sm[:, 16:24], scalar1=1.0 / 128, scalar2=EPS, op0=ALU.mult, op1=ALU.add),
                         reads=["sm"], writes=["sm"])
                    P.op("pool", lambda e: e.tensor_tensor(out=sm[:, 16:24], in0=sm[:, 24:32], in1=mhalf[:, 0:8], op=ALU.pow),
                         reads=["sm", "mhalf"], writes=["sm"])
                    P.op("dve", lambda e: e.tensor_tensor(out=oan.rearrange("p a b c -> p (a b) c"), in0=oa.rearrange("p a b c -> p (a b) c"),
                                                          in1=sm[:, 16:24].unsqueeze(2).to_broadcast([128, 8, 128]), op=ALU.mult),
                         reads=["oa", "sm"], writes=["oan"])
                items.append((0, pre))
                tb = bank[7].bitcast(BF16)[:, 0:256]
                for qs in range(4):
                    t = qc * 4 + qs
                    i = qs % 2
                    g0 = (6, 9, 19, 22)[qs]

                    def T1(qs=qs, t=t, i=i):
                        P.op("pool", lambda e: e.tensor_tensor(out=oan[:, qs], in0=oan[:, qs], in1=W2[:, l], op=ALU.mult),
                             reads=["oan", "W2"], writes=["oan"])
                        P.op("dve", lambda e: e.tensor_tensor(out=mixed[i][:], in0=oan[:, qs].rearrange("p a b -> p (a b)"),
                                                              in1=gate[:, t, :], op=ALU.mult),
                             reads=["oan", "gate"], writes=[("mixed", i)])

                    def T2(i=i):
                        for k in range(2):
                            P.op("pe", lambda e, k=k: e.transpose(out=tb[:, k * 128:(k + 1) * 128], in_=mixed[i][:, k * 128:(k + 1) * 128],
                                                                  identity=identb[:]),
                                 reads=[("mixed", i), "identb"], writes=PS(7), inc=(k == 1))
                        P.op("dve", lambda e: e.tensor_copy(out=mT[i][:].rearrange("p a b -> p (a b)"), in_=tb), reads=PS(7), writes=[("mT", i)])

                    def T4(half, t=t, i=i):
                        for k in range(2):
                            P.op("pe", lambda e, k=k: e.matmul(bank[7], lhsT=mT[i][:, k, :], rhs=wo[:, k, half * 512:(half + 1) * 512],
                                                               start=(k == 0), stop=(k == 1)),
                                 reads=[("mT", i), wok], writes=PS(7), inc=(k == 1))
                        P.op("dve", lambda e: e.tensor_tensor(out=xres[:, t, half * 512:(half + 1) * 512],
                                                              in0=xres[:, t, half * 512:(half + 1) * 512], in1=bank[7], op=ALU.add),
                             reads=PS(7) + [("x", t)], writes=[("x", t)])
                    items += [(g0, T1), (g0 + 2, T2), (g0 + 4, lambda T4=T4: T4(0)), (g0 + 5, lambda T4=T4: T4(1))]
                return items

            for qc in range(4):
                for hh in range(2):
                    h = 2 * hp + hh
                    seq = []
                    for kt in range(NT):
                        d = kt - 4 * qc
                        near = -1 <= d <= 4
                        seq.append((kt, near, d))

                    def emit_qk(j, hh=hh, qc=qc, h=h):
                        kt, near, d = seq[j]
                        b0 = (j % 2) * 2
                        for m in range(2):
                            P.op("pe", lambda e, m=m, kt=kt, b0=b0: e.matmul(
                                bank[b0 + m][:, :], lhsT=kT[m * 64:(m + 1) * 64, hh, kt * 128:(kt + 1) * 128],
                                rhs=qT[m * 64:(m + 1) * 64, hh, qc * 512:(qc + 1) * 512], start=True, stop=True),
                                reads=["kT", "qT"], writes=PS(b0 + m), inc=(m == 1))

                    def emit_exp(j, hh=hh, qc=qc, h=h):
                        kt, near, d = seq[j]
                        b0 = (j % 2) * 2
                        pt = PT[j % 3]
                        if near:
                            bias_ap = zero1[:, 0:1]
                        elif d > 4:
                            bias_ap = cfar[:, h:h + 1]
                        else:
                            bias_ap = cfar[:, 4 + h:5 + h]
                        P.op("act", lambda e: e.activation(out=pt.rearrange("p a b -> p (a b)"),
                                                           in_=psall[:, b0:b0 + 2, :].rearrange("p a b -> p (a b)"),
                                                           func=AF.Exp, bias=bias_ap, scale=1.0),
                             reads=PS(b0, b0 + 1) + ["cfar", "zero1"], writes=[("PT", j % 3, 0), ("PT", j % 3, 1)])
                        if near:
                            base = 512 - 128 * d
                            for m in range(2):
                                P.op("dve", lambda e, base=base, m=m: e.tensor_tensor(
                                    out=pt[:, m, :], in0=pt[:, m, :], in1=bmast[:, h, base:base + 512], op=ALU.mult),
                                    reads=["bmast"], writes=[("PT", j % 3, m)])

                    def emit_pv(j, hh=hh, qc=qc, h=h):
                        kt, near, d = seq[j]
                        pt = PT[j % 3]
                        for m in range(2):
                            for qs in range(4):
                                r = m * 4 + qs
                                first = (kt == 0 and r % 3 == 0)
                                P.op("pe", lambda e, m=m, qs=qs, r=r, first=first: e.matmul(
                                    acc(r), lhsT=pt[:, m, qs * 128:(qs + 1) * 128], rhs=V1[:, kt, hh, 0:129],
                                    start=first, stop=(kt == NT - 1), skip_group_check=True),
                                    reads=[("PT", j % 3, m), "V1"], writes=PS(4 + r // 3), inc=(m == 1 and qs == 3))

                    emit_qk(0)
                    for j in range(NT + 1):
                        if j + 1 < NT:
                            emit_qk(j + 1)
                        if j < NT:
                            emit_exp(j)
                        if j >= 1:
                            emit_pv(j - 1)
                        g_ = hh * NT + j
                        if j < NT:
                            for it_ in [x for x in pend if x[0] == g_]:
                                pend.remove(it_)
                                it_[1]()
                    P.op("act", lambda e: e.copy(out=accS[:, 0:3, :].rearrange("p a b -> p (a b)"), in_=bank[4][:, 0:387]), reads=PS(4), writes=["accS0"])
                    P.op("dve", lambda e: e.tensor_copy(out=accS[:, 3:6, :].rearrange("p a b -> p (a b)"), in_=bank[5][:, 0:387]), reads=PS(5), writes=["accS1"])
                    P.op("act", lambda e: e.copy(out=accS[:, 6:8, :].rearrange("p a b -> p (a b)"), in_=bank[6][:, 0:258]), reads=PS(6), writes=["accS2"])
                    P.op("dve", lambda e: e.reciprocal(out=sm[:, 0:8], in_=accS[:, :, 128]), reads=["accS0", "accS1", "accS2"], writes=["sm"])
                    P.op("dve", lambda e: e.tensor_scalar(out=sm[:, 4:8], in0=sm[:, 4:8], scalar1=nlam[:, l:l + 1], scalar2=None, op0=ALU.mult),
                         reads=["sm", "nlam"], writes=["sm"])
                    for qs in range(4):
                        r1 = 4 + qs
                        P.op("dve", lambda e, qs=qs, r1=r1: e.tensor_scalar(out=tmpA, in0=accS[:, r1, 0:128], scalar1=sm[:, r1:r1 + 1],
                                                                            scalar2=None, op0=ALU.mult),
                             reads=["accS0", "accS1", "accS2", "sm"], writes=["tmpA"])
                        P.op("dve", lambda e, qs=qs, hh=hh: e.scalar_tensor_tensor(out=oa[:, qs, hh, :], in0=accS[:, qs, 0:128], scalar=sm[:, qs:qs + 1],
                                                                                  in1=tmpA, op0=ALU.mult, op1=ALU.add),
                             reads=["accS0", "accS1", "accS2", "sm", "tmpA"], writes=["oa"])
                for it_ in sorted(pend, key=lambda x: x[0]):
                    it_[1]()
                pend = sched_chunk_tail(qc)
            for it_ in sorted(pend, key=lambda x: x[0]):
                it_[1]()
            pend = []

        def phase_B(l, hp):
            wqk_, wqkk = load_w("inqk", l, OFF["bq"] + hp * 128)
            wv_, wvk = load_w("in", l, OFF["bv"] + hp * 256)
            wg, wgk = load_w("in", l, OFF["bg"] + hp * 256)
            wo, wok = load_w("out", l, 512 + hp * 256)
            dma(lambda e: e.dma_start(out=cs, in_=cs_d), writes=["cs"])
            for s2 in range(2):
                P.op("pool", lambda e, s2=s2: e.memset(TTq2[s2], 0.0), writes=[("TTq2", s2)])
            for t in range(NT):
                bk = t % 4
                i2 = t % 2
                proj_tok(wqk_, wqkk, t, bk)
                P.op("act", lambda e, bk=bk, i2=i2: e.copy(out=qk32[i2][:, 0:128], in_=bank[bk][:, 0:128]), reads=PS(bk), writes=[("qk32", i2)])
                P.op("act", lambda e, bk=bk, i2=i2: e.mul(out=qk32[i2][:, 128:256], in_=bank[bk][:, 128:256], mul=0.125),
                     reads=PS(bk), writes=[("qk32", i2)])
                src4 = qk32[i2].rearrange("p (a b c) -> p a b c", a=4, b=2)
                cos_b = cs[:, 0, t, :].unsqueeze(1).to_broadcast([128, 4, 32])
                sin_b = cs[:, 1, t, :].unsqueeze(1).to_broadcast([128, 4, 32])
                t1 = src4[:, :, 0, :]
                t2 = src4[:, :, 1, :]
                rt = [rtmp[i2][:, i].rearrange("p (a c) -> p a c", a=4) for i in range(4)]
                dst4 = qkr[:, t, :].rearrange("p (a b c) -> p a b c", a=4, b=2)
                P.op("pool", lambda e, t1=t1, cos_b=cos_b, rt=rt: e.tensor_tensor(out=rt[0], in0=t1, in1=cos_b, op=ALU.mult),
                     reads=[("qk32", i2), "cs"], writes=[("rtmp", i2, 0)])
                P.op("pool", lambda e, t2=t2, sin_b=sin_b, rt=rt: e.tensor_tensor(out=rt[1], in0=t2, in1=sin_b, op=ALU.mult),
                     reads=[("qk32", i2), "cs"], writes=[("rtmp", i2, 1)])
                P.op("dve", lambda e, t1=t1, sin_b=sin_b, rt=rt: e.tensor_tensor(out=rt[2], in0=t1, in1=sin_b, op=ALU.mult),
                     reads=[("qk32", i2), "cs"], writes=[("rtmp", i2, 2)])
                P.op("dve", lambda e, t2=t2, cos_b=cos_b, rt=rt: e.tensor_tensor(out=rt[3], in0=t2, in1=cos_b, op=ALU.mult),
                     reads=[("qk32", i2), "cs"], writes=[("rtmp", i2, 3)])
                P.op("pool", lambda e, rt=rt, dst4=dst4: e.tensor_tensor(out=dst4[:, :, 0, :], in0=rt[0], in1=rt[1], op=ALU.subtract),
                     reads=[("rtmp", i2, 0), ("rtmp", i2, 1)], writes=[("qkr", t, 0)])
                P.op("dve", lambda e, rt=rt, dst4=dst4: e.tensor_tensor(out=dst4[:, :, 1, :], in0=rt[2], in1=rt[3], op=ALU.add),
                     reads=[("rtmp", i2, 2), ("rtmp", i2, 3)], writes=[("qkr", t, 1)])
            for t in range(NT):
                bk = t % 4
                proj_tok(wv_, wvk, t, bk)
                if t % 2 == 0:
                    P.op("act", lambda e, t=t, bk=bk: e.copy(out=vB[:, t, :], in_=bank[bk][:, 0:256]), reads=PS(bk), writes=[("vB", t)])
                else:
                    P.op("dve", lambda e, t=t, bk=bk: e.tensor_copy(out=vB[:, t, :], in_=bank[bk][:, 0:256]), reads=PS(bk), writes=[("vB", t)])
            for t in range(NT):
                bk = t % 4
                proj_tok(wg, wgk, t, bk)
                gate_evac(bk, t, t % 2)
            hoist_next()
            QKR = lambda t: [("qkr", t, 0), ("qkr", t, 1)]

            def kvp(t):
                return bank[t // 2][:, (t % 2) * 256:(t % 2) * 256 + 256]

            for t in range(NT):
                i2 = t % 2
                kq = qkr[:, t, 128:256].rearrange("p (a c) -> p a c", a=2)
                for dr in range(2):
                    c0 = 8 + dr * 4 + 2 * hp
                    P.op("pool", lambda e, dr=dr, c0=c0, kq=kq, i2=i2: e.tensor_tensor(
                        out=kdec[i2][:, :, dr, :], in0=kq, in1=tqk[:, l, c0:c0 + 2].unsqueeze(2).to_broadcast([128, 2, 64]), op=ALU.mult),
                        reads=QKR(t) + ["tqk"], writes=[("kdec", i2)])
                for hh in range(2):
                    P.op("pe", lambda e, hh=hh, t=t, i2=i2: e.matmul(kvp(t)[:, hh * 128:(hh + 1) * 128], lhsT=kdec[i2][:, hh].rearrange("p a b -> p (a b)"),
                                                                    rhs=vB[:, t, hh * 128:(hh + 1) * 128], start=True, stop=True),
                         reads=[("kdec", i2), ("vB", t)], writes=PS(t // 2), inc=(hh == 1))

            P.op("pool", lambda e: e.memset(Rcur, 0.0), writes=[("Rcur", 0), ("Rcur", 64)])
            for n_ in range(NT):
                for (lo, hi, t, dr) in ((0, 64, n_, 0), (64, 128, NT - 1 - n_, 1)):
                    P.op("act", lambda e, t=t, lo=lo, hi=hi: e.copy(out=Rst[lo:hi, t], in_=Rcur[lo:hi]), reads=[("Rcur", lo)], writes=[("Rst", t, lo)])
                    if n_ == NT - 1:
                        continue
                    for hh in range(2):
                        gc = l * 8 + dr * 4 + 2 * hp + hh
                        P.op("dve", lambda e, lo=lo, hi=hi, t=t, hh=hh, gc=gc: e.scalar_tensor_tensor(
                            out=Rcur[lo:hi, hh, :], in0=Rcur[lo:hi, hh, :], scalar=gsc[lo:hi, gc:gc + 1],
                            in1=kvp(t)[lo:hi, hh * 128:(hh + 1) * 128], op0=ALU.mult, op1=ALU.add),
                            reads=PS(t // 2) + [("Rcur", lo), "gsc"], writes=[("Rcur", lo)])

            def S0a(t):
                s2 = t % 2
                s3 = t % 3
                bTS = 2 * s2
                qq = qkr[:, t, 0:128].rearrange("p (a c) -> p a c", a=2)
                for dr in range(2):
                    c0 = dr * 4 + 2 * hp
                    P.op("pool", lambda e, dr=dr, c0=c0, qq=qq, s2=s2: e.tensor_tensor(
                        out=qdec[s2][:, :, dr, :], in0=qq, in1=tqk[:, l, c0:c0 + 2].unsqueeze(2).to_broadcast([128, 2, 64]), op=ALU.mult),
                        reads=QKR(t) + ["tqk"], writes=[("qdec", s2)])
                bT = bank[bTS].bitcast(BF16)
                srcs = [qdec[s2][:, 0].rearrange("p a b -> p (a b)"), qdec[s2][:, 1].rearrange("p a b -> p (a b)"),
                        qkr[:, t, 128:256], qkr[:, t, 0:128]]
                for i4 in range(4):
                    P.op("pe", lambda e, i4=i4, srcs=srcs, bT=bT: e.transpose(out=bT[:, i4 * 128:(i4 + 1) * 128], in_=srcs[i4], identity=identb[:]),
                         reads=[("qdec", s2), "identb"] + QKR(t), writes=PS(bTS), inc=(i4 == 3))
                P.op("act", lambda e, bT=bT, s3=s3: e.copy(out=TT[s3].rearrange("p a b -> p (a b)"), in_=bT[:, 0:384]),
                     reads=PS(bTS), writes=[("TT", s3)])
                for hh in range(2):
                    P.op("act", lambda e, bT=bT, s2=s2, hh=hh: e.copy(out=TTq2[s2][hh * 64:(hh + 1) * 64, hh, :],
                                                                       in_=bT[hh * 64:(hh + 1) * 64, 384:512]),
                         reads=PS(bTS), writes=[("TTq2", s2)])

            def S0b(t):
                s2 = t % 2
                s3 = t % 3
                bTS = 2 * s2
                P.op("pe", lambda e, s2=s2, s3=s3, bTS=bTS: e.matmul(bank[bTS][:, 256:512], lhsT=TT[s3][:, 2, :],
                                                                    rhs=TTq2[s2].rearrange("p a b -> p (a b)"), start=True, stop=True),
                     reads=[("TT", s3), ("TTq2", s2)], writes=PS(bTS), inc=True)
                P.op("dve", lambda e, s2=s2, bTS=bTS: e.tensor_tensor(out=innerT[s2], in0=bank[bTS][:, 256:512].rearrange("p (a b) -> p a b", a=2),
                                                                     in1=D2T[:, l, 2 * hp:2 * hp + 2, :], op=ALU.mult),
                     reads=PS(bTS) + ["D2T"], writes=[("innerT", s2)])

            def S0c(t):
                s2 = t % 2
                s3 = t % 3
                bO = 2 * s2 + 1
                for hh in range(2):
                    P.op("pe", lambda e, hh=hh, t=t, s2=s2, bO=bO: e.matmul(bank[bO][:, hh * 128:(hh + 1) * 128], lhsT=innerT[s2][:, hh, :],
                                                                           rhs=vB[:, t, hh * 128:(hh + 1) * 128], start=True, stop=False),
                         reads=[("innerT", s2), ("vB", t)], writes=PS(bO), inc=False)
                    P.op("pe", lambda e, hh=hh, t=t, s3=s3, bO=bO: e.matmul(bank[bO][:, hh * 128:(hh + 1) * 128], lhsT=TT[s3][:, hh, :],
                                                                           rhs=Rst[:, t, hh, :], start=False, stop=True),
                         reads=[("TT", s3), ("Rst", t, 0), ("Rst", t, 64)], writes=PS(bO), inc=(hh == 1))

            def S1a(t):
                s2 = t % 2
                s3 = t % 3
                bO = 2 * s2 + 1
                sc = 32 + 8 * s3
                for hh in range(2):
                    P.op("act", lambda e, hh=hh, bO=bO, sc=sc: e.activation(out=junk[:, hh * 128:(hh + 1) * 128], in_=bank[bO][:, hh * 128:(hh + 1) * 128],
                                                                          func=AF.Square, accum_out=sm[:, sc + hh:sc + hh + 1]),
                         reads=PS(bO), writes=["junk", ("smB", s3)])
                P.op("act", lambda e, bO=bO, s3=s3: e.copy(out=Osb[s3].rearrange("p a b -> p (a b)"), in_=bank[bO][:, 0:256]),
                     reads=PS(bO), writes=[("Osb", s3)])
                P.op("dve", lambda e, sc=sc: e.tensor_scalar(out=sm[:, sc + 2:sc + 4], in0=sm[:, sc:sc + 2], scalar1=4.0 / 128, scalar2=4.0 * EPS,
                                                             op0=ALU.mult, op1=ALU.add), reads=[("smB", s3)], writes=[("smB", s3)])

            def S1b(t):
                s3 = t % 3
                sc = 32 + 8 * s3
                P.op("pool", lambda e, sc=sc: e.tensor_tensor(out=sm[:, sc + 4:sc + 6], in0=sm[:, sc + 2:sc + 4], in1=mhalf[:, 0:2], op=ALU.pow),
                     reads=[("smB", s3), "mhalf"], writes=[("smB", s3)])

            def S1c(t):
                s2 = t % 2
                s3 = t % 3
                sc = 32 + 8 * s3
                P.op("dve", lambda e, s2=s2, s3=s3, sc=sc: e.tensor_tensor(out=btmp[s2], in0=Osb[s3],
                                                                          in1=sm[:, sc + 4:sc + 6].unsqueeze(2).to_broadcast([128, 2, 128]), op=ALU.mult),
                     reads=[("Osb", s3), ("smB", s3)], writes=[("btmp", s2)])

            def S1d(t):
                s2 = t % 2
                i = t % 2
                P.op("pool", lambda e, i=i, t=t, s2=s2: e.tensor_tensor(out=mixed[i][:], in0=btmp[s2].rearrange("p a b -> p (a b)"), in1=gate[:, t, :], op=ALU.mult),
                     reads=[("btmp", s2), "gate"], writes=[("mixed", i)])

            def S2(t):
                s2 = t % 2
                bO = 2 * s2 + 1
                i = t % 2
                tail(mixed[i], ("mixed", i), t, wo, wok, tb=bank[bO].bitcast(BF16)[:, 512:768], tbk=bO, ob=4 + 2 * s2, i=i)

            stages = [S0a, S0b, S0c, S1a, S1b, S1c, S1d, S2]
            for it in range(NT + len(stages) - 1):
                for k_, fn_ in enumerate(stages):
                    if 0 <= it - k_ < NT:
                        fn_(it - k_)

        def phase_C(l, gp):
            wu, wuk = load_w("in", l, OFF["cu"] + gp * 256)
            wg, wgk = load_w("in", l, OFF["cg"] + gp * 256)
            wo, wok = load_w("out", l, 1024 + gp * 256)
            for t in range(NT):
                bk = t % 4
                proj_tok(wu, wuk, t, bk)
                if t % 2 == 0:
                    P.op("act", lambda e, t=t, bk=bk: e.copy(out=uC[:, t, :], in_=bank[bk][:, 0:256]), reads=PS(bk), writes=[("uC", t)])
                else:
                    P.op("dve", lambda e, t=t, bk=bk: e.tensor_copy(out=uC[:, t, :], in_=bank[bk][:, 0:256]), reads=PS(bk), writes=[("uC", t)])
            for t in range(NT):
                bk = t % 4
                proj_tok(wg, wgk, t, bk)
                gate_evac(bk, t, t % 2)
            hoist_next()

            def S0(t):
                s2 = t % 2
                bP = 2 * s2
                bY = 2 * s2 + 1
                for gg in range(2):
                    g = 2 * gp + gg
                    parts = [(t, g * 5 + (3 if t == 0 else 4 if t == NT - 1 else 0))]
                    if t > 0:
                        parts.append((t - 1, g * 5 + 1))
                    if t < NT - 1:
                        parts.append((t + 1, g * 5 + 2))
                    for pi_, (tj, mi) in enumerate(parts):
                        P.op("pe", lambda e, gg=gg, tj=tj, mi=mi, pi_=pi_, np_=len(parts), bP=bP: e.matmul(
                            bank[bP][:, gg * 128:(gg + 1) * 128], lhsT=uC[:, tj, gg * 128:(gg + 1) * 128], rhs=poolm[:, mi, :],
                            start=(pi_ == 0), stop=(pi_ == np_ - 1)),
                            reads=[("uC", tj), "poolm"], writes=PS(bP), inc=(gg == 1 and pi_ == len(parts) - 1))
                P.op("act", lambda e, bP=bP, s2=s2: e.copy(out=pooledT[s2].rearrange("p a b -> p (a b)"), in_=bank[bP][:, 0:256]),
                     reads=PS(bP), writes=[("pooledT", s2)])
                for gg in range(2):
                    g = 2 * gp + gg
                    P.op("pe", lambda e, gg=gg, g=g, s2=s2, bY=bY: e.matmul(bank[bY][:, gg * 128:(gg + 1) * 128], lhsT=pooledT[s2][:, gg, :],
                                                                           rhs=poolw[:, l * 4 + g, :], start=True, stop=True),
                         reads=[("pooledT", s2), "poolw"], writes=PS(bY), inc=(gg == 1))

            def S1(t):
                s2 = t % 2
                bY = 2 * s2 + 1
                i = t % 2
                P.op("dve", lambda e, s2=s2, bY=bY: e.tensor_tensor(out=ytmp[s2], in0=bank[bY][:, 0:256], in1=psh[:, l, gp * 256:(gp + 1) * 256], op=ALU.mult),
                     reads=PS(bY) + ["psh"], writes=[("ytmp", s2)])
                P.op("pool", lambda e, t=t, i=i, s2=s2: e.tensor_tensor(out=mixed[i][:], in0=ytmp[s2], in1=gate[:, t, :], op=ALU.mult),
                     reads=[("ytmp", s2), "gate"], writes=[("mixed", i)])

            def S2(t):
                s2 = t % 2
                bY = 2 * s2 + 1
                i = t % 2
                tail(mixed[i], ("mixed", i), t, wo, wok, tb=bank[bY].bitcast(BF16)[:, 512:768], tbk=bY, ob=4 + 2 * s2, i=i)

            for it in range(NT + 2):
                if it < NT:
                    S0(it)
                if 0 <= it - 1 < NT:
                    S1(it - 1)
                if 0 <= it - 2 < NT:
                    S2(it - 2)

        for s_ in range(nseq):
            for l in range(nlayers):
                for ph in phases:
                    for hp in range(2):
                        sched["list"].append((ph, l, hp))
        sched["i"] = -1
        hoist_next()
        fns = {"A": phase_A, "B": phase_B, "C": phase_C}
        pi = 0
        for s in range(nseq):
            for t in range(NT):
                dma(lambda e, s=s, t=t: e.dma_start(out=xres[:, t, :], in_=x_d[s, t]), writes=[("x", t)])
            for l in range(nlayers):
                P.fence()
                rmsnorm_to_hT(l)
                for ph in phases:
                    for hp in range(2):
                        if hp == 0:
                            P.fence()
                        sched["i"] = pi
                        fns[ph](l, hp)
                        pi += 1
            P.fence()
            dma(lambda e: e.dma_start(out=nw, in_=nwb_d[2]), writes=["nw"])
            for t in range(NG):
                ms_sq(t)
            for t in range(NT):
                if t % NG == 0:
                    ms_stats(t // NG)
                if t + NG < NT:
                    ms_sq(t + NG)
                P.op("dve", lambda e, t=t: e.scalar_tensor_tensor(out=xres[:, t, :], in0=xres[:, t, :], scalar=ss[:, 0, t:t + 1], in1=nw,
                                                                  op0=ALU.mult, op1=ALU.mult),
                     reads=[("x", t), ("ss", t), "nw"], writes=[("x", t)])
                dma(lambda e, s=s, t=t: e.dma_start(out=y_d[s, t], in_=xres[:, t, :]), reads=[("x", t)], writes=[("y", s, t)])
        P.emit(nc, st)
    return nc


_NC_CACHE = {}


def kernel(**inputs):
    x = np.asarray(inputs["x"], np.float32)
    B = x.shape[0]
    ncores = 8
    nseq = B // ncores
    consts = _host_consts(inputs)
    key = (nseq,)
    if key not in _NC_CACHE:
        _NC_CACHE[key] = build_nc(nseq=nseq)
    nc = _NC_CACHE[key]
    w_in = np.ascontiguousarray(np.asarray(inputs["w_in"], np.float32))
    w_out = np.ascontiguousarray(np.asarray(inputs["w_out"], np.float32))
    in_maps = []
    for c in range(ncores):
        m = dict(consts)
        m["x"] = np.ascontiguousarray(x[c * nseq:(c + 1) * nseq].reshape(nseq, NT, 128, D))
        m["w_in"] = w_in
        m["w_out"] = w_out
        in_maps.append(m)
    res = run_bass_kernel_spmd(nc, in_maps, core_ids=list(range(ncores)))
    out = np.concatenate([np.asarray(r["y"]).reshape(nseq, S, D) for r in res.results], axis=0)
    return out.astype(np.float32)
```

```python
import contextlib
import math
import numpy as np
import concourse.bass as bass
import concourse.mybir as mybir
from concourse.bass_utils import run_bass_kernel_spmd

F32 = mybir.dt.float32
BF16 = mybir.dt.bfloat16
ALU = mybir.AluOpType
AF = mybir.ActivationFunctionType
AX = mybir.AxisListType

D = 1024
S = 2048
NT = S // 128
DIN = 4608
DMIX = 1536
EPS = 1e-6
OFF = dict(aq=0, ak=512, av=1024, ag=1536, bq=2048, bk=2304, bv=2560, bg=3072, cu=3584, cg=4096)
MW = 1152

ENGS = ("pe", "act", "dve", "pool", "sp")
CUT = 99
SEM_LIMIT = 30000


class Op:
    __slots__ = ("eng", "fn", "deps", "inc", "is_dma", "seq", "eidx", "sem", "val", "name",
                 "closer", "dslot", "nofence")


class Prog:
    def __init__(self):
        self.ops = []
        self.last_w = {}
        self.readers = {}
        self.pend = {e: set() for e in ENGS}
        self.fence_idx = 0

    def fence(self):
        deps = set()
        last = {}
        for o in self.ops[self.fence_idx:]:
            if o.is_dma:
                if not o.nofence:
                    deps.add(o.seq)
            else:
                last[o.eng] = o.seq
        deps |= set(last.values())
        for e in ENGS:
            self.pend[e] |= deps
        self.fence_idx = len(self.ops)

    def _add(self, eng, fn, reads, writes, inc, is_dma, name):
        o = Op()
        o.eng, o.fn, o.inc, o.is_dma, o.name = eng, fn, inc, is_dma, name
        o.seq = len(self.ops)
        o.closer = o.seq
        o.nofence = False
        deps = set(self.pend[eng])
        self.pend[eng] = set()
        reads = list(reads)
        writes = list(writes)
        ex = [r for r in reads if isinstance(r, tuple) and r[0] == "ps"]
        reads = [r for r in reads if r not in ex]
        writes = writes + [r for r in ex if r not in writes]
        for r in reads:
            if r in self.last_w:
                deps.add(self.last_w[r])
        for w in writes:
            if w in self.last_w:
                deps.add(self.last_w[w])
            for rd in self.readers.get(w, ()):
                deps.add(rd)
        for r in reads:
            self.readers.setdefault(r, []).append(o.seq)
        for w in writes:
            self.last_w[w] = o.seq
            self.readers[w] = []
        deps.discard(o.seq)
        o.deps = deps
        self.ops.append(o)
        return o

    def op(self, eng, fn, reads=(), writes=(), inc=True, name=""):
        return self._add(eng, fn, reads, writes, inc, False, name)

    def dma(self, fn, reads=(), writes=(), queue="sp", name=""):
        return self._add(queue, fn, reads, writes, True, True, name)

    def emit(self, nc, st, ndma_sems=8):
        ops = self.ops
        per_eng = {e: [] for e in ENGS}
        for o in ops:
            o.eidx = len(per_eng[o.eng])
            per_eng[o.eng].append(o)
        nsem = [0]

        def newsem(tag):
            nsem[0] += 1
            return st.enter_context(nc.semaphore("%s_%d" % (tag, nsem[0])))

        dsem = {q: [newsem("d" + q) for _ in range(ndma_sems)] for q in ("sp", "pool")}
        for e in ENGS:
            cnt = 0
            cur = newsem("s" + e)
            dcnt = [0] * ndma_sems
            nd = 0
            pending = []
            for o in per_eng[e]:
                if o.is_dma:
                    j = nd % ndma_sems
                    nd += 1
                    dcnt[j] += 16
                    o.sem, o.val, o.dslot = dsem[e][j], dcnt[j], j
                elif o.inc:
                    if cnt >= SEM_LIMIT:
                        cur = newsem("s" + e)
                        cnt = 0
                    cnt += 1
                    o.sem, o.val = cur, cnt
                    for p in pending:
                        p.sem, p.val, p.closer = cur, cnt, o.seq
                    pending = []
                else:
                    pending.append(o)
            assert not pending, "trailing no-inc ops on " + e
        blk = st.enter_context(nc.Block())

        def make(e):
            def body(engine):
                waited = {}
                last_dma = {}

                def need(p):
                    if waited.get(p.sem, 0) >= p.val:
                        return
                    waited[p.sem] = p.val
                    engine.wait_ge(p.sem, p.val)

                for o in per_eng[e]:
                    for d in sorted(o.deps):
                        p = ops[d]
                        if p.eng == e and not p.is_dma and not o.is_dma:
                            if e != "pe" and o.eidx - p.eidx <= 2:
                                need(p)
                            continue
                        assert p.closer < o.seq, (p.name, o.name)
                        need(p)
                    if o.is_dma:
                        j = o.dslot
                        if j in last_dma:
                            need(last_dma[j])
                        last_dma[j] = o
                        o.fn(engine).then_inc(o.sem, 16)
                    else:
                        ins = o.fn(engine)
                        if o.inc:
                            ins.then_inc(o.sem, 1)
                for pv in last_dma.values():
                    need(pv)
            return body

        blk.tensor(make("pe"))
        blk.scalar(make("act"))
        blk.vector(make("dve"))
        blk.gpsimd(make("pool"))
        blk.sync(make("sp"))


def _t5_bucket(rel):
    half, max_exact = 16, 8
    ret = np.where(rel > 0, half, 0)
    n = np.abs(rel)
    nf = np.maximum(n, 1).astype(np.float32)
    large = max_exact + (np.log(nf / np.float32(max_exact)) / np.float32(math.log(128 / max_exact))
                         * np.float32(half - max_exact)).astype(np.int32)
    large = np.minimum(large, half - 1)
    return ret + np.where(n < max_exact, n, large)


def _pool_mats():
    pm = np.zeros((128, 20, 128), np.float32)
    for g, w in enumerate((2, 4, 8, 16)):
        for v in range(5):
            t = {0: 5, 1: 5, 2: 5, 3: 0, 4: NT - 1}[v]
            for i in range(128):
                gi = t * 128 + i
                lo = min(max(gi - w // 2, 0), S)
                hi = min(max(gi + (w - w // 2), 0), S)
                cnt = float(hi - lo)
                for gj in range(lo, hi):
                    tj, j = divmod(gj, 128)
                    rel = tj - t
                    if v in (0, 3, 4) and rel == 0:
                        pm[j, g * 5 + v, i] += 1.0 / cnt
                    elif v == 1 and rel == -1:
                        pm[j, g * 5 + v, i] += 1.0 / cnt
                    elif v == 2 and rel == 1:
                        pm[j, g * 5 + v, i] += 1.0 / cnt
                if v in (0, 3, 4):
                    pm[i, g * 5 + v, i] -= 1.0
    return pm


def _host_consts(inp):
    c = {}
    bc = lambda a: np.ascontiguousarray(np.broadcast_to(a, (128,) + a.shape)).astype(np.float32)
    c["nwb"] = np.ascontiguousarray(np.stack([bc(inp["norm_w"][0]), bc(inp["norm_w"][1]),
                                               bc(inp["final_norm_w"])], 0))
    c["ident"] = np.eye(128, dtype=np.float32)
    p = np.arange(128)[:, None]
    cc = np.arange(MW)[None, :]
    bidx = _t5_bucket(p - cc + 512)
    rb = np.asarray(inp["rel_bias"], np.float32)
    c["bmaster"] = np.ascontiguousarray(rb[bidx].transpose(0, 2, 1))
    c["cfar"] = bc(np.concatenate([rb[31], rb[15]]))
    half = 32
    theta = (1.0 / (np.float32(10000.0) ** np.linspace(0.0, 1.0, half, dtype=np.float32))).astype(np.float32)
    ang = (np.arange(S, dtype=np.float32)[:, None] * theta[None, :]).astype(np.float32)
    cs = np.stack([np.cos(ang), np.sin(ang)], 0).astype(np.float32)
    c["cs"] = np.ascontiguousarray(cs.reshape(2, NT, 128, half).transpose(2, 0, 1, 3))
    m = np.arange(128, dtype=np.float32)[:, None]
    n = np.arange(128, dtype=np.float32)[None, :]
    c["retc"] = np.ascontiguousarray(np.stack([np.maximum(n - m, 0), np.maximum(m - n, 0)], 1))
    i = np.arange(128, dtype=np.float32)
    c["tokidx"] = np.ascontiguousarray(np.stack([i + 1, 128 - i, 127 - i, i], 1))
    c["dlam"] = bc(np.asarray(inp["diff_lambda"], np.float32).reshape(2, 256))
    c["subw"] = bc(np.asarray(inp["diff_subln_w"], np.float32))
    c["rdl"] = bc(np.asarray(inp["ret_decay_logit"], np.float32).reshape(16))
    c["pscale"] = bc(np.asarray(inp["pool_scale"], np.float32))
    c["poolw"] = np.ascontiguousarray(np.asarray(inp["pool_w"], np.float32))
    c["poolm"] = _pool_mats()
    return c


def build_nc(nseq=2, nlayers=2, phases="ABC"):
    nc = bass.Bass("TRN2", target_bir_lowering=False)
    din = lambda name, shape: nc.dram_tensor(name, list(shape), F32, kind="ExternalInput").ap()
    x_d = din("x", [nseq, NT, 128, D])
    win_d = din("w_in", [2, D, DIN])
    wout_d = din("w_out", [2, DMIX, D])
    nwb_d = din("nwb", [3, 128, D])
    ident_d = din("ident", [128, 128])
    bm_d = din("bmaster", [128, 4, MW])
    cfar_d = din("cfar", [128, 8])
    cs_d = din("cs", [128, 2, NT, 32])
    retc_d = din("retc", [128, 2, 128])
    tokidx_d = din("tokidx", [128, 4])
    dlam_d = din("dlam", [128, 2, 256])
    subw_d = din("subw", [128, 2, 128])
    rdl_d = din("rdl", [128, 16])
    pscale_d = din("pscale", [128, 2, 512])
    poolw_d = din("poolw", [2, 4, 128, 128])
    poolm_d = din("poolm", [128, 20, 128])
    y_d = nc.dram_tensor("y", [nseq, NT, 128, D], F32, kind="ExternalOutput").ap()

    P = Prog()
    with contextlib.ExitStack() as st:
        def sb(name, shape, dt=F32):
            return st.enter_context(nc.sbuf_tensor("s_" + name, list(shape), dt))

        xres = sb("xres", [128, NT, D])
        hT = sb("hT", [128, 8, S], BF16)
        NSLOT = 6
        wslot = [sb("wslot%d" % i, [128, 2048], BF16) for i in range(NSLOT)]
        bmast = sb("bmast", [128, 4, MW], BF16)
        identb = sb("identb", [128, 128], BF16)
        cfar = sb("cfar", [128, 8])
        zero1 = sb("zero1", [128, 1])
        mhalf = sb("mhalf", [128, 16])
        tokidx = sb("tokidx", [128, 4])
        rdl = sb("rdl", [128, 16])
        poolw = sb("poolw", [128, 8, 128], BF16)
        poolm = sb("poolm", [128, 20, 128], BF16)
        lg = sb("lg", [128, 16])
        tqk = sb("tqk", [128, 2, 16])
        gsc = sb("gsc", [128, 16])
        D2T = sb("D2T", [128, 2, 4, 128])
        W2 = sb("W2", [128, 2, 2, 128])
        psh = sb("psh", [128, 2, 512])
        nlam = sb("nlam", [128, 2])
        sm = sb("sm", [128, 64])
        ss = sb("ss", [128, 2, NT])
        hb = [sb("hb0", [128, D], BF16)] * 2
        junk = sb("junk", [128, D], BF16)
        mixed = [sb("mixed%d" % i, [128, 256], BF16) for i in range(2)]
        mT = [sb("mT%d" % i, [128, 2, 128], BF16) for i in range(2)]
        gate = sb("gate", [128, NT, 256], BF16)
        th = [sb("th%d" % i, [128, 256]) for i in range(2)]
        ARENA = 43 * 1024 + 768
        arena = sb("arena", [128, ARENA], mybir.dt.uint8)

        def carve(off, shape, dt):
            n = int(np.prod(shape))
            bpe = 2 if dt == BF16 else 4
            ap = arena[:, off:off + n * bpe].bitcast(dt)
            if len(shape) > 1:
                names = " ".join("d%d" % i for i in range(len(shape)))
                kw = {"d%d" % i: shape[i] for i in range(1, len(shape))}
                ap = ap.rearrange("p (%s) -> p %s" % (names, names), **kw)
            return ap, off + n * bpe

        o = 0
        nw, o = carve(o, [D], F32)
        identf, o = carve(o, [128], F32)
        retc, o = carve(o, [2, 128], F32)
        dlam, o = carve(o, [2, 256], F32)
        subw, o = carve(o, [2, 128], F32)
        scr, o = carve(o, [256], F32)
        nwF, o = carve(o, [D], F32)
        o = 0
        qT, o = carve(o, [2, S], BF16)
        kT, o = carve(o, [2, S], BF16)
        V1, o = carve(o, [NT, 2, 130], BF16)
        oa, o = carve(o, [4, 2, 128], F32)
        oan, o = carve(o, [4, 2, 128], F32)
        PT0, o = carve(o, [2, 512], BF16)
        PT1, o = carve(o, [2, 512], BF16)
        PT2, o = carve(o, [2, 512], BF16)
        PT = [PT0, PT1, PT2]
        tmpA, o = carve(o, [128], F32)
        accS, o = carve(o, [8, 129], F32)
        assert o <= ARENA, o
        o = 0
        qkr, o = carve(o, [NT, 256], BF16)
        vB, o = carve(o, [NT, 256], BF16)
        Rst, o = carve(o, [NT, 2, 128], BF16)
        cs, o = carve(o, [2, NT, 32], F32)
        Rcur, o = carve(o, [2, 128], F32)
        qk32 = []
        rtmp = []
        kdec = []
        qdec = []
        TT = []
        TTq2 = []
        innerT = []
        btmp = []
        for _i in range(2):
            a_, o = carve(o, [256], F32); qk32.append(a_)
            a_, o = carve(o, [4, 128], F32); rtmp.append(a_)
            a_, o = carve(o, [2, 2, 64], BF16); kdec.append(a_)
            a_, o = carve(o, [2, 2, 64], BF16); qdec.append(a_)
            a_, o = carve(o, [2, 128], BF16); TTq2.append(a_)
            a_, o = carve(o, [2, 128], BF16); innerT.append(a_)
            a_, o = carve(o, [2, 128], BF16); btmp.append(a_)
        Osb = []
        for _i in range(3):
            a_, o = carve(o, [3, 128], BF16); TT.append(a_)
            a_, o = carve(o, [2, 128], BF16); Osb.append(a_)
        assert o <= ARENA, o
        o = 0
        uC, o = carve(o, [NT, 256], BF16)
        pooledT = []
        ytmp = []
        for _i in range(2):
            a_, o = carve(o, [2, 128], BF16); pooledT.append(a_)
            a_, o = carve(o, [256], F32); ytmp.append(a_)
        assert o <= ARENA, o

        psall = st.enter_context(nc.psum_tensor("psall", [128, 8, 512], F32))
        bank = [psall[:, i, :] for i in range(8)]

        def PS(*idx):
            return [("ps", i) for i in idx]


        dma = P.dma
        dma(lambda e: e.dma_start(out=identf, in_=ident_d), writes=["identf"])
        dma(lambda e: e.dma_start(out=cfar[:], in_=cfar_d), writes=["cfar"])
        dma(lambda e: e.dma_start(out=retc, in_=retc_d), writes=["retc"])
        dma(lambda e: e.dma_start(out=tokidx[:], in_=tokidx_d), writes=["tokidx"])
        dma(lambda e: e.dma_start(out=dlam, in_=dlam_d), writes=["dlam"])
        dma(lambda e: e.dma_start(out=subw, in_=subw_d), writes=["subw"])
        dma(lambda e: e.dma_start(out=rdl[:], in_=rdl_d), writes=["rdl"])
        dma(lambda e: e.dma_start(out=psh[:], in_=pscale_d), writes=["psh"])
        dma(lambda e: e.dma_start(out=poolw[:].rearrange("c (l g) d -> c l g d", l=2),
                                  in_=poolw_d.rearrange("l g c d -> c l g d")),
            writes=["poolw"], queue="pool")
        dma(lambda e: e.dma_start(out=poolm[:], in_=poolm_d), writes=["poolm"], queue="pool")
        dma(lambda e: e.dma_start(out=bmast[:], in_=bm_d), writes=["bmast"], queue="pool")
        for h_ in range(4):
            P.op("act", lambda e, h_=h_: e.activation(out=bmast[:, h_, :], in_=bmast[:, h_, :], func=AF.Exp), reads=["bmast"], writes=["bmast"])

        P.op("pool", lambda e: e.memset(mhalf[:], -0.5), writes=["mhalf"])
        P.op("pool", lambda e: e.memset(zero1[:], 0.0), writes=["zero1"])
        P.op("dve", lambda e: e.tensor_copy(out=identb[:], in_=identf), reads=["identf"], writes=["identb"])

        lam_init = [0.8 - 0.6 * math.exp(-0.3 * l) for l in range(2)]
        for l in range(2):
            P.op("dve", lambda e, l=l: e.tensor_tensor(out=scr[:, 0:64], in0=dlam[:, l, 0:64], in1=dlam[:, l, 64:128], op=ALU.mult),
                 reads=["dlam"], writes=["junk"])
            P.op("dve", lambda e, l=l: e.tensor_tensor(out=scr[:, 64:128], in0=dlam[:, l, 128:192], in1=dlam[:, l, 192:256], op=ALU.mult),
                 reads=["dlam"], writes=["junk"])
            P.op("dve", lambda e: e.reduce_sum(out=sm[:, 0:2], in_=scr[:, 0:128].rearrange("p (a b) -> p a b", a=2), axis=AX.X),
                 reads=["junk"], writes=["sm"])
            P.op("act", lambda e: e.activation(out=sm[:, 2:4], in_=sm[:, 0:2], func=AF.Exp), reads=["sm"], writes=["sm"])
            P.op("dve", lambda e, l=l: e.tensor_scalar(out=sm[:, 4:5], in0=sm[:, 3:4], scalar1=-lam_init[l], scalar2=None, op0=ALU.add),
                 reads=["sm"], writes=["sm"])
            P.op("dve", lambda e, l=l: e.tensor_tensor(out=nlam[:, l:l + 1], in0=sm[:, 4:5], in1=sm[:, 2:3], op=ALU.subtract),
                 reads=["sm"], writes=["nlam"])
            for hh in range(2):
                P.op("dve", lambda e, l=l, hh=hh: e.tensor_scalar(out=W2[:, l, hh, :], in0=subw[:, l, :],
                                                                  scalar1=(1.0 - lam_init[l]) * 0.5, scalar2=None, op0=ALU.mult),
                     reads=["subw"], writes=["W2"])
            P.op("dve", lambda e, l=l: e.tensor_scalar(out=psh[:, l, :], in0=psh[:, l, :], scalar1=0.5, scalar2=None, op0=ALU.mult),
                 reads=["psh"], writes=["psh"])
        P.op("act", lambda e: e.activation(out=sm[:, 16:32], in_=rdl[:], func=AF.Exp, scale=-1.0), reads=["rdl"], writes=["sm"])
        P.op("dve", lambda e: e.tensor_scalar(out=sm[:, 32:48], in0=sm[:, 16:32], scalar1=1.0, scalar2=None, op0=ALU.add),
             reads=["sm"], writes=["sm"])
        P.op("act", lambda e: e.activation(out=sm[:, 16:32], in_=sm[:, 32:48], func=AF.Ln), reads=["sm"], writes=["sm"])
        P.op("dve", lambda e: e.tensor_scalar(out=lg[:], in0=sm[:, 16:32], scalar1=-1.0, scalar2=None, op0=ALU.mult),
             reads=["sm"], writes=["lg"])
        P.op("act", lambda e: e.activation(out=gsc[:], in_=lg[:], func=AF.Exp, scale=128.0), reads=["lg"], writes=["gsc"])
        for l in range(2):
            lf = l * 8
            lb = l * 8 + 4
            for (dst, src, ti) in ((0, lf, 0), (4, lb, 1), (8, lf, 2), (12, lb, 3)):
                P.op("dve", lambda e, l=l, dst=dst, src=src, ti=ti: e.tensor_scalar(
                    out=sm[:, 48 + dst:52 + dst], in0=lg[:, src:src + 4], scalar1=tokidx[:, ti:ti + 1], scalar2=None, op0=ALU.mult),
                    reads=["lg", "tokidx"], writes=["sm"])
            P.op("act", lambda e, l=l: e.activation(out=tqk[:, l, :], in_=sm[:, 48:64], func=AF.Exp), reads=["sm"], writes=["tqk"])
            for h in range(4):
                P.op("dve", lambda e, l=l, h=h: e.tensor_scalar(out=scr[:, 0:128], in0=retc[:, 0, :], scalar1=lg[:, l * 8 + h:l * 8 + h + 1],
                                                                scalar2=None, op0=ALU.mult), reads=["retc", "lg"], writes=["junk"])
                P.op("dve", lambda e, l=l, h=h: e.scalar_tensor_tensor(out=scr[:, 128:256], in0=retc[:, 1, :],
                                                                       scalar=lg[:, l * 8 + 4 + h:l * 8 + 5 + h], in1=scr[:, 0:128],
                                                                       op0=ALU.mult, op1=ALU.add), reads=["retc", "lg", "junk"], writes=["junk"])
                P.op("act", lambda e, l=l, h=h: e.activation(out=D2T[:, l, h, :], in_=scr[:, 128:256], func=AF.Exp),
                     reads=["junk"], writes=["D2T"])

        P.fence()
        wstate = {"n": 0}

        preloaded = {}
        sched = {"list": [], "i": 0}

        def phase_loads(ph, l, hp):
            if ph == "A":
                return [("in", l, OFF["aq"] + hp * 256), ("in", l, OFF["ak"] + hp * 256), ("in", l, OFF["av"] + hp * 256),
                        ("in", l, OFF["ag"] + hp * 256), ("out", l, hp * 256)]
            if ph == "B":
                return [("inqk", l, OFF["bq"] + hp * 128), ("in", l, OFF["bv"] + hp * 256), ("in", l, OFF["bg"] + hp * 256),
                        ("out", l, 512 + hp * 256)]
            return [("in", l, OFF["cu"] + hp * 256), ("in", l, OFF["cg"] + hp * 256), ("out", l, 1024 + hp * 256)]

        def hoist_next():
            i = sched["i"] + 1
            if i < len(sched["list"]):
                ph, l, hp = sched["list"][i]
                for k in phase_loads(ph, l, hp):
                    if k not in preloaded:
                        preloaded[k] = _load_w(*k)

        def load_w(kind, l, c0):
            k = (kind, l, c0)
            if k in preloaded:
                return preloaded.pop(k)
            return _load_w(kind, l, c0)

        def _load_w(kind, l, c0):
            n0 = len(P.ops)
            r = _load_w2(kind, l, c0)
            for o_ in P.ops[n0:]:
                o_.nofence = True
            return r

        def _load_w2(kind, l, c0):
            i = wstate["n"] % NSLOT
            wstate["n"] += 1
            sl = wslot[i]
            key = ("wslot", i)
            if kind == "in":
                v = sl[:].rearrange("p (k c) -> p k c", k=8)
                dma(lambda e: e.dma_start(out=v, in_=win_d[l, :, c0:c0 + 256].rearrange("(k p) c -> p k c", p=128)),
                    writes=[key], queue="pool")
            elif kind == "inqk":
                v = sl[:].rearrange("p (k c) -> p k c", k=8)
                dma(lambda e: e.dma_start(out=v[:, :, 0:128], in_=win_d[l, :, c0:c0 + 128].rearrange("(k p) c -> p k c", p=128)),
                    writes=[key], queue="pool")
                dma(lambda e: e.dma_start(out=v[:, :, 128:256], in_=win_d[l, :, c0 + 256:c0 + 384].rearrange("(k p) c -> p k c", p=128)),
                    reads=[key], writes=[key], queue="pool")
            else:
                v = sl[:].rearrange("p (k c) -> p k c", k=2)
                dma(lambda e: e.dma_start(out=v, in_=wout_d[l, c0:c0 + 256, :].rearrange("(k p) c -> p k c", p=128)),
                    writes=[key], queue="pool")
            return v, key

        cnt = {"tok": 0, "tail": 0, "ev": 0}

        def proj_tok(wv, wkey, t, bk, ncols=256):
            for k in range(8):
                P.op("pe", lambda e, k=k: e.matmul(bank[bk][:, 0:ncols], lhsT=hT[:, k, t * 128:(t + 1) * 128], rhs=wv[:, k, 0:ncols],
                                                   start=(k == 0), stop=(k == 7)),
                     reads=["hT", wkey], writes=PS(bk), inc=(k == 7))

        def gate_evac(bk, t, i):
            P.op("act", lambda e: e.activation(out=th[i][:], in_=bank[bk][:, 0:256], func=AF.Tanh, scale=0.5),
                 reads=PS(bk), writes=[("th", i)])
            P.op("dve", lambda e: e.scalar_tensor_tensor(out=gate[:, t, :], in0=th[i][:], scalar=1.0, in1=bank[bk][:, 0:256],
                                                         op0=ALU.add, op1=ALU.mult),
                 reads=PS(bk) + [("th", i)], writes=["gate"])

        def tail(mx, mxkey, t, wov, wokey, tb=None, tbk=7, ob=None, i=None):
            if i is None:
                i = cnt["tail"] % 2
                cnt["tail"] += 1
            if tb is None:
                tb = bank[7].bitcast(BF16)[:, 0:256]
            for k in range(2):
                P.op("pe", lambda e, k=k: e.transpose(out=tb[:, k * 128:(k + 1) * 128], in_=mx[:, k * 128:(k + 1) * 128], identity=identb[:]),
                     reads=[mxkey, "identb"], writes=PS(tbk), inc=(k == 1))
            P.op("act", lambda e: e.copy(out=mT[i][:].rearrange("p a b -> p (a b)"), in_=tb), reads=PS(tbk), writes=[("mT", i)])
            if ob is None:
                for half in range(2):
                    for k in range(2):
                        P.op("pe", lambda e, k=k, half=half: e.matmul(bank[7], lhsT=mT[i][:, k, :], rhs=wov[:, k, half * 512:(half + 1) * 512],
                                                                      start=(k == 0), stop=(k == 1)),
                             reads=[("mT", i), wokey], writes=PS(7), inc=(k == 1))
                    P.op("dve", lambda e, half=half: e.tensor_tensor(out=xres[:, t, half * 512:(half + 1) * 512],
                                                                     in0=xres[:, t, half * 512:(half + 1) * 512], in1=bank[7], op=ALU.add),
                         reads=PS(7) + [("x", t)], writes=[("x", t)])
            else:
                for half in range(2):
                    for k in range(2):
                        P.op("pe", lambda e, k=k, half=half: e.matmul(bank[ob + half], lhsT=mT[i][:, k, :], rhs=wov[:, k, half * 512:(half + 1) * 512],
                                                                      start=(k == 0), stop=(k == 1)),
                             reads=[("mT", i), wokey], writes=PS(ob + half), inc=(k == 1))
                P.op("dve", lambda e: e.tensor_tensor(out=xres[:, t, :], in0=xres[:, t, :],
                                                      in1=psall[:, ob:ob + 2, :].rearrange("p a b -> p (a b)"), op=ALU.add),
                     reads=PS(ob, ob + 1) + [("x", t)], writes=[("x", t)])

        NG = 4

        def ms_sq(t):
            P.op("act", lambda e, t=t: e.activation(out=junk[:], in_=xres[:, t, :], func=AF.Square, accum_out=ss[:, 0, t:t + 1]),
                 reads=[("x", t)], writes=["junk", ("ss", t)])

        def ms_stats(g):
            t0, t1 = g * NG, (g + 1) * NG
            P.op("dve", lambda e: e.tensor_scalar(out=ss[:, 1, t0:t1], in0=ss[:, 0, t0:t1], scalar1=1.0 / D, scalar2=EPS, op0=ALU.mult, op1=ALU.add),
                 reads=[("ss", t) for t in range(t0, t1)], writes=[("ss1", g)])
            P.op("pool", lambda e: e.tensor_tensor(out=ss[:, 0, t0:t1], in0=ss[:, 1, t0:t1], in1=mhalf[:, 0:NG], op=ALU.pow),
                 reads=[("ss1", g), "mhalf"], writes=[("ss", t) for t in range(t0, t1)])

        def rmsnorm_to_hT(l, load=True):
            if load:
                dma(lambda e: e.dma_start(out=nw, in_=nwb_d[l]), writes=["nw"])
            for t in range(NG):
                ms_sq(t)
            for t in range(NT):
                if t % NG == 0:
                    ms_stats(t // NG)
                if t + NG < NT:
                    ms_sq(t + NG)
                i = 0
                P.op("dve", lambda e, t=t, i=i: e.scalar_tensor_tensor(out=hb[i][:], in0=xres[:, t, :], scalar=ss[:, 0, t:t + 1], in1=nw,
                                                                       op0=ALU.mult, op1=ALU.mult),
                     reads=[("x", t), ("ss", t), "nw"], writes=[("hb", i)])
                bk = 5 + (t % 2)
                for k in range(8):
                    P.op("pe", lambda e, k=k, i=i, bk=bk: e.transpose(out=bank[bk][:].bitcast(BF16)[:, k * 128:(k + 1) * 128],
                                                                      in_=hb[i][:, k * 128:(k + 1) * 128], identity=identb[:]),
                         reads=[("hb", i), "identb"], writes=PS(bk), inc=(k == 7))
                if t % 2 == 0:
                    P.op("act", lambda e, t=t, bk=bk: e.copy(out=hT[:, :, t * 128:(t + 1) * 128],
                                                             in_=bank[bk][:].bitcast(BF16).rearrange("p (k c) -> p k c", k=8)),
                         reads=PS(bk), writes=[("hTw", t)])
                else:
                    P.op("dve", lambda e, t=t, bk=bk: e.tensor_copy(out=hT[:, :, t * 128:(t + 1) * 128],
                                                                    in_=bank[bk][:].bitcast(BF16).rearrange("p (k c) -> p k c", k=8)),
                         reads=PS(bk), writes=[("hTw", t)])

        def phase_A(l, hp):
            P.op("pool", lambda e: e.memset(V1[:, :, :, 128:129], 1.0), writes=["V1"])
            wq, wqk = load_w("in", l, OFF["aq"] + hp * 256)
            wk, wkk = load_w("in", l, OFF["ak"] + hp * 256)
            wv_, wvk = load_w("in", l, OFF["av"] + hp * 256)
            n = 0
            for (wv, wkey, dst, dkey, scl) in ((wq, wqk, qT, "qT", 0.125), (wk, wkk, kT, "kT", 1.0)):
                for hh in range(2):
                    for tc in range(4):
                        bk = n % 4
                        n += 1
                        for k in range(8):
                            P.op("pe", lambda e, k=k, bk=bk, wv=wv, hh=hh, tc=tc: e.matmul(
                                bank[bk][:, :], lhsT=wv[:, k, hh * 128:(hh + 1) * 128], rhs=hT[:, k, tc * 512:(tc + 1) * 512],
                                start=(k == 0), stop=(k == 7)), reads=["hT", wkey], writes=PS(bk), inc=(k == 7))
                        if n % 2 == 0:
                            P.op("act", lambda e, bk=bk, dst=dst, hh=hh, tc=tc, scl=scl: e.mul(out=dst[:, hh, tc * 512:(tc + 1) * 512],
                                                                                             in_=bank[bk][:, :], mul=scl),
                                 reads=PS(bk), writes=[dkey])
                        else:
                            P.op("dve", lambda e, bk=bk, dst=dst, hh=hh, tc=tc, scl=scl: e.tensor_scalar(
                                out=dst[:, hh, tc * 512:(tc + 1) * 512], in0=bank[bk][:, :], scalar1=scl, scalar2=None, op0=ALU.mult),
                                reads=PS(bk), writes=[dkey])
            wg, wgk = load_w("in", l, OFF["ag"] + hp * 256)
            wo, wok = load_w("out", l, hp * 256)
            for t in range(NT):
                bk = t % 4
                proj_tok(wv_, wvk, t, bk)
                eng = "act" if t % 2 == 0 else "dve"
                if eng == "act":
                    P.op("act", lambda e, t=t, bk=bk: e.copy(out=V1[:, t, :, 0:128], in_=bank[bk][:, 0:256].rearrange("p (a b) -> p a b", a=2)),
                         reads=PS(bk), writes=["V1"])
                else:
                    P.op("dve", lambda e, t=t, bk=bk: e.tensor_copy(out=V1[:, t, :, 0:128], in_=bank[bk][:, 0:256].rearrange("p (a b) -> p a b", a=2)),
                         reads=PS(bk), writes=["V1"])
            for t in range(NT):
                bk = t % 4
                proj_tok(wg, wgk, t, bk)
                gate_evac(bk, t, t % 2)

            hoist_next()

            def acc(r, c0=0, c1=129):
                return bank[4 + r // 3][:, (r % 3) * 129 + c0:(r % 3) * 129 + c1]

            pend = []

            def sched_chunk_tail(qc):
                items = []

                def pre():
                    P.op("pool", lambda e: e.tensor_tensor(out=oan, in0=oa, in1=oa, op=ALU.mult), reads=["oa"], writes=["oan"])
                    P.op("dve", lambda e: e.reduce_sum(out=sm[:, 16:24], in_=oan.rearrange("p a b c -> p (a b) c"), axis=AX.X),
                         reads=["oan"], writes=["sm"])
                    P.op("dve", lambda e: e.tensor_scalar(out=sm[:, 24:32], in0=sm[:, 16:24], scalar1=1.0 / 128, scalar2=EPS, op0=ALU.mult, op1=ALU.add),
                         reads=["sm"], writes=["sm"])
                    P.op("pool", lambda e: e.tensor_tensor(out=sm[:, 16:24], in0=sm[:, 24:32], in1=mhalf[:, 0:8], op=ALU.pow),
                         reads=["sm", "mhalf"], writes=["sm"])
                    P.op("dve", lambda e: e.tensor_tensor(out=oan.rearrange("p a b c -> p (a b) c"), in0=oa.rearrange("p a b c -> p (a b) c"),
                                                          in1=sm[:, 16:24].unsqueeze(2).to_broadcast([128, 8, 128]), op=ALU.mult),
                         reads=["oa", "sm"], writes=["oan"])
                items.append((0, pre))
                tb = bank[7].bitcast(BF16)[:, 0:256]
                for qs in range(4):
                    t = qc * 4 + qs
                    i = qs % 2
                    g0 = (6, 9, 19, 22)[qs]

                    def T1(qs=qs, t=t, i=i):
                        P.op("pool", lambda e: e.tensor_tensor(out=oan[:, qs], in0=oan[:, qs], in1=W2[:, l], op=ALU.mult),
                             reads=["oan", "W2"], writes=["oan"])
                        P.op("dve", lambda e: e.tensor_tensor(out=mixed[i][:], in0=oan[:, qs].rearrange("p a b -> p (a b)"),
                                                              in1=gate[:, t, :], op=ALU.mult),
                             reads=["oan", "gate"], writes=[("mixed", i)])

                    def T2(i=i):
                        for k in range(2):
                            P.op("pe", lambda e, k=k: e.transpose(out=tb[:, k * 128:(k + 1) * 128], in_=mixed[i][:, k * 128:(k + 1) * 128],
                                                                  identity=identb[:]),
                                 reads=[("mixed", i), "identb"], writes=PS(7), inc=(k == 1))
                        P.op("dve", lambda e: e.tensor_copy(out=mT[i][:].rearrange("p a b -> p (a b)"), in_=tb), reads=PS(7), writes=[("mT", i)])

                    def T4(half, t=t, i=i):
                        for k in range(2):
                            P.op("pe", lambda e, k=k: e.matmul(bank[7], lhsT=mT[i][:, k, :], rhs=wo[:, k, half * 512:(half + 1) * 512],
                                                               start=(k == 0), stop=(k == 1)),
                                 reads=[("mT", i), wok], writes=PS(7), inc=(k == 1))
                        P.op("dve", lambda e: e.tensor_tensor(out=xres[:, t, half * 512:(half + 1) * 512],
                                                              in0=xres[:, t, half * 512:(half + 1) * 512], in1=bank[7], op=ALU.add),
                             reads=PS(7) + [("x", t)], writes=[("x", t)])
                    items += [(g0, T1), (g0 + 2, T2), (g0 + 4, lambda T4=T4: T4(0)), (g0 + 5, lambda T4=T4: T4(1))]
                return items

            for qc in range(4):
                for hh in range(2):
                    h = 2 * hp + hh
                    seq = []
                    for kt in range(NT):
                        d = kt - 4 * qc
                        near = -1 <= d <= 4
                        seq.append((kt, near, d))

                    def emit_qk(j, hh=hh, qc=qc, h=h):
                        kt, near, d = seq[j]
                        b0 = (j % 2) * 2
                        for m in range(2):
                            P.op("pe", lambda e, m=m, kt=kt, b0=b0: e.matmul(
                                bank[b0 + m][:, :], lhsT=kT[m * 64:(m + 1) * 64, hh, kt * 128:(kt + 1) * 128],
                                rhs=qT[m * 64:(m + 1) * 64, hh, qc * 512:(qc + 1) * 512], start=True, stop=True),
                                reads=["kT", "qT"], writes=PS(b0 + m), inc=(m == 1))

                    def emit_exp(j, hh=hh, qc=qc, h=h):
                        kt, near, d = seq[j]
                        b0 = (j % 2) * 2
                        pt = PT[j % 3]
                        if near:
                            bias_ap = zero1[:, 0:1]
                        elif d > 4:
                            bias_ap = cfar[:, h:h + 1]
                        else:
                            bias_ap = cfar[:, 4 + h:5 + h]
                        P.op("act", lambda e: e.activation(out=pt.rearrange("p a b -> p (a b)"),
                                                           in_=psall[:, b0:b0 + 2, :].rearrange("p a b -> p (a b)"),
                                                           func=AF.Exp, bias=bias_ap, scale=1.0),
                             reads=PS(b0, b0 + 1) + ["cfar", "zero1"], writes=[("PT", j % 3, 0), ("PT", j % 3, 1)])
                        if near:
                            base = 512 - 128 * d
                            for m in range(2):
                                P.op("dve", lambda e, base=base, m=m: e.tensor_tensor(
                                    out=pt[:, m, :], in0=pt[:, m, :], in1=bmast[:, h, base:base + 512], op=ALU.mult),
                                    reads=["bmast"], writes=[("PT", j % 3, m)])

                    def emit_pv(j, hh=hh, qc=qc, h=h):
                        kt, near, d = seq[j]
                        pt = PT[j % 3]
                        for m in range(2):
                            for qs in range(4):
                                r = m * 4 + qs
                                first = (kt == 0 and r % 3 == 0)
                                P.op("pe", lambda e, m=m, qs=qs, r=r, first=first: e.matmul(
                                    acc(r), lhsT=pt[:, m, qs * 128:(qs + 1) * 128], rhs=V1[:, kt, hh, 0:129],
                                    start=first, stop=(kt == NT - 1), skip_group_check=True),
                                    reads=[("PT", j % 3, m), "V1"], writes=PS(4 + r // 3), inc=(m == 1 and qs == 3))

                    emit_qk(0)
                    for j in range(NT + 1):
                        if j + 1 < NT:
                            emit_qk(j + 1)
                        if j < NT:
                            emit_exp(j)
                        if j >= 1:
                            emit_pv(j - 1)
                        g_ = hh * NT + j
                        if j < NT:
                            for it_ in [x for x in pend if x[0] == g_]:
                                pend.remove(it_)
                                it_[1]()
                    P.op("act", lambda e: e.copy(out=accS[:, 0:3, :].rearrange("p a b -> p (a b)"), in_=bank[4][:, 0:387]), reads=PS(4), writes=["accS0"])
                    P.op("dve", lambda e: e.tensor_copy(out=accS[:, 3:6, :].rearrange("p a b -> p (a b)"), in_=bank[5][:, 0:387]), reads=PS(5), writes=["accS1"])
                    P.op("act", lambda e: e.copy(out=accS[:, 6:8, :].rearrange("p a b -> p (a b)"), in_=bank[6][:, 0:258]), reads=PS(6), writes=["accS2"])
                    P.op("dve", lambda e: e.reciprocal(out=sm[:, 0:8], in_=accS[:, :, 128]), reads=["accS0", "accS1", "accS2"], writes=["sm"])
                    P.op("dve", lambda e: e.tensor_scalar(out=sm[:, 4:8], in0=sm[:, 4:8], scalar1=nlam[:, l:l + 1], scalar2=None, op0=ALU.mult),
                         reads=["sm", "nlam"], writes=["sm"])
                    for qs in range(4):
                        r1 = 4 + qs
                        P.op("dve", lambda e, qs=qs, r1=r1: e.tensor_scalar(out=tmpA, in0=accS[:, r1, 0:128], scalar1=sm[:, r1:r1 + 1],
                                                                            scalar2=None, op0=ALU.mult),
                             reads=["accS0", "accS1", "accS2", "sm"], writes=["tmpA"])
                        P.op("dve", lambda e, qs=qs, hh=hh: e.scalar_tensor_tensor(out=oa[:, qs, hh, :], in0=accS[:, qs, 0:128], scalar=sm[:, qs:qs + 1],
                                                                                  in1=tmpA, op0=ALU.mult, op1=ALU.add),
                             reads=["accS0", "accS1", "accS2", "sm", "tmpA"], writes=["oa"])
                for it_ in sorted(pend, key=lambda x: x[0]):
                    it_[1]()
                pend = sched_chunk_tail(qc)
            for it_ in sorted(pend, key=lambda x: x[0]):
                it_[1]()
            pend = []

        def phase_B(l, hp):
            wqk_, wqkk = load_w("inqk", l, OFF["bq"] + hp * 128)
            wv_, wvk = load_w("in", l, OFF["bv"] + hp * 256)
            wg, wgk = load_w("in", l, OFF["bg"] + hp * 256)
            wo, wok = load_w("out", l, 512 + hp * 256)
            dma(lambda e: e.dma_start(out=cs, in_=cs_d), writes=["cs"])
            for s2 in range(2):
                P.op("pool", lambda e, s2=s2: e.memset(TTq2[s2], 0.0), writes=[("TTq2", s2)])
            for t in range(NT):
                bk = t % 4
                i2 = t % 2
                proj_tok(wqk_, wqkk, t, bk)
                P.op("act", lambda e, bk=bk, i2=i2: e.copy(out=qk32[i2][:, 0:128], in_=bank[bk][:, 0:128]), reads=PS(bk), writes=[("qk32", i2)])
                P.op("act", lambda e, bk=bk, i2=i2: e.mul(out=qk32[i2][:, 128:256], in_=bank[bk][:, 128:256], mul=0.125),
                     reads=PS(bk), writes=[("qk32", i2)])
                src4 = qk32[i2].rearrange("p (a b c) -> p a b c", a=4, b=2)
                cos_b = cs[:, 0, t, :].unsqueeze(1).to_broadcast([128, 4, 32])
                sin_b = cs[:, 1, t, :].unsqueeze(1).to_broadcast([128, 4, 32])
                t1 = src4[:, :, 0, :]
                t2 = src4[:, :, 1, :]
                rt = [rtmp[i2][:, i].rearrange("p (a c) -> p a c", a=4) for i in range(4)]
                dst4 = qkr[:, t, :].rearrange("p (a b c) -> p a b c", a=4, b=2)
                P.op("pool", lambda e, t1=t1, cos_b=cos_b, rt=rt: e.tensor_tensor(out=rt[0], in0=t1, in1=cos_b, op=ALU.mult),
                     reads=[("qk32", i2), "cs"], writes=[("rtmp", i2, 0)])
                P.op("pool", lambda e, t2=t2, sin_b=sin_b, rt=rt: e.tensor_tensor(out=rt[1], in0=t2, in1=sin_b, op=ALU.mult),
                     reads=[("qk32", i2), "cs"], writes=[("rtmp", i2, 1)])
                P.op("dve", lambda e, t1=t1, sin_b=sin_b, rt=rt: e.tensor_tensor(out=rt[2], in0=t1, in1=sin_b, op=ALU.mult),
                     reads=[("qk32", i2), "cs"], writes=[("rtmp", i2, 2)])
                P.op("dve", lambda e, t2=t2, cos_b=cos_b, rt=rt: e.tensor_tensor(out=rt[3], in0=t2, in1=cos_b, op=ALU.mult),
                     reads=[("qk32", i2), "cs"], writes=[("rtmp", i2, 3)])
                P.op("pool", lambda e, rt=rt, dst4=dst4: e.tensor_tensor(out=dst4[:, :, 0, :], in0=rt[0], in1=rt[1], op=ALU.subtract),
                     reads=[("rtmp", i2, 0), ("rtmp", i2, 1)], writes=[("qkr", t, 0)])
                P.op("dve", lambda e, rt=rt, dst4=dst4: e.tensor_tensor(out=dst4[:, :, 1, :], in0=rt[2], in1=rt[3], op=ALU.add),
                     reads=[("rtmp", i2, 2), ("rtmp", i2, 3)], writes=[("qkr", t, 1)])
            for t in range(NT):
                bk = t % 4
                proj_tok(wv_, wvk, t, bk)
                if t % 2 == 0:
                    P.op("act", lambda e, t=t, bk=bk: e.copy(out=vB[:, t, :], in_=bank[bk][:, 0:256]), reads=PS(bk), writes=[("vB", t)])
                else:
                    P.op("dve", lambda e, t=t, bk=bk: e.tensor_copy(out=vB[:, t, :], in_=bank[bk][:, 0:256]), reads=PS(bk), writes=[("vB", t)])
            for t in range(NT):
                bk = t % 4
                proj_tok(wg, wgk, t, bk)
                gate_evac(bk, t, t % 2)
            hoist_next()
            QKR = lambda t: [("qkr", t, 0), ("qkr", t, 1)]

            def kvp(t):
                return bank[t // 2][:, (t % 2) * 256:(t % 2) * 256 + 256]

            for t in range(NT):
                i2 = t % 2
                kq = qkr[:, t, 128:256].rearrange("p (a c) -> p a c", a=2)
                for dr in range(2):
                    c0 = 8 + dr * 4 + 2 * hp
                    P.op("pool", lambda e, dr=dr, c0=c0, kq=kq, i2=i2: e.tensor_tensor(
                        out=kdec[i2][:, :, dr, :], in0=kq, in1=tqk[:, l, c0:c0 + 2].unsqueeze(2).to_broadcast([128, 2, 64]), op=ALU.mult),
                        reads=QKR(t) + ["tqk"], writes=[("kdec", i2)])
                for hh in range(2):
                    P.op("pe", lambda e, hh=hh, t=t, i2=i2: e.matmul(kvp(t)[:, hh * 128:(hh + 1) * 128], lhsT=kdec[i2][:, hh].rearrange("p a b -> p (a b)"),
                                                                    rhs=vB[:, t, hh * 128:(hh + 1) * 128], start=True, stop=True),
                         reads=[("kdec", i2), ("vB", t)], writes=PS(t // 2), inc=(hh == 1))

            P.op("pool", lambda e: e.memset(Rcur, 0.0), writes=[("Rcur", 0), ("Rcur", 64)])
            for n_ in range(NT):
                for (lo, hi, t, dr) in ((0, 64, n_, 0), (64, 128, NT - 1 - n_, 1)):
                    P.op("act", lambda e, t=t, lo=lo, hi=hi: e.copy(out=Rst[lo:hi, t], in_=Rcur[lo:hi]), reads=[("Rcur", lo)], writes=[("Rst", t, lo)])
                    if n_ == NT - 1:
                        continue
                    for hh in range(2):
                        gc = l * 8 + dr * 4 + 2 * hp + hh
                        P.op("dve", lambda e, lo=lo, hi=hi, t=t, hh=hh, gc=gc: e.scalar_tensor_tensor(
                            out=Rcur[lo:hi, hh, :], in0=Rcur[lo:hi, hh, :], scalar=gsc[lo:hi, gc:gc + 1],
                            in1=kvp(t)[lo:hi, hh * 128:(hh + 1) * 128], op0=ALU.mult, op1=ALU.add),
                            reads=PS(t // 2) + [("Rcur", lo), "gsc"], writes=[("Rcur", lo)])

            def S0a(t):
                s2 = t % 2
                s3 = t % 3
                bTS = 2 * s2
                qq = qkr[:, t, 0:128].rearrange("p (a c) -> p a c", a=2)
                for dr in range(2):
                    c0 = dr * 4 + 2 * hp
                    P.op("pool", lambda e, dr=dr, c0=c0, qq=qq, s2=s2: e.tensor_tensor(
                        out=qdec[s2][:, :, dr, :], in0=qq, in1=tqk[:, l, c0:c0 + 2].unsqueeze(2).to_broadcast([128, 2, 64]), op=ALU.mult),
                        reads=QKR(t) + ["tqk"], writes=[("qdec", s2)])
                bT = bank[bTS].bitcast(BF16)
                srcs = [qdec[s2][:, 0].rearrange("p a b -> p (a b)"), qdec[s2][:, 1].rearrange("p a b -> p (a b)"),
                        qkr[:, t, 128:256], qkr[:, t, 0:128]]
                for i4 in range(4):
                    P.op("pe", lambda e, i4=i4, srcs=srcs, bT=bT: e.transpose(out=bT[:, i4 * 128:(i4 + 1) * 128], in_=srcs[i4], identity=identb[:]),
                         reads=[("qdec", s2), "identb"] + QKR(t), writes=PS(bTS), inc=(i4 == 3))
                P.op("act", lambda e, bT=bT, s3=s3: e.copy(out=TT[s3].rearrange("p a b -> p (a b)"), in_=bT[:, 0:384]),
                     reads=PS(bTS), writes=[("TT", s3)])
                for hh in range(2):
                    P.op("act", lambda e, bT=bT, s2=s2, hh=hh: e.copy(out=TTq2[s2][hh * 64:(hh + 1) * 64, hh, :],
                                                                       in_=bT[hh * 64:(hh + 1) * 64, 384:512]),
                         reads=PS(bTS), writes=[("TTq2", s2)])

            def S0b(t):
                s2 = t % 2
                s3 = t % 3
                bTS = 2 * s2
                P.op("pe", lambda e, s2=s2, s3=s3, bTS=bTS: e.matmul(bank[bTS][:, 256:512], lhsT=TT[s3][:, 2, :],
                                                                    rhs=TTq2[s2].rearrange("p a b -> p (a b)"), start=True, stop=True),
                     reads=[("TT", s3), ("TTq2", s2)], writes=PS(bTS), inc=True)
                P.op("dve", lambda e, s2=s2, bTS=bTS: e.tensor_tensor(out=innerT[s2], in0=bank[bTS][:, 256:512].rearrange("p (a b) -> p a b", a=2),
                                                                     in1=D2T[:, l, 2 * hp:2 * hp + 2, :], op=ALU.mult),
                     reads=PS(bTS) + ["D2T"], writes=[("innerT", s2)])

            def S0c(t):
                s2 = t % 2
                s3 = t % 3
                bO = 2 * s2 + 1
                for hh in range(2):
                    P.op("pe", lambda e, hh=hh, t=t, s2=s2, bO=bO: e.matmul(bank[bO][:, hh * 128:(hh + 1) * 128], lhsT=innerT[s2][:, hh, :],
                                                                           rhs=vB[:, t, hh * 128:(hh + 1) * 128], start=True, stop=False),
                         reads=[("innerT", s2), ("vB", t)], writes=PS(bO), inc=False)
                    P.op("pe", lambda e, hh=hh, t=t, s3=s3, bO=bO: e.matmul(bank[bO][:, hh * 128:(hh + 1) * 128], lhsT=TT[s3][:, hh, :],
                                                                           rhs=Rst[:, t, hh, :], start=False, stop=True),
                         reads=[("TT", s3), ("Rst", t, 0), ("Rst", t, 64)], writes=PS(bO), inc=(hh == 1))

            def S1a(t):
                s2 = t % 2
                s3 = t % 3
                bO = 2 * s2 + 1
                sc = 32 + 8 * s3
                for hh in range(2):
                    P.op("act", lambda e, hh=hh, bO=bO, sc=sc: e.activation(out=junk[:, hh * 128:(hh + 1) * 128], in_=bank[bO][:, hh * 128:(hh + 1) * 128],
                                                                          func=AF.Square, accum_out=sm[:, sc + hh:sc + hh + 1]),
                         reads=PS(bO), writes=["junk", ("smB", s3)])
                P.op("act", lambda e, bO=bO, s3=s3: e.copy(out=Osb[s3].rearrange("p a b -> p (a b)"), in_=bank[bO][:, 0:256]),
                     reads=PS(bO), writes=[("Osb", s3)])
                P.op("dve", lambda e, sc=sc: e.tensor_scalar(out=sm[:, sc + 2:sc + 4], in0=sm[:, sc:sc + 2], scalar1=4.0 / 128, scalar2=4.0 * EPS,
                                                             op0=ALU.mult, op1=ALU.add), reads=[("smB", s3)], writes=[("smB", s3)])

            def S1b(t):
                s3 = t % 3
                sc = 32 + 8 * s3
                P.op("pool", lambda e, sc=sc: e.tensor_tensor(out=sm[:, sc + 4:sc + 6], in0=sm[:, sc + 2:sc + 4], in1=mhalf[:, 0:2], op=ALU.pow),
                     reads=[("smB", s3), "mhalf"], writes=[("smB", s3)])

            def S1c(t):
                s2 = t % 2
                s3 = t % 3
                sc = 32 + 8 * s3
                P.op("dve", lambda e, s2=s2, s3=s3, sc=sc: e.tensor_tensor(out=btmp[s2], in0=Osb[s3],
                                                                          in1=sm[:, sc + 4:sc + 6].unsqueeze(2).to_broadcast([128, 2, 128]), op=ALU.mult),
                     reads=[("Osb", s3), ("smB", s3)], writes=[("btmp", s2)])

            def S1d(t):
                s2 = t % 2
                i = t % 2
                P.op("pool", lambda e, i=i, t=t, s2=s2: e.tensor_tensor(out=mixed[i][:], in0=btmp[s2].rearrange("p a b -> p (a b)"), in1=gate[:, t, :], op=ALU.mult),
                     reads=[("btmp", s2), "gate"], writes=[("mixed", i)])

            def S2(t):
                s2 = t % 2
                bO = 2 * s2 + 1
                i = t % 2
                tail(mixed[i], ("mixed", i), t, wo, wok, tb=bank[bO].bitcast(BF16)[:, 512:768], tbk=bO, ob=4 + 2 * s2, i=i)

            stages = [S0a, S0b, S0c, S1a, S1b, S1c, S1d, S2]
            for it in range(NT + len(stages) - 1):
                for k_, fn_ in enumerate(stages):
                    if 0 <= it - k_ < NT:
                        fn_(it - k_)

        def phase_C(l, gp):
            wu, wuk = load_w("in", l, OFF["cu"] + gp * 256)
            wg, wgk = load_w("in", l, OFF["cg"] + gp * 256)
            wo, wok = load_w("out", l, 1024 + gp * 256)
            for t in range(NT):
                bk = t % 4
                proj_tok(wu, wuk, t, bk)
                if t % 2 == 0:
                    P.op("act", lambda e, t=t, bk=bk: e.copy(out=uC[:, t, :], in_=bank[bk][:, 0:256]), reads=PS(bk), writes=[("uC", t)])
                else:
                    P.op("dve", lambda e, t=t, bk=bk: e.tensor_copy(out=uC[:, t, :], in_=bank[bk][:, 0:256]), reads=PS(bk), writes=[("uC", t)])
            for t in range(NT):
                bk = t % 4
                proj_tok(wg, wgk, t, bk)
                gate_evac(bk, t, t % 2)
            hoist_next()

            def S0(t):
                s2 = t % 2
                bP = 2 * s2
                bY = 2 * s2 + 1
                for gg in range(2):
                    g = 2 * gp + gg
                    parts = [(t, g * 5 + (3 if t == 0 else 4 if t == NT - 1 else 0))]
                    if t > 0:
                        parts.append((t - 1, g * 5 + 1))
                    if t < NT - 1:
                        parts.append((t + 1, g * 5 + 2))
                    for pi_, (tj, mi) in enumerate(parts):
                        P.op("pe", lambda e, gg=gg, tj=tj, mi=mi, pi_=pi_, np_=len(parts), bP=bP: e.matmul(
                            bank[bP][:, gg * 128:(gg + 1) * 128], lhsT=uC[:, tj, gg * 128:(gg + 1) * 128], rhs=poolm[:, mi, :],
                            start=(pi_ == 0), stop=(pi_ == np_ - 1)),
                            reads=[("uC", tj), "poolm"], writes=PS(bP), inc=(gg == 1 and pi_ == len(parts) - 1))
                P.op("act", lambda e, bP=bP, s2=s2: e.copy(out=pooledT[s2].rearrange("p a b -> p (a b)"), in_=bank[bP][:, 0:256]),
                     reads=PS(bP), writes=[("pooledT", s2)])
                for gg in range(2):
                    g = 2 * gp + gg
                    P.op("pe", lambda e, gg=gg, g=g, s2=s2, bY=bY: e.matmul(bank[bY][:, gg * 128:(gg + 1) * 128], lhsT=pooledT[s2][:, gg, :],
                                                                           rhs=poolw[:, l * 4 + g, :], start=True, stop=True),
                         reads=[("pooledT", s2), "poolw"], writes=PS(bY), inc=(gg == 1))

            def S1(t):
                s2 = t % 2
                bY = 2 * s2 + 1
                i = t % 2
                P.op("dve", lambda e, s2=s2, bY=bY: e.tensor_tensor(out=ytmp[s2], in0=bank[bY][:, 0:256], in1=psh[:, l, gp * 256:(gp + 1) * 256], op=ALU.mult),
                     reads=PS(bY) + ["psh"], writes=[("ytmp", s2)])
                P.op("pool", lambda e, t=t, i=i, s2=s2: e.tensor_tensor(out=mixed[i][:], in0=ytmp[s2], in1=gate[:, t, :], op=ALU.mult),
                     reads=[("ytmp", s2), "gate"], writes=[("mixed", i)])

            def S2(t):
                s2 = t % 2
                bY = 2 * s2 + 1
                i = t % 2
                tail(mixed[i], ("mixed", i), t, wo, wok, tb=bank[bY].bitcast(BF16)[:, 512:768], tbk=bY, ob=4 + 2 * s2, i=i)

            for it in range(NT + 2):
                if it < NT:
                    S0(it)
                if 0 <= it - 1 < NT:
                    S1(it - 1)
                if 0 <= it - 2 < NT:
                    S2(it - 2)

        for s_ in range(nseq):
            for l in range(nlayers):
                for ph in phases:
                    for hp in range(2):
                        sched["list"].append((ph, l, hp))
        sched["i"] = -1
        hoist_next()
        fns = {"A": phase_A, "B": phase_B, "C": phase_C}
        pi = 0
        for s in range(nseq):
            dma(lambda e: e.dma_start(out=nw, in_=nwb_d[0]), writes=["nw"])
            for t in range(NT):
                dma(lambda e, s=s, t=t: e.dma_start(out=xres[:, t, :], in_=x_d[s, t]), writes=[("x", t)])
            for l in range(nlayers):
                if l > 0:
                    P.fence()
                rmsnorm_to_hT(l, load=(l > 0))
                for ph in phases:
                    for hp in range(2):
                        if hp == 0:
                            P.fence()
                        sched["i"] = pi
                        fns[ph](l, hp)
                        pi += 1
            P.fence()
            dma(lambda e: e.dma_start(out=nwF, in_=nwb_d[2]), writes=["nwF"])
            for t in range(NG):
                ms_sq(t)
            for t in range(NT):
                if t % NG == 0:
                    ms_stats(t // NG)
                if t + NG < NT:
                    ms_sq(t + NG)
                P.op("dve", lambda e, t=t: e.scalar_tensor_tensor(out=xres[:, t, :], in0=xres[:, t, :], scalar=ss[:, 0, t:t + 1], in1=nwF,
                                                                  op0=ALU.mult, op1=ALU.mult),
                     reads=[("x", t), ("ss", t), "nwF"], writes=[("x", t)])
                dma(lambda e, s=s, t=t: e.dma_start(out=y_d[s, t], in_=xres[:, t, :]), reads=[("x", t)], writes=[("y", s, t)])
        P.emit(nc, st)
    return nc


_NC_CACHE = {}


def kernel(**inputs):
    x = np.asarray(inputs["x"], np.float32)
    B = x.shape[0]
    ncores = 8
    nseq = B // ncores
    consts = _host_consts(inputs)
    key = (nseq,)
    if key not in _NC_CACHE:
        _NC_CACHE[key] = build_nc(nseq=nseq)
    nc = _NC_CACHE[key]
    w_in = np.ascontiguousarray(np.asarray(inputs["w_in"], np.float32))
    w_out = np.ascontiguousarray(np.asarray(inputs["w_out"], np.float32))
    in_maps = []
    for c in range(ncores):
        m = dict(consts)
        m["x"] = np.ascontiguousarray(x[c * nseq:(c + 1) * nseq].reshape(nseq, NT, 128, D))
        m["w_in"] = w_in
        m["w_out"] = w_out
        in_maps.append(m)
    res = run_bass_kernel_spmd(nc, in_maps, core_ids=list(range(ncores)))
    out = np.concatenate([np.asarray(r["y"]).reshape(nseq, S, D) for r in res.results], axis=0)
    return out.astype(np.float32)
```
